# Optimizing a Trainium2 kernel written in Bass

```python
import math
import jax, jax.numpy as jnp
from jax import lax
import numpy as np

D_MODEL = 1024
BATCH = 4
SEQ = 4096
DEPTH = 2
DEC_BATCH = 32
DEC_SEQ = 8
PAST_LEN = 8192
PAGE_SIZE = 128

N_EVEN = (DEPTH + 1) // 2
N_ODD = DEPTH // 2
EPS = 1e-6
F32 = jnp.float32

POOL_WINDOWS = (2, 4, 8, 16)
N_POOL_GROUPS = len(POOL_WINDOWS)
POOL_GROUP_DIM = D_MODEL // 8
D_POOL = N_POOL_GROUPS * POOL_GROUP_DIM
POOL_HIST = max(POOL_WINDOWS) - 1

DN_HEADS = 4
DN_HEAD_DIM = 128
D_DN = DN_HEADS * DN_HEAD_DIM
DN_CONV = 4
DN_CHUNK = 64

OFF_Q = D_POOL
OFF_GATE = D_POOL + 3 * D_DN
OFF_BETA = D_POOL + 4 * D_DN
OFF_ALPHA = OFF_BETA + DN_HEADS
D_IN_EVEN = OFF_ALPHA + DN_HEADS
D_MIX_EVEN = D_POOL + D_DN

WIN_CONFIGS = ((128, 1), (512, 4), (2048, 16))
N_WIN = len(WIN_CONFIGS)
SWA_HEADS = 8
SWA_HEAD_DIM = 64
D_SWA = SWA_HEADS * SWA_HEAD_DIM
ROT_DIM = SWA_HEAD_DIM // 4
ROPE_THETA = 500000.0
SWA_BLOCK = 128

N_MEM = 256
MEM_HEADS = 4
MEM_HEAD_DIM = D_MODEL // MEM_HEADS

D_FF = 2816
FFN_CONV = 3

kernel_name = 'hybrid_pool_delta_dilated_decoder_step'


def rmsnorm(x, g):
    xf = x.astype(F32)
    y = xf * lax.rsqrt(jnp.mean(xf * xf, axis=-1, keepdims=True) + EPS)
    return (y * g.astype(F32)).astype(x.dtype)


def l2norm(x):
    return x * lax.rsqrt(jnp.sum(x * x, axis=-1, keepdims=True) + EPS)


def causal_dwconv(x_ext, w):
    return lax.conv_general_dilated(x_ext, w[:, None, :].astype(x_ext.dtype), (1,), 'VALID',
                                    dimension_numbers=('NWC', 'WIO', 'NWC'),
                                    feature_group_count=x_ext.shape[-1])


def partial_rope(x, pos):
    half = ROT_DIM // 2
    inv_freq = ROPE_THETA ** (-jnp.arange(half, dtype=F32) / half)
    ang = pos.astype(F32)[:, None] * inv_freq[None, :]
    shape = (1, pos.shape[0]) + (1,) * (x.ndim - 3) + (half,)
    cos, sin = jnp.cos(ang).reshape(shape), jnp.sin(ang).reshape(shape)
    xr = x[..., :ROT_DIM].astype(F32)
    x1, x2 = xr[..., :half], xr[..., half:]
    rot = jnp.concatenate([x1 * cos - x2 * sin, x2 * cos + x1 * sin], axis=-1).astype(x.dtype)
    return jnp.concatenate([rot, x[..., ROT_DIM:]], axis=-1)


def pool_mix(a_ext, n_hist, p0, w_pool, pool_scale):
    B, L, _ = a_ext.shape
    T = L - n_hist
    af = a_ext.astype(F32)
    cs = jnp.pad(jnp.cumsum(af, axis=1), ((0, 0), (1, 0), (0, 0)))
    idx = n_hist + jnp.arange(T)
    pos = p0 + jnp.arange(T)
    cur = af[:, n_hist:]
    groups = []
    for gi, w in enumerate(POOL_WINDOWS):
        sl = slice(gi * POOL_GROUP_DIM, (gi + 1) * POOL_GROUP_DIM)
        lo = jnp.maximum(idx + 1 - w, 0)
        cnt = jnp.minimum(pos + 1, w).astype(F32)[None, :, None]
        groups.append((cs[:, idx + 1, sl] - cs[:, lo, sl]) / cnt - cur[:, :, sl])
    z = jnp.stack(groups, axis=2)
    y = jnp.einsum('btgc,gcd->btgd', z, w_pool.astype(F32))
    y = y * pool_scale.astype(F32).reshape(N_POOL_GROUPS, POOL_GROUP_DIM)
    return y.reshape(B, T, D_POOL).astype(a_ext.dtype)


def gated_delta_rule(q, k, v, g, beta, s0):
    B, T, H, K = q.shape
    V = v.shape[-1]
    C = math.gcd(T, DN_CHUNK)
    n = T // C

    def blk(t):
        t = t.reshape((B, n, C) + t.shape[2:])
        return jnp.moveaxis(jnp.moveaxis(t, 1, 0), 3, 2)

    qc, kc, vc, gc, bc = blk(q), blk(k), blk(v), blk(g), blk(beta)
    G = jnp.cumsum(gc, axis=-1)
    i = jnp.arange(C)
    incl = i[:, None] >= i[None, :]
    strict = i[:, None] > i[None, :]
    decay = jnp.exp(jnp.where(incl, G[..., :, None] - G[..., None, :], -jnp.inf))
    a = jnp.where(strict, jnp.einsum('nbhik,nbhjk->nbhij', kc, kc) * decay * bc[..., :, None], 0.0)
    rhs = jnp.concatenate([vc * bc[..., None], kc * (bc * jnp.exp(G))[..., None]], axis=-1)
    sol = lax.linalg.triangular_solve(a, rhs, left_side=True, lower=True, unit_diagonal=True)
    u, w = sol[..., :V], sol[..., V:]
    qk = jnp.einsum('nbhik,nbhjk->nbhij', qc, kc) * decay
    qg = qc * jnp.exp(G)[..., None]
    kg = kc * jnp.exp(G[..., -1:] - G)[..., None]
    g_last = jnp.exp(G[..., -1])

    def step(s, xs):
        u_c, w_c, qk_c, qg_c, kg_c, gl_c = xs
        v_new = u_c - jnp.einsum('bhck,bhkv->bhcv', w_c, s)
        o = jnp.einsum('bhck,bhkv->bhcv', qg_c, s) + jnp.einsum('bhij,bhjv->bhiv', qk_c, v_new)
        s = s * gl_c[..., None, None] + jnp.einsum('bhck,bhcv->bhkv', kg_c, v_new)
        return s, o

    s_T, o = lax.scan(step, s0, (u, w, qk, qg, kg, g_last))
    o = o.transpose(1, 0, 3, 2, 4).reshape(B, T, H, V)
    return o, s_T


def even_mixer(h, pool_hist, conv_hist, s0, p0, w_in, w_pool, pool_scale, conv_w, a_log, dt_bias,
               norm_w, w_out):
    B, T, _ = h.shape
    proj = h @ w_in
    a_in = proj[..., :OFF_Q]
    qkv = proj[..., OFF_Q:OFF_GATE]
    gate = proj[..., OFF_GATE:OFF_BETA].astype(F32).reshape(B, T, DN_HEADS, DN_HEAD_DIM)
    b_raw = proj[..., OFF_BETA:OFF_ALPHA].astype(F32)
    a_raw = proj[..., OFF_ALPHA:].astype(F32)
    a_ext = jnp.concatenate([pool_hist, a_in], axis=1)
    y_pool = pool_mix(a_ext, pool_hist.shape[1], p0, w_pool, pool_scale)
    qkv_ext = jnp.concatenate([conv_hist, qkv], axis=1)
    qkv_c = jax.nn.silu(causal_dwconv(qkv_ext, conv_w).astype(F32)).reshape(B, T, 3, DN_HEADS, DN_HEAD_DIM)
    q = l2norm(qkv_c[:, :, 0]) * DN_HEAD_DIM ** -0.5
    k = l2norm(qkv_c[:, :, 1])
    v = qkv_c[:, :, 2]
    beta = jax.nn.sigmoid(b_raw)
    g = -jnp.exp(a_log.astype(F32)) * jax.nn.softplus(a_raw + dt_bias.astype(F32))
    o, s_new = gated_delta_rule(q, k, v, g, beta, s0.astype(F32))
    o = o * lax.rsqrt(jnp.mean(o * o, axis=-1, keepdims=True) + EPS) * norm_w.astype(F32) * jax.nn.silu(gate)
    mix = jnp.concatenate([y_pool, o.reshape(B, T, D_DN).astype(h.dtype)], axis=-1)
    return mix @ w_out, a_ext[:, -POOL_HIST:], qkv_ext[:, -(DN_CONV - 1):], s_new


def swa_project(h, pos, w_qkv):
    B, T, _ = h.shape
    qkv = (h @ w_qkv).reshape(B, T, N_WIN, 3, SWA_HEADS, SWA_HEAD_DIM)
    return partial_rope(qkv[:, :, :, 0], pos), partial_rope(qkv[:, :, :, 1], pos), qkv[:, :, :, 2]


def dilated_group_prompt(q, k, v, window, dilation):
    B, S, H, E = q.shape
    span = window // dilation
    L = S // dilation
    nb = -(-L // SWA_BLOCK)
    Lp = nb * SWA_BLOCK

    def sub(t):
        t = t.reshape(B, L, dilation, H, E).transpose(0, 2, 1, 3, 4)
        return jnp.pad(t, ((0, 0), (0, 0), (0, Lp - L), (0, 0), (0, 0)))

    def band(t):
        t = jnp.pad(sub(t), ((0, 0), (0, 0), (SWA_BLOCK, 0), (0, 0), (0, 0)))
        t = t.reshape(B, dilation, nb + 1, SWA_BLOCK, H, E)
        return jnp.concatenate([t[:, :, :-1], t[:, :, 1:]], axis=3)

    qb = sub(q).reshape(B, dilation, nb, SWA_BLOCK, H, E)
    kb, vb = band(k), band(v)
    s = jnp.einsum('bdnqhe,bdnkhe->bdnhqk', qb, kb, preferred_element_type=F32) * E ** -0.5
    blk_i = jnp.arange(nb)[:, None, None]
    qi = jnp.arange(SWA_BLOCK)[None, :, None]
    kj = jnp.arange(2 * SWA_BLOCK)[None, None, :]
    dist = SWA_BLOCK + qi - kj
    valid = (dist >= 0) & (dist <= span) & ((blk_i - 1) * SWA_BLOCK + kj >= 0)
    s = jnp.where(valid[None, None, :, None], s, -jnp.inf)
    m = jnp.max(s, axis=-1)
    p = jnp.exp(s - m[..., None])
    den = jnp.sum(p, axis=-1)
    num = jnp.einsum('bdnhqk,bdnkhe->bdnqhe', p, vb.astype(F32))

    def unsub(t):
        t = t.reshape((B, dilation, Lp) + t.shape[4:])[:, :, :L]
        t = jnp.swapaxes(t, 1, 2)
        return t.reshape((B, S) + t.shape[3:])

    return unsub(num), unsub(jnp.swapaxes(m, 3, 4)), unsub(jnp.swapaxes(den, 3, 4))


def dilated_group_sample(q, k_ext, v_ext, n_hist, window, dilation):
    T, E = q.shape[1], q.shape[-1]
    span = window // dilation
    idx = n_hist + jnp.arange(T)[:, None] - dilation * jnp.arange(span + 1)[None, :]
    valid = idx >= 0
    idx_c = jnp.maximum(idx, 0)
    kg = jnp.take(k_ext, idx_c, axis=1)
    vg = jnp.take(v_ext, idx_c, axis=1)
    s = jnp.einsum('bthe,btjhe->bthj', q, kg, preferred_element_type=F32) * E ** -0.5
    s = jnp.where(valid[None, :, None, :], s, -jnp.inf)
    m = jnp.max(s, axis=-1)
    p = jnp.exp(s - m[..., None])
    den = jnp.sum(p, axis=-1)
    num = jnp.einsum('bthj,btjhe->bthe', p, vg.astype(F32))
    return num, m, den


def combine_groups(parts):
    num = jnp.stack([pt[0] for pt in parts])
    m = jnp.stack([pt[1] for pt in parts])
    den = jnp.stack([pt[2] for pt in parts])
    w = jnp.exp(m - jnp.max(m, axis=0, keepdims=True))
    return jnp.sum(w[..., None] * num, axis=0) / jnp.sum(w * den, axis=0)[..., None]


def odd_mixer_prompt(h, w_qkv, w_out):
    B, S, _ = h.shape
    q, k, v = swa_project(h, jnp.arange(S), w_qkv)
    parts = [dilated_group_prompt(q[:, :, gi], k[:, :, gi], v[:, :, gi], wd, dl)
             for gi, (wd, dl) in enumerate(WIN_CONFIGS)]
    o = combine_groups(parts)
    y = o.reshape(B, S, D_SWA).astype(h.dtype) @ w_out
    rows = [jnp.stack([k[:, S - min(wd, S):, gi], v[:, S - min(wd, S):, gi]], axis=2)
            for gi, (wd, _) in enumerate(WIN_CONFIGS)]
    return y, rows


def odd_mixer_sample(h, bufs, w_qkv, w_out):
    B, T, _ = h.shape
    q, k, v = swa_project(h, PAST_LEN + jnp.arange(T), w_qkv)
    parts, rows = [], []
    for gi, (wd, dl) in enumerate(WIN_CONFIGS):
        buf = bufs[gi]
        k_ext = jnp.concatenate([buf[:, :, 0], k[:, :, gi]], axis=1)
        v_ext = jnp.concatenate([buf[:, :, 1], v[:, :, gi]], axis=1)
        parts.append(dilated_group_sample(q[:, :, gi], k_ext, v_ext, buf.shape[1], wd, dl))
        rows.append(jnp.stack([k[:, :, gi], v[:, :, gi]], axis=2))
    o = combine_groups(parts)
    y = o.reshape(B, T, D_SWA).astype(h.dtype) @ w_out
    return y, rows


def mem_kv(mem, g, w_k, w_v):
    B, N, _ = mem.shape
    mn = rmsnorm(mem, g)
    return ((mn @ w_k).reshape(B, N, MEM_HEADS, MEM_HEAD_DIM),
            (mn @ w_v).reshape(B, N, MEM_HEADS, MEM_HEAD_DIM))


def mem_attend(h, k, v, w_q, w_o):
    B, T, _ = h.shape
    q = (h @ w_q).reshape(B, T, MEM_HEADS, MEM_HEAD_DIM)
    s = jnp.einsum('bthe,bnhe->bhtn', q, k, preferred_element_type=F32) * MEM_HEAD_DIM ** -0.5
    p = jax.nn.softmax(s, axis=-1)
    o = jnp.einsum('bhtn,bnhe->bthe', p, v.astype(F32)).reshape(B, T, D_MODEL).astype(h.dtype)
    return o @ w_o


def conv_ffn(h, hist, w_up, conv_w, w_down):
    u = h @ w_up
    u_ext = jnp.concatenate([hist, u], axis=1)
    c = causal_dwconv(u_ext, conv_w)
    y = (jax.nn.silu(c[..., :D_FF]) * c[..., D_FF:]) @ w_down
    return y, u_ext[:, -(FFN_CONV - 1):]


def setup_inputs(seed: int = 0) -> dict:
    key = jax.random.key(seed)
    ks = iter(jax.random.split(key, 40))

    def nrm(shape, scale=1.0):
        return jax.random.normal(next(ks), shape, F32) * scale

    def gain(shape):
        return 1.0 + 0.02 * jax.random.normal(next(ks), shape, F32)

    d = D_MODEL
    inp = {
        'x_prompt': nrm((BATCH, SEQ, d)),
        'x_sample': nrm((DEC_BATCH, DEC_SEQ, d)),
        'state_pool': nrm((N_EVEN, DEC_BATCH, POOL_HIST, D_POOL)),
        'state_dn_conv': nrm((N_EVEN, DEC_BATCH, DN_CONV - 1, 3 * D_DN)),
        'state_dn': nrm((N_EVEN, DEC_BATCH, DN_HEADS, DN_HEAD_DIM, DN_HEAD_DIM), 0.3),
        'cache_win_w128': nrm((N_ODD, DEC_BATCH, min(WIN_CONFIGS[0][0], PAST_LEN), 2, SWA_HEADS, SWA_HEAD_DIM)),
        'cache_win_w512': nrm((N_ODD, DEC_BATCH, min(WIN_CONFIGS[1][0], PAST_LEN), 2, SWA_HEADS, SWA_HEAD_DIM)),
        'cache_win_w2048': nrm((N_ODD, DEC_BATCH, min(WIN_CONFIGS[2][0], PAST_LEN), 2, SWA_HEADS, SWA_HEAD_DIM)),
        'cache_mem_k': nrm((DEPTH, DEC_BATCH, N_MEM, MEM_HEADS, MEM_HEAD_DIM)),
        'cache_mem_v': nrm((DEPTH, DEC_BATCH, N_MEM, MEM_HEADS, MEM_HEAD_DIM)),
        'state_ffn_conv': nrm((DEPTH, DEC_BATCH, FFN_CONV - 1, 2 * D_FF)),
        'mem_prompt': nrm((BATCH, N_MEM, d)),
        'g_mix': gain((DEPTH, d)),
        'w_in_ab': nrm((N_EVEN, d, D_IN_EVEN), d ** -0.5),
        'w_pool': nrm((N_EVEN, N_POOL_GROUPS, POOL_GROUP_DIM, POOL_GROUP_DIM), POOL_GROUP_DIM ** -0.5),
        'pool_scale': gain((N_EVEN, D_POOL)),
        'dn_conv_w': nrm((N_EVEN, DN_CONV, 3 * D_DN), DN_CONV ** -0.5),
        'dn_a_log': jnp.log(jax.random.uniform(next(ks), (N_EVEN, DN_HEADS), F32, 1.0, 16.0)),
        'dn_dt_bias': jnp.log(jnp.expm1(jax.random.uniform(next(ks), (N_EVEN, DN_HEADS), F32, 1e-3, 1e-1))),
        'dn_norm_w': gain((N_EVEN, DN_HEAD_DIM)),
        'w_out_ab': nrm((N_EVEN, D_MIX_EVEN, d), D_MIX_EVEN ** -0.5),
        'w_qkv_c': nrm((N_ODD, d, N_WIN * 3 * D_SWA), d ** -0.5),
        'w_out_c': nrm((N_ODD, D_SWA, d), D_SWA ** -0.5),
        'g_mem_q': gain((DEPTH, d)),
        'g_mem_kv': gain((DEPTH, d)),
        'w_mem_q': nrm((DEPTH, d, d), d ** -0.5),
        'w_mem_k': nrm((DEPTH, d, d), d ** -0.5),
        'w_mem_v': nrm((DEPTH, d, d), d ** -0.5),
        'w_mem_o': nrm((DEPTH, d, d), d ** -0.5),
        'g_ffn': gain((DEPTH, d)),
        'w_up': nrm((DEPTH, d, 2 * D_FF), d ** -0.5),
        'ffn_conv_w': nrm((DEPTH, FFN_CONV, 2 * D_FF), FFN_CONV ** -0.5),
        'w_down': nrm((DEPTH, D_FF, d), D_FF ** -0.5),
        'g_final': gain((d,)),
    }
    return inp


def reference(x_prompt, x_sample, state_pool, state_dn_conv, state_dn, cache_win_w128, cache_win_w512,
              cache_win_w2048, cache_mem_k, cache_mem_v, state_ffn_conv, mem_prompt, g_mix, w_in_ab, w_pool,
              pool_scale, dn_conv_w, dn_a_log, dn_dt_bias, dn_norm_w, w_out_ab, w_qkv_c, w_out_c, g_mem_q,
              g_mem_kv, w_mem_q, w_mem_k, w_mem_v, w_mem_o, g_ffn, w_up, ffn_conv_w, w_down, g_final):
    xp, xs = x_prompt, x_sample
    Bp = xp.shape[0]
    dt = xp.dtype
    win_caches = (cache_win_w128, cache_win_w512, cache_win_w2048)
    pool_p, pool_s, dconv_p, dconv_s, dn_p, dn_s = [], [], [], [], [], []
    win_p = [[] for _ in WIN_CONFIGS]
    win_s = [[] for _ in WIN_CONFIGS]
    memk_p, memv_p, fconv_p, fconv_s = [], [], [], []
    for layer in range(DEPTH):
        hp = rmsnorm(xp, g_mix[layer])
        hs = rmsnorm(xs, g_mix[layer])
        if layer % 2 == 0:
            e = layer // 2
            wts = (w_in_ab[e], w_pool[e], pool_scale[e], dn_conv_w[e], dn_a_log[e], dn_dt_bias[e],
                   dn_norm_w[e], w_out_ab[e])
            yp, a_p, c_p, s_p = even_mixer(
                hp, jnp.zeros((Bp, 0, D_POOL), dt), jnp.zeros((Bp, DN_CONV - 1, 3 * D_DN), dt),
                jnp.zeros((Bp, DN_HEADS, DN_HEAD_DIM, DN_HEAD_DIM), F32), 0, *wts)
            ys, a_s, c_s, s_s = even_mixer(hs, state_pool[e], state_dn_conv[e], state_dn[e], PAST_LEN, *wts)
            pool_p.append(a_p); pool_s.append(a_s)
            dconv_p.append(c_p); dconv_s.append(c_s)
            dn_p.append(s_p); dn_s.append(s_s)
        else:
            o = layer // 2
            yp, rows_p = odd_mixer_prompt(hp, w_qkv_c[o], w_out_c[o])
            ys, rows_s = odd_mixer_sample(hs, [c[o] for c in win_caches], w_qkv_c[o], w_out_c[o])
            for gi in range(N_WIN):
                win_p[gi].append(rows_p[gi]); win_s[gi].append(rows_s[gi])
        xp = xp + yp
        xs = xs + ys
        mk, mv = mem_kv(mem_prompt, g_mem_kv[layer], w_mem_k[layer], w_mem_v[layer])
        memk_p.append(mk); memv_p.append(mv)
        xp = xp + mem_attend(rmsnorm(xp, g_mem_q[layer]), mk, mv, w_mem_q[layer], w_mem_o[layer])
        xs = xs + mem_attend(rmsnorm(xs, g_mem_q[layer]), cache_mem_k[layer], cache_mem_v[layer],
                             w_mem_q[layer], w_mem_o[layer])
        yp, f_p = conv_ffn(rmsnorm(xp, g_ffn[layer]), jnp.zeros((Bp, FFN_CONV - 1, 2 * D_FF), dt),
                           w_up[layer], ffn_conv_w[layer], w_down[layer])
        ys, f_s = conv_ffn(rmsnorm(xs, g_ffn[layer]), state_ffn_conv[layer], w_up[layer], ffn_conv_w[layer],
                           w_down[layer])
        fconv_p.append(f_p); fconv_s.append(f_s)
        xp = xp + yp
        xs = xs + ys
    y_prompt = rmsnorm(xp, g_final)
    y_sample = rmsnorm(xs, g_final)
    return (y_prompt, y_sample,
            jnp.stack(pool_p), jnp.stack(pool_s),
            jnp.stack(dconv_p), jnp.stack(dconv_s),
            jnp.stack(dn_p), jnp.stack(dn_s),
            jnp.stack(win_p[0]), jnp.stack(win_s[0]),
            jnp.stack(win_p[1]), jnp.stack(win_s[1]),
            jnp.stack(win_p[2]), jnp.stack(win_s[2]),
            jnp.stack(memk_p), jnp.stack(memv_p),
            jnp.stack(fconv_p), jnp.stack(fconv_s))
```

```python
import numpy as np
import concourse.bass as bass
import concourse.mybir as mybir

F32 = mybir.dt.float32
BF16 = mybir.dt.bfloat16
AF = mybir.ActivationFunctionType
ALU = mybir.AluOpType
AX = mybir.AxisListType


class TT:
    __slots__ = ("ap", "w", "r", "name", "psum")

    def __init__(self, ap, name=""):
        self.ap = ap
        self.psum = False
        self.w = None
        self.r = {}
        self.name = name

    def __getitem__(self, idx):
        return self.ap[idx]


class FW:
    ENG = ("pe", "act", "dve", "pool", "sp")

    def __init__(self, nc, ndma=6):
        self.nc = nc
        self.prog = {e: [] for e in self.ENG}
        self.cnt = {e: 0 for e in self.ENG if e != "sp"}
        self.seen = {e: {} for e in self.ENG}
        self.sems = {}
        self.ndma = ndma
        self.dma_rr = {"sp": 0, "pool": 0, "act": 0}
        self.dma_tot = {}
        self.ctx = []
        self.out_tokens = []
        import os
        self.maxops = int(os.environ.get("FW_MAXOPS", "100000000"))
        self.total = 0
        self.log = []

    def sem(self, key):
        if key not in self.sems:
            cm = self.nc.semaphore("s_%s" % str(key).replace(" ", ""))
            self.sems[key] = cm.__enter__()
            self.ctx.append(cm)
        return self.sems[key]

    def sb(self, name, shape, dt=F32):
        cm = self.nc.sbuf_tensor(name, list(shape), dt)
        t = cm.__enter__()
        self.ctx.append(cm)
        return TT(t, name)

    def ps(self, name, shape, dt=F32):
        cm = self.nc.psum_tensor(name, list(shape), dt)
        t = cm.__enter__()
        self.ctx.append(cm)
        tt = TT(t, name)
        tt.psum = True
        return tt

    def _waits(self, eng, reads, writes, skip_self_pe=False):
        need = {}
        for t in reads:
            if t.w is not None:
                k, v = t.w
                need[k] = max(need.get(k, 0), v)
            if t.psum:
                for k, v in t.r.items():
                    if k != eng:
                        need[k] = max(need.get(k, 0), v)
        for t in writes:
            if t.w is not None:
                k, v = t.w
                need[k] = max(need.get(k, 0), v)
            for k, v in t.r.items():
                need[k] = max(need.get(k, 0), v)
        out = []
        seen = self.seen[eng]
        for k, v in need.items():
            if eng == "pe" and k == "pe":
                continue
            if seen.get(k, 0) >= v:
                continue
            seen[k] = v
            out.append((k, v))
        return out

    def op(self, eng, fn, reads=(), writes=()):
        self.total += 1
        if self.total > self.maxops:
            return None
        reads = [t for t in reads if isinstance(t, TT)]
        writes = [t for t in writes if isinstance(t, TT)]
        waits = self._waits(eng, reads, writes)
        self.cnt[eng] += 1
        v = self.cnt[eng]
        self.prog[eng].append((waits, fn, (eng, v, 1)))
        tok = (eng, v)
        for t in writes:
            t.w = tok
            t.r = {}
        for t in reads:
            if t in writes:
                continue
            t.r[eng] = max(t.r.get(eng, 0), v)
        return tok

    def dma(self, q, out_ap, in_ap, reads=(), writes=(), is_output=False, **kw):
        self.total += 1
        if self.total > self.maxops:
            return None
        reads = [t for t in reads if isinstance(t, TT)]
        writes = [t for t in writes if isinstance(t, TT)]
        waits = self._waits(q, reads, writes)
        i = self.dma_rr[q]
        self.dma_rr[q] = (i + 1) % self.ndma
        key = "d%s%d" % (q, i)
        prev = self.dma_tot.get(key, 0)
        if prev > 0 and self.seen[q].get(key, 0) < prev:
            self.seen[q][key] = prev
            waits.append((key, prev))
        v = prev + 16
        self.dma_tot[key] = v
        fn = (lambda e, o=out_ap, a=in_ap, k=kw: e.dma_start(out=o, in_=a, **k))
        self.prog[q].append((waits, fn, (key, v, 16)))
        tok = (key, v)
        for t in writes:
            t.w = tok
            t.r = {}
        for t in reads:
            t.r[key] = max(t.r.get(key, 0), v)
        if is_output:
            self.out_tokens.append(tok)
        return tok

    def finish(self):
        nc = self.nc
        fin = {}
        for k, v in self.out_tokens:
            fin[k] = max(fin.get(k, 0), v)
        for k, v in self.dma_tot.items():
            fin[k] = max(fin.get(k, 0), v)
        for e, c in self.cnt.items():
            if c > 0:
                fin[e] = c
        for k in list(fin.keys()):
            self.sem(k)
        for e in self.ENG:
            for waits, fn, (k, v, inc) in self.prog[e]:
                self.sem(k)
                for (wk, wv) in waits:
                    self.sem(wk)
        sems = self.sems
        prog = self.prog

        def run(e, h, final=False):
            for waits, fn, (k, v, inc) in prog[e]:
                for (wk, wv) in waits:
                    h.wait_ge(sems[wk], wv)
                fn(h).then_inc(sems[k], inc)
            if final:
                for k, v in fin.items():
                    h.wait_ge(sems[k], v)

        with nc.Block() as block:
            @block.sync
            def _(h):
                run("sp", h, final=True)

            @block.tensor
            def _(h):
                run("pe", h)

            @block.scalar
            def _(h):
                run("act", h)

            @block.vector
            def _(h):
                run("dve", h)

            @block.gpsimd
            def _(h):
                run("pool", h)
        for cm in reversed(self.ctx):
            cm.__exit__(None, None, None)

from concourse.bass_utils import run_bass_kernel_spmd

BIG = 30000.0
EPS = 1e-6
WINS = ((128, 1), (512, 4), (2048, 16))


def build(S=4096, NS=4, phases=('A0', 'B0', 'C0', 'A1', 'B1', 'C1')):
    nc = bass.Bass("TRN2", target_bir_lowering=False)
    fw = FW(nc)
    NT = S // 128
    I_, O_ = {}, {}

    def din(name, shape):
        I_[name] = nc.dram_tensor(name, list(shape), F32, kind="ExternalInput").ap()
        return I_[name]

    def dout(name, shape):
        O_[name] = nc.dram_tensor(name, list(shape), F32, kind="ExternalOutput").ap()
        return O_[name]

    xp = din("xp", [S, 1024]); xs_d = din("xs", [NS * 8, 1024]); memp = din("memp", [256, 1024])
    st_pool = din("st_pool", [NS, 15, 512]); st_dconv = din("st_dconv", [NS, 3, 1536])
    st_dn = din("st_dn", [NS, 4, 128, 128])
    cwin = [din("cw%d" % w, [NS, w, 1024]) for w, _ in WINS]
    cmk = din("cmk", [2, NS, 256, 1024]); cmv = din("cmv", [2, NS, 256, 1024])
    st_ffn = din("st_ffn", [2, NS, 2, 5632])
    g_mix = din("g_mix", [2, 1024]); w_in = din("w_in", [1024, 2568]); w_pool = din("w_pool", [512, 128])
    pool_scale = din("pool_scale", [1, 512]); dn_conv_w = din("dn_conv_w", [4, 1536])
    dn_a_log = din("dn_a_log", [1, 4]); dn_dt_bias = din("dn_dt_bias", [1, 4]); dn_norm_w = din("dn_norm_w", [1, 128])
    w_out_ab = din("w_out_ab", [1024, 1024]); w_qkv = din("w_qkv", [1024, 4608]); w_out_c = din("w_out_c", [512, 1024])
    g_mem_q = din("g_mem_q", [2, 1024]); g_mem_kv = din("g_mem_kv", [2, 1024])
    w_mem = {k: din("w_mem_" + k, [2, 1024, 1024]) for k in "qkvo"}
    g_ffn = din("g_ffn", [2, 1024]); w_up = din("w_up", [2, 1024, 5632]); ffn_cw = din("ffn_cw", [2, 3, 5632])
    w_down = din("w_down", [2, 2816, 1024]); g_final = din("g_final", [1, 1024])
    c_ident = din("c_ident", [128, 128]); c_mge = din("c_mge", [128, 128]); c_nmle = din("c_nmle", [128, 128])
    c_nmlt = din("c_nmlt", [128, 128]); c_LT = din("c_LT", [128, 128])
    c_invc = din("c_invc", [2, 128, 512])
    c_cos = din("c_cos", [S + 8, 8]); c_sin = din("c_sin", [S + 8, 8])
    c_mb = din("c_mb", [2, 128, 128]); c_sm = din("c_sm", [128, 24 * 8])

    y_p = dout("y_p", [S, 1024]); y_s = dout("y_s", [NS * 8, 1024])
    o_pool_p = dout("o_pool_p", [15, 512]); o_pool_s = dout("o_pool_s", [NS, 15, 512])
    o_dconv_p = dout("o_dconv_p", [3, 1536]); o_dconv_s = dout("o_dconv_s", [NS, 3, 1536])
    o_dn_p = dout("o_dn_p", [4, 128, 128]); o_dn_s = dout("o_dn_s", [NS, 4, 128, 128])
    o_win_p = [dout("o_w%d_p" % w, [min(w, S), 1024]) for w, _ in WINS]
    o_win_s = [dout("o_w%d_s" % w, [NS, 8, 1024]) for w, _ in WINS]
    o_memk = dout("o_memk", [2, 256, 1024]); o_memv = dout("o_memv", [2, 256, 1024])
    o_ffn_p = dout("o_ffn_p", [2, 2, 5632]); o_ffn_s = dout("o_ffn_s", [2, NS, 2, 5632])

    QT_d = [nc.dram_tensor("QT%d" % g, [512, S], BF16).ap() for g in range(3)]
    KT_d = [nc.dram_tensor("KT%d" % g, [512, S], BF16).ap() for g in range(3)]
    V_d = [nc.dram_tensor("Vd%d" % g, [S, 512], BF16).ap() for g in range(3)]
    qkv_tt = [TT(None, "qkvscr%d" % t) for t in range(NT)]
    xres = [TT(None, "xres%d" % t) for t in range(NT)]

    rr = {"ps": 0, "ev": 0, "q": 0}
    PS = [fw.ps("ps%d" % i, [128, 512]) for i in range(8)]

    def nps():
        p = PS[rr["ps"] % 8]
        rr["ps"] += 1
        return p

    def ev_eng():
        rr["ev"] += 1
        return "act" if rr["ev"] % 2 else "dve"

    def copy(dst, src, reads, writes, eng=None, scale=None):
        eng = eng or ev_eng()
        if scale is not None and not isinstance(scale, float) and eng == "act":
            eng = "dve"
        if eng == "act":
            if scale is None:
                fw.op("act", lambda e: e.copy(out=dst, in_=src), reads, writes)
            else:
                fw.op("act", lambda e: e.mul(out=dst, in_=src, mul=scale), reads, writes)
        else:
            if scale is None:
                fw.op(eng, lambda e: e.tensor_copy(out=dst, in_=src), reads, writes)
            else:
                fw.op(eng, lambda e: e.tensor_scalar(out=dst, in0=src, scalar1=scale, scalar2=None, op0=ALU.mult), reads, writes)

    def tt_op(eng, dst, a, b, op, reads, writes):
        fw.op(eng, lambda e: e.tensor_tensor(out=dst, in0=a, in1=b, op=op), reads, writes)

    def stt(eng, dst, a, sc, b, op0, op1, reads, writes):
        if eng == "pool":
            tmp = sm("ptmp", [128, 128])
            nparts, nfree = a.shape[0], a.shape[1]
            fw.op("pool", lambda e: e.tensor_scalar(out=tmp[:nparts, :nfree], in0=a, scalar1=sc, scalar2=None, op0=op0), reads, [tmp])
            fw.op("pool", lambda e: e.tensor_tensor(out=dst, in0=tmp[:nparts, :nfree], in1=b, op=op1), list(reads) + [tmp], writes)
            return
        fw.op(eng, lambda e: e.scalar_tensor_tensor(out=dst, in0=a, scalar=sc, in1=b, op0=op0, op1=op1), reads, writes)

    def act(dst, src, func, reads, writes, bias=None, scale=None):
        kw = {}
        if bias is not None:
            kw["bias"] = bias
        if scale is not None:
            kw["scale"] = scale
        fw.op("act", lambda e: e.activation(out=dst, in_=src, func=func, **kw), reads, writes)

    def mm(ps_ap, lhsT, rhs, start, stop, reads, writes):
        fw.op("pe", lambda e: e.matmul(ps_ap, lhsT=lhsT, rhs=rhs, start=start, stop=stop), reads, writes)

    def tr(ps_ap, in_ap, k, reads, writes):
        fw.op("pe", lambda e: e.transpose(ps_ap, in_ap, ident[:k, :k]), list(reads) + [ident], writes)

    ident = fw.sb("ident", [128, 128]); fw.dma("sp", ident[:], c_ident[:], writes=[ident])
    m_ge = fw.sb("m_ge", [128, 128]); fw.dma("sp", m_ge[:], c_mge[:], writes=[m_ge])
    nm_le = fw.sb("nm_le", [128, 128]); fw.dma("sp", nm_le[:], c_nmle[:], writes=[nm_le])
    nm_lt = fw.sb("nm_lt", [128, 128]); fw.dma("sp", nm_lt[:], c_nmlt[:], writes=[nm_lt])
    LT = fw.sb("LT", [128, 128]); fw.dma("sp", LT[:], c_LT[:], writes=[LT])
    ones = fw.sb("ones", [128, 128]); fw.op("pool", lambda e: e.memset(ones[:], 1.0), writes=[ones])
    gfin = fw.sb("gfin", [128, 1024]); fw.dma("sp", gfin[:], g_final[0:1, :].partition_broadcast(128) if False else g_final.to_broadcast([128, 1024]), writes=[gfin])
    gbc = fw.sb("gbc", [128, 1024])

    def load_g(src_row):
        fw.dma("sp", gbc[:], src_row.to_broadcast([128, 1024]), writes=[gbc])

    xsres = [TT(None, "xsres%d" % s) for s in range(NS)]

    ARENA = 67584
    arena = fw.sb("arena", [128, ARENA], BF16)
    arena_f = arena.ap.bitcast(F32)

    def barrier_tt(tt):
        tt.r = dict(fw.cnt)
        for k, v in fw.dma_tot.items():
            tt.r[k] = v
        return tt

    def wload(name, dram2d, off, K, N):
        KC = K // 128
        tt = barrier_tt(TT(arena.ap[:, off:off + KC * N].rearrange("p (k n) -> p k n", n=N), name))
        chunks = []
        for kc in range(KC):
            c = TT(tt.ap[:, kc, :], "%s_%d" % (name, kc))
            c.r = dict(tt.r)
            fw.dma("pool", c.ap, dram2d[kc * 128:(kc + 1) * 128, :], writes=[c])
            chunks.append(c)
        return chunks, off + KC * N

    xt_rot = [fw.sb("xt%d" % i, [128, 1024]) for i in range(2)]
    hn = fw.sb("hn", [128, 1024])
    hT4 = fw.sb("hT4", [128, 8, 512], BF16)
    hT_rot = [TT(hT4.ap[:, :, 0:256], "hTa"), TT(hT4.ap[:, :, 256:512], "hTb")]
    ssq = [fw.sb("ssq%d" % i, [128, 1]) for i in range(2)]
    cnt = {"xt": 0, "hT": 0, "ss": 0}

    def rstd_of(src_ap, ntok, width, reads):
        ss = ssq[cnt["ss"] % 2]; cnt["ss"] += 1
        act(hn[:ntok, :width], src_ap, AF.Square, reads, [hn])
        fw.op("dve", lambda e: e.reduce_sum(out=ss[:ntok, :], in_=hn[:ntok, :width], axis=AX.X), [hn], [ss])
        act(ss[:ntok, :], ss[:ntok, :], AF.Sqrt, [ss], [ss], bias=EPS, scale=1.0 / width)
        fw.op("dve", lambda e: e.reciprocal(out=ss[:ntok, :], in_=ss[:ntok, :]), [ss], [ss])
        return ss

    def norm_tok(xt, ntok, g_tt):
        ss = rstd_of(xt[:ntok, :], ntok, 1024, [xt])
        stt("dve", hn[:ntok, :], xt[:ntok, :], ss[:ntok, 0:1], g_tt[:ntok, :], ALU.mult, ALU.mult, [xt, ss, g_tt], [hn])
        return hn

    def to_fm(src, ntok, dst, nchunks=8, c0=0):
        for b in range(0, nchunks, 4):
            nb = min(4, nchunks - b)
            ps = nps()
            for j in range(nb):
                tr(ps[:, j * 128:j * 128 + ntok], src[:ntok, (b + j) * 128:(b + j + 1) * 128], ntok, [src], [ps])
            copy(dst[:, b:b + nb, c0:c0 + ntok], ps[:, 0:nb * 128].rearrange("p (j t) -> p j t", t=128)[:, :, :ntok], [ps], [dst])

    def norm_T(xt, ntok, g_tt):
        norm_tok(xt, ntok, g_tt)
        hT = hT_rot[cnt["hT"] % 2]; cnt["hT"] += 1
        to_fm(hn, ntok, hT)
        return hT

    def lin_fm(ps_ap, W, col0, M, hT, ntok, wr):
        KC = len(W)
        for kc in range(KC):
            mm(ps_ap, W[kc][:, col0:col0 + M], hT[:, kc, :ntok], kc == 0, kc == KC - 1, [W[kc], hT], [wr])

    def lin_tm(ps_ap, hT, W, col0, N, ntok, wr):
        KC = len(W)
        for kc in range(KC):
            mm(ps_ap, hT[:, kc, :ntok], W[kc][:, col0:col0 + N], kc == 0, kc == KC - 1, [W[kc], hT], [wr])

    STW = 1024
    stage = fw.sb("stage", [16, STW])
    stage_o = fw.sb("stage_o", [16, STW])

    def cols_from_rows(dst_fn, dst_tts, src_dram, k, C):
        per = STW // 128
        for c0 in range(0, C, per):
            n = min(per, C - c0)
            fw.dma("sp", stage[:k, :n * 128], src_dram[:, c0 * 128:(c0 + n) * 128], writes=[stage])
            ps = nps()
            for j in range(n):
                tr(ps[:, j * 16:j * 16 + k], stage[:k, j * 128:(j + 1) * 128], k, [stage], [ps])
            for j in range(n):
                copy(dst_fn(c0 + j), ps[:, j * 16:j * 16 + k], [ps], [dst_tts[c0 + j] if isinstance(dst_tts, list) else dst_tts])

    def rows_from_cols(dst_dram, src_fn, src_tts, k, C):
        per = STW // 128
        for p0 in range(0, C, per):
            np_ = min(per, C - p0)
            for c0 in range(p0, p0 + np_, 4):
                n = min(4, p0 + np_ - c0)
                ps = nps()
                for j in range(n):
                    st = src_tts[c0 + j] if isinstance(src_tts, list) else src_tts
                    fw.op("pe", lambda e, j=j, c=c0 + j, ps=ps: e.transpose(ps[:k, j * 128:(j + 1) * 128], src_fn(c), ident[:, :]), [st, ident], [ps])
                copy(stage_o[:k, (c0 - p0) * 128:(c0 - p0 + n) * 128], ps[:k, :n * 128], [ps], [stage_o])
            fw.dma("pool", dst_dram[:, p0 * 128:(p0 + np_) * 128], stage_o[:k, :np_ * 128], reads=[stage_o], is_output=True)

    def load_x(t, layer0):
        xt = xt_rot[cnt["xt"] % 2]; cnt["xt"] += 1
        if layer0:
            fw.dma("sp", xt[:, :], xp[t * 128:(t + 1) * 128, :], writes=[xt])
        else:
            fw.dma("sp", xt[:, :], y_p[t * 128:(t + 1) * 128, :], reads=[xres[t]], writes=[xt])
        return xt

    def load_xs(s_, layer0=False):
        xt = xt_rot[cnt["xt"] % 2]; cnt["xt"] += 1
        if layer0:
            fw.dma("sp", xt[:8, :], xs_d[s_ * 8:(s_ + 1) * 8, :], writes=[xt])
        else:
            fw.dma("sp", xt[:8, :], y_s[s_ * 8:(s_ + 1) * 8, :], reads=[xsres[s_]], writes=[xt])
        return xt

    def store_xs(s_, xt):
        fw.dma("sp", y_s[s_ * 8:(s_ + 1) * 8, :], xt[:8, :], reads=[xt], writes=[xsres[s_]], is_output=True)

    def store_x(t, xt):
        fw.dma("sp", y_p[t * 128:(t + 1) * 128, :], xt[:, :], reads=[xt], writes=[xres[t]], is_output=True)

    def add_res(xt, ntok, half, ps):
        tt_op("dve", xt[:ntok, half * 512:(half + 1) * 512], xt[:ntok, half * 512:(half + 1) * 512], ps[:ntok, :512], ALU.add, [xt, ps], [xt])

    SCRB = 37632
    scratch = fw.sb("scratch", [128, SCRB // 2], BF16)
    scr_f = scratch.ap.bitcast(F32)
    small = {}
    scr = {"off": 0}

    def sm_reset():
        print('scratch used', scr['off'])
        small.clear()
        scr["off"] = 0

    def sm(name, shape, dt=F32):
        if name in small:
            return small[name]
        esz = 4 if dt == F32 else 2
        n = 1
        for d_ in shape[1:]:
            n *= d_
        nbytes = (n * esz + 7) // 8 * 8
        off = scr["off"]
        assert off + nbytes <= SCRB, ("scratch overflow", name, off, nbytes)
        scr["off"] = off + nbytes
        base = scr_f if dt == F32 else scratch.ap
        ap = base[:shape[0], off // esz: off // esz + n]
        if len(shape) == 3:
            ap = ap.rearrange("p (a b) -> p a b", b=shape[2])
        elif len(shape) == 4:
            ap = ap.rearrange("p (a b c) -> p a b c", b=shape[2], c=shape[3])
        elif len(shape) == 5:
            ap = ap.rearrange("p (a b c d) -> p a b c d", b=shape[2], c=shape[3], d=shape[4])
        tt = barrier_tt(TT(ap, name))
        small[name] = tt
        return tt

    def phase_A0():
        off = 0
        sm_reset()
        Win, off = wload("w_in", w_in[:, 0:2560], off, 1024, 2560)
        Wab, off = wload("w_ab", w_in[:, 2560:2568], off, 1024, 8)
        Wout, off = wload("w_outab", w_out_ab, off, 1024, 1024)
        Wpool, off = wload("w_pool", w_pool, off, 512, 128)
        load_g(g_mix[0:1, :])
        cw = sm("dn_cw", [128, 12, 4]); cols_from_rows(lambda c: cw[:, c, :], cw, dn_conv_w, 4, 12)
        psc = sm("pool_sc", [128, 4, 1]); cols_from_rows(lambda c: psc[:, c, :], psc, pool_scale, 1, 4)
        dtb = sm("dtb", [128, 4]); fw.dma("sp", dtb[:], dn_dt_bias.to_broadcast([128, 4]), writes=[dtb])
        nea = sm("nea", [128, 4]); fw.dma("sp", nea[:], dn_a_log.to_broadcast([128, 4]), writes=[nea])
        act(nea[:], nea[:], AF.Exp, [nea], [nea])
        copy(nea[:], nea[:], [nea], [nea], eng="dve", scale=-1.0)
        nw = sm("normw", [128, 128]); fw.dma("sp", nw[:], dn_norm_w.to_broadcast([128, 128]), writes=[nw])
        invc = sm("invc", [128, 512]); fw.dma("sp", invc[:], c_invc[0], writes=[invc])
        aext = [sm("aext%d" % g, [128, 15 + 128]) for g in range(4)]
        qext = [sm("qext%d" % c, [128, 3 + 128]) for c in range(12)]
        qc = [sm("qc%d" % c, [128, 128]) for c in range(12)]
        S = [sm("S%d" % h, [128, 128]) for h in range(4)]
        mixT = sm("mixT", [128, 8, 128], BF16)
        gate_s = sm("gate_s", [128, 512]); ofin = sm("ofin", [128, 512])
        pa = [sm("pa%d" % i, [128, 143]) for i in range(2)]
        zb = sm("zb", [128, 128], BF16)
        col = {n: sm("col_" + n, [128, 4]) for n in ["beta", "lnb", "g", "G", "negG", "Gl", "eG", "beG", "nbeta", "ekg", "glast", "egl"]}
        gt = {n: sm("gt_" + n, [128, 128]) for n in ["sq", "rn"]}
        gnames = ["Ds", "DTb", "DTi", "B0", "B1", "R0", "R1", "Tt", "QKd", "X0v", "X0k", "kg", "nwT", "vnew", "tmpB"]
        gth = []
        aoff = 14720
        for h_ in range(4):
            d_ = {}
            for nm in gnames:
                d_[nm] = barrier_tt(TT(arena_f[:, aoff:aoff + 128], "gt%d_%s" % (h_, nm)))
                aoff += 128
            gth.append(d_)

        def tile(xt, ntok, tile0, write_x):
            n = ntok
            hT = norm_T(xt, n, gbc)
            ps = nps(); lin_tm(ps[:n, :512], hT, Win, 2048, 512, n, ps)
            act(gate_s[:n, :], ps[:n, :512], AF.Silu, [ps], [gate_s])
            ps = nps(); lin_tm(ps[:n, :8], hT, Wab, 0, 8, n, ps)
            act(col["beta"][:n, :], ps[:n, 0:4], AF.Sigmoid, [ps], [col["beta"]])
            tt_op("dve", col["g"][:n, :], ps[:n, 4:8], dtb[:n, :], ALU.add, [ps, dtb], [col["g"]])
            act(col["g"][:n, :], col["g"][:n, :], AF.Exp, [col["g"]], [col["g"]])
            act(col["g"][:n, :], col["g"][:n, :], AF.Ln, [col["g"]], [col["g"]], bias=1.0)
            tt_op("dve", col["g"][:n, :], col["g"][:n, :], nea[:n, :], ALU.mult, [col["g"], nea], [col["g"]])
            act(col["lnb"][:n, :], col["beta"][:n, :], AF.Ln, [col["beta"]], [col["lnb"]])
            ps = nps()
            mm(ps[:n, 0:4], LT[:n, :n], col["g"][:n, :], True, True, [LT, col["g"]], [ps])
            copy(col["G"][:n, :], ps[:n, 0:4], [ps], [col["G"]], eng="dve")
            ps = nps()
            mm(ps[:, 0:4], ones[:n, :], col["g"][:n, :], True, True, [ones, col["g"]], [ps])
            copy(col["glast"][:, :], ps[:, 0:4], [ps], [col["glast"]], eng="dve")
            act(col["egl"][:, :], col["glast"][:, :], AF.Exp, [col["glast"]], [col["egl"]])
            copy(col["negG"][:n, :], col["G"][:n, :], [col["G"]], [col["negG"]], eng="dve", scale=-1.0)
            tt_op("dve", col["Gl"][:n, :], col["G"][:n, :], col["lnb"][:n, :], ALU.add, [col["G"], col["lnb"]], [col["Gl"]])
            act(col["eG"][:n, :], col["G"][:n, :], AF.Exp, [col["G"]], [col["eG"]])
            tt_op("dve", col["beG"][:n, :], col["eG"][:n, :], col["beta"][:n, :], ALU.mult, [col["eG"], col["beta"]], [col["beG"]])
            copy(col["nbeta"][:n, :], col["beta"][:n, :], [col["beta"]], [col["nbeta"]], eng="dve", scale=-1.0)
            tt_op("dve", col["ekg"][:n, :], col["glast"][:n, :], col["G"][:n, :], ALU.subtract, [col["glast"], col["G"]], [col["ekg"]])
            act(col["ekg"][:n, :], col["ekg"][:n, :], AF.Exp, [col["ekg"]], [col["ekg"]])
            for g in range(4):
                ps = nps(); lin_fm(ps[:, :n], Win, g * 128, 128, hT, n, ps)
                copy(aext[g][:, 15:15 + n], ps[:, :n], [ps], [aext[g]])
            for c in range(12):
                ps = nps(); lin_fm(ps[:, :n], Win, 512 + c * 128, 128, hT, n, ps)
                copy(qext[c][:, 3:3 + n], ps[:, :n], [ps], [qext[c]])
            Lx = 15 + n
            for g in range(4):
                cur = aext[g]
                sh = 1
                for step in range(g + 1):
                    dst = pa[step % 2]
                    eng = "dve"
                    tt_op(eng, dst[:, sh:Lx], cur[:, sh:Lx], cur[:, 0:Lx - sh], ALU.add, [cur], [dst])
                    cur = dst
                    sh *= 2
                nxt = pa[(g + 1) % 2]
                if tile0:
                    tt_op("dve", nxt[:, 15:Lx], cur[:, 15:Lx], invc[:, g * 128:g * 128 + n], ALU.mult, [cur, invc], [nxt])
                else:
                    copy(nxt[:, 15:Lx], cur[:, 15:Lx], [cur], [nxt], eng="dve", scale=1.0 / (2 ** (g + 1)))
                tt_op("dve", zb[:, :n], nxt[:, 15:Lx], aext[g][:, 15:Lx], ALU.subtract, [nxt, aext[g]], [zb])
                ps = nps()
                mm(ps[:, :n], Wpool[g][:, :], zb[:, :n], True, True, [Wpool[g], zb], [ps])
                copy(mixT[:, g, :n], ps[:, :n], [ps], [mixT], eng="act", scale=psc[:, g, 0:1])
            for c in range(12):
                fw.op("act", lambda e, c=c: e.mul(out=qc[c][:, :n], in_=qext[c][:, 0:n], mul=cw[:, c, 0:1]), [qext[c], cw], [qc[c]])
                for k in range(1, 4):
                    stt("dve", qc[c][:, :n], qext[c][:, k:k + n], cw[:, c, k:k + 1], qc[c][:, :n], ALU.mult, ALU.add, [qext[c], cw, qc[c]], [qc[c]])
                act(qc[c][:, :n], qc[c][:, :n], AF.Silu, [qc[c]], [qc[c]])
            for c in range(8):
                act(gt["sq"][:, :n], qc[c][:, :n], AF.Square, [qc[c]], [gt["sq"]])
                ps = nps()
                mm(ps[:, :n], ones[:, :], gt["sq"][:, :n], True, True, [ones, gt["sq"]], [ps])
                sc = 128.0 if c < 4 else 1.0
                act(gt["rn"][:, :n], ps[:, :n], AF.Sqrt, [ps], [gt["rn"]], bias=EPS * sc, scale=sc)
                fw.op("dve", lambda e: e.reciprocal(out=gt["rn"][:, :n], in_=gt["rn"][:, :n]), [gt["rn"]], [gt["rn"]])
                tt_op("dve", qc[c][:, :n], qc[c][:, :n], gt["rn"][:, :n], ALU.mult, [qc[c], gt["rn"]], [qc[c]])
            nlev = 7 if n == 128 else 3
            PSH = [dict() for _ in range(4)]

            def st_A(h):
                g_ = gth[h]
                qT, kT = qc[h], qc[4 + h]
                pKK = nps(); mm(pKK[:n, :n], kT[:, :n], kT[:, :n], True, True, [kT], [pKK])
                pQK = nps(); mm(pQK[:n, :n], kT[:, :n], qT[:, :n], True, True, [kT, qT], [pQK])
                p1 = nps()
                mm(p1[:n, :n], col["G"][:n, h:h + 1].to_broadcast([n, n]), ident[:n, :n], True, False, [col["G"], ident], [p1])
                mm(p1[:n, :n], ident[:n, :n], m_ge[:n, :n], False, True, [ident, m_ge], [p1])
                act(g_["Ds"][:n, :n], p1[:n, :n], AF.Exp, [p1, col["G"]], [g_["Ds"]], bias=col["G"][:n, h:h + 1], scale=-1.0)
                p2 = nps()
                mm(p2[:n, :n], col["Gl"][:n, h:h + 1].to_broadcast([n, n]), ident[:n, :n], True, False, [col["Gl"], ident], [p2])
                mm(p2[:n, :n], ident[:n, :n], nm_le[:n, :n], False, True, [ident, nm_le], [p2])
                act(g_["DTb"][:n, :n], p2[:n, :n], AF.Exp, [p2, col["negG"]], [g_["DTb"]], bias=col["negG"][:n, h:h + 1], scale=1.0)
                p3 = nps()
                mm(p3[:n, :n], col["G"][:n, h:h + 1].to_broadcast([n, n]), ident[:n, :n], True, False, [col["G"], ident], [p3])
                mm(p3[:n, :n], ident[:n, :n], nm_lt[:n, :n], False, True, [ident, nm_lt], [p3])
                act(g_["DTi"][:n, :n], p3[:n, :n], AF.Exp, [p3, col["negG"]], [g_["DTi"]], bias=col["negG"][:n, h:h + 1], scale=1.0)
                stt("dve", g_["B0"][:n, :n], pKK[:n, :n], col["nbeta"][:n, h:h + 1], g_["Ds"][:n, :n], ALU.mult, ALU.mult, [pKK, col["nbeta"], g_["Ds"]], [g_["B0"]])
                stt("dve", g_["R0"][:n, :n], pKK[:n, :n], -1.0, g_["DTb"][:n, :n], ALU.mult, ALU.mult, [pKK, g_["DTb"]], [g_["R0"]])
                tt_op("dve", g_["QKd"][:n, :n], pQK[:n, :n], g_["DTi"][:n, :n], ALU.mult, [pQK, g_["DTi"]], [g_["QKd"]])
                tt_op("dve", g_["Tt"][:n, :n], g_["R0"][:n, :n], ident[:n, :n], ALU.add, [g_["R0"], ident], [g_["Tt"]])

            def st_L(h, k):
                g_ = gth[h]
                Bm = [g_["B0"], g_["B1"]]; Rm = [g_["R0"], g_["R1"]]; Tt = g_["Tt"]
                a, b = (k - 1) % 2, k % 2
                pP = nps(); mm(pP[:n, :n], Rm[a][:n, :n], Bm[a][:n, :n], True, True, [Rm[a], Bm[a]], [pP])
                if k < nlev - 1:
                    pR = nps(); mm(pR[:n, :n], Bm[a][:n, :n], Rm[a][:n, :n], True, True, [Rm[a], Bm[a]], [pR])
                copy(Bm[b][:n, :n], pP[:n, :n], [pP], [Bm[b]], eng="act")
                if k < nlev - 1:
                    copy(Rm[b][:n, :n], pR[:n, :n], [pR], [Rm[b]], eng="dve")
                pT = nps(); mm(pT[:n, :n], Bm[b][:n, :n], Tt[:n, :n], True, True, [Bm[b], Tt], [pT])
                tt_op("dve", Tt[:n, :n], Tt[:n, :n], pT[:n, :n], ALU.add, [Tt, pT], [Tt])

            def st_V(h):
                g_ = gth[h]
                kT, vT = qc[4 + h], qc[8 + h]
                Tt = g_["Tt"]
                pV = nps(); mm(pV[:n, :128], vT[:, :n], ident[:, :], True, True, [vT, ident], [pV])
                copy(g_["X0v"][:n, :], pV[:n, :128], [pV, col["beta"]], [g_["X0v"]], eng="dve", scale=col["beta"][:n, h:h + 1])
                pK = nps(); mm(pK[:n, :128], kT[:, :n], ident[:, :], True, True, [kT, ident], [pK])
                copy(g_["X0k"][:n, :], pK[:n, :128], [pK, col["beG"]], [g_["X0k"]], eng="dve", scale=col["beG"][:n, h:h + 1])
                copy(g_["kg"][:n, :], pK[:n, :128], [pK, col["ekg"]], [g_["kg"]], eng="dve", scale=col["ekg"][:n, h:h + 1])
                pW = nps(); mm(pW[:, :n], g_["X0k"][:n, :], Tt[:n, :n], True, True, [g_["X0k"], Tt], [pW])
                copy(g_["nwT"][:, :n], pW[:, :n], [pW], [g_["nwT"]], eng="act", scale=-1.0)

            def st_N(h):
                g_ = gth[h]
                qT = qc[h]
                Tt = g_["Tt"]
                pN = nps()
                mm(pN[:n, :128], Tt[:n, :n], g_["X0v"][:n, :], True, False, [Tt, g_["X0v"]], [pN])
                mm(pN[:n, :128], g_["nwT"][:, :n], S[h][:, :], False, True, [g_["nwT"], S[h]], [pN])
                copy(g_["vnew"][:n, :], pN[:n, :128], [pN], [g_["vnew"]], eng="act")
                pA = nps(); mm(pA[:n, :128], qT[:, :n], S[h][:, :], True, True, [qT, S[h]], [pA])
                pB = nps(); mm(pB[:n, :128], g_["QKd"][:n, :n], g_["vnew"][:n, :], True, True, [g_["QKd"], g_["vnew"]], [pB])
                copy(g_["tmpB"][:n, :], pB[:n, :128], [pB], [g_["tmpB"]], eng="act")
                stt("dve", ofin[:n, h * 128:(h + 1) * 128], pA[:n, :128], col["eG"][:n, h:h + 1], g_["tmpB"][:n, :], ALU.mult, ALU.add, [pA, col["eG"], g_["tmpB"]], [ofin])
                pS = nps(); mm(pS[:, :128], g_["kg"][:n, :], g_["vnew"][:n, :], True, True, [g_["kg"], g_["vnew"]], [pS])
                stt("dve", S[h][:, :], S[h][:, :], col["egl"][:, h:h + 1], pS[:, :128], ALU.mult, ALU.add, [S[h], col["egl"], pS], [S[h]])

            for h in range(4):
                st_A(h)
            for k in range(1, nlev):
                for h in range(4):
                    st_L(h, k)
            for h in range(4):
                st_V(h)
            for h in range(4):
                st_N(h)
            for h in range(4):
                ss = rstd_of(ofin[:n, h * 128:(h + 1) * 128], n, 128, [ofin])
                stt("dve", ofin[:n, h * 128:(h + 1) * 128], ofin[:n, h * 128:(h + 1) * 128], ss[:n, 0:1], nw[:n, :], ALU.mult, ALU.mult, [ofin, ss, nw], [ofin])
                tt_op("dve", ofin[:n, h * 128:(h + 1) * 128], ofin[:n, h * 128:(h + 1) * 128], gate_s[:n, h * 128:(h + 1) * 128], ALU.mult, [ofin, gate_s], [ofin])
            ps = nps()
            for j in range(4):
                tr(ps[:, j * 128:j * 128 + n], ofin[:n, j * 128:(j + 1) * 128], n, [ofin], [ps])
            copy(mixT[:, 4:8, :n], ps[:, :].rearrange("p (j t) -> p j t", t=128)[:, :, :n], [ps], [mixT])
            for half in range(2):
                ps = nps()
                for c in range(8):
                    mm(ps[:n, :512], mixT[:, c, :n], Wout[c][:, half * 512:(half + 1) * 512], c == 0, c == 7, [mixT, Wout[c]], [ps])
                add_res(xt, n, half, ps)
            for g in range(4):
                copy(aext[g][:, 0:15], aext[g][:, n:n + 15], [aext[g]], [aext[g]], eng="pool")
            for c in range(12):
                copy(qext[c][:, 0:3], qext[c][:, n:n + 3], [qext[c]], [qext[c]], eng="pool")

        for g in range(4):
            fw.op("pool", lambda e, g=g: e.memset(aext[g][:, :], 0.0), writes=[aext[g]])
        for c in range(12):
            fw.op("pool", lambda e, c=c: e.memset(qext[c][:, :], 0.0), writes=[qext[c]])
        for h in range(4):
            fw.op("pool", lambda e, h=h: e.memset(S[h][:, :], 0.0), writes=[S[h]])
        for t in range(NT):
            xt = load_x(t, True)
            tile(xt, 128, t == 0, True)
            store_x(t, xt)
        rows_from_cols(o_pool_p[:, :], lambda c: aext[c][:, 0:15], aext, 15, 4)
        rows_from_cols(o_dconv_p[:, :], lambda c: qext[c][:, 0:3], qext, 3, 12)
        for h in range(4):
            fw.dma("pool", o_dn_p[h], S[h][:, :], reads=[S[h]], is_output=True)
        for s in range(NS):
            cols_from_rows(lambda c: aext[c][:, 0:15], aext, st_pool[s], 15, 4)
            cols_from_rows(lambda c: qext[c][:, 0:3], qext, st_dconv[s], 3, 12)
            for h in range(4):
                fw.dma("sp", S[h][:, :], st_dn[s, h], writes=[S[h]])
            xs_t = load_xs(s, True)
            tile(xs_t, 8, False, False)
            store_xs(s, xs_t)
            rows_from_cols(o_pool_s[s], lambda c: aext[c][:, 0:15], aext, 15, 4)
            rows_from_cols(o_dconv_s[s], lambda c: qext[c][:, 0:3], qext, 3, 12)
            for h in range(4):
                fw.dma("pool", o_dn_s[s, h], S[h][:, :], reads=[S[h]], is_output=True)

    def phase_B(l):
        off = 0
        sm_reset()
        Wq, off = wload("wq", w_mem["q"][l], off, 1024, 1024)
        Wk, off = wload("wk", w_mem["k"][l], off, 1024, 1024)
        Wv, off = wload("wv", w_mem["v"][l], off, 1024, 1024)
        Wo, off = wload("wo", w_mem["o"][l], off, 1024, 1024)
        KTp = sm("KTp", [128, 8, 256], BF16); Vaug = sm("Vaug", [128, 2, 4, 257], BF16)
        KTs = sm("KTs", [128, 8, 256], BF16); Vaugs = sm("Vaugs", [128, 2, 4, 257], BF16)
        fw.op("pool", lambda e: e.memset(Vaug[:, :, :, 256:257], 1.0), writes=[Vaug])
        fw.op("pool", lambda e: e.memset(Vaugs[:, :, :, 256:257], 1.0), writes=[Vaugs])
        qT = sm("qT", [128, 8, 128], BF16); pT = sm("pT", [128, 2, 128], BF16)
        otok = sm("otok", [128, 1024]); oT = sm("oT", [128, 8, 128], BF16)
        rd = sm("rd", [128, 1]); kvs = sm("kvs", [128, 512]); kraw = sm("kraw", [128, 1024])
        load_g(g_mem_kv[l:l + 1, :])
        for nb in range(2):
            xt = xt_rot[cnt["xt"] % 2]; cnt["xt"] += 1
            fw.dma("sp", xt[:, :], memp[nb * 128:(nb + 1) * 128, :], writes=[xt])
            mT = norm_T(xt, 128, gbc)
            for ec in range(8):
                ps = nps(); lin_fm(ps[:, :128], Wk, ec * 128, 128, mT, 128, ps)
                copy(KTp[:, ec, nb * 128:(nb + 1) * 128], ps[:, :128], [ps], [KTp])
            for half in range(2):
                ps = nps(); lin_tm(ps[:, :512], mT, Wk, half * 512, 512, 128, ps)
                copy(kvs[:, :], ps[:, :512], [ps], [kvs])
                fw.dma("pool", o_memk[l, nb * 128:(nb + 1) * 128, half * 512:(half + 1) * 512], kvs[:, :], reads=[kvs], is_output=True)
                ps = nps(); lin_tm(ps[:, :512], mT, Wv, half * 512, 512, 128, ps)
                copy(kvs[:, :], ps[:, :512], [ps], [kvs])
                copy(Vaug[:, nb, 2 * half:2 * half + 2, 0:256], ps[:, :512].rearrange("p (h e) -> p h e", e=256), [ps], [Vaug])
                fw.dma("pool", o_memv[l, nb * 128:(nb + 1) * 128, half * 512:(half + 1) * 512], kvs[:, :], reads=[kvs], is_output=True)
        load_g(g_mem_q[l:l + 1, :])

        def tile(xt, n, KT, VA):
            hT = norm_T(xt, n, gbc)
            for ec in range(8):
                ps = nps(); lin_fm(ps[:, :n], Wq, ec * 128, 128, hT, n, ps)
                copy(qT[:, ec, :n], ps[:, :n], [ps], [qT])
            for h in range(4):
                ps = nps()
                for nb in range(2):
                    for ec in range(2):
                        mm(ps[:, nb * 128:nb * 128 + n], KT[:, 2 * h + ec, nb * 128:(nb + 1) * 128], qT[:, 2 * h + ec, :n], ec == 0, ec == 1, [KT, qT], [ps])
                act(pT[:, :, :n], ps[:, 0:256].rearrange("p (b t) -> p b t", t=128)[:, :, :n], AF.Exp, [ps], [pT], scale=1.0 / 16.0)
                ps2 = nps()
                for nb in range(2):
                    mm(ps2[:n, 0:257], pT[:, nb, :n], VA[:, nb, h, :], nb == 0, nb == 1, [pT, VA], [ps2])
                fw.op("dve", lambda e, ps2=ps2: e.reciprocal(out=rd[:n, :], in_=ps2[:n, 256:257]), [ps2], [rd])
                copy(otok[:n, h * 256:(h + 1) * 256], ps2[:n, 0:256], [ps2, rd], [otok], scale=rd[:n, 0:1])
            to_fm(otok, n, oT)
            for half in range(2):
                ps = nps(); lin_tm(ps[:n, :512], oT, Wo, half * 512, 512, n, ps)
                add_res(xt, n, half, ps)

        for t in range(NT):
            xt = load_x(t, False)
            tile(xt, 128, KTp, Vaug)
            store_x(t, xt)
        for s in range(NS):
            for nb in range(2):
                fw.dma("sp", kraw[:, :], cmk[l, s, nb * 128:(nb + 1) * 128, :], writes=[kraw])
                for b in range(2):
                    ps = nps()
                    for j in range(4):
                        tr(ps[:, j * 128:(j + 1) * 128], kraw[:, (4 * b + j) * 128:(4 * b + j + 1) * 128], 128, [kraw], [ps])
                    copy(KTs[:, 4 * b:4 * b + 4, nb * 128:(nb + 1) * 128], ps[:, :].rearrange("p (j t) -> p j t", t=128), [ps], [KTs])
                fw.dma("pool", Vaugs[:, nb, :, 0:256], cmv[l, s, nb * 128:(nb + 1) * 128, :].rearrange("n (h e) -> n h e", e=256), writes=[Vaugs])
            xs_t = load_xs(s)
            tile(xs_t, 8, KTs, Vaugs)
            store_xs(s, xs_t)

    def phase_C(l, final):
        off = 0
        sm_reset()
        Wup, off = wload("wup", w_up[l], off, 1024, 5632)
        Wdn, off = wload("wdn", w_down[l], off, 2816, 1024)
        load_g(g_ffn[l:l + 1, :])
        cwf = sm("cwf", [128, 44, 3]); cols_from_rows(lambda c: cwf[:, c, :], cwf, ffn_cw[l], 3, 44)
        hist = sm("fhist", [128, 44, 2])
        actT = sm("actT", [128, 22, 512], BF16)
        extA = [sm("extA%d" % i, [128, 514]) for i in range(2)]
        extB = [sm("extB%d" % i, [128, 514]) for i in range(2)]
        tA = sm("ftA", [128, 512]); tB = sm("ftB", [128, 512])
        hTs = [hT_rot[0], hT_rot[1]]

        def stile(loaders, n, finish):
            ntl = len(loaders)
            Wd = ntl * n
            for i, ld in enumerate(loaders):
                xt = ld()
                norm_tok(xt, n, gbc)
                half_tt = hTs[(i * n) // 256]
                for b in range(0, 8, 4):
                    ps = nps()
                    for j in range(4):
                        tr(ps[:, j * 128:j * 128 + n], hn[:n, (b + j) * 128:(b + j + 1) * 128], n, [hn], [ps])
                    copy(hT4[:, b:b + 4, i * n:i * n + n], ps[:, 0:512].rearrange("p (j t) -> p j t", t=128)[:, :, :n], [ps], [half_tt])
            rd = [hTs[0]] + ([hTs[1]] if Wd > 256 else [])
            for j in range(22):
                ea, eb = extA[j % 2], extB[j % 2]
                for (c, ext, ev) in ((j, ea, "act"), (j + 22, eb, "act")):
                    ps = nps()
                    for kc in range(8):
                        mm(ps[:, :Wd], Wup[kc][:, c * 128:(c + 1) * 128], hT4[:, kc, :Wd], kc == 0, kc == 7, [Wup[kc]] + rd, [ps])
                    copy(ext[:, 0:2], hist[:, c, :], [hist], [ext], eng="pool")
                    copy(ext[:, 2:2 + Wd], ps[:, :Wd], [ps], [ext], eng=ev)
                    copy(hist[:, c, :], ext[:, Wd:Wd + 2], [ext], [hist], eng="pool")
                c2 = j + 22
                fw.op("act", lambda e, ea=ea, j=j: e.mul(out=tA[:, :Wd], in_=ea[:, 0:Wd], mul=cwf[:, j, 0:1]), [ea, cwf], [tA])
                fw.op("act", lambda e, eb=eb, c2=c2: e.mul(out=tB[:, :Wd], in_=eb[:, 0:Wd], mul=cwf[:, c2, 0:1]), [eb, cwf], [tB])
                for k in (1, 2):
                    stt("dve", tA[:, :Wd], ea[:, k:k + Wd], cwf[:, j, k:k + 1], tA[:, :Wd], ALU.mult, ALU.add, [ea, cwf, tA], [tA])
                    stt("dve", tB[:, :Wd], eb[:, k:k + Wd], cwf[:, c2, k:k + 1], tB[:, :Wd], ALU.mult, ALU.add, [eb, cwf, tB], [tB])
                act(tA[:, :Wd], tA[:, :Wd], AF.Silu, [tA], [tA])
                tt_op("dve", actT[:, j, :Wd], tA[:, :Wd], tB[:, :Wd], ALU.mult, [tA, tB], [actT])
            for i, ld in enumerate(loaders):
                xt = ld()
                for half in range(2):
                    ps = nps()
                    for j in range(22):
                        mm(ps[:n, :512], actT[:, j, i * n:i * n + n], Wdn[j][:, half * 512:(half + 1) * 512], j == 0, j == 21, [actT, Wdn[j]], [ps])
                    add_res(xt, n, half, ps)
                finish(i, xt)

        fw.op("pool", lambda e: e.memset(hist[:, :, :], 0.0), writes=[hist])
        for t0 in range(0, NT, 4):
            def fin(i, xt, t0=t0):
                t = t0 + i
                if final:
                    norm_tok(xt, 128, gfin)
                    fw.dma("sp", y_p[t * 128:(t + 1) * 128, :], hn[:, :], reads=[hn], writes=[xres[t]], is_output=True)
                else:
                    store_x(t, xt)
            stile([(lambda t=t0 + i: load_x(t, False)) for i in range(4)], 128, fin)
        rows_from_cols(o_ffn_p[l], lambda c: hist[:, c, :], hist, 2, 44)
        for s_ in range(NS):
            cols_from_rows(lambda c: hist[:, c, :], hist, st_ffn[l, s_], 2, 44)

            def fin_s(i, xt, s_=s_):
                if final:
                    norm_tok(xt, 8, gfin)
                    fw.dma("sp", y_s[s_ * 8:(s_ + 1) * 8, :], hn[:8, :], reads=[hn], writes=[xsres[s_]], is_output=True)
                else:
                    store_xs(s_, xt)
            stile([lambda s_=s_: load_xs(s_)], 8, fin_s)
            rows_from_cols(o_ffn_s[l, s_], lambda c: hist[:, c, :], hist, 2, 44)

    def phase_A1():
        off = 0
        sm_reset()
        Wqkv, off = wload("wqkv", w_qkv, off, 1024, 4608)
        Woc, off = wload("woc", w_out_c, off, 512, 1024)
        load_g(g_mix[1:2, :])
        qk = sm("qk_tok", [128, 512]); vt = sm("v_tok", [128, 512]); vbf = sm("v_bf", [128, 512], BF16)
        cs = sm("cs_t", [128, 8]); sn = sm("sn_t", [128, 8])
        r1 = sm("rp1", [128, 8, 8]); r2 = sm("rp2", [128, 8, 8]); r3 = sm("rp3", [128, 8, 8])
        fmT = sm("fmT", [128, 4, 128], BF16)
        mb = sm("mb", [128, 2, 128], BF16); fw.dma("pool", mb[:], c_mb.rearrange("a p n -> p a n"), writes=[mb])
        smk = sm("smk", [128, 24, 8], BF16); fw.dma("pool", smk[:], c_sm.rearrange("p (b t) -> p b t", t=8), writes=[smk])
        ident_bf = sm("ident_bf", [128, 128], BF16); copy(ident_bf[:], ident[:], [ident], [ident_bf], eng="dve")
        sQT = [[sm("sQT%d_%d" % (s, g), [128, 4, 8], BF16) for g in range(3)] for s in range(NS)]
        sKT = [[sm("sKT%d_%d" % (s, g), [128, 4, 8], BF16) for g in range(3)] for s in range(NS)]
        sV_d = nc.dram_tensor("sV_d", [NS, 3, 8, 512], BF16).ap()
        sV_tt = TT(None, "sV_tt")
        sVn = sm("sVn", [8, 512], BF16)

        def rope(n, pos0):
            fw.dma("sp", cs[:n, :], c_cos[pos0:pos0 + n, :], writes=[cs])
            fw.dma("sp", sn[:n, :], c_sin[pos0:pos0 + n, :], writes=[sn])
            X = qk[:n, :].rearrange("p (h e) -> p h e", e=64)
            x1 = X[:, :, 0:8]; x2 = X[:, :, 8:16]
            cb = cs[:n, :].unsqueeze(1).to_broadcast([n, 8, 8]); sb_ = sn[:n, :].unsqueeze(1).to_broadcast([n, 8, 8])
            tt_op("dve", r1[:n], x1, sb_, ALU.mult, [qk, sn], [r1])
            tt_op("dve", r2[:n], x2, sb_, ALU.mult, [qk, sn], [r2])
            tt_op("dve", r3[:n], x2, cb, ALU.mult, [qk, cs], [r3])
            tt_op("dve", x1, x1, cb, ALU.mult, [qk, cs], [qk])
            tt_op("dve", x1, x1, r2[:n], ALU.subtract, [qk, r2], [qk])
            tt_op("dve", x2, r3[:n], r1[:n], ALU.add, [r3, r1], [qk])

        def proj_tile(xt, n, pos0, t, s):
            hT = norm_T(xt, n, gbc)
            for g in range(3):
                W = WINS[g][0]
                for si in range(3):
                    ps = nps(); lin_tm(ps[:n, :512], hT, Wqkv, g * 1536 + si * 512, 512, n, ps)
                    if si < 2:
                        copy(qk[:n, :], ps[:n, :512], [ps], [qk])
                        rope(n, pos0)
                        to_fm(qk, n, fmT, nchunks=4)
                        if s is None:
                            dst = (QT_d if si == 0 else KT_d)[g]
                            fw.dma("pool", dst.rearrange("(c p) s -> p c s", p=128)[:, :, t * 128:(t + 1) * 128], fmT[:, :, :], reads=[fmT], writes=[qkv_tt[t]])
                        else:
                            copy((sQT if si == 0 else sKT)[s][g][:, :, :], fmT[:, :, :8], [fmT], [(sQT if si == 0 else sKT)[s][g]], eng="pool")
                        if si == 1:
                            if s is None:
                                lo = S - min(W, S)
                                if (t + 1) * 128 > lo:
                                    fw.dma("pool", o_win_p[g][t * 128 - lo:(t + 1) * 128 - lo, 0:512], qk[:, :], reads=[qk], is_output=True)
                            else:
                                fw.dma("pool", o_win_s[g][s, :, 0:512], qk[:8, :], reads=[qk], is_output=True)
                    else:
                        copy(vt[:n, :], ps[:n, :512], [ps], [vt])
                        copy(vbf[:n, :], ps[:n, :512], [ps], [vbf])
                        if s is None:
                            fw.dma("pool", V_d[g][t * 128:(t + 1) * 128, :], vbf[:, :], reads=[vbf], writes=[qkv_tt[t]])
                            lo = S - min(W, S)
                            if (t + 1) * 128 > lo:
                                fw.dma("pool", o_win_p[g][t * 128 - lo:(t + 1) * 128 - lo, 512:1024], vt[:, :], reads=[vt], is_output=True)
                        else:
                            fw.dma("sp", sV_d[s, g], vbf[:8, :], reads=[vbf], writes=[sV_tt])
                            fw.dma("pool", o_win_s[g][s, :, 512:1024], vt[:8, :], reads=[vt], is_output=True)

        for t in range(NT):
            xt = load_x(t, False)
            proj_tile(xt, 128, t * 128, t, None)
        for s in range(NS):
            proj_tile(load_xs(s), 8, S, None, s)

        print('A1 proj done at op', fw.total)
        HALF = 2048
        NH = S // HALF
        accN = barrier_tt(TT(arena_f[:, 0:4 * HALF].rearrange("p (c s) -> p c s", s=HALF), "accN"))
        accD = barrier_tt(TT(arena_f[:, 4 * HALF:8 * HALF].rearrange("p (c s) -> p c s", s=HALF), "accD"))
        QTs = barrier_tt(TT(arena.ap[:, 40960:40960 + 8192].rearrange("p (c s) -> p c s", s=2048), "QTs"))
        KTs_ = barrier_tt(TT(arena.ap[:, 49152:49152 + 16384].rearrange("p (c s) -> p c s", s=4096), "KTs"))
        Vst = [sm("Vst%d" % i, [128, 512], BF16) for i in range(2)]
        Vpad = [sm("Vpad%d" % i, [128, 2, 4, 2, 128], BF16) for i in range(2)]
        onespad = sm("onespad", [128, 2, 128], BF16)
        fw.op("pool", lambda e: e.memset(onespad[:], 0.0), writes=[onespad])
        fw.op("pool", lambda e: e.memset(onespad[:, 0, 0:64], 1.0), writes=[onespad])
        fw.op("pool", lambda e: e.memset(onespad[:, 1, 64:128], 1.0), writes=[onespad])
        for i in range(2):
            fw.op("pool", lambda e, i=i: e.memset(Vpad[i][:], 0.0), writes=[Vpad[i]])
        PT = [sm("PT%d" % i, [128, 2, 128], BF16) for i in range(2)]
        rec = sm("rec", [128, 4, 128]); oTb = sm("oTb", [128, 4, 128], BF16)
        vcnt = [0]
        for hf in range(NH):
            for g in range(3):
                W, d = WINS[g]
                SB = 128 * d
                for sbk in range(HALF // SB):
                    q_sb = hf * (HALF // SB) + sbk
                    base = q_sb * SB
                    lb = base - hf * HALF
                    fw.dma("sp", QTs[:, :, 0:SB], QT_d[g].rearrange("(c p) s -> p c s", p=128)[:, :, base:base + SB], reads=qkv_tt, writes=[QTs])
                    if q_sb > 0:
                        fw.dma("sp", KTs_[:, :, 0:2 * SB], KT_d[g].rearrange("(c p) s -> p c s", p=128)[:, :, base - SB:base + SB], reads=qkv_tt, writes=[KTs_])
                    else:
                        fw.dma("sp", KTs_[:, :, SB:2 * SB], KT_d[g].rearrange("(c p) s -> p c s", p=128)[:, :, base:base + SB], reads=qkv_tt, writes=[KTs_])
                    kbs = (0, 1) if q_sb > 0 else (1,)
                    for r in range(d):
                        vp = Vpad[vcnt[0] % 2]; vcnt[0] += 1
                        for kb in kbs:
                            p0 = base - SB + r if kb == 0 else base + r
                            vs = Vst[kb]
                            fw.dma("sp", vs[:, :], V_d[g][p0:p0 + 127 * d + 1:d, :], reads=qkv_tt, writes=[vs])
                            for hh in range(2):
                                copy(vp[:, kb, :, hh, hh * 64:(hh + 1) * 64], vs[:, :].rearrange("p (c h e) -> p c h e", h=2, e=64)[:, :, hh, :], [vs], [vp], eng="pool")
                        for c in range(4):
                            for hh in range(2):
                                ps = nps()
                                pl = slice(hh * 64, (hh + 1) * 64)
                                for kb in kbs:
                                    ksl = slice(r, SB, d) if kb == 0 else slice(SB + r, 2 * SB, d)
                                    mm(ps[:, kb * 128:(kb + 1) * 128], KTs_[pl, c, ksl], QTs[pl, c, r:SB:d], True, False, [KTs_, QTs], [ps])
                                    mm(ps[:, kb * 128:(kb + 1) * 128], ident_bf[:, :], mb[:, kb, :], False, True, [ident_bf, mb], [ps])
                                k0 = kbs[0]
                                act(PT[hh][:, k0:2, :], ps[:, k0 * 128:256].rearrange("p (b t) -> p b t", t=128), AF.Exp, [ps], [PT[hh]], scale=0.125)
                            psN = nps(); psD = nps()
                            seq = [(hh, kb) for hh in range(2) for kb in kbs]
                            for i, (hh, kb) in enumerate(seq):
                                mm(psN[:, :128], vp[:, kb, c, hh, :], PT[hh][:, kb, :], i == 0, i == len(seq) - 1, [vp, PT[hh]], [psN])
                            for i, (hh, kb) in enumerate(seq):
                                mm(psD[:, :128], onespad[:, hh, :], PT[hh][:, kb, :], i == 0, i == len(seq) - 1, [onespad, PT[hh]], [psD])
                            dN = accN[:, c, lb + r:lb + SB:d]; dD = accD[:, c, lb + r:lb + SB:d]
                            if g == 0:
                                copy(dN, psN[:, :128], [psN], [accN], eng="act")
                                copy(dD, psD[:, :128], [psD], [accD], eng="dve")
                            else:
                                tt_op("dve", dN, dN, psN[:, :128], ALU.add, [accN, psN], [accN])
                                tt_op("dve", dD, dD, psD[:, :128], ALU.add, [accD, psD], [accD])
            for tl in range(HALF // 128):
                t = hf * (HALF // 128) + tl
                fw.op("dve", lambda e, tl=tl: e.reciprocal(out=rec[:, :, :], in_=accD[:, :, tl * 128:(tl + 1) * 128]), [accD], [rec])
                tt_op("dve", oTb[:, :, :], accN[:, :, tl * 128:(tl + 1) * 128], rec[:, :, :], ALU.mult, [accN, rec], [oTb])
                xt = load_x(t, False)
                for half in range(2):
                    ps = nps()
                    for c in range(4):
                        mm(ps[:, :512], oTb[:, c, :], Woc[c][:, half * 512:(half + 1) * 512], c == 0, c == 3, [oTb, Woc[c]], [ps])
                    add_res(xt, 128, half, ps)
                store_x(t, xt)
        print('A1 prompt attn done at op', fw.total)
        craw = sm("craw", [128, 1024]); KTb = sm("KTb", [128, 4, 128], BF16)
        sVp = [sm("sVp%d" % i, [128, 4, 2, 128], BF16) for i in range(2)]
        for i in range(2):
            fw.op("pool", lambda e, i=i: e.memset(sVp[i][:], 0.0), writes=[sVp[i]])
        PTs = sm("PTs", [128, 8, 8], BF16)
        aN = sm("aNs", [128, 32]); aD = sm("aDs", [128, 32]); oTs = sm("oTs", [128, 32], BF16)
        bases = [0, 1, 5]
        bcnt = [0]
        for s in range(NS):
            xs_t = load_xs(s)
            first = True
            for g in range(3):
                W, d = WINS[g]
                nblk = W // 128
                for kb in range(nblk + 1):
                    newblk = (kb == nblk)
                    nk = 8 if newblk else 128
                    vp = sVp[bcnt[0] % 2]; bcnt[0] += 1
                    if not newblk:
                        fw.dma("sp", craw[:, :], cwin[g][s, kb * 128:(kb + 1) * 128, :], writes=[craw])
                        ps = nps()
                        for j in range(4):
                            tr(ps[:, j * 128:(j + 1) * 128], craw[:, j * 128:(j + 1) * 128], 128, [craw], [ps])
                        copy(KTb[:, :, :], ps[:, :].rearrange("p (j t) -> p j t", t=128), [ps], [KTb])
                        for hh in range(2):
                            copy(vp[:, :, hh, hh * 64:(hh + 1) * 64], craw[:, 512:1024].rearrange("p (c h e) -> p c h e", h=2, e=64)[:, :, hh, :], [craw], [vp], eng="pool")
                        ktt, kap = KTb, KTb
                        bi = bases[g] + kb
                    else:
                        fw.dma("sp", sVn[:8, :], sV_d[s, g], reads=[sV_tt], writes=[sVn])
                        for hh in range(2):
                            copy(vp[:8, :, hh, hh * 64:(hh + 1) * 64], sVn[:8, :].rearrange("p (c h e) -> p c h e", h=2, e=64)[:, :, hh, :], [sVn], [vp], eng="pool")
                        ktt, kap = sKT[s][g], sKT[s][g]
                        bi = 21 + g
                        k2 = sm("k2n", [64, 2, 4, 8], BF16); q2 = sm("q2n", [64, 2, 4, 8], BF16)
                        for hh in range(2):
                            copy(k2[:, hh, :, :], sKT[s][g][hh * 64:(hh + 1) * 64, :, :], [sKT[s][g]], [k2], eng="pool")
                            copy(q2[:, hh, :, :], sQT[s][g][hh * 64:(hh + 1) * 64, :, :], [sQT[s][g]], [q2], eng="pool")
                    ps = nps()
                    for c in range(4):
                        for hh in range(2):
                            pl = slice(hh * 64, (hh + 1) * 64)
                            hd = 2 * c + hh
                            if newblk:
                                mm(ps[:nk, hd * 8:(hd + 1) * 8], k2[:, hh, c, :nk], q2[:, hh, c, :8], True, True, [k2, q2], [ps])
                            else:
                                mm(ps[:nk, hd * 8:(hd + 1) * 8], kap[pl, c, :nk], sQT[s][g][pl, c, :8], True, True, [ktt, sQT[s][g]], [ps])
                    act(PTs[:nk, :, :], ps[:nk, 0:64].rearrange("p (h t) -> p h t", t=8), AF.Exp, [ps], [PTs], scale=0.125)
                    tt_op("dve", PTs[:nk, :, :], PTs[:nk, :, :], smk[:nk, bi, :].unsqueeze(1).to_broadcast([nk, 8, 8]), ALU.mult, [PTs, smk], [PTs])
                    psN = nps(); psD = nps()
                    for c in range(4):
                        for hh in range(2):
                            hd = 2 * c + hh
                            mm(psN[:, c * 8:(c + 1) * 8], vp[:nk, c, hh, :], PTs[:nk, hd, :], hh == 0, hh == 1, [vp, PTs], [psN])
                    for c in range(4):
                        for hh in range(2):
                            hd = 2 * c + hh
                            mm(psD[:, c * 8:(c + 1) * 8], onespad[:nk, hh, :], PTs[:nk, hd, :], hh == 0, hh == 1, [onespad, PTs], [psD])
                    if first:
                        copy(aN[:, :], psN[:, :32], [psN], [aN], eng="act")
                        copy(aD[:, :], psD[:, :32], [psD], [aD], eng="dve")
                        first = False
                    else:
                        tt_op("dve", aN[:, :], aN[:, :], psN[:, :32], ALU.add, [aN, psN], [aN])
                        tt_op("dve", aD[:, :], aD[:, :], psD[:, :32], ALU.add, [aD, psD], [aD])
            fw.op("dve", lambda e: e.reciprocal(out=aD[:, :], in_=aD[:, :]), [aD], [aD])
            tt_op("dve", oTs[:, :], aN[:, :], aD[:, :], ALU.mult, [aN, aD], [oTs])
            for half in range(2):
                ps = nps()
                for c in range(4):
                    mm(ps[:8, :512], oTs[:, c * 8:(c + 1) * 8], Woc[c][:, half * 512:(half + 1) * 512], c == 0, c == 3, [oTs, Woc[c]], [ps])
                add_res(xs_t, 8, half, ps)
            store_xs(s, xs_t)

    if 'A0' in phases: phase_A0()
    if 'B0' in phases: phase_B(0)
    if 'C0' in phases: phase_C(0, False)
    if 'A1' in phases: phase_A1()
    if 'B1' in phases: phase_B(1)
    if 'C1' in phases: phase_C(1, True)
    fw.finish()
    return nc, fw


def _consts(S):
    c = {}
    r = np.arange(128)[:, None]; cc = np.arange(128)[None, :]
    c["c_ident"] = np.eye(128, dtype=np.float32)
    c["c_mge"] = (BIG * (cc >= r)).astype(np.float32)
    c["c_nmle"] = (-BIG * (cc <= r)).astype(np.float32)
    c["c_nmlt"] = (-BIG * (cc < r)).astype(np.float32)
    c["c_LT"] = (r <= cc).astype(np.float32)
    invc = np.zeros((2, 128, 512), np.float32)
    for g, w in enumerate((2, 4, 8, 16)):
        pos = np.arange(128)
        invc[0, :, g * 128:(g + 1) * 128] = (1.0 / np.minimum(pos + 1, w))[None, :]
        invc[1, :, g * 128:(g + 1) * 128] = 1.0 / w
    c["c_invc"] = invc
    half = 8
    inv_freq = (np.float32(500000.0) ** (-np.arange(half, dtype=np.float32) / np.float32(half))).astype(np.float32)
    pos = np.concatenate([np.arange(S), 8192 + np.arange(8)]).astype(np.float32)
    ang = (pos[:, None] * inv_freq[None, :]).astype(np.float32)
    c["c_cos"] = np.cos(ang.astype(np.float64)).astype(np.float32)
    c["c_sin"] = np.sin(ang.astype(np.float64)).astype(np.float32)
    NEG = -240000.0
    j = np.arange(128)[:, None]; i = np.arange(128)[None, :]
    mb = np.zeros((2, 128, 128), np.float32)
    mb[0] = np.where(j >= i, 0.0, NEG)
    mb[1] = np.where(j <= i, 0.0, NEG)
    c["c_mb"] = mb
    smk = np.zeros((128, 24, 8), np.float32)
    bases = [0, 1, 5]
    for g, (W, d) in enumerate(WINS):
        for kb in range(W // 128):
            e = kb * 128 + np.arange(128)[:, None]
            t = np.arange(8)[None, :]
            diff = W + t - e
            smk[:, bases[g] + kb, :] = ((diff % d == 0) & (diff // d >= 1) & (diff // d <= 128)).astype(np.float32)
        p = np.arange(128)[:, None]; t = np.arange(8)[None, :]
        smk[:, 21 + g, :] = ((p <= t) & ((t - p) % d == 0) & (p < 8)).astype(np.float32)
    c["c_sm"] = smk.reshape(128, 24 * 8)
    return c


_CACHE = {}


def _core_inputs(inp, c, S, NS, consts):
    b = c % 4
    ss = slice(NS * c, NS * (c + 1))
    f = lambda a: np.ascontiguousarray(a, dtype=np.float32)
    m = {
        "xp": f(inp["x_prompt"][b]), "xs": f(inp["x_sample"][ss].reshape(NS * 8, 1024)), "memp": f(inp["mem_prompt"][b]),
        "st_pool": f(inp["state_pool"][0, ss]), "st_dconv": f(inp["state_dn_conv"][0, ss]), "st_dn": f(inp["state_dn"][0, ss]),
        "cw128": f(inp["cache_win_w128"][0, ss].reshape(NS, 128, 1024)),
        "cw512": f(inp["cache_win_w512"][0, ss].reshape(NS, 512, 1024)),
        "cw2048": f(inp["cache_win_w2048"][0, ss].reshape(NS, 2048, 1024)),
        "cmk": f(inp["cache_mem_k"][:, ss].reshape(2, NS, 256, 1024)), "cmv": f(inp["cache_mem_v"][:, ss].reshape(2, NS, 256, 1024)),
        "st_ffn": f(inp["state_ffn_conv"][:, ss]),
        "g_mix": f(inp["g_mix"]), "w_in": f(inp["w_in_ab"][0]), "w_pool": f(inp["w_pool"][0].reshape(512, 128)),
        "pool_scale": f(inp["pool_scale"]), "dn_conv_w": f(inp["dn_conv_w"][0]), "dn_a_log": f(inp["dn_a_log"]),
        "dn_dt_bias": f(inp["dn_dt_bias"]), "dn_norm_w": f(inp["dn_norm_w"]), "w_out_ab": f(inp["w_out_ab"][0]),
        "w_qkv": f(inp["w_qkv_c"][0]), "w_out_c": f(inp["w_out_c"][0]), "g_mem_q": f(inp["g_mem_q"]), "g_mem_kv": f(inp["g_mem_kv"]),
        "w_mem_q": f(inp["w_mem_q"]), "w_mem_k": f(inp["w_mem_k"]), "w_mem_v": f(inp["w_mem_v"]), "w_mem_o": f(inp["w_mem_o"]),
        "g_ffn": f(inp["g_ffn"]), "w_up": f(inp["w_up"]), "ffn_cw": f(inp["ffn_conv_w"]), "w_down": f(inp["w_down"]),
        "g_final": f(inp["g_final"].reshape(1, 1024)),
    }
    m.update(consts)
    return m


def kernel(**inp):
    S = inp["x_prompt"].shape[1]
    NS = 4
    if S not in _CACHE:
        _CACHE[S] = build(S, NS)[0]
    nc = _CACHE[S]
    consts = _consts(S)
    in_maps = [_core_inputs(inp, c, S, NS, consts) for c in range(8)]
    res = run_bass_kernel_spmd(nc, in_maps, core_ids=list(range(8)))
    R = res.results
    f = np.float32
    B = 4
    cat_p = lambda k: np.stack([R[b][k] for b in range(B)])
    cat_s = lambda k: np.concatenate([R[c][k] for c in range(8)], axis=0)
    y_p = cat_p("y_p")
    y_s = cat_s("y_s").reshape(32, 8, 1024)
    outs = [y_p, y_s,
            cat_p("o_pool_p")[None], cat_s("o_pool_s")[None],
            cat_p("o_dconv_p")[None], cat_s("o_dconv_s")[None],
            cat_p("o_dn_p")[None], cat_s("o_dn_s")[None]]
    for w, _ in WINS:
        wp = min(w, S)
        outs.append(cat_p("o_w%d_p" % w).reshape(1, B, wp, 2, 8, 64))
        outs.append(cat_s("o_w%d_s" % w).reshape(1, 32, 8, 2, 8, 64))
    outs.append(np.stack([R[b]["o_memk"] for b in range(B)], axis=1).reshape(2, B, 256, 4, 256))
    outs.append(np.stack([R[b]["o_memv"] for b in range(B)], axis=1).reshape(2, B, 256, 4, 256))
    outs.append(np.stack([R[b]["o_ffn_p"] for b in range(B)], axis=1))
    outs.append(np.concatenate([R[c]["o_ffn_s"] for c in range(8)], axis=1))
    return tuple(np.ascontiguousarray(o, dtype=f) for o in outs)
```

```python
import numpy as np
import concourse.bass as bass
import concourse.mybir as mybir

F32 = mybir.dt.float32
BF16 = mybir.dt.bfloat16
AF = mybir.ActivationFunctionType
ALU = mybir.AluOpType
AX = mybir.AxisListType


class TT:
    __slots__ = ("ap", "w", "r", "name", "psum")

    def __init__(self, ap, name=""):
        self.ap = ap
        self.psum = False
        self.w = None
        self.r = {}
        self.name = name

    def __getitem__(self, idx):
        return self.ap[idx]


class FW:
    ENG = ("pe", "act", "dve", "pool", "sp")

    def __init__(self, nc, ndma=6):
        self.nc = nc
        self.prog = {e: [] for e in self.ENG}
        self.cnt = {e: 0 for e in self.ENG if e != "sp"}
        self.seen = {e: {} for e in self.ENG}
        self.sems = {}
        self.ndma = ndma
        self.dma_rr = {"sp": 0, "pool": 0, "act": 0}
        self.dma_tot = {}
        self.ctx = []
        self.out_tokens = []
        import os
        self.maxops = int(os.environ.get("FW_MAXOPS", "100000000"))
        self.total = 0
        self.log = []

    def sem(self, key):
        if key not in self.sems:
            cm = self.nc.semaphore("s_%s" % str(key).replace(" ", ""))
            self.sems[key] = cm.__enter__()
            self.ctx.append(cm)
        return self.sems[key]

    def sb(self, name, shape, dt=F32):
        cm = self.nc.sbuf_tensor(name, list(shape), dt)
        t = cm.__enter__()
        self.ctx.append(cm)
        return TT(t, name)

    def ps(self, name, shape, dt=F32):
        cm = self.nc.psum_tensor(name, list(shape), dt)
        t = cm.__enter__()
        self.ctx.append(cm)
        tt = TT(t, name)
        tt.psum = True
        return tt

    def _waits(self, eng, reads, writes, skip_self_pe=False):
        need = {}
        for t in reads:
            if t.w is not None:
                k, v = t.w
                need[k] = max(need.get(k, 0), v)
            if t.psum:
                for k, v in t.r.items():
                    if k != eng:
                        need[k] = max(need.get(k, 0), v)
        for t in writes:
            if t.w is not None:
                k, v = t.w
                need[k] = max(need.get(k, 0), v)
            for k, v in t.r.items():
                need[k] = max(need.get(k, 0), v)
        out = []
        seen = self.seen[eng]
        for k, v in need.items():
            if eng == "pe" and k == "pe":
                continue
            if seen.get(k, 0) >= v:
                continue
            seen[k] = v
            out.append((k, v))
        return out

    def op(self, eng, fn, reads=(), writes=()):
        self.total += 1
        if self.total > self.maxops:
            return None
        reads = [t for t in reads if isinstance(t, TT)]
        writes = [t for t in writes if isinstance(t, TT)]
        waits = self._waits(eng, reads, writes)
        self.cnt[eng] += 1
        v = self.cnt[eng]
        self.prog[eng].append((waits, fn, (eng, v, 1)))
        tok = (eng, v)
        for t in writes:
            t.w = tok
            t.r = {}
        for t in reads:
            if t in writes:
                continue
            t.r[eng] = max(t.r.get(eng, 0), v)
        return tok

    def dma(self, q, out_ap, in_ap, reads=(), writes=(), is_output=False, **kw):
        self.total += 1
        if self.total > self.maxops:
            return None
        reads = [t for t in reads if isinstance(t, TT)]
        writes = [t for t in writes if isinstance(t, TT)]
        waits = self._waits(q, reads, writes)
        i = self.dma_rr[q]
        self.dma_rr[q] = (i + 1) % self.ndma
        key = "d%s%d" % (q, i)
        prev = self.dma_tot.get(key, 0)
        if prev > 0 and self.seen[q].get(key, 0) < prev:
            self.seen[q][key] = prev
            waits.append((key, prev))
        v = prev + 16
        self.dma_tot[key] = v
        fn = (lambda e, o=out_ap, a=in_ap, k=kw: e.dma_start(out=o, in_=a, **k))
        self.prog[q].append((waits, fn, (key, v, 16)))
        tok = (key, v)
        for t in writes:
            t.w = tok
            t.r = {}
        for t in reads:
            t.r[key] = max(t.r.get(key, 0), v)
        if is_output:
            self.out_tokens.append(tok)
        return tok

    def finish(self):
        nc = self.nc
        fin = {}
        for k, v in self.out_tokens:
            fin[k] = max(fin.get(k, 0), v)
        for k, v in self.dma_tot.items():
            fin[k] = max(fin.get(k, 0), v)
        for e, c in self.cnt.items():
            if c > 0:
                fin[e] = c
        for k in list(fin.keys()):
            self.sem(k)
        for e in self.ENG:
            for waits, fn, (k, v, inc) in self.prog[e]:
                self.sem(k)
                for (wk, wv) in waits:
                    self.sem(wk)
        sems = self.sems
        prog = self.prog

        def run(e, h, final=False):
            for waits, fn, (k, v, inc) in prog[e]:
                for (wk, wv) in waits:
                    h.wait_ge(sems[wk], wv)
                fn(h).then_inc(sems[k], inc)
            if final:
                for k, v in fin.items():
                    h.wait_ge(sems[k], v)

        with nc.Block() as block:
            @block.sync
            def _(h):
                run("sp", h, final=True)

            @block.tensor
            def _(h):
                run("pe", h)

            @block.scalar
            def _(h):
                run("act", h)

            @block.vector
            def _(h):
                run("dve", h)

            @block.gpsimd
            def _(h):
                run("pool", h)
        for cm in reversed(self.ctx):
            cm.__exit__(None, None, None)

from concourse.bass_utils import run_bass_kernel_spmd

BIG = 30000.0
EPS = 1e-6
WINS = ((128, 1), (512, 4), (2048, 16))


def build(S=4096, NS=4, phases=('A0', 'B0', 'C0', 'A1', 'B1', 'C1')):
    nc = bass.Bass("TRN2", target_bir_lowering=False)
    fw = FW(nc)
    NT = S // 128
    I_, O_ = {}, {}

    def din(name, shape):
        I_[name] = nc.dram_tensor(name, list(shape), F32, kind="ExternalInput").ap()
        return I_[name]

    def dout(name, shape):
        O_[name] = nc.dram_tensor(name, list(shape), F32, kind="ExternalOutput").ap()
        return O_[name]

    xp = din("xp", [S, 1024]); xs_d = din("xs", [NS * 8, 1024]); memp = din("memp", [256, 1024])
    st_pool = din("st_pool", [NS, 15, 512]); st_dconv = din("st_dconv", [NS, 3, 1536])
    st_dn = din("st_dn", [NS, 4, 128, 128])
    cwin = [din("cw%d" % w, [NS, w, 1024]) for w, _ in WINS]
    cmk = din("cmk", [2, NS, 256, 1024]); cmv = din("cmv", [2, NS, 256, 1024])
    st_ffn = din("st_ffn", [2, NS, 2, 5632])
    g_mix = din("g_mix", [2, 1024]); w_in = din("w_in", [1024, 2568]); w_pool = din("w_pool", [512, 128])
    pool_scale = din("pool_scale", [1, 512]); dn_conv_w = din("dn_conv_w", [4, 1536])
    dn_a_log = din("dn_a_log", [1, 4]); dn_dt_bias = din("dn_dt_bias", [1, 4]); dn_norm_w = din("dn_norm_w", [1, 128])
    w_out_ab = din("w_out_ab", [1024, 1024]); w_qkv = din("w_qkv", [1024, 4608]); w_out_c = din("w_out_c", [512, 1024])
    g_mem_q = din("g_mem_q", [2, 1024]); g_mem_kv = din("g_mem_kv", [2, 1024])
    w_mem = {k: din("w_mem_" + k, [2, 1024, 1024]) for k in "qkvo"}
    g_ffn = din("g_ffn", [2, 1024]); w_up = din("w_up", [2, 1024, 5632]); ffn_cw = din("ffn_cw", [2, 3, 5632])
    w_down = din("w_down", [2, 2816, 1024]); g_final = din("g_final", [1, 1024])
    c_ident = din("c_ident", [128, 128]); c_mge = din("c_mge", [128, 128]); c_nmle = din("c_nmle", [128, 128])
    c_nmlt = din("c_nmlt", [128, 128]); c_LT = din("c_LT", [128, 128])
    c_invc = din("c_invc", [2, 128, 512])
    c_cos = din("c_cos", [S + 8, 8]); c_sin = din("c_sin", [S + 8, 8])
    c_mb = din("c_mb", [2, 128, 128]); c_sm = din("c_sm", [128, 24 * 8])

    y_p = dout("y_p", [S, 1024]); y_s = dout("y_s", [NS * 8, 1024])
    o_pool_p = dout("o_pool_p", [15, 512]); o_pool_s = dout("o_pool_s", [NS, 15, 512])
    o_dconv_p = dout("o_dconv_p", [3, 1536]); o_dconv_s = dout("o_dconv_s", [NS, 3, 1536])
    o_dn_p = dout("o_dn_p", [4, 128, 128]); o_dn_s = dout("o_dn_s", [NS, 4, 128, 128])
    o_win_p = [dout("o_w%d_p" % w, [min(w, S), 1024]) for w, _ in WINS]
    o_win_s = [dout("o_w%d_s" % w, [NS, 8, 1024]) for w, _ in WINS]
    o_memk = dout("o_memk", [2, 256, 1024]); o_memv = dout("o_memv", [2, 256, 1024])
    o_ffn_p = dout("o_ffn_p", [2, 2, 5632]); o_ffn_s = dout("o_ffn_s", [2, NS, 2, 5632])

    QT_d = [nc.dram_tensor("QT%d" % g, [512, S], BF16).ap() for g in range(3)]
    KT_d = [nc.dram_tensor("KT%d" % g, [512, S], BF16).ap() for g in range(3)]
    V_d = [nc.dram_tensor("Vd%d" % g, [S, 512], BF16).ap() for g in range(3)]
    qkv_tt = [TT(None, "qkvscr%d" % t) for t in range(NT)]
    xres = [TT(None, "xres%d" % t) for t in range(NT)]

    rr = {"ps": 0, "ev": 0, "q": 0}
    PS = [fw.ps("ps%d" % i, [128, 512]) for i in range(8)]

    def nps():
        p = PS[rr["ps"] % 8]
        rr["ps"] += 1
        return p

    def ev_eng():
        rr["ev"] += 1
        return "act" if rr["ev"] % 2 else "dve"

    def copy(dst, src, reads, writes, eng=None, scale=None):
        eng = eng or ev_eng()
        if scale is not None and not isinstance(scale, float) and eng == "act":
            eng = "dve"
        if eng == "act":
            if scale is None:
                fw.op("act", lambda e: e.copy(out=dst, in_=src), reads, writes)
            else:
                fw.op("act", lambda e: e.mul(out=dst, in_=src, mul=scale), reads, writes)
        else:
            if scale is None:
                fw.op(eng, lambda e: e.tensor_copy(out=dst, in_=src), reads, writes)
            else:
                fw.op(eng, lambda e: e.tensor_scalar(out=dst, in0=src, scalar1=scale, scalar2=None, op0=ALU.mult), reads, writes)

    def tt_op(eng, dst, a, b, op, reads, writes):
        fw.op(eng, lambda e: e.tensor_tensor(out=dst, in0=a, in1=b, op=op), reads, writes)

    def stt(eng, dst, a, sc, b, op0, op1, reads, writes):
        if eng == "pool":
            tmp = sm("ptmp", [128, 128])
            nparts, nfree = a.shape[0], a.shape[1]
            fw.op("pool", lambda e: e.tensor_scalar(out=tmp[:nparts, :nfree], in0=a, scalar1=sc, scalar2=None, op0=op0), reads, [tmp])
            fw.op("pool", lambda e: e.tensor_tensor(out=dst, in0=tmp[:nparts, :nfree], in1=b, op=op1), list(reads) + [tmp], writes)
            return
        fw.op(eng, lambda e: e.scalar_tensor_tensor(out=dst, in0=a, scalar=sc, in1=b, op0=op0, op1=op1), reads, writes)

    def act(dst, src, func, reads, writes, bias=None, scale=None):
        kw = {}
        if bias is not None:
            kw["bias"] = bias
        if scale is not None:
            kw["scale"] = scale
        fw.op("act", lambda e: e.activation(out=dst, in_=src, func=func, **kw), reads, writes)

    def mm(ps_ap, lhsT, rhs, start, stop, reads, writes):
        fw.op("pe", lambda e: e.matmul(ps_ap, lhsT=lhsT, rhs=rhs, start=start, stop=stop), reads, writes)

    def tr(ps_ap, in_ap, k, reads, writes):
        fw.op("pe", lambda e: e.transpose(ps_ap, in_ap, ident[:k, :k]), list(reads) + [ident], writes)

    ident = fw.sb("ident", [128, 128]); fw.dma("sp", ident[:], c_ident[:], writes=[ident])
    m_ge = fw.sb("m_ge", [128, 128]); fw.dma("sp", m_ge[:], c_mge[:], writes=[m_ge])
    nm_le = fw.sb("nm_le", [128, 128]); fw.dma("sp", nm_le[:], c_nmle[:], writes=[nm_le])
    nm_lt = fw.sb("nm_lt", [128, 128]); fw.dma("sp", nm_lt[:], c_nmlt[:], writes=[nm_lt])
    LT = fw.sb("LT", [128, 128]); fw.dma("sp", LT[:], c_LT[:], writes=[LT])
    ones = fw.sb("ones", [128, 128]); fw.op("pool", lambda e: e.memset(ones[:], 1.0), writes=[ones])
    gfin = fw.sb("gfin", [128, 1024]); fw.dma("sp", gfin[:], g_final[0:1, :].partition_broadcast(128) if False else g_final.to_broadcast([128, 1024]), writes=[gfin])
    gbc = fw.sb("gbc", [128, 1024])

    def load_g(src_row):
        fw.dma("sp", gbc[:], src_row.to_broadcast([128, 1024]), writes=[gbc])

    xsres = [TT(None, "xsres%d" % s) for s in range(NS)]

    ARENA = 67584
    arena = fw.sb("arena", [128, ARENA], BF16)
    arena_f = arena.ap.bitcast(F32)

    def barrier_tt(tt):
        tt.r = dict(fw.cnt)
        for k, v in fw.dma_tot.items():
            tt.r[k] = v
        return tt

    wl = {"i": 0}

    def wload(name, dram2d, off, K, N):
        KC = K // 128
        tt = barrier_tt(TT(arena.ap[:, off:off + KC * N].rearrange("p (k n) -> p k n", n=N), name))
        chunks = []
        stg = [xt_rot[0], xt_rot[1], hn]
        for kc in range(KC):
            c = TT(tt.ap[:, kc, :], "%s_%d" % (name, kc))
            c.r = dict(tt.r)
            if N < 64:
                fw.dma("pool", c.ap, dram2d[kc * 128:(kc + 1) * 128, :], writes=[c])
            else:
                eng = "act" if kc % 2 else "dve"
                for c0 in range(0, N, 1024):
                    w_ = min(1024, N - c0)
                    st = stg[wl["i"] % 3]; wl["i"] += 1
                    fw.dma("sp", st[:, :w_], dram2d[kc * 128:(kc + 1) * 128, c0:c0 + w_], writes=[st])
                    if eng == "act":
                        fw.op("act", lambda e, st=st, c=c, c0=c0, w_=w_: e.copy(out=c.ap[:, c0:c0 + w_], in_=st[:, :w_]), [st], [c])
                    else:
                        fw.op("dve", lambda e, st=st, c=c, c0=c0, w_=w_: e.tensor_copy(out=c.ap[:, c0:c0 + w_], in_=st[:, :w_]), [st], [c])
            chunks.append(c)
        return chunks, off + KC * N

    xt_rot = [fw.sb("xt%d" % i, [128, 1024]) for i in range(2)]
    hn = fw.sb("hn", [128, 1024])
    hT4 = fw.sb("hT4", [128, 8, 512], BF16)
    hT_rot = [TT(hT4.ap[:, :, 0:256], "hTa"), TT(hT4.ap[:, :, 256:512], "hTb")]
    ssq = [fw.sb("ssq%d" % i, [128, 1]) for i in range(2)]
    cnt = {"xt": 0, "hT": 0, "ss": 0}

    def rstd_of(src_ap, ntok, width, reads):
        ss = ssq[cnt["ss"] % 2]; cnt["ss"] += 1
        act(hn[:ntok, :width], src_ap, AF.Square, reads, [hn])
        fw.op("dve", lambda e: e.reduce_sum(out=ss[:ntok, :], in_=hn[:ntok, :width], axis=AX.X), [hn], [ss])
        act(ss[:ntok, :], ss[:ntok, :], AF.Sqrt, [ss], [ss], bias=EPS, scale=1.0 / width)
        fw.op("dve", lambda e: e.reciprocal(out=ss[:ntok, :], in_=ss[:ntok, :]), [ss], [ss])
        return ss

    def norm_tok(xt, ntok, g_tt):
        ss = rstd_of(xt[:ntok, :], ntok, 1024, [xt])
        stt("dve", hn[:ntok, :], xt[:ntok, :], ss[:ntok, 0:1], g_tt[:ntok, :], ALU.mult, ALU.mult, [xt, ss, g_tt], [hn])
        return hn

    def to_fm(src, ntok, dst, nchunks=8, c0=0):
        for b in range(0, nchunks, 4):
            nb = min(4, nchunks - b)
            ps = nps()
            for j in range(nb):
                tr(ps[:, j * 128:j * 128 + ntok], src[:ntok, (b + j) * 128:(b + j + 1) * 128], ntok, [src], [ps])
            copy(dst[:, b:b + nb, c0:c0 + ntok], ps[:, 0:nb * 128].rearrange("p (j t) -> p j t", t=128)[:, :, :ntok], [ps], [dst])

    def norm_T(xt, ntok, g_tt):
        norm_tok(xt, ntok, g_tt)
        hT = hT_rot[cnt["hT"] % 2]; cnt["hT"] += 1
        to_fm(hn, ntok, hT)
        return hT

    def lin_fm(ps_ap, W, col0, M, hT, ntok, wr):
        KC = len(W)
        for kc in range(KC):
            mm(ps_ap, W[kc][:, col0:col0 + M], hT[:, kc, :ntok], kc == 0, kc == KC - 1, [W[kc], hT], [wr])

    def lin_tm(ps_ap, hT, W, col0, N, ntok, wr, o=0, rd=None):
        KC = len(W)
        for kc in range(KC):
            mm(ps_ap, hT[:, kc, o:o + ntok], W[kc][:, col0:col0 + N], kc == 0, kc == KC - 1, [W[kc]] + (rd if rd is not None else [hT]), [wr])

    STW = 1024
    stage = fw.sb("stage", [16, STW])
    stage_o = fw.sb("stage_o", [16, STW])

    def cols_from_rows(dst_fn, dst_tts, src_dram, k, C):
        per = STW // 128
        for c0 in range(0, C, per):
            n = min(per, C - c0)
            fw.dma("sp", stage[:k, :n * 128], src_dram[:, c0 * 128:(c0 + n) * 128], writes=[stage])
            ps = nps()
            for j in range(n):
                tr(ps[:, j * 16:j * 16 + k], stage[:k, j * 128:(j + 1) * 128], k, [stage], [ps])
            for j in range(n):
                copy(dst_fn(c0 + j), ps[:, j * 16:j * 16 + k], [ps], [dst_tts[c0 + j] if isinstance(dst_tts, list) else dst_tts])

    def rows_from_cols(dst_dram, src_fn, src_tts, k, C):
        per = STW // 128
        for p0 in range(0, C, per):
            np_ = min(per, C - p0)
            for c0 in range(p0, p0 + np_, 4):
                n = min(4, p0 + np_ - c0)
                ps = nps()
                for j in range(n):
                    st = src_tts[c0 + j] if isinstance(src_tts, list) else src_tts
                    fw.op("pe", lambda e, j=j, c=c0 + j, ps=ps: e.transpose(ps[:k, j * 128:(j + 1) * 128], src_fn(c), ident[:, :]), [st, ident], [ps])
                copy(stage_o[:k, (c0 - p0) * 128:(c0 - p0 + n) * 128], ps[:k, :n * 128], [ps], [stage_o])
            fw.dma("pool", dst_dram[:, p0 * 128:(p0 + np_) * 128], stage_o[:k, :np_ * 128], reads=[stage_o], is_output=True)

    def load_x(t, layer0):
        xt = xt_rot[cnt["xt"] % 2]; cnt["xt"] += 1
        if layer0:
            fw.dma("sp", xt[:, :], xp[t * 128:(t + 1) * 128, :], writes=[xt])
        else:
            fw.dma("sp", xt[:, :], y_p[t * 128:(t + 1) * 128, :], reads=[xres[t]], writes=[xt])
        return xt

    def load_xs(s_, layer0=False):
        xt = xt_rot[cnt["xt"] % 2]; cnt["xt"] += 1
        if layer0:
            fw.dma("sp", xt[:8, :], xs_d[s_ * 8:(s_ + 1) * 8, :], writes=[xt])
        else:
            fw.dma("sp", xt[:8, :], y_s[s_ * 8:(s_ + 1) * 8, :], reads=[xsres[s_]], writes=[xt])
        return xt

    def store_xs(s_, xt):
        fw.dma("sp", y_s[s_ * 8:(s_ + 1) * 8, :], xt[:8, :], reads=[xt], writes=[xsres[s_]], is_output=True)

    def store_x(t, xt):
        fw.dma("sp", y_p[t * 128:(t + 1) * 128, :], xt[:, :], reads=[xt], writes=[xres[t]], is_output=True)

    def add_res(xt, ntok, half, ps):
        tt_op("dve", xt[:ntok, half * 512:(half + 1) * 512], xt[:ntok, half * 512:(half + 1) * 512], ps[:ntok, :512], ALU.add, [xt, ps], [xt])

    SCRB = 37632
    scratch = fw.sb("scratch", [128, SCRB // 2], BF16)
    scr_f = scratch.ap.bitcast(F32)
    small = {}
    scr = {"off": 0}

    def sm_reset():
        print('scratch used', scr['off'])
        small.clear()
        scr["off"] = 0

    def sm(name, shape, dt=F32):
        if name in small:
            return small[name]
        esz = 4 if dt == F32 else 2
        n = 1
        for d_ in shape[1:]:
            n *= d_
        nbytes = (n * esz + 7) // 8 * 8
        off = scr["off"]
        assert off + nbytes <= SCRB, ("scratch overflow", name, off, nbytes)
        scr["off"] = off + nbytes
        base = scr_f if dt == F32 else scratch.ap
        ap = base[:shape[0], off // esz: off // esz + n]
        if len(shape) == 3:
            ap = ap.rearrange("p (a b) -> p a b", b=shape[2])
        elif len(shape) == 4:
            ap = ap.rearrange("p (a b c) -> p a b c", b=shape[2], c=shape[3])
        elif len(shape) == 5:
            ap = ap.rearrange("p (a b c d) -> p a b c d", b=shape[2], c=shape[3], d=shape[4])
        tt = barrier_tt(TT(ap, name))
        small[name] = tt
        return tt

    def phase_A0():
        off = 0
        sm_reset()
        Win, off = wload("w_in", w_in[:, 0:2560], off, 1024, 2560)
        Wab, off = wload("w_ab", w_in[:, 2560:2568], off, 1024, 8)
        Wout, off = wload("w_outab", w_out_ab, off, 1024, 1024)
        Wpool, off = wload("w_pool", w_pool, off, 512, 128)
        load_g(g_mix[0:1, :])
        cw = sm("dn_cw", [128, 12, 4]); cols_from_rows(lambda c: cw[:, c, :], cw, dn_conv_w, 4, 12)
        psc = sm("pool_sc", [128, 4, 1]); cols_from_rows(lambda c: psc[:, c, :], psc, pool_scale, 1, 4)
        dtb = sm("dtb", [128, 4]); fw.dma("sp", dtb[:], dn_dt_bias.to_broadcast([128, 4]), writes=[dtb])
        nea = sm("nea", [128, 4]); fw.dma("sp", nea[:], dn_a_log.to_broadcast([128, 4]), writes=[nea])
        act(nea[:], nea[:], AF.Exp, [nea], [nea])
        copy(nea[:], nea[:], [nea], [nea], eng="dve", scale=-1.0)
        nw = sm("normw", [128, 128]); fw.dma("sp", nw[:], dn_norm_w.to_broadcast([128, 128]), writes=[nw])
        invc = sm("invc", [128, 512]); fw.dma("sp", invc[:], c_invc[0], writes=[invc])
        aext = []; qext = []
        xoff = 22400
        for g in range(4):
            aext.append(barrier_tt(TT(arena_f[:, xoff:xoff + 527], "aext%d" % g))); xoff += 527
        for c in range(12):
            qext.append(barrier_tt(TT(arena_f[:, xoff:xoff + 515], "qext%d" % c))); xoff += 515
        assert xoff <= 33792
        qc = [sm("qc%d" % c, [128, 128]) for c in range(12)]
        S = [sm("S%d" % h, [128, 128]) for h in range(4)]
        mixT = sm("mixT", [128, 8, 128], BF16)
        gate_s = sm("gate_s", [128, 512]); ofin = sm("ofin", [128, 512])
        pa = [sm("pa%d" % i, [128, 143]) for i in range(2)]
        zb = sm("zb", [128, 128], BF16)
        col = {n: sm("col_" + n, [128, 4]) for n in ["beta", "lnb", "g", "G", "negG", "Gl", "eG", "beG", "nbeta", "ekg", "glast", "egl"]}
        gt = {n: sm("gt_" + n, [128, 128]) for n in ["sq", "rn"]}
        gnames = ["Ds", "DTb", "DTi", "B0", "B1", "R0", "R1", "Tt", "QKd", "X0v", "X0k", "kg", "nwT", "vnew", "tmpB"]
        gth = []
        aoff = 14720
        for h_ in range(4):
            d_ = {}
            for nm in gnames:
                d_[nm] = barrier_tt(TT(arena_f[:, aoff:aoff + 128], "gt%d_%s" % (h_, nm)))
                aoff += 128
            gth.append(d_)

        def tile(xt, ntok, tile0, o, rd):
            n = ntok
            ps = nps(); lin_tm(ps[:n, :512], hT4, Win, 2048, 512, n, ps, o=o, rd=rd)
            act(gate_s[:n, :], ps[:n, :512], AF.Silu, [ps], [gate_s])
            ps = nps(); lin_tm(ps[:n, :8], hT4, Wab, 0, 8, n, ps, o=o, rd=rd)
            act(col["beta"][:n, :], ps[:n, 0:4], AF.Sigmoid, [ps], [col["beta"]])
            tt_op("dve", col["g"][:n, :], ps[:n, 4:8], dtb[:n, :], ALU.add, [ps, dtb], [col["g"]])
            act(col["g"][:n, :], col["g"][:n, :], AF.Exp, [col["g"]], [col["g"]])
            act(col["g"][:n, :], col["g"][:n, :], AF.Ln, [col["g"]], [col["g"]], bias=1.0)
            tt_op("dve", col["g"][:n, :], col["g"][:n, :], nea[:n, :], ALU.mult, [col["g"], nea], [col["g"]])
            act(col["lnb"][:n, :], col["beta"][:n, :], AF.Ln, [col["beta"]], [col["lnb"]])
            ps = nps()
            mm(ps[:n, 0:4], LT[:n, :n], col["g"][:n, :], True, True, [LT, col["g"]], [ps])
            copy(col["G"][:n, :], ps[:n, 0:4], [ps], [col["G"]], eng="dve")
            ps = nps()
            mm(ps[:, 0:4], ones[:n, :], col["g"][:n, :], True, True, [ones, col["g"]], [ps])
            copy(col["glast"][:, :], ps[:, 0:4], [ps], [col["glast"]], eng="dve")
            act(col["egl"][:, :], col["glast"][:, :], AF.Exp, [col["glast"]], [col["egl"]])
            copy(col["negG"][:n, :], col["G"][:n, :], [col["G"]], [col["negG"]], eng="dve", scale=-1.0)
            tt_op("dve", col["Gl"][:n, :], col["G"][:n, :], col["lnb"][:n, :], ALU.add, [col["G"], col["lnb"]], [col["Gl"]])
            act(col["eG"][:n, :], col["G"][:n, :], AF.Exp, [col["G"]], [col["eG"]])
            tt_op("dve", col["beG"][:n, :], col["eG"][:n, :], col["beta"][:n, :], ALU.mult, [col["eG"], col["beta"]], [col["beG"]])
            copy(col["nbeta"][:n, :], col["beta"][:n, :], [col["beta"]], [col["nbeta"]], eng="dve", scale=-1.0)
            tt_op("dve", col["ekg"][:n, :], col["glast"][:n, :], col["G"][:n, :], ALU.subtract, [col["glast"], col["G"]], [col["ekg"]])
            act(col["ekg"][:n, :], col["ekg"][:n, :], AF.Exp, [col["ekg"]], [col["ekg"]])
            Lx = 15 + n
            for g in range(4):
                cur = aext[g]
                cb = o
                sh = 1
                for step in range(g + 1):
                    dst = pa[step % 2]
                    eng = "dve"
                    tt_op(eng, dst[:, sh:Lx], cur[:, cb + sh:cb + Lx], cur[:, cb:cb + Lx - sh], ALU.add, [cur], [dst])
                    cur = dst
                    cb = 0
                    sh *= 2
                nxt = pa[(g + 1) % 2]
                if tile0:
                    tt_op("dve", nxt[:, 15:Lx], cur[:, 15:Lx], invc[:, g * 128:g * 128 + n], ALU.mult, [cur, invc], [nxt])
                else:
                    copy(nxt[:, 15:Lx], cur[:, 15:Lx], [cur], [nxt], eng="dve", scale=1.0 / (2 ** (g + 1)))
                tt_op("dve", zb[:, :n], nxt[:, 15:Lx], aext[g][:, o + 15:o + Lx], ALU.subtract, [nxt, aext[g]], [zb])
                ps = nps()
                mm(ps[:, :n], Wpool[g][:, :], zb[:, :n], True, True, [Wpool[g], zb], [ps])
                copy(mixT[:, g, :n], ps[:, :n], [ps], [mixT], eng="act", scale=psc[:, g, 0:1])
            for c in range(12):
                fw.op("act", lambda e, c=c: e.mul(out=qc[c][:, :n], in_=qext[c][:, o:o + n], mul=cw[:, c, 0:1]), [qext[c], cw], [qc[c]])
                for k in range(1, 4):
                    stt("dve", qc[c][:, :n], qext[c][:, o + k:o + k + n], cw[:, c, k:k + 1], qc[c][:, :n], ALU.mult, ALU.add, [qext[c], cw, qc[c]], [qc[c]])
                act(qc[c][:, :n], qc[c][:, :n], AF.Silu, [qc[c]], [qc[c]])
            for c in range(8):
                act(gt["sq"][:, :n], qc[c][:, :n], AF.Square, [qc[c]], [gt["sq"]])
                ps = nps()
                mm(ps[:, :n], ones[:, :], gt["sq"][:, :n], True, True, [ones, gt["sq"]], [ps])
                sc = 128.0 if c < 4 else 1.0
                act(gt["rn"][:, :n], ps[:, :n], AF.Sqrt, [ps], [gt["rn"]], bias=EPS * sc, scale=sc)
                fw.op("dve", lambda e: e.reciprocal(out=gt["rn"][:, :n], in_=gt["rn"][:, :n]), [gt["rn"]], [gt["rn"]])
                tt_op("dve", qc[c][:, :n], qc[c][:, :n], gt["rn"][:, :n], ALU.mult, [qc[c], gt["rn"]], [qc[c]])
            nlev = 7 if n == 128 else 3
            PSH = [dict() for _ in range(4)]

            def st_A(h):
                g_ = gth[h]
                qT, kT = qc[h], qc[4 + h]
                pKK = nps(); mm(pKK[:n, :n], kT[:, :n], kT[:, :n], True, True, [kT], [pKK])
                pQK = nps(); mm(pQK[:n, :n], kT[:, :n], qT[:, :n], True, True, [kT, qT], [pQK])
                p1 = nps()
                mm(p1[:n, :n], col["G"][:n, h:h + 1].to_broadcast([n, n]), ident[:n, :n], True, False, [col["G"], ident], [p1])
                mm(p1[:n, :n], ident[:n, :n], m_ge[:n, :n], False, True, [ident, m_ge], [p1])
                act(g_["Ds"][:n, :n], p1[:n, :n], AF.Exp, [p1, col["G"]], [g_["Ds"]], bias=col["G"][:n, h:h + 1], scale=-1.0)
                p2 = nps()
                mm(p2[:n, :n], col["Gl"][:n, h:h + 1].to_broadcast([n, n]), ident[:n, :n], True, False, [col["Gl"], ident], [p2])
                mm(p2[:n, :n], ident[:n, :n], nm_le[:n, :n], False, True, [ident, nm_le], [p2])
                act(g_["DTb"][:n, :n], p2[:n, :n], AF.Exp, [p2, col["negG"]], [g_["DTb"]], bias=col["negG"][:n, h:h + 1], scale=1.0)
                p3 = nps()
                mm(p3[:n, :n], col["G"][:n, h:h + 1].to_broadcast([n, n]), ident[:n, :n], True, False, [col["G"], ident], [p3])
                mm(p3[:n, :n], ident[:n, :n], nm_lt[:n, :n], False, True, [ident, nm_lt], [p3])
                act(g_["DTi"][:n, :n], p3[:n, :n], AF.Exp, [p3, col["negG"]], [g_["DTi"]], bias=col["negG"][:n, h:h + 1], scale=1.0)
                stt("dve", g_["B0"][:n, :n], pKK[:n, :n], col["nbeta"][:n, h:h + 1], g_["Ds"][:n, :n], ALU.mult, ALU.mult, [pKK, col["nbeta"], g_["Ds"]], [g_["B0"]])
                stt("dve", g_["R0"][:n, :n], pKK[:n, :n], -1.0, g_["DTb"][:n, :n], ALU.mult, ALU.mult, [pKK, g_["DTb"]], [g_["R0"]])
                tt_op("dve", g_["QKd"][:n, :n], pQK[:n, :n], g_["DTi"][:n, :n], ALU.mult, [pQK, g_["DTi"]], [g_["QKd"]])
                tt_op("dve", g_["Tt"][:n, :n], g_["R0"][:n, :n], ident[:n, :n], ALU.add, [g_["R0"], ident], [g_["Tt"]])

            def st_L(h, k):
                g_ = gth[h]
                Bm = [g_["B0"], g_["B1"]]; Rm = [g_["R0"], g_["R1"]]; Tt = g_["Tt"]
                a, b = (k - 1) % 2, k % 2
                pP = nps(); mm(pP[:n, :n], Rm[a][:n, :n], Bm[a][:n, :n], True, True, [Rm[a], Bm[a]], [pP])
                if k < nlev - 1:
                    pR = nps(); mm(pR[:n, :n], Bm[a][:n, :n], Rm[a][:n, :n], True, True, [Rm[a], Bm[a]], [pR])
                copy(Bm[b][:n, :n], pP[:n, :n], [pP], [Bm[b]], eng="act")
                if k < nlev - 1:
                    copy(Rm[b][:n, :n], pR[:n, :n], [pR], [Rm[b]], eng="dve")
                pT = nps(); mm(pT[:n, :n], Bm[b][:n, :n], Tt[:n, :n], True, True, [Bm[b], Tt], [pT])
                tt_op("dve", Tt[:n, :n], Tt[:n, :n], pT[:n, :n], ALU.add, [Tt, pT], [Tt])

            def st_V(h):
                g_ = gth[h]
                kT, vT = qc[4 + h], qc[8 + h]
                Tt = g_["Tt"]
                pV = nps(); mm(pV[:n, :128], vT[:, :n], ident[:, :], True, True, [vT, ident], [pV])
                copy(g_["X0v"][:n, :], pV[:n, :128], [pV, col["beta"]], [g_["X0v"]], eng="dve", scale=col["beta"][:n, h:h + 1])
                pK = nps(); mm(pK[:n, :128], kT[:, :n], ident[:, :], True, True, [kT, ident], [pK])
                copy(g_["X0k"][:n, :], pK[:n, :128], [pK, col["beG"]], [g_["X0k"]], eng="dve", scale=col["beG"][:n, h:h + 1])
                copy(g_["kg"][:n, :], pK[:n, :128], [pK, col["ekg"]], [g_["kg"]], eng="dve", scale=col["ekg"][:n, h:h + 1])
                pW = nps(); mm(pW[:, :n], g_["X0k"][:n, :], Tt[:n, :n], True, True, [g_["X0k"], Tt], [pW])
                copy(g_["nwT"][:, :n], pW[:, :n], [pW], [g_["nwT"]], eng="act", scale=-1.0)

            def st_N(h):
                g_ = gth[h]
                qT = qc[h]
                Tt = g_["Tt"]
                pN = nps()
                mm(pN[:n, :128], Tt[:n, :n], g_["X0v"][:n, :], True, False, [Tt, g_["X0v"]], [pN])
                mm(pN[:n, :128], g_["nwT"][:, :n], S[h][:, :], False, True, [g_["nwT"], S[h]], [pN])
                copy(g_["vnew"][:n, :], pN[:n, :128], [pN], [g_["vnew"]], eng="act")
                pA = nps(); mm(pA[:n, :128], qT[:, :n], S[h][:, :], True, True, [qT, S[h]], [pA])
                pB = nps(); mm(pB[:n, :128], g_["QKd"][:n, :n], g_["vnew"][:n, :], True, True, [g_["QKd"], g_["vnew"]], [pB])
                copy(g_["tmpB"][:n, :], pB[:n, :128], [pB], [g_["tmpB"]], eng="act")
                stt("dve", ofin[:n, h * 128:(h + 1) * 128], pA[:n, :128], col["eG"][:n, h:h + 1], g_["tmpB"][:n, :], ALU.mult, ALU.add, [pA, col["eG"], g_["tmpB"]], [ofin])
                pS = nps(); mm(pS[:, :128], g_["kg"][:n, :], g_["vnew"][:n, :], True, True, [g_["kg"], g_["vnew"]], [pS])
                stt("dve", S[h][:, :], S[h][:, :], col["egl"][:, h:h + 1], pS[:, :128], ALU.mult, ALU.add, [S[h], col["egl"], pS], [S[h]])

            for h in range(4):
                st_A(h)
            for k in range(1, nlev):
                for h in range(4):
                    st_L(h, k)
            for h in range(4):
                st_V(h)
            for h in range(4):
                st_N(h)
            for h in range(4):
                ss = rstd_of(ofin[:n, h * 128:(h + 1) * 128], n, 128, [ofin])
                stt("dve", ofin[:n, h * 128:(h + 1) * 128], ofin[:n, h * 128:(h + 1) * 128], ss[:n, 0:1], nw[:n, :], ALU.mult, ALU.mult, [ofin, ss, nw], [ofin])
                tt_op("dve", ofin[:n, h * 128:(h + 1) * 128], ofin[:n, h * 128:(h + 1) * 128], gate_s[:n, h * 128:(h + 1) * 128], ALU.mult, [ofin, gate_s], [ofin])
            ps = nps()
            for j in range(4):
                tr(ps[:, j * 128:j * 128 + n], ofin[:n, j * 128:(j + 1) * 128], n, [ofin], [ps])
            copy(mixT[:, 4:8, :n], ps[:, :].rearrange("p (j t) -> p j t", t=128)[:, :, :n], [ps], [mixT])
            for half in range(2):
                ps = nps()
                for c in range(8):
                    mm(ps[:n, :512], mixT[:, c, :n], Wout[c][:, half * 512:(half + 1) * 512], c == 0, c == 7, [mixT, Wout[c]], [ps])
                add_res(xt, n, half, ps)

        def stile(loaders, n, first, finish):
            ntl = len(loaders)
            Wd = ntl * n
            for i, ld in enumerate(loaders):
                xt = ld()
                norm_tok(xt, n, gbc)
                half_tt = hT_rot[(i * n) // 256]
                for b in range(0, 8, 4):
                    ps = nps()
                    for j in range(4):
                        tr(ps[:, j * 128:j * 128 + n], hn[:n, (b + j) * 128:(b + j + 1) * 128], n, [hn], [ps])
                    copy(hT4[:, b:b + 4, i * n:i * n + n], ps[:, 0:512].rearrange("p (j t) -> p j t", t=128)[:, :, :n], [ps], [half_tt])
            rd = [hT_rot[0]] + ([hT_rot[1]] if Wd > 256 else [])
            for g in range(4):
                ps = nps()
                for kc in range(8):
                    mm(ps[:, :Wd], Win[kc][:, g * 128:(g + 1) * 128], hT4[:, kc, :Wd], kc == 0, kc == 7, [Win[kc]] + rd, [ps])
                copy(aext[g][:, 15:15 + Wd], ps[:, :Wd], [ps], [aext[g]])
            for c in range(12):
                ps = nps()
                for kc in range(8):
                    mm(ps[:, :Wd], Win[kc][:, 512 + c * 128:512 + (c + 1) * 128], hT4[:, kc, :Wd], kc == 0, kc == 7, [Win[kc]] + rd, [ps])
                copy(qext[c][:, 3:3 + Wd], ps[:, :Wd], [ps], [qext[c]])
            for i, ld in enumerate(loaders):
                xt = ld()
                tile(xt, n, first and i == 0, i * n, rd)
                finish(i, xt)
            for g in range(4):
                copy(aext[g][:, 0:15], aext[g][:, Wd:Wd + 15], [aext[g]], [aext[g]], eng="pool")
            for c in range(12):
                copy(qext[c][:, 0:3], qext[c][:, Wd:Wd + 3], [qext[c]], [qext[c]], eng="pool")

        for g in range(4):
            fw.op("pool", lambda e, g=g: e.memset(aext[g][:, :], 0.0), writes=[aext[g]])
        for c in range(12):
            fw.op("pool", lambda e, c=c: e.memset(qext[c][:, :], 0.0), writes=[qext[c]])
        for h in range(4):
            fw.op("pool", lambda e, h=h: e.memset(S[h][:, :], 0.0), writes=[S[h]])
        for t0 in range(0, NT, 4):
            stile([(lambda t=t0 + i: load_x(t, True)) for i in range(4)], 128, t0 == 0, lambda i, xt, t0=t0: store_x(t0 + i, xt))
        rows_from_cols(o_pool_p[:, :], lambda c: aext[c][:, 0:15], aext, 15, 4)
        rows_from_cols(o_dconv_p[:, :], lambda c: qext[c][:, 0:3], qext, 3, 12)
        for h in range(4):
            fw.dma("pool", o_dn_p[h], S[h][:, :], reads=[S[h]], is_output=True)
        for s in range(NS):
            cols_from_rows(lambda c: aext[c][:, 0:15], aext, st_pool[s], 15, 4)
            cols_from_rows(lambda c: qext[c][:, 0:3], qext, st_dconv[s], 3, 12)
            for h in range(4):
                fw.dma("sp", S[h][:, :], st_dn[s, h], writes=[S[h]])
            stile([lambda s=s: load_xs(s, True)], 8, False, lambda i, xt, s=s: store_xs(s, xt))
            rows_from_cols(o_pool_s[s], lambda c: aext[c][:, 0:15], aext, 15, 4)
            rows_from_cols(o_dconv_s[s], lambda c: qext[c][:, 0:3], qext, 3, 12)
            for h in range(4):
                fw.dma("pool", o_dn_s[s, h], S[h][:, :], reads=[S[h]], is_output=True)

    def phase_B(l):
        off = 0
        sm_reset()
        Wq, off = wload("wq", w_mem["q"][l], off, 1024, 1024)
        Wk, off = wload("wk", w_mem["k"][l], off, 1024, 1024)
        Wv, off = wload("wv", w_mem["v"][l], off, 1024, 1024)
        Wo, off = wload("wo", w_mem["o"][l], off, 1024, 1024)
        KTp = sm("KTp", [128, 8, 256], BF16); Vaug = sm("Vaug", [128, 2, 4, 257], BF16)
        KTs = sm("KTs", [128, 8, 256], BF16); Vaugs = sm("Vaugs", [128, 2, 4, 257], BF16)
        fw.op("pool", lambda e: e.memset(Vaug[:, :, :, 256:257], 1.0), writes=[Vaug])
        fw.op("pool", lambda e: e.memset(Vaugs[:, :, :, 256:257], 1.0), writes=[Vaugs])
        qT = sm("qT", [128, 8, 128], BF16); pT = sm("pT", [128, 2, 128], BF16)
        otok = sm("otok", [128, 1024]); oT = sm("oT", [128, 8, 128], BF16)
        rd = sm("rd", [128, 1]); kvs = sm("kvs", [128, 512]); kraw = sm("kraw", [128, 1024])
        load_g(g_mem_kv[l:l + 1, :])
        for nb in range(2):
            xt = xt_rot[cnt["xt"] % 2]; cnt["xt"] += 1
            fw.dma("sp", xt[:, :], memp[nb * 128:(nb + 1) * 128, :], writes=[xt])
            mT = norm_T(xt, 128, gbc)
            for ec in range(8):
                ps = nps(); lin_fm(ps[:, :128], Wk, ec * 128, 128, mT, 128, ps)
                copy(KTp[:, ec, nb * 128:(nb + 1) * 128], ps[:, :128], [ps], [KTp])
            for half in range(2):
                ps = nps(); lin_tm(ps[:, :512], mT, Wk, half * 512, 512, 128, ps)
                copy(kvs[:, :], ps[:, :512], [ps], [kvs])
                fw.dma("pool", o_memk[l, nb * 128:(nb + 1) * 128, half * 512:(half + 1) * 512], kvs[:, :], reads=[kvs], is_output=True)
                ps = nps(); lin_tm(ps[:, :512], mT, Wv, half * 512, 512, 128, ps)
                copy(kvs[:, :], ps[:, :512], [ps], [kvs])
                copy(Vaug[:, nb, 2 * half:2 * half + 2, 0:256], ps[:, :512].rearrange("p (h e) -> p h e", e=256), [ps], [Vaug])
                fw.dma("pool", o_memv[l, nb * 128:(nb + 1) * 128, half * 512:(half + 1) * 512], kvs[:, :], reads=[kvs], is_output=True)
        load_g(g_mem_q[l:l + 1, :])

        def tile(xt, n, KT, VA):
            hT = norm_T(xt, n, gbc)
            for ec in range(8):
                ps = nps(); lin_fm(ps[:, :n], Wq, ec * 128, 128, hT, n, ps)
                copy(qT[:, ec, :n], ps[:, :n], [ps], [qT])
            for h in range(4):
                ps = nps()
                for nb in range(2):
                    for ec in range(2):
                        mm(ps[:, nb * 128:nb * 128 + n], KT[:, 2 * h + ec, nb * 128:(nb + 1) * 128], qT[:, 2 * h + ec, :n], ec == 0, ec == 1, [KT, qT], [ps])
                act(pT[:, :, :n], ps[:, 0:256].rearrange("p (b t) -> p b t", t=128)[:, :, :n], AF.Exp, [ps], [pT], scale=1.0 / 16.0)
                ps2 = nps()
                for nb in range(2):
                    mm(ps2[:n, 0:257], pT[:, nb, :n], VA[:, nb, h, :], nb == 0, nb == 1, [pT, VA], [ps2])
                fw.op("dve", lambda e, ps2=ps2: e.reciprocal(out=rd[:n, :], in_=ps2[:n, 256:257]), [ps2], [rd])
                copy(otok[:n, h * 256:(h + 1) * 256], ps2[:n, 0:256], [ps2, rd], [otok], scale=rd[:n, 0:1])
            to_fm(otok, n, oT)
            for half in range(2):
                ps = nps(); lin_tm(ps[:n, :512], oT, Wo, half * 512, 512, n, ps)
                add_res(xt, n, half, ps)

        for t in range(NT):
            xt = load_x(t, False)
            tile(xt, 128, KTp, Vaug)
            store_x(t, xt)
        for s in range(NS):
            for nb in range(2):
                fw.dma("sp", kraw[:, :], cmk[l, s, nb * 128:(nb + 1) * 128, :], writes=[kraw])
                for b in range(2):
                    ps = nps()
                    for j in range(4):
                        tr(ps[:, j * 128:(j + 1) * 128], kraw[:, (4 * b + j) * 128:(4 * b + j + 1) * 128], 128, [kraw], [ps])
                    copy(KTs[:, 4 * b:4 * b + 4, nb * 128:(nb + 1) * 128], ps[:, :].rearrange("p (j t) -> p j t", t=128), [ps], [KTs])
                fw.dma("pool", Vaugs[:, nb, :, 0:256], cmv[l, s, nb * 128:(nb + 1) * 128, :].rearrange("n (h e) -> n h e", e=256), writes=[Vaugs])
            xs_t = load_xs(s)
            tile(xs_t, 8, KTs, Vaugs)
            store_xs(s, xs_t)

    def phase_C(l, final):
        off = 0
        sm_reset()
        Wup, off = wload("wup", w_up[l], off, 1024, 5632)
        Wdn, off = wload("wdn", w_down[l], off, 2816, 1024)
        load_g(g_ffn[l:l + 1, :])
        cwf = sm("cwf", [128, 44, 3]); cols_from_rows(lambda c: cwf[:, c, :], cwf, ffn_cw[l], 3, 44)
        hist = sm("fhist", [128, 44, 2])
        actT = sm("actT", [128, 22, 512], BF16)
        extA = [sm("extA%d" % i, [128, 514]) for i in range(2)]
        extB = [sm("extB%d" % i, [128, 514]) for i in range(2)]
        tA = sm("ftA", [128, 512]); tB = sm("ftB", [128, 512])
        hTs = [hT_rot[0], hT_rot[1]]

        def stile(loaders, n, finish):
            ntl = len(loaders)
            Wd = ntl * n
            for i, ld in enumerate(loaders):
                xt = ld()
                norm_tok(xt, n, gbc)
                half_tt = hTs[(i * n) // 256]
                for b in range(0, 8, 4):
                    ps = nps()
                    for j in range(4):
                        tr(ps[:, j * 128:j * 128 + n], hn[:n, (b + j) * 128:(b + j + 1) * 128], n, [hn], [ps])
                    copy(hT4[:, b:b + 4, i * n:i * n + n], ps[:, 0:512].rearrange("p (j t) -> p j t", t=128)[:, :, :n], [ps], [half_tt])
            rd = [hTs[0]] + ([hTs[1]] if Wd > 256 else [])
            for j in range(22):
                ea, eb = extA[j % 2], extB[j % 2]
                for (c, ext, ev) in ((j, ea, "act"), (j + 22, eb, "act")):
                    ps = nps()
                    for kc in range(8):
                        mm(ps[:, :Wd], Wup[kc][:, c * 128:(c + 1) * 128], hT4[:, kc, :Wd], kc == 0, kc == 7, [Wup[kc]] + rd, [ps])
                    copy(ext[:, 0:2], hist[:, c, :], [hist], [ext], eng="pool")
                    copy(ext[:, 2:2 + Wd], ps[:, :Wd], [ps], [ext], eng=ev)
                    copy(hist[:, c, :], ext[:, Wd:Wd + 2], [ext], [hist], eng="pool")
                c2 = j + 22
                fw.op("act", lambda e, ea=ea, j=j: e.mul(out=tA[:, :Wd], in_=ea[:, 0:Wd], mul=cwf[:, j, 0:1]), [ea, cwf], [tA])
                fw.op("act", lambda e, eb=eb, c2=c2: e.mul(out=tB[:, :Wd], in_=eb[:, 0:Wd], mul=cwf[:, c2, 0:1]), [eb, cwf], [tB])
                for k in (1, 2):
                    stt("dve", tA[:, :Wd], ea[:, k:k + Wd], cwf[:, j, k:k + 1], tA[:, :Wd], ALU.mult, ALU.add, [ea, cwf, tA], [tA])
                    stt("dve", tB[:, :Wd], eb[:, k:k + Wd], cwf[:, c2, k:k + 1], tB[:, :Wd], ALU.mult, ALU.add, [eb, cwf, tB], [tB])
                act(tA[:, :Wd], tA[:, :Wd], AF.Silu, [tA], [tA])
                tt_op("dve", actT[:, j, :Wd], tA[:, :Wd], tB[:, :Wd], ALU.mult, [tA, tB], [actT])
            for i, ld in enumerate(loaders):
                xt = ld()
                for half in range(2):
                    ps = nps()
                    for j in range(22):
                        mm(ps[:n, :512], actT[:, j, i * n:i * n + n], Wdn[j][:, half * 512:(half + 1) * 512], j == 0, j == 21, [actT, Wdn[j]], [ps])
                    add_res(xt, n, half, ps)
                finish(i, xt)

        fw.op("pool", lambda e: e.memset(hist[:, :, :], 0.0), writes=[hist])
        for t0 in range(0, NT, 4):
            def fin(i, xt, t0=t0):
                t = t0 + i
                if final:
                    norm_tok(xt, 128, gfin)
                    fw.dma("sp", y_p[t * 128:(t + 1) * 128, :], hn[:, :], reads=[hn], writes=[xres[t]], is_output=True)
                else:
                    store_x(t, xt)
            stile([(lambda t=t0 + i: load_x(t, False)) for i in range(4)], 128, fin)
        rows_from_cols(o_ffn_p[l], lambda c: hist[:, c, :], hist, 2, 44)
        for s_ in range(NS):
            cols_from_rows(lambda c: hist[:, c, :], hist, st_ffn[l, s_], 2, 44)

            def fin_s(i, xt, s_=s_):
                if final:
                    norm_tok(xt, 8, gfin)
                    fw.dma("sp", y_s[s_ * 8:(s_ + 1) * 8, :], hn[:8, :], reads=[hn], writes=[xsres[s_]], is_output=True)
                else:
                    store_xs(s_, xt)
            stile([lambda s_=s_: load_xs(s_)], 8, fin_s)
            rows_from_cols(o_ffn_s[l, s_], lambda c: hist[:, c, :], hist, 2, 44)

    def phase_A1():
        off = 0
        sm_reset()
        Wqkv, off = wload("wqkv", w_qkv, off, 1024, 4608)
        Woc, off = wload("woc", w_out_c, off, 512, 1024)
        load_g(g_mix[1:2, :])
        qk = sm("qk_tok", [128, 512]); vt = sm("v_tok", [128, 512]); vbf = sm("v_bf", [128, 512], BF16)
        cs = sm("cs_t", [128, 8]); sn = sm("sn_t", [128, 8])
        r1 = sm("rp1", [128, 8, 8]); r2 = sm("rp2", [128, 8, 8]); r3 = sm("rp3", [128, 8, 8])
        fmT = sm("fmT", [128, 4, 128], BF16)
        mb = sm("mb", [128, 2, 128], BF16); fw.dma("pool", mb[:], c_mb.rearrange("a p n -> p a n"), writes=[mb])
        smk = sm("smk", [128, 24, 8], BF16); fw.dma("pool", smk[:], c_sm.rearrange("p (b t) -> p b t", t=8), writes=[smk])
        ident_bf = sm("ident_bf", [128, 128], BF16); copy(ident_bf[:], ident[:], [ident], [ident_bf], eng="dve")
        sQT = [[sm("sQT%d_%d" % (s, g), [128, 4, 8], BF16) for g in range(3)] for s in range(NS)]
        sKT = [[sm("sKT%d_%d" % (s, g), [128, 4, 8], BF16) for g in range(3)] for s in range(NS)]
        sV_d = nc.dram_tensor("sV_d", [NS, 3, 8, 512], BF16).ap()
        sV_tt = TT(None, "sV_tt")
        sVn = sm("sVn", [8, 512], BF16)

        def rope(n, pos0):
            fw.dma("sp", cs[:n, :], c_cos[pos0:pos0 + n, :], writes=[cs])
            fw.dma("sp", sn[:n, :], c_sin[pos0:pos0 + n, :], writes=[sn])
            X = qk[:n, :].rearrange("p (h e) -> p h e", e=64)
            x1 = X[:, :, 0:8]; x2 = X[:, :, 8:16]
            cb = cs[:n, :].unsqueeze(1).to_broadcast([n, 8, 8]); sb_ = sn[:n, :].unsqueeze(1).to_broadcast([n, 8, 8])
            tt_op("dve", r1[:n], x1, sb_, ALU.mult, [qk, sn], [r1])
            tt_op("dve", r2[:n], x2, sb_, ALU.mult, [qk, sn], [r2])
            tt_op("dve", r3[:n], x2, cb, ALU.mult, [qk, cs], [r3])
            tt_op("dve", x1, x1, cb, ALU.mult, [qk, cs], [qk])
            tt_op("dve", x1, x1, r2[:n], ALU.subtract, [qk, r2], [qk])
            tt_op("dve", x2, r3[:n], r1[:n], ALU.add, [r3, r1], [qk])

        def proj_tile(xt, n, pos0, t, s):
            hT = norm_T(xt, n, gbc)
            for g in range(3):
                W = WINS[g][0]
                for si in range(3):
                    ps = nps(); lin_tm(ps[:n, :512], hT, Wqkv, g * 1536 + si * 512, 512, n, ps)
                    if si < 2:
                        copy(qk[:n, :], ps[:n, :512], [ps], [qk])
                        rope(n, pos0)
                        to_fm(qk, n, fmT, nchunks=4)
                        if s is None:
                            dst = (QT_d if si == 0 else KT_d)[g]
                            fw.dma("pool", dst.rearrange("(c p) s -> p c s", p=128)[:, :, t * 128:(t + 1) * 128], fmT[:, :, :], reads=[fmT], writes=[qkv_tt[t]])
                        else:
                            copy((sQT if si == 0 else sKT)[s][g][:, :, :], fmT[:, :, :8], [fmT], [(sQT if si == 0 else sKT)[s][g]], eng="pool")
                        if si == 1:
                            if s is None:
                                lo = S - min(W, S)
                                if (t + 1) * 128 > lo:
                                    fw.dma("pool", o_win_p[g][t * 128 - lo:(t + 1) * 128 - lo, 0:512], qk[:, :], reads=[qk], is_output=True)
                            else:
                                fw.dma("pool", o_win_s[g][s, :, 0:512], qk[:8, :], reads=[qk], is_output=True)
                    else:
                        copy(vt[:n, :], ps[:n, :512], [ps], [vt])
                        copy(vbf[:n, :], ps[:n, :512], [ps], [vbf])
                        if s is None:
                            fw.dma("pool", V_d[g][t * 128:(t + 1) * 128, :], vbf[:, :], reads=[vbf], writes=[qkv_tt[t]])
                            lo = S - min(W, S)
                            if (t + 1) * 128 > lo:
                                fw.dma("pool", o_win_p[g][t * 128 - lo:(t + 1) * 128 - lo, 512:1024], vt[:, :], reads=[vt], is_output=True)
                        else:
                            fw.dma("sp", sV_d[s, g], vbf[:8, :], reads=[vbf], writes=[sV_tt])
                            fw.dma("pool", o_win_s[g][s, :, 512:1024], vt[:8, :], reads=[vt], is_output=True)

        for t in range(NT):
            xt = load_x(t, False)
            proj_tile(xt, 128, t * 128, t, None)
        for s in range(NS):
            proj_tile(load_xs(s), 8, S, None, s)

        print('A1 proj done at op', fw.total)
        HALF = 2048
        NH = S // HALF
        accN = barrier_tt(TT(arena_f[:, 0:4 * HALF].rearrange("p (c s) -> p c s", s=HALF), "accN"))
        accD = barrier_tt(TT(arena_f[:, 4 * HALF:8 * HALF].rearrange("p (c s) -> p c s", s=HALF), "accD"))
        QTs = barrier_tt(TT(arena.ap[:, 40960:40960 + 8192].rearrange("p (c s) -> p c s", s=2048), "QTs"))
        KTs_ = barrier_tt(TT(arena.ap[:, 49152:49152 + 16384].rearrange("p (c s) -> p c s", s=4096), "KTs"))
        Vst = [sm("Vst%d" % i, [128, 512], BF16) for i in range(2)]
        Vpad = [sm("Vpad%d" % i, [128, 2, 4, 2, 128], BF16) for i in range(2)]
        onespad = sm("onespad", [128, 2, 128], BF16)
        fw.op("pool", lambda e: e.memset(onespad[:], 0.0), writes=[onespad])
        fw.op("pool", lambda e: e.memset(onespad[:, 0, 0:64], 1.0), writes=[onespad])
        fw.op("pool", lambda e: e.memset(onespad[:, 1, 64:128], 1.0), writes=[onespad])
        for i in range(2):
            fw.op("pool", lambda e, i=i: e.memset(Vpad[i][:], 0.0), writes=[Vpad[i]])
        PTR = [sm("PT%d" % i, [128, 2, 128], BF16) for i in range(5)]
        ptc = [0]
        rec = sm("rec", [128, 4, 128]); oTb = sm("oTb", [128, 4, 128], BF16)
        vcnt = [0]
        for hf in range(NH):
            for g in range(3):
                W, d = WINS[g]
                SB = 128 * d
                for sbk in range(HALF // SB):
                    q_sb = hf * (HALF // SB) + sbk
                    base = q_sb * SB
                    lb = base - hf * HALF
                    fw.dma("sp", QTs[:, :, 0:SB], QT_d[g].rearrange("(c p) s -> p c s", p=128)[:, :, base:base + SB], reads=qkv_tt, writes=[QTs])
                    if q_sb > 0:
                        fw.dma("sp", KTs_[:, :, 0:2 * SB], KT_d[g].rearrange("(c p) s -> p c s", p=128)[:, :, base - SB:base + SB], reads=qkv_tt, writes=[KTs_])
                    else:
                        fw.dma("sp", KTs_[:, :, SB:2 * SB], KT_d[g].rearrange("(c p) s -> p c s", p=128)[:, :, base:base + SB], reads=qkv_tt, writes=[KTs_])
                    kbs = (0, 1) if q_sb > 0 else (1,)
                    for r in range(d):
                        vp = Vpad[vcnt[0] % 2]; vcnt[0] += 1
                        for kb in kbs:
                            p0 = base - SB + r if kb == 0 else base + r
                            vs = Vst[kb]
                            fw.dma("sp", vs[:, :], V_d[g][p0:p0 + 127 * d + 1:d, :], reads=qkv_tt, writes=[vs])
                            for hh in range(2):
                                copy(vp[:, kb, :, hh, hh * 64:(hh + 1) * 64], vs[:, :].rearrange("p (c h e) -> p c h e", h=2, e=64)[:, :, hh, :], [vs], [vp], eng="pool")
                        for c in range(4):
                            PT = [PTR[(ptc[0] + i_) % 5] for i_ in range(2)]
                            ptc[0] += 2
                            for hh in range(2):
                                ps = nps()
                                pl = slice(hh * 64, (hh + 1) * 64)
                                for kb in kbs:
                                    ksl = slice(r, SB, d) if kb == 0 else slice(SB + r, 2 * SB, d)
                                    mm(ps[:, kb * 128:(kb + 1) * 128], KTs_[pl, c, ksl], QTs[pl, c, r:SB:d], True, False, [KTs_, QTs], [ps])
                                    mm(ps[:, kb * 128:(kb + 1) * 128], ident_bf[:, :], mb[:, kb, :], False, True, [ident_bf, mb], [ps])
                                k0 = kbs[0]
                                act(PT[hh][:, k0:2, :], ps[:, k0 * 128:256].rearrange("p (b t) -> p b t", t=128), AF.Exp, [ps], [PT[hh]], scale=0.125)
                            psN = nps(); psD = nps()
                            seq = [(hh, kb) for hh in range(2) for kb in kbs]
                            for i, (hh, kb) in enumerate(seq):
                                mm(psN[:, :128], vp[:, kb, c, hh, :], PT[hh][:, kb, :], i == 0, i == len(seq) - 1, [vp, PT[hh]], [psN])
                            for i, (hh, kb) in enumerate(seq):
                                mm(psD[:, :128], onespad[:, hh, :], PT[hh][:, kb, :], i == 0, i == len(seq) - 1, [onespad, PT[hh]], [psD])
                            dN = accN[:, c, lb + r:lb + SB:d]; dD = accD[:, c, lb + r:lb + SB:d]
                            if g == 0:
                                copy(dN, psN[:, :128], [psN], [accN], eng="act")
                                copy(dD, psD[:, :128], [psD], [accD], eng="dve")
                            else:
                                tt_op("dve", dN, dN, psN[:, :128], ALU.add, [accN, psN], [accN])
                                tt_op("dve", dD, dD, psD[:, :128], ALU.add, [accD, psD], [accD])
            for tl in range(HALF // 128):
                t = hf * (HALF // 128) + tl
                fw.op("dve", lambda e, tl=tl: e.reciprocal(out=rec[:, :, :], in_=accD[:, :, tl * 128:(tl + 1) * 128]), [accD], [rec])
                tt_op("dve", oTb[:, :, :], accN[:, :, tl * 128:(tl + 1) * 128], rec[:, :, :], ALU.mult, [accN, rec], [oTb])
                xt = load_x(t, False)
                for half in range(2):
                    ps = nps()
                    for c in range(4):
                        mm(ps[:, :512], oTb[:, c, :], Woc[c][:, half * 512:(half + 1) * 512], c == 0, c == 3, [oTb, Woc[c]], [ps])
                    add_res(xt, 128, half, ps)
                store_x(t, xt)
        print('A1 prompt attn done at op', fw.total)
        craw = sm("craw", [128, 1024]); KTb = sm("KTb", [128, 4, 128], BF16)
        sVp = [sm("sVp%d" % i, [128, 4, 2, 128], BF16) for i in range(2)]
        for i in range(2):
            fw.op("pool", lambda e, i=i: e.memset(sVp[i][:], 0.0), writes=[sVp[i]])
        PTs = sm("PTs", [128, 8, 8], BF16)
        aN = sm("aNs", [128, 32]); aD = sm("aDs", [128, 32]); oTs = sm("oTs", [128, 32], BF16)
        bases = [0, 1, 5]
        bcnt = [0]
        for s in range(NS):
            xs_t = load_xs(s)
            first = True
            for g in range(3):
                W, d = WINS[g]
                nblk = W // 128
                for kb in range(nblk + 1):
                    newblk = (kb == nblk)
                    nk = 8 if newblk else 128
                    vp = sVp[bcnt[0] % 2]; bcnt[0] += 1
                    if not newblk:
                        fw.dma("sp", craw[:, :], cwin[g][s, kb * 128:(kb + 1) * 128, :], writes=[craw])
                        ps = nps()
                        for j in range(4):
                            tr(ps[:, j * 128:(j + 1) * 128], craw[:, j * 128:(j + 1) * 128], 128, [craw], [ps])
                        copy(KTb[:, :, :], ps[:, :].rearrange("p (j t) -> p j t", t=128), [ps], [KTb])
                        for hh in range(2):
                            copy(vp[:, :, hh, hh * 64:(hh + 1) * 64], craw[:, 512:1024].rearrange("p (c h e) -> p c h e", h=2, e=64)[:, :, hh, :], [craw], [vp], eng="pool")
                        ktt, kap = KTb, KTb
                        bi = bases[g] + kb
                    else:
                        fw.dma("sp", sVn[:8, :], sV_d[s, g], reads=[sV_tt], writes=[sVn])
                        for hh in range(2):
                            copy(vp[:8, :, hh, hh * 64:(hh + 1) * 64], sVn[:8, :].rearrange("p (c h e) -> p c h e", h=2, e=64)[:, :, hh, :], [sVn], [vp], eng="pool")
                        ktt, kap = sKT[s][g], sKT[s][g]
                        bi = 21 + g
                        k2 = sm("k2n", [64, 2, 4, 8], BF16); q2 = sm("q2n", [64, 2, 4, 8], BF16)
                        for hh in range(2):
                            copy(k2[:, hh, :, :], sKT[s][g][hh * 64:(hh + 1) * 64, :, :], [sKT[s][g]], [k2], eng="pool")
                            copy(q2[:, hh, :, :], sQT[s][g][hh * 64:(hh + 1) * 64, :, :], [sQT[s][g]], [q2], eng="pool")
                    ps = nps()
                    for c in range(4):
                        for hh in range(2):
                            pl = slice(hh * 64, (hh + 1) * 64)
                            hd = 2 * c + hh
                            if newblk:
                                mm(ps[:nk, hd * 8:(hd + 1) * 8], k2[:, hh, c, :nk], q2[:, hh, c, :8], True, True, [k2, q2], [ps])
                            else:
                                mm(ps[:nk, hd * 8:(hd + 1) * 8], kap[pl, c, :nk], sQT[s][g][pl, c, :8], True, True, [ktt, sQT[s][g]], [ps])
                    act(PTs[:nk, :, :], ps[:nk, 0:64].rearrange("p (h t) -> p h t", t=8), AF.Exp, [ps], [PTs], scale=0.125)
                    tt_op("dve", PTs[:nk, :, :], PTs[:nk, :, :], smk[:nk, bi, :].unsqueeze(1).to_broadcast([nk, 8, 8]), ALU.mult, [PTs, smk], [PTs])
                    psN = nps(); psD = nps()
                    for c in range(4):
                        for hh in range(2):
                            hd = 2 * c + hh
                            mm(psN[:, c * 8:(c + 1) * 8], vp[:nk, c, hh, :], PTs[:nk, hd, :], hh == 0, hh == 1, [vp, PTs], [psN])
                    for c in range(4):
                        for hh in range(2):
                            hd = 2 * c + hh
                            mm(psD[:, c * 8:(c + 1) * 8], onespad[:nk, hh, :], PTs[:nk, hd, :], hh == 0, hh == 1, [onespad, PTs], [psD])
                    if first:
                        copy(aN[:, :], psN[:, :32], [psN], [aN], eng="act")
                        copy(aD[:, :], psD[:, :32], [psD], [aD], eng="dve")
                        first = False
                    else:
                        tt_op("dve", aN[:, :], aN[:, :], psN[:, :32], ALU.add, [aN, psN], [aN])
                        tt_op("dve", aD[:, :], aD[:, :], psD[:, :32], ALU.add, [aD, psD], [aD])
            fw.op("dve", lambda e: e.reciprocal(out=aD[:, :], in_=aD[:, :]), [aD], [aD])
            tt_op("dve", oTs[:, :], aN[:, :], aD[:, :], ALU.mult, [aN, aD], [oTs])
            for half in range(2):
                ps = nps()
                for c in range(4):
                    mm(ps[:8, :512], oTs[:, c * 8:(c + 1) * 8], Woc[c][:, half * 512:(half + 1) * 512], c == 0, c == 3, [oTs, Woc[c]], [ps])
                add_res(xs_t, 8, half, ps)
            store_xs(s, xs_t)

    if 'A0' in phases: phase_A0()
    if 'B0' in phases: phase_B(0)
    if 'C0' in phases: phase_C(0, False)
    if 'A1' in phases: phase_A1()
    if 'B1' in phases: phase_B(1)
    if 'C1' in phases: phase_C(1, True)
    fw.finish()
    return nc, fw


def _consts(S):
    c = {}
    r = np.arange(128)[:, None]; cc = np.arange(128)[None, :]
    c["c_ident"] = np.eye(128, dtype=np.float32)
    c["c_mge"] = (BIG * (cc >= r)).astype(np.float32)
    c["c_nmle"] = (-BIG * (cc <= r)).astype(np.float32)
    c["c_nmlt"] = (-BIG * (cc < r)).astype(np.float32)
    c["c_LT"] = (r <= cc).astype(np.float32)
    invc = np.zeros((2, 128, 512), np.float32)
    for g, w in enumerate((2, 4, 8, 16)):
        pos = np.arange(128)
        invc[0, :, g * 128:(g + 1) * 128] = (1.0 / np.minimum(pos + 1, w))[None, :]
        invc[1, :, g * 128:(g + 1) * 128] = 1.0 / w
    c["c_invc"] = invc
    half = 8
    inv_freq = (np.float32(500000.0) ** (-np.arange(half, dtype=np.float32) / np.float32(half))).astype(np.float32)
    pos = np.concatenate([np.arange(S), 8192 + np.arange(8)]).astype(np.float32)
    ang = (pos[:, None] * inv_freq[None, :]).astype(np.float32)
    c["c_cos"] = np.cos(ang.astype(np.float64)).astype(np.float32)
    c["c_sin"] = np.sin(ang.astype(np.float64)).astype(np.float32)
    NEG = -240000.0
    j = np.arange(128)[:, None]; i = np.arange(128)[None, :]
    mb = np.zeros((2, 128, 128), np.float32)
    mb[0] = np.where(j >= i, 0.0, NEG)
    mb[1] = np.where(j <= i, 0.0, NEG)
    c["c_mb"] = mb
    smk = np.zeros((128, 24, 8), np.float32)
    bases = [0, 1, 5]
    for g, (W, d) in enumerate(WINS):
        for kb in range(W // 128):
            e = kb * 128 + np.arange(128)[:, None]
            t = np.arange(8)[None, :]
            diff = W + t - e
            smk[:, bases[g] + kb, :] = ((diff % d == 0) & (diff // d >= 1) & (diff // d <= 128)).astype(np.float32)
        p = np.arange(128)[:, None]; t = np.arange(8)[None, :]
        smk[:, 21 + g, :] = ((p <= t) & ((t - p) % d == 0) & (p < 8)).astype(np.float32)
    c["c_sm"] = smk.reshape(128, 24 * 8)
    return c


_CACHE = {}


def _core_inputs(inp, c, S, NS, consts):
    b = c % 4
    ss = slice(NS * c, NS * (c + 1))
    f = lambda a: np.ascontiguousarray(a, dtype=np.float32)
    m = {
        "xp": f(inp["x_prompt"][b]), "xs": f(inp["x_sample"][ss].reshape(NS * 8, 1024)), "memp": f(inp["mem_prompt"][b]),
        "st_pool": f(inp["state_pool"][0, ss]), "st_dconv": f(inp["state_dn_conv"][0, ss]), "st_dn": f(inp["state_dn"][0, ss]),
        "cw128": f(inp["cache_win_w128"][0, ss].reshape(NS, 128, 1024)),
        "cw512": f(inp["cache_win_w512"][0, ss].reshape(NS, 512, 1024)),
        "cw2048": f(inp["cache_win_w2048"][0, ss].reshape(NS, 2048, 1024)),
        "cmk": f(inp["cache_mem_k"][:, ss].reshape(2, NS, 256, 1024)), "cmv": f(inp["cache_mem_v"][:, ss].reshape(2, NS, 256, 1024)),
        "st_ffn": f(inp["state_ffn_conv"][:, ss]),
        "g_mix": f(inp["g_mix"]), "w_in": f(inp["w_in_ab"][0]), "w_pool": f(inp["w_pool"][0].reshape(512, 128)),
        "pool_scale": f(inp["pool_scale"]), "dn_conv_w": f(inp["dn_conv_w"][0]), "dn_a_log": f(inp["dn_a_log"]),
        "dn_dt_bias": f(inp["dn_dt_bias"]), "dn_norm_w": f(inp["dn_norm_w"]), "w_out_ab": f(inp["w_out_ab"][0]),
        "w_qkv": f(inp["w_qkv_c"][0]), "w_out_c": f(inp["w_out_c"][0]), "g_mem_q": f(inp["g_mem_q"]), "g_mem_kv": f(inp["g_mem_kv"]),
        "w_mem_q": f(inp["w_mem_q"]), "w_mem_k": f(inp["w_mem_k"]), "w_mem_v": f(inp["w_mem_v"]), "w_mem_o": f(inp["w_mem_o"]),
        "g_ffn": f(inp["g_ffn"]), "w_up": f(inp["w_up"]), "ffn_cw": f(inp["ffn_conv_w"]), "w_down": f(inp["w_down"]),
        "g_final": f(inp["g_final"].reshape(1, 1024)),
    }
    m.update(consts)
    return m


def kernel(**inp):
    S = inp["x_prompt"].shape[1]
    NS = 4
    if S not in _CACHE:
        _CACHE[S] = build(S, NS)[0]
    nc = _CACHE[S]
    consts = _consts(S)
    in_maps = [_core_inputs(inp, c, S, NS, consts) for c in range(8)]
    res = run_bass_kernel_spmd(nc, in_maps, core_ids=list(range(8)))
    R = res.results
    f = np.float32
    B = 4
    cat_p = lambda k: np.stack([R[b][k] for b in range(B)])
    cat_s = lambda k: np.concatenate([R[c][k] for c in range(8)], axis=0)
    y_p = cat_p("y_p")
    y_s = cat_s("y_s").reshape(32, 8, 1024)
    outs = [y_p, y_s,
            cat_p("o_pool_p")[None], cat_s("o_pool_s")[None],
            cat_p("o_dconv_p")[None], cat_s("o_dconv_s")[None],
            cat_p("o_dn_p")[None], cat_s("o_dn_s")[None]]
    for w, _ in WINS:
        wp = min(w, S)
        outs.append(cat_p("o_w%d_p" % w).reshape(1, B, wp, 2, 8, 64))
        outs.append(cat_s("o_w%d_s" % w).reshape(1, 32, 8, 2, 8, 64))
    outs.append(np.stack([R[b]["o_memk"] for b in range(B)], axis=1).reshape(2, B, 256, 4, 256))
    outs.append(np.stack([R[b]["o_memv"] for b in range(B)], axis=1).reshape(2, B, 256, 4, 256))
    outs.append(np.stack([R[b]["o_ffn_p"] for b in range(B)], axis=1))
    outs.append(np.concatenate([R[c]["o_ffn_s"] for c in range(8)], axis=1))
    return tuple(np.ascontiguousarray(o, dtype=f) for o in outs)
```

```python
import numpy as np
import concourse.bass as bass
import concourse.mybir as mybir

F32 = mybir.dt.float32
BF16 = mybir.dt.bfloat16
AF = mybir.ActivationFunctionType
ALU = mybir.AluOpType
AX = mybir.AxisListType


class TT:
    __slots__ = ("ap", "w", "r", "name", "psum")

    def __init__(self, ap, name=""):
        self.ap = ap
        self.psum = False
        self.w = None
        self.r = {}
        self.name = name

    def __getitem__(self, idx):
        return self.ap[idx]


class FW:
    ENG = ("pe", "act", "dve", "pool", "sp")

    def __init__(self, nc, ndma=6):
        self.nc = nc
        self.prog = {e: [] for e in self.ENG}
        self.cnt = {e: 0 for e in self.ENG if e != "sp"}
        self.seen = {e: {} for e in self.ENG}
        self.sems = {}
        self.ndma = ndma
        self.dma_rr = {"sp": 0, "pool": 0, "act": 0}
        self.dma_tot = {}
        self.ctx = []
        self.out_tokens = []
        import os
        self.maxops = int(os.environ.get("FW_MAXOPS", "100000000"))
        self.total = 0
        self.log = []

    def sem(self, key):
        if key not in self.sems:
            cm = self.nc.semaphore("s_%s" % str(key).replace(" ", ""))
            self.sems[key] = cm.__enter__()
            self.ctx.append(cm)
        return self.sems[key]

    def sb(self, name, shape, dt=F32):
        cm = self.nc.sbuf_tensor(name, list(shape), dt)
        t = cm.__enter__()
        self.ctx.append(cm)
        return TT(t, name)

    def ps(self, name, shape, dt=F32):
        cm = self.nc.psum_tensor(name, list(shape), dt)
        t = cm.__enter__()
        self.ctx.append(cm)
        tt = TT(t, name)
        tt.psum = True
        return tt

    def _waits(self, eng, reads, writes, skip_self_pe=False):
        need = {}
        for t in reads:
            if t.w is not None:
                k, v = t.w
                need[k] = max(need.get(k, 0), v)
            if t.psum:
                for k, v in t.r.items():
                    if k != eng:
                        need[k] = max(need.get(k, 0), v)
        for t in writes:
            if t.w is not None:
                k, v = t.w
                need[k] = max(need.get(k, 0), v)
            for k, v in t.r.items():
                need[k] = max(need.get(k, 0), v)
        out = []
        seen = self.seen[eng]
        for k, v in need.items():
            if eng == "pe" and k == "pe":
                continue
            if seen.get(k, 0) >= v:
                continue
            seen[k] = v
            out.append((k, v))
        return out

    def op(self, eng, fn, reads=(), writes=()):
        self.total += 1
        if self.total > self.maxops:
            return None
        reads = [t for t in reads if isinstance(t, TT)]
        writes = [t for t in writes if isinstance(t, TT)]
        waits = self._waits(eng, reads, writes)
        self.cnt[eng] += 1
        v = self.cnt[eng]
        self.prog[eng].append((waits, fn, (eng, v, 1)))
        tok = (eng, v)
        for t in writes:
            t.w = tok
            t.r = {}
        for t in reads:
            if t in writes:
                continue
            t.r[eng] = max(t.r.get(eng, 0), v)
        return tok

    def dma(self, q, out_ap, in_ap, reads=(), writes=(), is_output=False, **kw):
        self.total += 1
        if self.total > self.maxops:
            return None
        reads = [t for t in reads if isinstance(t, TT)]
        writes = [t for t in writes if isinstance(t, TT)]
        waits = self._waits(q, reads, writes)
        i = self.dma_rr[q]
        self.dma_rr[q] = (i + 1) % self.ndma
        key = "d%s%d" % (q, i)
        prev = self.dma_tot.get(key, 0)
        if prev > 0 and self.seen[q].get(key, 0) < prev:
            self.seen[q][key] = prev
            waits.append((key, prev))
        v = prev + 16
        self.dma_tot[key] = v
        fn = (lambda e, o=out_ap, a=in_ap, k=kw: e.dma_start(out=o, in_=a, **k))
        self.prog[q].append((waits, fn, (key, v, 16)))
        tok = (key, v)
        for t in writes:
            t.w = tok
            t.r = {}
        for t in reads:
            t.r[key] = max(t.r.get(key, 0), v)
        if is_output:
            self.out_tokens.append(tok)
        return tok

    def finish(self):
        nc = self.nc
        fin = {}
        for k, v in self.out_tokens:
            fin[k] = max(fin.get(k, 0), v)
        for k, v in self.dma_tot.items():
            fin[k] = max(fin.get(k, 0), v)
        for e, c in self.cnt.items():
            if c > 0:
                fin[e] = c
        for k in list(fin.keys()):
            self.sem(k)
        for e in self.ENG:
            for waits, fn, (k, v, inc) in self.prog[e]:
                self.sem(k)
                for (wk, wv) in waits:
                    self.sem(wk)
        sems = self.sems
        prog = self.prog

        def run(e, h, final=False):
            for waits, fn, (k, v, inc) in prog[e]:
                for (wk, wv) in waits:
                    h.wait_ge(sems[wk], wv)
                fn(h).then_inc(sems[k], inc)
            if final:
                for k, v in fin.items():
                    h.wait_ge(sems[k], v)

        with nc.Block() as block:
            @block.sync
            def _(h):
                run("sp", h, final=True)

            @block.tensor
            def _(h):
                run("pe", h)

            @block.scalar
            def _(h):
                run("act", h)

            @block.vector
            def _(h):
                run("dve", h)

            @block.gpsimd
            def _(h):
                run("pool", h)
        for cm in reversed(self.ctx):
            cm.__exit__(None, None, None)

from concourse.bass_utils import run_bass_kernel_spmd

BIG = 30000.0
EPS = 1e-6
WINS = ((128, 1), (512, 4), (2048, 16))


def build(S=4096, NS=4, phases=('A0', 'B0', 'C0', 'A1', 'B1', 'C1')):
    nc = bass.Bass("TRN2", target_bir_lowering=False)
    fw = FW(nc)
    NT = S // 128
    I_, O_ = {}, {}

    def din(name, shape):
        I_[name] = nc.dram_tensor(name, list(shape), F32, kind="ExternalInput").ap()
        return I_[name]

    def dout(name, shape):
        O_[name] = nc.dram_tensor(name, list(shape), F32, kind="ExternalOutput").ap()
        return O_[name]

    xp = din("xp", [S, 1024]); xs_d = din("xs", [NS * 8, 1024]); memp = din("memp", [256, 1024])
    st_pool = din("st_pool", [NS, 15, 512]); st_dconv = din("st_dconv", [NS, 3, 1536])
    st_dn = din("st_dn", [NS, 4, 128, 128])
    cwin = [din("cw%d" % w, [NS, w, 1024]) for w, _ in WINS]
    cmk = din("cmk", [2, NS, 256, 1024]); cmv = din("cmv", [2, NS, 256, 1024])
    st_ffn = din("st_ffn", [2, NS, 2, 5632])
    g_mix = din("g_mix", [2, 1024]); w_in = din("w_in", [1024, 2568]); w_pool = din("w_pool", [512, 128])
    pool_scale = din("pool_scale", [1, 512]); dn_conv_w = din("dn_conv_w", [4, 1536])
    dn_a_log = din("dn_a_log", [1, 4]); dn_dt_bias = din("dn_dt_bias", [1, 4]); dn_norm_w = din("dn_norm_w", [1, 128])
    w_out_ab = din("w_out_ab", [1024, 1024]); w_qkv = din("w_qkv", [1024, 4608]); w_out_c = din("w_out_c", [512, 1024])
    g_mem_q = din("g_mem_q", [2, 1024]); g_mem_kv = din("g_mem_kv", [2, 1024])
    w_mem = {k: din("w_mem_" + k, [2, 1024, 1024]) for k in "qkvo"}
    g_ffn = din("g_ffn", [2, 1024]); w_up = din("w_up", [2, 1024, 5632]); ffn_cw = din("ffn_cw", [2, 3, 5632])
    w_down = din("w_down", [2, 2816, 1024]); g_final = din("g_final", [1, 1024])
    c_ident = din("c_ident", [128, 128]); c_mge = din("c_mge", [128, 128]); c_nmle = din("c_nmle", [128, 128])
    c_nmlt = din("c_nmlt", [128, 128]); c_LT = din("c_LT", [128, 128])
    c_invc = din("c_invc", [2, 128, 512])
    c_cos = din("c_cos", [S + 8, 8]); c_sin = din("c_sin", [S + 8, 8])
    c_mb = din("c_mb", [2, 128, 128]); c_sm = din("c_sm", [128, 24 * 8])

    y_p = dout("y_p", [S, 1024]); y_s = dout("y_s", [NS * 8, 1024])
    o_pool_p = dout("o_pool_p", [15, 512]); o_pool_s = dout("o_pool_s", [NS, 15, 512])
    o_dconv_p = dout("o_dconv_p", [3, 1536]); o_dconv_s = dout("o_dconv_s", [NS, 3, 1536])
    o_dn_p = dout("o_dn_p", [4, 128, 128]); o_dn_s = dout("o_dn_s", [NS, 4, 128, 128])
    o_win_p = [dout("o_w%d_p" % w, [min(w, S), 1024]) for w, _ in WINS]
    o_win_s = [dout("o_w%d_s" % w, [NS, 8, 1024]) for w, _ in WINS]
    o_memk = dout("o_memk", [2, 256, 1024]); o_memv = dout("o_memv", [2, 256, 1024])
    o_ffn_p = dout("o_ffn_p", [2, 2, 5632]); o_ffn_s = dout("o_ffn_s", [2, NS, 2, 5632])

    QT_d = [nc.dram_tensor("QT%d" % g, [512, S], BF16).ap() for g in range(3)]
    KT_d = [nc.dram_tensor("KT%d" % g, [512, S], BF16).ap() for g in range(3)]
    V_d = [nc.dram_tensor("Vd%d" % g, [S, 512], BF16).ap() for g in range(3)]
    qkv_tt = [TT(None, "qkvscr%d" % t) for t in range(NT)]
    xres = [TT(None, "xres%d" % t) for t in range(NT)]

    rr = {"ps": 0, "ev": 0, "q": 0}
    PS = [fw.ps("ps%d" % i, [128, 512]) for i in range(8)]

    def nps():
        p = PS[rr["ps"] % 8]
        rr["ps"] += 1
        return p

    def ev_eng():
        rr["ev"] += 1
        return "act" if rr["ev"] % 2 else "dve"

    def copy(dst, src, reads, writes, eng=None, scale=None):
        eng = eng or ev_eng()
        if scale is not None and not isinstance(scale, float) and eng == "act":
            eng = "dve"
        if eng == "act":
            if scale is None:
                fw.op("act", lambda e: e.copy(out=dst, in_=src), reads, writes)
            else:
                fw.op("act", lambda e: e.mul(out=dst, in_=src, mul=scale), reads, writes)
        else:
            if scale is None:
                fw.op(eng, lambda e: e.tensor_copy(out=dst, in_=src), reads, writes)
            else:
                fw.op(eng, lambda e: e.tensor_scalar(out=dst, in0=src, scalar1=scale, scalar2=None, op0=ALU.mult), reads, writes)

    def tt_op(eng, dst, a, b, op, reads, writes):
        fw.op(eng, lambda e: e.tensor_tensor(out=dst, in0=a, in1=b, op=op), reads, writes)

    def stt(eng, dst, a, sc, b, op0, op1, reads, writes):
        if eng == "pool":
            tmp = sm("ptmp", [128, 128])
            nparts, nfree = a.shape[0], a.shape[1]
            fw.op("pool", lambda e: e.tensor_scalar(out=tmp[:nparts, :nfree], in0=a, scalar1=sc, scalar2=None, op0=op0), reads, [tmp])
            fw.op("pool", lambda e: e.tensor_tensor(out=dst, in0=tmp[:nparts, :nfree], in1=b, op=op1), list(reads) + [tmp], writes)
            return
        fw.op(eng, lambda e: e.scalar_tensor_tensor(out=dst, in0=a, scalar=sc, in1=b, op0=op0, op1=op1), reads, writes)

    def act(dst, src, func, reads, writes, bias=None, scale=None):
        kw = {}
        if bias is not None:
            kw["bias"] = bias
        if scale is not None:
            kw["scale"] = scale
        fw.op("act", lambda e: e.activation(out=dst, in_=src, func=func, **kw), reads, writes)

    def mm(ps_ap, lhsT, rhs, start, stop, reads, writes):
        fw.op("pe", lambda e: e.matmul(ps_ap, lhsT=lhsT, rhs=rhs, start=start, stop=stop), reads, writes)

    def tr(ps_ap, in_ap, k, reads, writes):
        fw.op("pe", lambda e: e.transpose(ps_ap, in_ap, ident[:k, :k]), list(reads) + [ident], writes)

    ident = fw.sb("ident", [128, 128]); fw.dma("sp", ident[:], c_ident[:], writes=[ident])
    m_ge = fw.sb("m_ge", [128, 128]); fw.dma("sp", m_ge[:], c_mge[:], writes=[m_ge])
    nm_le = fw.sb("nm_le", [128, 128]); fw.dma("sp", nm_le[:], c_nmle[:], writes=[nm_le])
    nm_lt = fw.sb("nm_lt", [128, 128]); fw.dma("sp", nm_lt[:], c_nmlt[:], writes=[nm_lt])
    LT = fw.sb("LT", [128, 128]); fw.dma("sp", LT[:], c_LT[:], writes=[LT])
    ones = fw.sb("ones", [128, 128]); fw.op("pool", lambda e: e.memset(ones[:], 1.0), writes=[ones])
    gfin = fw.sb("gfin", [128, 1024]); fw.dma("sp", gfin[:], g_final[0:1, :].partition_broadcast(128) if False else g_final.to_broadcast([128, 1024]), writes=[gfin])
    gbc = fw.sb("gbc", [128, 1024])

    def load_g(src_row):
        fw.dma("sp", gbc[:], src_row.to_broadcast([128, 1024]), writes=[gbc])

    xsres = [TT(None, "xsres%d" % s) for s in range(NS)]

    ARENA = 67584
    arena = fw.sb("arena", [128, ARENA], BF16)
    arena_f = arena.ap.bitcast(F32)

    def barrier_tt(tt):
        tt.r = dict(fw.cnt)
        for k, v in fw.dma_tot.items():
            tt.r[k] = v
        return tt

    wl = {"i": 0}

    def wload(name, dram2d, off, K, N):
        KC = K // 128
        tt = barrier_tt(TT(arena.ap[:, off:off + KC * N].rearrange("p (k n) -> p k n", n=N), name))
        chunks = []
        stg = [xt_rot[0], xt_rot[1], hn]
        for kc in range(KC):
            c = TT(tt.ap[:, kc, :], "%s_%d" % (name, kc))
            c.r = dict(tt.r)
            if N < 64:
                fw.dma("pool", c.ap, dram2d[kc * 128:(kc + 1) * 128, :], writes=[c])
            else:
                eng = "act" if kc % 2 else "dve"
                for c0 in range(0, N, 1024):
                    w_ = min(1024, N - c0)
                    st = stg[wl["i"] % 3]; wl["i"] += 1
                    fw.dma("sp", st[:, :w_], dram2d[kc * 128:(kc + 1) * 128, c0:c0 + w_], writes=[st])
                    if eng == "act":
                        fw.op("act", lambda e, st=st, c=c, c0=c0, w_=w_: e.copy(out=c.ap[:, c0:c0 + w_], in_=st[:, :w_]), [st], [c])
                    else:
                        fw.op("dve", lambda e, st=st, c=c, c0=c0, w_=w_: e.tensor_copy(out=c.ap[:, c0:c0 + w_], in_=st[:, :w_]), [st], [c])
            chunks.append(c)
        return chunks, off + KC * N

    xt_rot = [fw.sb("xt%d" % i, [128, 1024]) for i in range(2)]
    hn = fw.sb("hn", [128, 1024])
    hT4 = fw.sb("hT4", [128, 8, 512], BF16)
    hT_rot = [TT(hT4.ap[:, :, 0:256], "hTa"), TT(hT4.ap[:, :, 256:512], "hTb")]
    ssq = [fw.sb("ssq%d" % i, [128, 1]) for i in range(2)]
    cnt = {"xt": 0, "hT": 0, "ss": 0}

    def rstd_of(src_ap, ntok, width, reads):
        ss = ssq[cnt["ss"] % 2]; cnt["ss"] += 1
        act(hn[:ntok, :width], src_ap, AF.Square, reads, [hn])
        fw.op("dve", lambda e: e.reduce_sum(out=ss[:ntok, :], in_=hn[:ntok, :width], axis=AX.X), [hn], [ss])
        act(ss[:ntok, :], ss[:ntok, :], AF.Sqrt, [ss], [ss], bias=EPS, scale=1.0 / width)
        fw.op("dve", lambda e: e.reciprocal(out=ss[:ntok, :], in_=ss[:ntok, :]), [ss], [ss])
        return ss

    def norm_tok(xt, ntok, g_tt):
        ss = rstd_of(xt[:ntok, :], ntok, 1024, [xt])
        stt("dve", hn[:ntok, :], xt[:ntok, :], ss[:ntok, 0:1], g_tt[:ntok, :], ALU.mult, ALU.mult, [xt, ss, g_tt], [hn])
        return hn

    def to_fm(src, ntok, dst, nchunks=8, c0=0):
        for b in range(0, nchunks, 4):
            nb = min(4, nchunks - b)
            ps = nps()
            for j in range(nb):
                tr(ps[:, j * 128:j * 128 + ntok], src[:ntok, (b + j) * 128:(b + j + 1) * 128], ntok, [src], [ps])
            copy(dst[:, b:b + nb, c0:c0 + ntok], ps[:, 0:nb * 128].rearrange("p (j t) -> p j t", t=128)[:, :, :ntok], [ps], [dst])

    def norm_T(xt, ntok, g_tt):
        norm_tok(xt, ntok, g_tt)
        hT = hT_rot[cnt["hT"] % 2]; cnt["hT"] += 1
        to_fm(hn, ntok, hT)
        return hT

    def lin_fm(ps_ap, W, col0, M, hT, ntok, wr):
        KC = len(W)
        for kc in range(KC):
            mm(ps_ap, W[kc][:, col0:col0 + M], hT[:, kc, :ntok], kc == 0, kc == KC - 1, [W[kc], hT], [wr])

    def lin_tm(ps_ap, hT, W, col0, N, ntok, wr, o=0, rd=None):
        KC = len(W)
        for kc in range(KC):
            mm(ps_ap, hT[:, kc, o:o + ntok], W[kc][:, col0:col0 + N], kc == 0, kc == KC - 1, [W[kc]] + (rd if rd is not None else [hT]), [wr])

    STW = 1024
    stage = fw.sb("stage", [16, STW])
    stage_o = fw.sb("stage_o", [16, STW])

    def cols_from_rows(dst_fn, dst_tts, src_dram, k, C):
        per = STW // 128
        for c0 in range(0, C, per):
            n = min(per, C - c0)
            fw.dma("sp", stage[:k, :n * 128], src_dram[:, c0 * 128:(c0 + n) * 128], writes=[stage])
            ps = nps()
            for j in range(n):
                tr(ps[:, j * 16:j * 16 + k], stage[:k, j * 128:(j + 1) * 128], k, [stage], [ps])
            for j in range(n):
                copy(dst_fn(c0 + j), ps[:, j * 16:j * 16 + k], [ps], [dst_tts[c0 + j] if isinstance(dst_tts, list) else dst_tts])

    def rows_from_cols(dst_dram, src_fn, src_tts, k, C):
        per = STW // 128
        for p0 in range(0, C, per):
            np_ = min(per, C - p0)
            for c0 in range(p0, p0 + np_, 4):
                n = min(4, p0 + np_ - c0)
                ps = nps()
                for j in range(n):
                    st = src_tts[c0 + j] if isinstance(src_tts, list) else src_tts
                    fw.op("pe", lambda e, j=j, c=c0 + j, ps=ps: e.transpose(ps[:k, j * 128:(j + 1) * 128], src_fn(c), ident[:, :]), [st, ident], [ps])
                copy(stage_o[:k, (c0 - p0) * 128:(c0 - p0 + n) * 128], ps[:k, :n * 128], [ps], [stage_o])
            fw.dma("pool", dst_dram[:, p0 * 128:(p0 + np_) * 128], stage_o[:k, :np_ * 128], reads=[stage_o], is_output=True)

    def load_x(t, layer0):
        xt = xt_rot[cnt["xt"] % 2]; cnt["xt"] += 1
        if layer0:
            fw.dma("sp", xt[:, :], xp[t * 128:(t + 1) * 128, :], writes=[xt])
        else:
            fw.dma("sp", xt[:, :], y_p[t * 128:(t + 1) * 128, :], reads=[xres[t]], writes=[xt])
        return xt

    def load_xs(s_, layer0=False):
        xt = xt_rot[cnt["xt"] % 2]; cnt["xt"] += 1
        if layer0:
            fw.dma("sp", xt[:8, :], xs_d[s_ * 8:(s_ + 1) * 8, :], writes=[xt])
        else:
            fw.dma("sp", xt[:8, :], y_s[s_ * 8:(s_ + 1) * 8, :], reads=[xsres[s_]], writes=[xt])
        return xt

    def store_xs(s_, xt):
        fw.dma("sp", y_s[s_ * 8:(s_ + 1) * 8, :], xt[:8, :], reads=[xt], writes=[xsres[s_]], is_output=True)

    def store_x(t, xt):
        fw.dma("sp", y_p[t * 128:(t + 1) * 128, :], xt[:, :], reads=[xt], writes=[xres[t]], is_output=True)

    def add_res(xt, ntok, half, ps):
        tt_op("dve", xt[:ntok, half * 512:(half + 1) * 512], xt[:ntok, half * 512:(half + 1) * 512], ps[:ntok, :512], ALU.add, [xt, ps], [xt])

    SCRB = 37632
    scratch = fw.sb("scratch", [128, SCRB // 2], BF16)
    scr_f = scratch.ap.bitcast(F32)
    small = {}
    scr = {"off": 0}

    def sm_reset():
        print('scratch used', scr['off'])
        small.clear()
        scr["off"] = 0

    def sm(name, shape, dt=F32):
        if name in small:
            return small[name]
        esz = 4 if dt == F32 else 2
        n = 1
        for d_ in shape[1:]:
            n *= d_
        nbytes = (n * esz + 7) // 8 * 8
        off = scr["off"]
        assert off + nbytes <= SCRB, ("scratch overflow", name, off, nbytes)
        scr["off"] = off + nbytes
        base = scr_f if dt == F32 else scratch.ap
        ap = base[:shape[0], off // esz: off // esz + n]
        if len(shape) == 3:
            ap = ap.rearrange("p (a b) -> p a b", b=shape[2])
        elif len(shape) == 4:
            ap = ap.rearrange("p (a b c) -> p a b c", b=shape[2], c=shape[3])
        elif len(shape) == 5:
            ap = ap.rearrange("p (a b c d) -> p a b c d", b=shape[2], c=shape[3], d=shape[4])
        tt = barrier_tt(TT(ap, name))
        small[name] = tt
        return tt

    def phase_A0():
        off = 0
        sm_reset()
        Win, off = wload("w_in", w_in[:, 0:2560], off, 1024, 2560)
        Wab, off = wload("w_ab", w_in[:, 2560:2568], off, 1024, 8)
        Wout, off = wload("w_outab", w_out_ab, off, 1024, 1024)
        Wpool, off = wload("w_pool", w_pool, off, 512, 128)
        load_g(g_mix[0:1, :])
        cw = sm("dn_cw", [128, 12, 4]); cols_from_rows(lambda c: cw[:, c, :], cw, dn_conv_w, 4, 12)
        psc = sm("pool_sc", [128, 4, 1]); cols_from_rows(lambda c: psc[:, c, :], psc, pool_scale, 1, 4)
        dtb = sm("dtb", [128, 4]); fw.dma("sp", dtb[:], dn_dt_bias.to_broadcast([128, 4]), writes=[dtb])
        nea = sm("nea", [128, 4]); fw.dma("sp", nea[:], dn_a_log.to_broadcast([128, 4]), writes=[nea])
        act(nea[:], nea[:], AF.Exp, [nea], [nea])
        copy(nea[:], nea[:], [nea], [nea], eng="dve", scale=-1.0)
        nw = sm("normw", [128, 128]); fw.dma("sp", nw[:], dn_norm_w.to_broadcast([128, 128]), writes=[nw])
        invc = sm("invc", [128, 512]); fw.dma("sp", invc[:], c_invc[0], writes=[invc])
        aext = []; qext = []
        xoff = 22400
        for g in range(4):
            aext.append(barrier_tt(TT(arena_f[:, xoff:xoff + 527], "aext%d" % g))); xoff += 527
        for c in range(12):
            qext.append(barrier_tt(TT(arena_f[:, xoff:xoff + 515], "qext%d" % c))); xoff += 515
        assert xoff <= 33792
        qc = [sm("qc%d" % c, [128, 128]) for c in range(12)]
        S = [sm("S%d" % h, [128, 128]) for h in range(4)]
        mixT = sm("mixT", [128, 8, 128], BF16)
        gate_s = sm("gate_s", [128, 512]); ofin = sm("ofin", [128, 512])
        pa = [sm("pa%d" % i, [128, 143]) for i in range(2)]
        zb = sm("zb", [128, 128], BF16)
        col = {n: sm("col_" + n, [128, 4]) for n in ["beta", "lnb", "g", "G", "negG", "Gl", "eG", "beG", "nbeta", "ekg", "glast", "egl"]}
        gt = {n: sm("gt_" + n, [128, 128]) for n in ["sq", "rn"]}
        gnames = ["Ds", "DTb", "DTi", "B0", "B1", "R0", "R1", "Tt", "QKd", "X0v", "X0k", "kg", "nwT", "vnew", "tmpB"]
        gth = []
        aoff = 14720
        for h_ in range(4):
            d_ = {}
            for nm in gnames:
                d_[nm] = barrier_tt(TT(arena_f[:, aoff:aoff + 128], "gt%d_%s" % (h_, nm)))
                aoff += 128
            gth.append(d_)

        def tile(xt, ntok, tile0, o, rd):
            n = ntok
            ps = nps(); lin_tm(ps[:n, :512], hT4, Win, 2048, 512, n, ps, o=o, rd=rd)
            act(gate_s[:n, :], ps[:n, :512], AF.Silu, [ps], [gate_s])
            ps = nps(); lin_tm(ps[:n, :8], hT4, Wab, 0, 8, n, ps, o=o, rd=rd)
            act(col["beta"][:n, :], ps[:n, 0:4], AF.Sigmoid, [ps], [col["beta"]])
            tt_op("dve", col["g"][:n, :], ps[:n, 4:8], dtb[:n, :], ALU.add, [ps, dtb], [col["g"]])
            act(col["g"][:n, :], col["g"][:n, :], AF.Exp, [col["g"]], [col["g"]])
            act(col["g"][:n, :], col["g"][:n, :], AF.Ln, [col["g"]], [col["g"]], bias=1.0)
            tt_op("dve", col["g"][:n, :], col["g"][:n, :], nea[:n, :], ALU.mult, [col["g"], nea], [col["g"]])
            act(col["lnb"][:n, :], col["beta"][:n, :], AF.Ln, [col["beta"]], [col["lnb"]])
            ps = nps()
            mm(ps[:n, 0:4], LT[:n, :n], col["g"][:n, :], True, True, [LT, col["g"]], [ps])
            copy(col["G"][:n, :], ps[:n, 0:4], [ps], [col["G"]], eng="dve")
            ps = nps()
            mm(ps[:, 0:4], ones[:n, :], col["g"][:n, :], True, True, [ones, col["g"]], [ps])
            copy(col["glast"][:, :], ps[:, 0:4], [ps], [col["glast"]], eng="dve")
            act(col["egl"][:, :], col["glast"][:, :], AF.Exp, [col["glast"]], [col["egl"]])
            copy(col["negG"][:n, :], col["G"][:n, :], [col["G"]], [col["negG"]], eng="dve", scale=-1.0)
            tt_op("dve", col["Gl"][:n, :], col["G"][:n, :], col["lnb"][:n, :], ALU.add, [col["G"], col["lnb"]], [col["Gl"]])
            act(col["eG"][:n, :], col["G"][:n, :], AF.Exp, [col["G"]], [col["eG"]])
            tt_op("dve", col["beG"][:n, :], col["eG"][:n, :], col["beta"][:n, :], ALU.mult, [col["eG"], col["beta"]], [col["beG"]])
            copy(col["nbeta"][:n, :], col["beta"][:n, :], [col["beta"]], [col["nbeta"]], eng="dve", scale=-1.0)
            tt_op("dve", col["ekg"][:n, :], col["glast"][:n, :], col["G"][:n, :], ALU.subtract, [col["glast"], col["G"]], [col["ekg"]])
            act(col["ekg"][:n, :], col["ekg"][:n, :], AF.Exp, [col["ekg"]], [col["ekg"]])
            Lx = 15 + n
            for g in range(4):
                cur = aext[g]
                cb = o
                sh = 1
                for step in range(g + 1):
                    dst = pa[step % 2]
                    eng = "dve"
                    tt_op(eng, dst[:, sh:Lx], cur[:, cb + sh:cb + Lx], cur[:, cb:cb + Lx - sh], ALU.add, [cur], [dst])
                    cur = dst
                    cb = 0
                    sh *= 2
                nxt = pa[(g + 1) % 2]
                if tile0:
                    tt_op("dve", nxt[:, 15:Lx], cur[:, 15:Lx], invc[:, g * 128:g * 128 + n], ALU.mult, [cur, invc], [nxt])
                else:
                    copy(nxt[:, 15:Lx], cur[:, 15:Lx], [cur], [nxt], eng="dve", scale=1.0 / (2 ** (g + 1)))
                tt_op("dve", zb[:, :n], nxt[:, 15:Lx], aext[g][:, o + 15:o + Lx], ALU.subtract, [nxt, aext[g]], [zb])
                ps = nps()
                mm(ps[:, :n], Wpool[g][:, :], zb[:, :n], True, True, [Wpool[g], zb], [ps])
                copy(mixT[:, g, :n], ps[:, :n], [ps], [mixT], eng="act", scale=psc[:, g, 0:1])
            for c in range(12):
                fw.op("act", lambda e, c=c: e.mul(out=qc[c][:, :n], in_=qext[c][:, o:o + n], mul=cw[:, c, 0:1]), [qext[c], cw], [qc[c]])
                for k in range(1, 4):
                    stt("dve", qc[c][:, :n], qext[c][:, o + k:o + k + n], cw[:, c, k:k + 1], qc[c][:, :n], ALU.mult, ALU.add, [qext[c], cw, qc[c]], [qc[c]])
                act(qc[c][:, :n], qc[c][:, :n], AF.Silu, [qc[c]], [qc[c]])
            for c in range(8):
                act(gt["sq"][:, :n], qc[c][:, :n], AF.Square, [qc[c]], [gt["sq"]])
                ps = nps()
                mm(ps[:, :n], ones[:, :], gt["sq"][:, :n], True, True, [ones, gt["sq"]], [ps])
                sc = 128.0 if c < 4 else 1.0
                act(gt["rn"][:, :n], ps[:, :n], AF.Sqrt, [ps], [gt["rn"]], bias=EPS * sc, scale=sc)
                fw.op("dve", lambda e: e.reciprocal(out=gt["rn"][:, :n], in_=gt["rn"][:, :n]), [gt["rn"]], [gt["rn"]])
                tt_op("dve", qc[c][:, :n], qc[c][:, :n], gt["rn"][:, :n], ALU.mult, [qc[c], gt["rn"]], [qc[c]])
            nlev = 7 if n == 128 else 3
            PSH = [dict() for _ in range(4)]

            def st_A(h):
                g_ = gth[h]
                qT, kT = qc[h], qc[4 + h]
                pKK = nps(); mm(pKK[:n, :n], kT[:, :n], kT[:, :n], True, True, [kT], [pKK])
                pQK = nps(); mm(pQK[:n, :n], kT[:, :n], qT[:, :n], True, True, [kT, qT], [pQK])
                p1 = nps()
                mm(p1[:n, :n], col["G"][:n, h:h + 1].to_broadcast([n, n]), ident[:n, :n], True, False, [col["G"], ident], [p1])
                mm(p1[:n, :n], ident[:n, :n], m_ge[:n, :n], False, True, [ident, m_ge], [p1])
                act(g_["Ds"][:n, :n], p1[:n, :n], AF.Exp, [p1, col["G"]], [g_["Ds"]], bias=col["G"][:n, h:h + 1], scale=-1.0)
                p2 = nps()
                mm(p2[:n, :n], col["Gl"][:n, h:h + 1].to_broadcast([n, n]), ident[:n, :n], True, False, [col["Gl"], ident], [p2])
                mm(p2[:n, :n], ident[:n, :n], nm_le[:n, :n], False, True, [ident, nm_le], [p2])
                act(g_["DTb"][:n, :n], p2[:n, :n], AF.Exp, [p2, col["negG"]], [g_["DTb"]], bias=col["negG"][:n, h:h + 1], scale=1.0)
                p3 = nps()
                mm(p3[:n, :n], col["G"][:n, h:h + 1].to_broadcast([n, n]), ident[:n, :n], True, False, [col["G"], ident], [p3])
                mm(p3[:n, :n], ident[:n, :n], nm_lt[:n, :n], False, True, [ident, nm_lt], [p3])
                act(g_["DTi"][:n, :n], p3[:n, :n], AF.Exp, [p3, col["negG"]], [g_["DTi"]], bias=col["negG"][:n, h:h + 1], scale=1.0)
                stt("dve", g_["B0"][:n, :n], pKK[:n, :n], col["nbeta"][:n, h:h + 1], g_["Ds"][:n, :n], ALU.mult, ALU.mult, [pKK, col["nbeta"], g_["Ds"]], [g_["B0"]])
                stt("dve", g_["R0"][:n, :n], pKK[:n, :n], -1.0, g_["DTb"][:n, :n], ALU.mult, ALU.mult, [pKK, g_["DTb"]], [g_["R0"]])
                tt_op("dve", g_["QKd"][:n, :n], pQK[:n, :n], g_["DTi"][:n, :n], ALU.mult, [pQK, g_["DTi"]], [g_["QKd"]])
                tt_op("dve", g_["Tt"][:n, :n], g_["R0"][:n, :n], ident[:n, :n], ALU.add, [g_["R0"], ident], [g_["Tt"]])

            def st_L(h, k):
                g_ = gth[h]
                Bm = [g_["B0"], g_["B1"]]; Rm = [g_["R0"], g_["R1"]]; Tt = g_["Tt"]
                a, b = (k - 1) % 2, k % 2
                pP = nps(); mm(pP[:n, :n], Rm[a][:n, :n], Bm[a][:n, :n], True, True, [Rm[a], Bm[a]], [pP])
                if k < nlev - 1:
                    pR = nps(); mm(pR[:n, :n], Bm[a][:n, :n], Rm[a][:n, :n], True, True, [Rm[a], Bm[a]], [pR])
                copy(Bm[b][:n, :n], pP[:n, :n], [pP], [Bm[b]], eng="act")
                if k < nlev - 1:
                    copy(Rm[b][:n, :n], pR[:n, :n], [pR], [Rm[b]], eng="dve")
                pT = nps(); mm(pT[:n, :n], Bm[b][:n, :n], Tt[:n, :n], True, True, [Bm[b], Tt], [pT])
                tt_op("dve", Tt[:n, :n], Tt[:n, :n], pT[:n, :n], ALU.add, [Tt, pT], [Tt])

            def st_V(h):
                g_ = gth[h]
                kT, vT = qc[4 + h], qc[8 + h]
                Tt = g_["Tt"]
                pV = nps(); mm(pV[:n, :128], vT[:, :n], ident[:, :], True, True, [vT, ident], [pV])
                copy(g_["X0v"][:n, :], pV[:n, :128], [pV, col["beta"]], [g_["X0v"]], eng="dve", scale=col["beta"][:n, h:h + 1])
                pK = nps(); mm(pK[:n, :128], kT[:, :n], ident[:, :], True, True, [kT, ident], [pK])
                copy(g_["X0k"][:n, :], pK[:n, :128], [pK, col["beG"]], [g_["X0k"]], eng="dve", scale=col["beG"][:n, h:h + 1])
                copy(g_["kg"][:n, :], pK[:n, :128], [pK, col["ekg"]], [g_["kg"]], eng="dve", scale=col["ekg"][:n, h:h + 1])
                pW = nps(); mm(pW[:, :n], g_["X0k"][:n, :], Tt[:n, :n], True, True, [g_["X0k"], Tt], [pW])
                copy(g_["nwT"][:, :n], pW[:, :n], [pW], [g_["nwT"]], eng="act", scale=-1.0)

            def st_N(h):
                g_ = gth[h]
                qT = qc[h]
                Tt = g_["Tt"]
                pN = nps()
                mm(pN[:n, :128], Tt[:n, :n], g_["X0v"][:n, :], True, False, [Tt, g_["X0v"]], [pN])
                mm(pN[:n, :128], g_["nwT"][:, :n], S[h][:, :], False, True, [g_["nwT"], S[h]], [pN])
                copy(g_["vnew"][:n, :], pN[:n, :128], [pN], [g_["vnew"]], eng="act")
                pA = nps(); mm(pA[:n, :128], qT[:, :n], S[h][:, :], True, True, [qT, S[h]], [pA])
                pB = nps(); mm(pB[:n, :128], g_["QKd"][:n, :n], g_["vnew"][:n, :], True, True, [g_["QKd"], g_["vnew"]], [pB])
                copy(g_["tmpB"][:n, :], pB[:n, :128], [pB], [g_["tmpB"]], eng="act")
                stt("dve", ofin[:n, h * 128:(h + 1) * 128], pA[:n, :128], col["eG"][:n, h:h + 1], g_["tmpB"][:n, :], ALU.mult, ALU.add, [pA, col["eG"], g_["tmpB"]], [ofin])
                pS = nps(); mm(pS[:, :128], g_["kg"][:n, :], g_["vnew"][:n, :], True, True, [g_["kg"], g_["vnew"]], [pS])
                stt("dve", S[h][:, :], S[h][:, :], col["egl"][:, h:h + 1], pS[:, :128], ALU.mult, ALU.add, [S[h], col["egl"], pS], [S[h]])

            for h in range(4):
                st_A(h)
            for k in range(1, nlev):
                for h in range(4):
                    st_L(h, k)
            for h in range(4):
                st_V(h)
            for h in range(4):
                st_N(h)
            for h in range(4):
                ss = rstd_of(ofin[:n, h * 128:(h + 1) * 128], n, 128, [ofin])
                stt("dve", ofin[:n, h * 128:(h + 1) * 128], ofin[:n, h * 128:(h + 1) * 128], ss[:n, 0:1], nw[:n, :], ALU.mult, ALU.mult, [ofin, ss, nw], [ofin])
                tt_op("dve", ofin[:n, h * 128:(h + 1) * 128], ofin[:n, h * 128:(h + 1) * 128], gate_s[:n, h * 128:(h + 1) * 128], ALU.mult, [ofin, gate_s], [ofin])
            ps = nps()
            for j in range(4):
                tr(ps[:, j * 128:j * 128 + n], ofin[:n, j * 128:(j + 1) * 128], n, [ofin], [ps])
            copy(mixT[:, 4:8, :n], ps[:, :].rearrange("p (j t) -> p j t", t=128)[:, :, :n], [ps], [mixT])
            for half in range(2):
                ps = nps()
                for c in range(8):
                    mm(ps[:n, :512], mixT[:, c, :n], Wout[c][:, half * 512:(half + 1) * 512], c == 0, c == 7, [mixT, Wout[c]], [ps])
                add_res(xt, n, half, ps)

        def stile(loaders, n, first, finish):
            ntl = len(loaders)
            Wd = ntl * n
            for i, ld in enumerate(loaders):
                xt = ld()
                norm_tok(xt, n, gbc)
                half_tt = hT_rot[(i * n) // 256]
                for b in range(0, 8, 4):
                    ps = nps()
                    for j in range(4):
                        tr(ps[:, j * 128:j * 128 + n], hn[:n, (b + j) * 128:(b + j + 1) * 128], n, [hn], [ps])
                    copy(hT4[:, b:b + 4, i * n:i * n + n], ps[:, 0:512].rearrange("p (j t) -> p j t", t=128)[:, :, :n], [ps], [half_tt])
            rd = [hT_rot[0]] + ([hT_rot[1]] if Wd > 256 else [])
            for g in range(4):
                ps = nps()
                for kc in range(8):
                    mm(ps[:, :Wd], Win[kc][:, g * 128:(g + 1) * 128], hT4[:, kc, :Wd], kc == 0, kc == 7, [Win[kc]] + rd, [ps])
                copy(aext[g][:, 15:15 + Wd], ps[:, :Wd], [ps], [aext[g]])
            for c in range(12):
                ps = nps()
                for kc in range(8):
                    mm(ps[:, :Wd], Win[kc][:, 512 + c * 128:512 + (c + 1) * 128], hT4[:, kc, :Wd], kc == 0, kc == 7, [Win[kc]] + rd, [ps])
                copy(qext[c][:, 3:3 + Wd], ps[:, :Wd], [ps], [qext[c]])
            for i, ld in enumerate(loaders):
                xt = ld()
                tile(xt, n, first and i == 0, i * n, rd)
                finish(i, xt)
            for g in range(4):
                copy(aext[g][:, 0:15], aext[g][:, Wd:Wd + 15], [aext[g]], [aext[g]], eng="pool")
            for c in range(12):
                copy(qext[c][:, 0:3], qext[c][:, Wd:Wd + 3], [qext[c]], [qext[c]], eng="pool")

        for g in range(4):
            fw.op("pool", lambda e, g=g: e.memset(aext[g][:, :], 0.0), writes=[aext[g]])
        for c in range(12):
            fw.op("pool", lambda e, c=c: e.memset(qext[c][:, :], 0.0), writes=[qext[c]])
        for h in range(4):
            fw.op("pool", lambda e, h=h: e.memset(S[h][:, :], 0.0), writes=[S[h]])
        for t0 in range(0, NT, 4):
            stile([(lambda t=t0 + i: load_x(t, True)) for i in range(4)], 128, t0 == 0, lambda i, xt, t0=t0: store_x(t0 + i, xt))
        rows_from_cols(o_pool_p[:, :], lambda c: aext[c][:, 0:15], aext, 15, 4)
        rows_from_cols(o_dconv_p[:, :], lambda c: qext[c][:, 0:3], qext, 3, 12)
        for h in range(4):
            fw.dma("pool", o_dn_p[h], S[h][:, :], reads=[S[h]], is_output=True)
        for s in range(NS):
            cols_from_rows(lambda c: aext[c][:, 0:15], aext, st_pool[s], 15, 4)
            cols_from_rows(lambda c: qext[c][:, 0:3], qext, st_dconv[s], 3, 12)
            for h in range(4):
                fw.dma("sp", S[h][:, :], st_dn[s, h], writes=[S[h]])
            stile([lambda s=s: load_xs(s, True)], 8, False, lambda i, xt, s=s: store_xs(s, xt))
            rows_from_cols(o_pool_s[s], lambda c: aext[c][:, 0:15], aext, 15, 4)
            rows_from_cols(o_dconv_s[s], lambda c: qext[c][:, 0:3], qext, 3, 12)
            for h in range(4):
                fw.dma("pool", o_dn_s[s, h], S[h][:, :], reads=[S[h]], is_output=True)

    def phase_B(l):
        off = 0
        sm_reset()
        Wq, off = wload("wq", w_mem["q"][l], off, 1024, 1024)
        Wk, off = wload("wk", w_mem["k"][l], off, 1024, 1024)
        Wv, off = wload("wv", w_mem["v"][l], off, 1024, 1024)
        Wo, off = wload("wo", w_mem["o"][l], off, 1024, 1024)
        KTp = sm("KTp", [128, 8, 256], BF16); Vaug = sm("Vaug", [128, 2, 4, 257], BF16)
        KTs = sm("KTs", [128, 8, 256], BF16); Vaugs = sm("Vaugs", [128, 2, 4, 257], BF16)
        fw.op("pool", lambda e: e.memset(Vaug[:, :, :, 256:257], 1.0), writes=[Vaug])
        fw.op("pool", lambda e: e.memset(Vaugs[:, :, :, 256:257], 1.0), writes=[Vaugs])
        qT = sm("qT", [128, 8, 128], BF16); pT = sm("pT", [128, 2, 128], BF16)
        otok = sm("otok", [128, 1024]); oT = sm("oT", [128, 8, 128], BF16)
        rd = sm("rd", [128, 1]); kvs = sm("kvs", [128, 512]); kraw = sm("kraw", [128, 1024])
        load_g(g_mem_kv[l:l + 1, :])
        for nb in range(2):
            xt = xt_rot[cnt["xt"] % 2]; cnt["xt"] += 1
            fw.dma("sp", xt[:, :], memp[nb * 128:(nb + 1) * 128, :], writes=[xt])
            mT = norm_T(xt, 128, gbc)
            for ec in range(8):
                ps = nps(); lin_fm(ps[:, :128], Wk, ec * 128, 128, mT, 128, ps)
                copy(KTp[:, ec, nb * 128:(nb + 1) * 128], ps[:, :128], [ps], [KTp])
            for half in range(2):
                ps = nps(); lin_tm(ps[:, :512], mT, Wk, half * 512, 512, 128, ps)
                copy(kvs[:, :], ps[:, :512], [ps], [kvs])
                fw.dma("pool", o_memk[l, nb * 128:(nb + 1) * 128, half * 512:(half + 1) * 512], kvs[:, :], reads=[kvs], is_output=True)
                ps = nps(); lin_tm(ps[:, :512], mT, Wv, half * 512, 512, 128, ps)
                copy(kvs[:, :], ps[:, :512], [ps], [kvs])
                copy(Vaug[:, nb, 2 * half:2 * half + 2, 0:256], ps[:, :512].rearrange("p (h e) -> p h e", e=256), [ps], [Vaug])
                fw.dma("pool", o_memv[l, nb * 128:(nb + 1) * 128, half * 512:(half + 1) * 512], kvs[:, :], reads=[kvs], is_output=True)
        load_g(g_mem_q[l:l + 1, :])

        def tile(xt, n, KT, VA):
            hT = norm_T(xt, n, gbc)
            for ec in range(8):
                ps = nps(); lin_fm(ps[:, :n], Wq, ec * 128, 128, hT, n, ps)
                copy(qT[:, ec, :n], ps[:, :n], [ps], [qT])
            for h in range(4):
                ps = nps()
                for nb in range(2):
                    for ec in range(2):
                        mm(ps[:, nb * 128:nb * 128 + n], KT[:, 2 * h + ec, nb * 128:(nb + 1) * 128], qT[:, 2 * h + ec, :n], ec == 0, ec == 1, [KT, qT], [ps])
                act(pT[:, :, :n], ps[:, 0:256].rearrange("p (b t) -> p b t", t=128)[:, :, :n], AF.Exp, [ps], [pT], scale=1.0 / 16.0)
                ps2 = nps()
                for nb in range(2):
                    mm(ps2[:n, 0:257], pT[:, nb, :n], VA[:, nb, h, :], nb == 0, nb == 1, [pT, VA], [ps2])
                fw.op("dve", lambda e, ps2=ps2: e.reciprocal(out=rd[:n, :], in_=ps2[:n, 256:257]), [ps2], [rd])
                copy(otok[:n, h * 256:(h + 1) * 256], ps2[:n, 0:256], [ps2, rd], [otok], scale=rd[:n, 0:1])
            to_fm(otok, n, oT)
            for half in range(2):
                ps = nps(); lin_tm(ps[:n, :512], oT, Wo, half * 512, 512, n, ps)
                add_res(xt, n, half, ps)

        for t in range(NT):
            xt = load_x(t, False)
            tile(xt, 128, KTp, Vaug)
            store_x(t, xt)
        for s in range(NS):
            for nb in range(2):
                fw.dma("sp", kraw[:, :], cmk[l, s, nb * 128:(nb + 1) * 128, :], writes=[kraw])
                for b in range(2):
                    ps = nps()
                    for j in range(4):
                        tr(ps[:, j * 128:(j + 1) * 128], kraw[:, (4 * b + j) * 128:(4 * b + j + 1) * 128], 128, [kraw], [ps])
                    copy(KTs[:, 4 * b:4 * b + 4, nb * 128:(nb + 1) * 128], ps[:, :].rearrange("p (j t) -> p j t", t=128), [ps], [KTs])
                fw.dma("pool", Vaugs[:, nb, :, 0:256], cmv[l, s, nb * 128:(nb + 1) * 128, :].rearrange("n (h e) -> n h e", e=256), writes=[Vaugs])
            xs_t = load_xs(s)
            tile(xs_t, 8, KTs, Vaugs)
            store_xs(s, xs_t)

    def phase_C(l, final):
        off = 0
        sm_reset()
        Wup, off = wload("wup", w_up[l], off, 1024, 5632)
        Wdn, off = wload("wdn", w_down[l], off, 2816, 1024)
        load_g(g_ffn[l:l + 1, :])
        cwf = sm("cwf", [128, 44, 3]); cols_from_rows(lambda c: cwf[:, c, :], cwf, ffn_cw[l], 3, 44)
        hist = sm("fhist", [128, 44, 2])
        actT = sm("actT", [128, 22, 512], BF16)
        extA = [sm("extA%d" % i, [128, 514]) for i in range(2)]
        extB = [sm("extB%d" % i, [128, 514]) for i in range(2)]
        tA = sm("ftA", [128, 512]); tB = sm("ftB", [128, 512])
        hTs = [hT_rot[0], hT_rot[1]]

        def stile(loaders, n, finish):
            ntl = len(loaders)
            Wd = ntl * n
            for i, ld in enumerate(loaders):
                xt = ld()
                norm_tok(xt, n, gbc)
                half_tt = hTs[(i * n) // 256]
                for b in range(0, 8, 4):
                    ps = nps()
                    for j in range(4):
                        tr(ps[:, j * 128:j * 128 + n], hn[:n, (b + j) * 128:(b + j + 1) * 128], n, [hn], [ps])
                    copy(hT4[:, b:b + 4, i * n:i * n + n], ps[:, 0:512].rearrange("p (j t) -> p j t", t=128)[:, :, :n], [ps], [half_tt])
            rd = [hTs[0]] + ([hTs[1]] if Wd > 256 else [])
            for j in range(22):
                ea, eb = extA[j % 2], extB[j % 2]
                for (c, ext, ev) in ((j, ea, "act"), (j + 22, eb, "act")):
                    ps = nps()
                    for kc in range(8):
                        mm(ps[:, :Wd], Wup[kc][:, c * 128:(c + 1) * 128], hT4[:, kc, :Wd], kc == 0, kc == 7, [Wup[kc]] + rd, [ps])
                    copy(ext[:, 0:2], hist[:, c, :], [hist], [ext], eng="pool")
                    copy(ext[:, 2:2 + Wd], ps[:, :Wd], [ps], [ext], eng=ev)
                    copy(hist[:, c, :], ext[:, Wd:Wd + 2], [ext], [hist], eng="pool")
                c2 = j + 22
                fw.op("act", lambda e, ea=ea, j=j: e.mul(out=tA[:, :Wd], in_=ea[:, 0:Wd], mul=cwf[:, j, 0:1]), [ea, cwf], [tA])
                fw.op("act", lambda e, eb=eb, c2=c2: e.mul(out=tB[:, :Wd], in_=eb[:, 0:Wd], mul=cwf[:, c2, 0:1]), [eb, cwf], [tB])
                for k in (1, 2):
                    stt("dve", tA[:, :Wd], ea[:, k:k + Wd], cwf[:, j, k:k + 1], tA[:, :Wd], ALU.mult, ALU.add, [ea, cwf, tA], [tA])
                    stt("dve", tB[:, :Wd], eb[:, k:k + Wd], cwf[:, c2, k:k + 1], tB[:, :Wd], ALU.mult, ALU.add, [eb, cwf, tB], [tB])
                act(tA[:, :Wd], tA[:, :Wd], AF.Silu, [tA], [tA])
                tt_op("dve", actT[:, j, :Wd], tA[:, :Wd], tB[:, :Wd], ALU.mult, [tA, tB], [actT])
            for i, ld in enumerate(loaders):
                xt = ld()
                for half in range(2):
                    ps = nps()
                    for j in range(22):
                        mm(ps[:n, :512], actT[:, j, i * n:i * n + n], Wdn[j][:, half * 512:(half + 1) * 512], j == 0, j == 21, [actT, Wdn[j]], [ps])
                    add_res(xt, n, half, ps)
                finish(i, xt)

        fw.op("pool", lambda e: e.memset(hist[:, :, :], 0.0), writes=[hist])
        for t0 in range(0, NT, 4):
            def fin(i, xt, t0=t0):
                t = t0 + i
                if final:
                    norm_tok(xt, 128, gfin)
                    fw.dma("sp", y_p[t * 128:(t + 1) * 128, :], hn[:, :], reads=[hn], writes=[xres[t]], is_output=True)
                else:
                    store_x(t, xt)
            stile([(lambda t=t0 + i: load_x(t, False)) for i in range(4)], 128, fin)
        rows_from_cols(o_ffn_p[l], lambda c: hist[:, c, :], hist, 2, 44)
        for s_ in range(NS):
            cols_from_rows(lambda c: hist[:, c, :], hist, st_ffn[l, s_], 2, 44)

            def fin_s(i, xt, s_=s_):
                if final:
                    norm_tok(xt, 8, gfin)
                    fw.dma("sp", y_s[s_ * 8:(s_ + 1) * 8, :], hn[:8, :], reads=[hn], writes=[xsres[s_]], is_output=True)
                else:
                    store_xs(s_, xt)
            stile([lambda s_=s_: load_xs(s_)], 8, fin_s)
            rows_from_cols(o_ffn_s[l, s_], lambda c: hist[:, c, :], hist, 2, 44)

    def phase_A1():
        off = 0
        sm_reset()
        Wqkv, off = wload("wqkv", w_qkv, off, 1024, 4608)
        Woc, off = wload("woc", w_out_c, off, 512, 1024)
        load_g(g_mix[1:2, :])
        qk = sm("qk_tok", [128, 512]); vt = sm("v_tok", [128, 512]); vbf = sm("v_bf", [128, 512], BF16)
        cs = sm("cs_t", [128, 8]); sn = sm("sn_t", [128, 8])
        r1 = sm("rp1", [128, 8, 8]); r2 = sm("rp2", [128, 8, 8]); r3 = sm("rp3", [128, 8, 8])
        fmT = sm("fmT", [128, 4, 128], BF16)
        mb = sm("mb", [128, 2, 128], BF16); fw.dma("pool", mb[:], c_mb.rearrange("a p n -> p a n"), writes=[mb])
        smk = sm("smk", [128, 24, 8], BF16); fw.dma("pool", smk[:], c_sm.rearrange("p (b t) -> p b t", t=8), writes=[smk])
        ident_bf = sm("ident_bf", [128, 128], BF16); copy(ident_bf[:], ident[:], [ident], [ident_bf], eng="dve")
        sQT = [[sm("sQT%d_%d" % (s, g), [128, 4, 8], BF16) for g in range(3)] for s in range(NS)]
        sKT = [[sm("sKT%d_%d" % (s, g), [128, 4, 8], BF16) for g in range(3)] for s in range(NS)]
        sV_d = nc.dram_tensor("sV_d", [NS, 3, 8, 512], BF16).ap()
        sV_tt = TT(None, "sV_tt")
        sVn = sm("sVn", [8, 512], BF16)

        def rope(n, pos0):
            fw.dma("sp", cs[:n, :], c_cos[pos0:pos0 + n, :], writes=[cs])
            fw.dma("sp", sn[:n, :], c_sin[pos0:pos0 + n, :], writes=[sn])
            X = qk[:n, :].rearrange("p (h e) -> p h e", e=64)
            x1 = X[:, :, 0:8]; x2 = X[:, :, 8:16]
            cb = cs[:n, :].unsqueeze(1).to_broadcast([n, 8, 8]); sb_ = sn[:n, :].unsqueeze(1).to_broadcast([n, 8, 8])
            tt_op("dve", r1[:n], x1, sb_, ALU.mult, [qk, sn], [r1])
            tt_op("dve", r2[:n], x2, sb_, ALU.mult, [qk, sn], [r2])
            tt_op("dve", r3[:n], x2, cb, ALU.mult, [qk, cs], [r3])
            tt_op("dve", x1, x1, cb, ALU.mult, [qk, cs], [qk])
            tt_op("dve", x1, x1, r2[:n], ALU.subtract, [qk, r2], [qk])
            tt_op("dve", x2, r3[:n], r1[:n], ALU.add, [r3, r1], [qk])

        def proj_tile(xt, n, pos0, t, s):
            hT = norm_T(xt, n, gbc)
            for g in range(3):
                W = WINS[g][0]
                for si in range(3):
                    ps = nps(); lin_tm(ps[:n, :512], hT, Wqkv, g * 1536 + si * 512, 512, n, ps)
                    if si < 2:
                        copy(qk[:n, :], ps[:n, :512], [ps], [qk])
                        rope(n, pos0)
                        to_fm(qk, n, fmT, nchunks=4)
                        if s is None:
                            dst = (QT_d if si == 0 else KT_d)[g]
                            fw.dma("pool", dst.rearrange("(c p) s -> p c s", p=128)[:, :, t * 128:(t + 1) * 128], fmT[:, :, :], reads=[fmT], writes=[qkv_tt[t]])
                        else:
                            copy((sQT if si == 0 else sKT)[s][g][:, :, :], fmT[:, :, :8], [fmT], [(sQT if si == 0 else sKT)[s][g]], eng="pool")
                        if si == 1:
                            if s is None:
                                lo = S - min(W, S)
                                if (t + 1) * 128 > lo:
                                    fw.dma("pool", o_win_p[g][t * 128 - lo:(t + 1) * 128 - lo, 0:512], qk[:, :], reads=[qk], is_output=True)
                            else:
                                fw.dma("pool", o_win_s[g][s, :, 0:512], qk[:8, :], reads=[qk], is_output=True)
                    else:
                        copy(vt[:n, :], ps[:n, :512], [ps], [vt])
                        copy(vbf[:n, :], ps[:n, :512], [ps], [vbf])
                        if s is None:
                            fw.dma("pool", V_d[g][t * 128:(t + 1) * 128, :], vbf[:, :], reads=[vbf], writes=[qkv_tt[t]])
                            lo = S - min(W, S)
                            if (t + 1) * 128 > lo:
                                fw.dma("pool", o_win_p[g][t * 128 - lo:(t + 1) * 128 - lo, 512:1024], vt[:, :], reads=[vt], is_output=True)
                        else:
                            fw.dma("sp", sV_d[s, g], vbf[:8, :], reads=[vbf], writes=[sV_tt])
                            fw.dma("pool", o_win_s[g][s, :, 512:1024], vt[:8, :], reads=[vt], is_output=True)

        for t in range(NT):
            xt = load_x(t, False)
            proj_tile(xt, 128, t * 128, t, None)
        for s in range(NS):
            proj_tile(load_xs(s), 8, S, None, s)

        print('A1 proj done at op', fw.total)
        HALF = 2048
        NH = S // HALF
        accN = barrier_tt(TT(arena_f[:, 0:4 * HALF].rearrange("p (c s) -> p c s", s=HALF), "accN"))
        accD = barrier_tt(TT(arena_f[:, 4 * HALF:8 * HALF].rearrange("p (c s) -> p c s", s=HALF), "accD"))
        QTs = barrier_tt(TT(arena.ap[:, 40960:40960 + 8192].rearrange("p (c s) -> p c s", s=2048), "QTs"))
        KTs_ = barrier_tt(TT(arena.ap[:, 49152:49152 + 16384].rearrange("p (c s) -> p c s", s=4096), "KTs"))
        Vst = [sm("Vst%d" % i, [128, 512], BF16) for i in range(2)]
        Vpad = [sm("Vpad%d" % i, [128, 2, 4, 2, 128], BF16) for i in range(2)]
        onespad = sm("onespad", [128, 2, 128], BF16)
        fw.op("pool", lambda e: e.memset(onespad[:], 0.0), writes=[onespad])
        fw.op("pool", lambda e: e.memset(onespad[:, 0, 0:64], 1.0), writes=[onespad])
        fw.op("pool", lambda e: e.memset(onespad[:, 1, 64:128], 1.0), writes=[onespad])
        for i in range(2):
            fw.op("pool", lambda e, i=i: e.memset(Vpad[i][:], 0.0), writes=[Vpad[i]])
        PTR = [sm("PT%d" % i, [128, 2, 128], BF16) for i in range(5)]
        ptc = [0]
        rec = sm("rec", [128, 4, 128]); oTb = sm("oTb", [128, 4, 128], BF16)
        vcnt = [0]
        for hf in range(NH):
            for g in range(3):
                W, d = WINS[g]
                SB = 128 * d
                for sbk in range(HALF // SB):
                    q_sb = hf * (HALF // SB) + sbk
                    base = q_sb * SB
                    lb = base - hf * HALF
                    fw.dma("sp", QTs[:, :, 0:SB], QT_d[g].rearrange("(c p) s -> p c s", p=128)[:, :, base:base + SB], reads=qkv_tt, writes=[QTs])
                    if q_sb > 0:
                        fw.dma("sp", KTs_[:, :, 0:2 * SB], KT_d[g].rearrange("(c p) s -> p c s", p=128)[:, :, base - SB:base + SB], reads=qkv_tt, writes=[KTs_])
                    else:
                        fw.dma("sp", KTs_[:, :, SB:2 * SB], KT_d[g].rearrange("(c p) s -> p c s", p=128)[:, :, base:base + SB], reads=qkv_tt, writes=[KTs_])
                    kbs = (0, 1) if q_sb > 0 else (1,)
                    for r in range(d):
                        vp = Vpad[vcnt[0] % 2]; vcnt[0] += 1
                        for kb in kbs:
                            p0 = base - SB + r if kb == 0 else base + r
                            vs = Vst[kb]
                            fw.dma("sp", vs[:, :], V_d[g][p0:p0 + 127 * d + 1:d, :], reads=qkv_tt, writes=[vs])
                            for hh in range(2):
                                copy(vp[:, kb, :, hh, hh * 64:(hh + 1) * 64], vs[:, :].rearrange("p (c h e) -> p c h e", h=2, e=64)[:, :, hh, :], [vs], [vp], eng="pool")
                        def a_scores(c):
                            PT = [PTR[(ptc[0] + i_) % 5] for i_ in range(2)]
                            ptc[0] += 2
                            for hh in range(2):
                                ps = nps()
                                pl = slice(hh * 64, (hh + 1) * 64)
                                for kb in kbs:
                                    ksl = slice(r, SB, d) if kb == 0 else slice(SB + r, 2 * SB, d)
                                    mm(ps[:, kb * 128:(kb + 1) * 128], KTs_[pl, c, ksl], QTs[pl, c, r:SB:d], True, False, [KTs_, QTs], [ps])
                                    mm(ps[:, kb * 128:(kb + 1) * 128], ident_bf[:, :], mb[:, kb, :], False, True, [ident_bf, mb], [ps])
                                k0 = kbs[0]
                                act(PT[hh][:, k0:2, :], ps[:, k0 * 128:256].rearrange("p (b t) -> p b t", t=128), AF.Exp, [ps], [PT[hh]], scale=0.125)
                            return PT

                        def a_pv(c, PT):
                            psN = nps(); psD = nps()
                            seq = [(hh, kb) for hh in range(2) for kb in kbs]
                            for i, (hh, kb) in enumerate(seq):
                                mm(psN[:, :128], vp[:, kb, c, hh, :], PT[hh][:, kb, :], i == 0, i == len(seq) - 1, [vp, PT[hh]], [psN])
                            for i, (hh, kb) in enumerate(seq):
                                mm(psD[:, :128], onespad[:, hh, :], PT[hh][:, kb, :], i == 0, i == len(seq) - 1, [onespad, PT[hh]], [psD])
                            dN = accN[:, c, lb + r:lb + SB:d]; dD = accD[:, c, lb + r:lb + SB:d]
                            if g == 0:
                                copy(dN, psN[:, :128], [psN], [accN], eng="act")
                                copy(dD, psD[:, :128], [psD], [accD], eng="dve")
                            else:
                                tt_op("dve", dN, dN, psN[:, :128], ALU.add, [accN, psN], [accN])
                                tt_op("dve", dD, dD, psD[:, :128], ALU.add, [accD, psD], [accD])

                        pts = {0: a_scores(0)}
                        for c in range(4):
                            if c + 1 < 4:
                                pts[c + 1] = a_scores(c + 1)
                            a_pv(c, pts[c])
            for tl in range(HALF // 128):
                t = hf * (HALF // 128) + tl
                fw.op("dve", lambda e, tl=tl: e.reciprocal(out=rec[:, :, :], in_=accD[:, :, tl * 128:(tl + 1) * 128]), [accD], [rec])
                tt_op("dve", oTb[:, :, :], accN[:, :, tl * 128:(tl + 1) * 128], rec[:, :, :], ALU.mult, [accN, rec], [oTb])
                xt = load_x(t, False)
                for half in range(2):
                    ps = nps()
                    for c in range(4):
                        mm(ps[:, :512], oTb[:, c, :], Woc[c][:, half * 512:(half + 1) * 512], c == 0, c == 3, [oTb, Woc[c]], [ps])
                    add_res(xt, 128, half, ps)
                store_x(t, xt)
        print('A1 prompt attn done at op', fw.total)
        craw = sm("craw", [128, 1024]); KTb = sm("KTb", [128, 4, 128], BF16)
        sVp = [sm("sVp%d" % i, [128, 4, 2, 128], BF16) for i in range(2)]
        for i in range(2):
            fw.op("pool", lambda e, i=i: e.memset(sVp[i][:], 0.0), writes=[sVp[i]])
        PTs = sm("PTs", [128, 8, 8], BF16)
        aN = sm("aNs", [128, 32]); aD = sm("aDs", [128, 32]); oTs = sm("oTs", [128, 32], BF16)
        bases = [0, 1, 5]
        bcnt = [0]
        for s in range(NS):
            xs_t = load_xs(s)
            first = True
            for g in range(3):
                W, d = WINS[g]
                nblk = W // 128
                for kb in range(nblk + 1):
                    newblk = (kb == nblk)
                    nk = 8 if newblk else 128
                    vp = sVp[bcnt[0] % 2]; bcnt[0] += 1
                    if not newblk:
                        fw.dma("sp", craw[:, :], cwin[g][s, kb * 128:(kb + 1) * 128, :], writes=[craw])
                        ps = nps()
                        for j in range(4):
                            tr(ps[:, j * 128:(j + 1) * 128], craw[:, j * 128:(j + 1) * 128], 128, [craw], [ps])
                        copy(KTb[:, :, :], ps[:, :].rearrange("p (j t) -> p j t", t=128), [ps], [KTb])
                        for hh in range(2):
                            copy(vp[:, :, hh, hh * 64:(hh + 1) * 64], craw[:, 512:1024].rearrange("p (c h e) -> p c h e", h=2, e=64)[:, :, hh, :], [craw], [vp], eng="pool")
                        ktt, kap = KTb, KTb
                        bi = bases[g] + kb
                    else:
                        fw.dma("sp", sVn[:8, :], sV_d[s, g], reads=[sV_tt], writes=[sVn])
                        for hh in range(2):
                            copy(vp[:8, :, hh, hh * 64:(hh + 1) * 64], sVn[:8, :].rearrange("p (c h e) -> p c h e", h=2, e=64)[:, :, hh, :], [sVn], [vp], eng="pool")
                        ktt, kap = sKT[s][g], sKT[s][g]
                        bi = 21 + g
                        k2 = sm("k2n", [64, 2, 4, 8], BF16); q2 = sm("q2n", [64, 2, 4, 8], BF16)
                        for hh in range(2):
                            copy(k2[:, hh, :, :], sKT[s][g][hh * 64:(hh + 1) * 64, :, :], [sKT[s][g]], [k2], eng="pool")
                            copy(q2[:, hh, :, :], sQT[s][g][hh * 64:(hh + 1) * 64, :, :], [sQT[s][g]], [q2], eng="pool")
                    ps = nps()
                    for c in range(4):
                        for hh in range(2):
                            pl = slice(hh * 64, (hh + 1) * 64)
                            hd = 2 * c + hh
                            if newblk:
                                mm(ps[:nk, hd * 8:(hd + 1) * 8], k2[:, hh, c, :nk], q2[:, hh, c, :8], True, True, [k2, q2], [ps])
                            else:
                                mm(ps[:nk, hd * 8:(hd + 1) * 8], kap[pl, c, :nk], sQT[s][g][pl, c, :8], True, True, [ktt, sQT[s][g]], [ps])
                    act(PTs[:nk, :, :], ps[:nk, 0:64].rearrange("p (h t) -> p h t", t=8), AF.Exp, [ps], [PTs], scale=0.125)
                    tt_op("dve", PTs[:nk, :, :], PTs[:nk, :, :], smk[:nk, bi, :].unsqueeze(1).to_broadcast([nk, 8, 8]), ALU.mult, [PTs, smk], [PTs])
                    psN = nps(); psD = nps()
                    for c in range(4):
                        for hh in range(2):
                            hd = 2 * c + hh
                            mm(psN[:, c * 8:(c + 1) * 8], vp[:nk, c, hh, :], PTs[:nk, hd, :], hh == 0, hh == 1, [vp, PTs], [psN])
                    for c in range(4):
                        for hh in range(2):
                            hd = 2 * c + hh
                            mm(psD[:, c * 8:(c + 1) * 8], onespad[:nk, hh, :], PTs[:nk, hd, :], hh == 0, hh == 1, [onespad, PTs], [psD])
                    if first:
                        copy(aN[:, :], psN[:, :32], [psN], [aN], eng="act")
                        copy(aD[:, :], psD[:, :32], [psD], [aD], eng="dve")
                        first = False
                    else:
                        tt_op("dve", aN[:, :], aN[:, :], psN[:, :32], ALU.add, [aN, psN], [aN])
                        tt_op("dve", aD[:, :], aD[:, :], psD[:, :32], ALU.add, [aD, psD], [aD])
            fw.op("dve", lambda e: e.reciprocal(out=aD[:, :], in_=aD[:, :]), [aD], [aD])
            tt_op("dve", oTs[:, :], aN[:, :], aD[:, :], ALU.mult, [aN, aD], [oTs])
            for half in range(2):
                ps = nps()
                for c in range(4):
                    mm(ps[:8, :512], oTs[:, c * 8:(c + 1) * 8], Woc[c][:, half * 512:(half + 1) * 512], c == 0, c == 3, [oTs, Woc[c]], [ps])
                add_res(xs_t, 8, half, ps)
            store_xs(s, xs_t)

    if 'A0' in phases: phase_A0()
    if 'B0' in phases: phase_B(0)
    if 'C0' in phases: phase_C(0, False)
    if 'A1' in phases: phase_A1()
    if 'B1' in phases: phase_B(1)
    if 'C1' in phases: phase_C(1, True)
    fw.finish()
    return nc, fw


def _consts(S):
    c = {}
    r = np.arange(128)[:, None]; cc = np.arange(128)[None, :]
    c["c_ident"] = np.eye(128, dtype=np.float32)
    c["c_mge"] = (BIG * (cc >= r)).astype(np.float32)
    c["c_nmle"] = (-BIG * (cc <= r)).astype(np.float32)
    c["c_nmlt"] = (-BIG * (cc < r)).astype(np.float32)
    c["c_LT"] = (r <= cc).astype(np.float32)
    invc = np.zeros((2, 128, 512), np.float32)
    for g, w in enumerate((2, 4, 8, 16)):
        pos = np.arange(128)
        invc[0, :, g * 128:(g + 1) * 128] = (1.0 / np.minimum(pos + 1, w))[None, :]
        invc[1, :, g * 128:(g + 1) * 128] = 1.0 / w
    c["c_invc"] = invc
    half = 8
    inv_freq = (np.float32(500000.0) ** (-np.arange(half, dtype=np.float32) / np.float32(half))).astype(np.float32)
    pos = np.concatenate([np.arange(S), 8192 + np.arange(8)]).astype(np.float32)
    ang = (pos[:, None] * inv_freq[None, :]).astype(np.float32)
    c["c_cos"] = np.cos(ang.astype(np.float64)).astype(np.float32)
    c["c_sin"] = np.sin(ang.astype(np.float64)).astype(np.float32)
    NEG = -240000.0
    j = np.arange(128)[:, None]; i = np.arange(128)[None, :]
    mb = np.zeros((2, 128, 128), np.float32)
    mb[0] = np.where(j >= i, 0.0, NEG)
    mb[1] = np.where(j <= i, 0.0, NEG)
    c["c_mb"] = mb
    smk = np.zeros((128, 24, 8), np.float32)
    bases = [0, 1, 5]
    for g, (W, d) in enumerate(WINS):
        for kb in range(W // 128):
            e = kb * 128 + np.arange(128)[:, None]
            t = np.arange(8)[None, :]
            diff = W + t - e
            smk[:, bases[g] + kb, :] = ((diff % d == 0) & (diff // d >= 1) & (diff // d <= 128)).astype(np.float32)
        p = np.arange(128)[:, None]; t = np.arange(8)[None, :]
        smk[:, 21 + g, :] = ((p <= t) & ((t - p) % d == 0) & (p < 8)).astype(np.float32)
    c["c_sm"] = smk.reshape(128, 24 * 8)
    return c


_CACHE = {}


def _core_inputs(inp, c, S, NS, consts):
    b = c % 4
    ss = slice(NS * c, NS * (c + 1))
    f = lambda a: np.ascontiguousarray(a, dtype=np.float32)
    m = {
        "xp": f(inp["x_prompt"][b]), "xs": f(inp["x_sample"][ss].reshape(NS * 8, 1024)), "memp": f(inp["mem_prompt"][b]),
        "st_pool": f(inp["state_pool"][0, ss]), "st_dconv": f(inp["state_dn_conv"][0, ss]), "st_dn": f(inp["state_dn"][0, ss]),
        "cw128": f(inp["cache_win_w128"][0, ss].reshape(NS, 128, 1024)),
        "cw512": f(inp["cache_win_w512"][0, ss].reshape(NS, 512, 1024)),
        "cw2048": f(inp["cache_win_w2048"][0, ss].reshape(NS, 2048, 1024)),
        "cmk": f(inp["cache_mem_k"][:, ss].reshape(2, NS, 256, 1024)), "cmv": f(inp["cache_mem_v"][:, ss].reshape(2, NS, 256, 1024)),
        "st_ffn": f(inp["state_ffn_conv"][:, ss]),
        "g_mix": f(inp["g_mix"]), "w_in": f(inp["w_in_ab"][0]), "w_pool": f(inp["w_pool"][0].reshape(512, 128)),
        "pool_scale": f(inp["pool_scale"]), "dn_conv_w": f(inp["dn_conv_w"][0]), "dn_a_log": f(inp["dn_a_log"]),
        "dn_dt_bias": f(inp["dn_dt_bias"]), "dn_norm_w": f(inp["dn_norm_w"]), "w_out_ab": f(inp["w_out_ab"][0]),
        "w_qkv": f(inp["w_qkv_c"][0]), "w_out_c": f(inp["w_out_c"][0]), "g_mem_q": f(inp["g_mem_q"]), "g_mem_kv": f(inp["g_mem_kv"]),
        "w_mem_q": f(inp["w_mem_q"]), "w_mem_k": f(inp["w_mem_k"]), "w_mem_v": f(inp["w_mem_v"]), "w_mem_o": f(inp["w_mem_o"]),
        "g_ffn": f(inp["g_ffn"]), "w_up": f(inp["w_up"]), "ffn_cw": f(inp["ffn_conv_w"]), "w_down": f(inp["w_down"]),
        "g_final": f(inp["g_final"].reshape(1, 1024)),
    }
    m.update(consts)
    return m


def kernel(**inp):
    S = inp["x_prompt"].shape[1]
    NS = 4
    if S not in _CACHE:
        _CACHE[S] = build(S, NS)[0]
    nc = _CACHE[S]
    consts = _consts(S)
    in_maps = [_core_inputs(inp, c, S, NS, consts) for c in range(8)]
    res = run_bass_kernel_spmd(nc, in_maps, core_ids=list(range(8)))
    R = res.results
    f = np.float32
    B = 4
    cat_p = lambda k: np.stack([R[b][k] for b in range(B)])
    cat_s = lambda k: np.concatenate([R[c][k] for c in range(8)], axis=0)
    y_p = cat_p("y_p")
    y_s = cat_s("y_s").reshape(32, 8, 1024)
    outs = [y_p, y_s,
            cat_p("o_pool_p")[None], cat_s("o_pool_s")[None],
            cat_p("o_dconv_p")[None], cat_s("o_dconv_s")[None],
            cat_p("o_dn_p")[None], cat_s("o_dn_s")[None]]
    for w, _ in WINS:
        wp = min(w, S)
        outs.append(cat_p("o_w%d_p" % w).reshape(1, B, wp, 2, 8, 64))
        outs.append(cat_s("o_w%d_s" % w).reshape(1, 32, 8, 2, 8, 64))
    outs.append(np.stack([R[b]["o_memk"] for b in range(B)], axis=1).reshape(2, B, 256, 4, 256))
    outs.append(np.stack([R[b]["o_memv"] for b in range(B)], axis=1).reshape(2, B, 256, 4, 256))
    outs.append(np.stack([R[b]["o_ffn_p"] for b in range(B)], axis=1))
    outs.append(np.concatenate([R[c]["o_ffn_s"] for c in range(8)], axis=1))
    return tuple(np.ascontiguousarray(o, dtype=f) for o in outs)
```

```python
import numpy as np
import concourse.bass as bass
import concourse.mybir as mybir

F32 = mybir.dt.float32
BF16 = mybir.dt.bfloat16
AF = mybir.ActivationFunctionType
ALU = mybir.AluOpType
AX = mybir.AxisListType


class TT:
    __slots__ = ("ap", "w", "r", "name", "psum")

    def __init__(self, ap, name=""):
        self.ap = ap
        self.psum = False
        self.w = None
        self.r = {}
        self.name = name

    def __getitem__(self, idx):
        return self.ap[idx]


class FW:
    ENG = ("pe", "act", "dve", "pool", "sp")

    def __init__(self, nc, ndma=6):
        self.nc = nc
        self.prog = {e: [] for e in self.ENG}
        self.cnt = {e: 0 for e in self.ENG if e != "sp"}
        self.seen = {e: {} for e in self.ENG}
        self.sems = {}
        self.ndma = ndma
        self.dma_rr = {"sp": 0, "pool": 0, "act": 0}
        self.dma_tot = {}
        self.ctx = []
        self.out_tokens = []
        import os
        self.maxops = int(os.environ.get("FW_MAXOPS", "100000000"))
        self.total = 0
        self.log = []

    def sem(self, key):
        if key not in self.sems:
            cm = self.nc.semaphore("s_%s" % str(key).replace(" ", ""))
            self.sems[key] = cm.__enter__()
            self.ctx.append(cm)
        return self.sems[key]

    def sb(self, name, shape, dt=F32):
        cm = self.nc.sbuf_tensor(name, list(shape), dt)
        t = cm.__enter__()
        self.ctx.append(cm)
        return TT(t, name)

    def ps(self, name, shape, dt=F32):
        cm = self.nc.psum_tensor(name, list(shape), dt)
        t = cm.__enter__()
        self.ctx.append(cm)
        tt = TT(t, name)
        tt.psum = True
        return tt

    def _waits(self, eng, reads, writes, skip_self_pe=False):
        need = {}
        for t in reads:
            if t.w is not None:
                k, v = t.w
                need[k] = max(need.get(k, 0), v)
            if t.psum:
                for k, v in t.r.items():
                    if k != eng:
                        need[k] = max(need.get(k, 0), v)
        for t in writes:
            if t.w is not None:
                k, v = t.w
                need[k] = max(need.get(k, 0), v)
            for k, v in t.r.items():
                need[k] = max(need.get(k, 0), v)
        out = []
        seen = self.seen[eng]
        for k, v in need.items():
            if eng == "pe" and k == "pe":
                continue
            if seen.get(k, 0) >= v:
                continue
            seen[k] = v
            out.append((k, v))
        return out

    def op(self, eng, fn, reads=(), writes=()):
        self.total += 1
        if self.total > self.maxops:
            return None
        reads = [t for t in reads if isinstance(t, TT)]
        writes = [t for t in writes if isinstance(t, TT)]
        waits = self._waits(eng, reads, writes)
        self.cnt[eng] += 1
        v = self.cnt[eng]
        self.prog[eng].append((waits, fn, (eng, v, 1)))
        tok = (eng, v)
        for t in writes:
            t.w = tok
            t.r = {}
        for t in reads:
            if t in writes:
                continue
            t.r[eng] = max(t.r.get(eng, 0), v)
        return tok

    def dma(self, q, out_ap, in_ap, reads=(), writes=(), is_output=False, **kw):
        self.total += 1
        if self.total > self.maxops:
            return None
        reads = [t for t in reads if isinstance(t, TT)]
        writes = [t for t in writes if isinstance(t, TT)]
        waits = self._waits(q, reads, writes)
        i = self.dma_rr[q]
        self.dma_rr[q] = (i + 1) % self.ndma
        key = "d%s%d" % (q, i)
        prev = self.dma_tot.get(key, 0)
        if prev > 0 and self.seen[q].get(key, 0) < prev:
            self.seen[q][key] = prev
            waits.append((key, prev))
        v = prev + 16
        self.dma_tot[key] = v
        fn = (lambda e, o=out_ap, a=in_ap, k=kw: e.dma_start(out=o, in_=a, **k))
        self.prog[q].append((waits, fn, (key, v, 16)))
        tok = (key, v)
        for t in writes:
            t.w = tok
            t.r = {}
        for t in reads:
            t.r[key] = max(t.r.get(key, 0), v)
        if is_output:
            self.out_tokens.append(tok)
        return tok

    def finish(self):
        nc = self.nc
        fin = {}
        for k, v in self.out_tokens:
            fin[k] = max(fin.get(k, 0), v)
        for k, v in self.dma_tot.items():
            fin[k] = max(fin.get(k, 0), v)
        for e, c in self.cnt.items():
            if c > 0:
                fin[e] = c
        for k in list(fin.keys()):
            self.sem(k)
        for e in self.ENG:
            for waits, fn, (k, v, inc) in self.prog[e]:
                self.sem(k)
                for (wk, wv) in waits:
                    self.sem(wk)
        sems = self.sems
        prog = self.prog

        def run(e, h, final=False):
            for waits, fn, (k, v, inc) in prog[e]:
                for (wk, wv) in waits:
                    h.wait_ge(sems[wk], wv)
                fn(h).then_inc(sems[k], inc)
            if final:
                for k, v in fin.items():
                    h.wait_ge(sems[k], v)

        with nc.Block() as block:
            @block.sync
            def _(h):
                run("sp", h, final=True)

            @block.tensor
            def _(h):
                run("pe", h)

            @block.scalar
            def _(h):
                run("act", h)

            @block.vector
            def _(h):
                run("dve", h)

            @block.gpsimd
            def _(h):
                run("pool", h)
        for cm in reversed(self.ctx):
            cm.__exit__(None, None, None)

from concourse.bass_utils import run_bass_kernel_spmd

BIG = 30000.0
EPS = 1e-6
WINS = ((128, 1), (512, 4), (2048, 16))


def build(S=4096, NS=4, phases=('A0', 'B0', 'C0', 'A1', 'B1', 'C1')):
    nc = bass.Bass("TRN2", target_bir_lowering=False)
    fw = FW(nc)
    NT = S // 128
    I_, O_ = {}, {}

    def din(name, shape):
        I_[name] = nc.dram_tensor(name, list(shape), F32, kind="ExternalInput").ap()
        return I_[name]

    def dout(name, shape):
        O_[name] = nc.dram_tensor(name, list(shape), F32, kind="ExternalOutput").ap()
        return O_[name]

    xp = din("xp", [S, 1024]); xs_d = din("xs", [NS * 8, 1024]); memp = din("memp", [256, 1024])
    st_pool = din("st_pool", [NS, 15, 512]); st_dconv = din("st_dconv", [NS, 3, 1536])
    st_dn = din("st_dn", [NS, 4, 128, 128])
    cwin = [din("cw%d" % w, [NS, w, 1024]) for w, _ in WINS]
    cmk = din("cmk", [2, NS, 256, 1024]); cmv = din("cmv", [2, NS, 256, 1024])
    st_ffn = din("st_ffn", [2, NS, 2, 5632])
    g_mix = din("g_mix", [2, 1024]); w_in = din("w_in", [1024, 2568]); w_pool = din("w_pool", [512, 128])
    pool_scale = din("pool_scale", [1, 512]); dn_conv_w = din("dn_conv_w", [4, 1536])
    dn_a_log = din("dn_a_log", [1, 4]); dn_dt_bias = din("dn_dt_bias", [1, 4]); dn_norm_w = din("dn_norm_w", [1, 128])
    w_out_ab = din("w_out_ab", [1024, 1024]); w_qkv = din("w_qkv", [1024, 4608]); w_out_c = din("w_out_c", [512, 1024])
    g_mem_q = din("g_mem_q", [2, 1024]); g_mem_kv = din("g_mem_kv", [2, 1024])
    w_mem = {k: din("w_mem_" + k, [2, 1024, 1024]) for k in "qkvo"}
    g_ffn = din("g_ffn", [2, 1024]); w_up = din("w_up", [2, 1024, 5632]); ffn_cw = din("ffn_cw", [2, 3, 5632])
    w_down = din("w_down", [2, 2816, 1024]); g_final = din("g_final", [1, 1024])
    c_ident = din("c_ident", [128, 128]); c_mge = din("c_mge", [128, 128]); c_nmle = din("c_nmle", [128, 128])
    c_nmlt = din("c_nmlt", [128, 128]); c_LT = din("c_LT", [128, 128])
    c_invc = din("c_invc", [2, 128, 512])
    c_cos = din("c_cos", [S + 8, 8]); c_sin = din("c_sin", [S + 8, 8])
    c_mb = din("c_mb", [2, 128, 128]); c_sm = din("c_sm", [128, 24 * 8])

    y_p = dout("y_p", [S, 1024]); y_s = dout("y_s", [NS * 8, 1024])
    o_pool_p = dout("o_pool_p", [15, 512]); o_pool_s = dout("o_pool_s", [NS, 15, 512])
    o_dconv_p = dout("o_dconv_p", [3, 1536]); o_dconv_s = dout("o_dconv_s", [NS, 3, 1536])
    o_dn_p = dout("o_dn_p", [4, 128, 128]); o_dn_s = dout("o_dn_s", [NS, 4, 128, 128])
    o_win_p = [dout("o_w%d_p" % w, [min(w, S), 1024]) for w, _ in WINS]
    o_win_s = [dout("o_w%d_s" % w, [NS, 8, 1024]) for w, _ in WINS]
    o_memk = dout("o_memk", [2, 256, 1024]); o_memv = dout("o_memv", [2, 256, 1024])
    o_ffn_p = dout("o_ffn_p", [2, 2, 5632]); o_ffn_s = dout("o_ffn_s", [2, NS, 2, 5632])

    QT_d = [nc.dram_tensor("QT%d" % g, [512, S], BF16).ap() for g in range(3)]
    KT_d = [nc.dram_tensor("KT%d" % g, [512, S], BF16).ap() for g in range(3)]
    V_d = [nc.dram_tensor("Vd%d" % g, [S, 512], BF16).ap() for g in range(3)]
    qkv_tt = [TT(None, "qkvscr%d" % t) for t in range(NT)]
    xres = [TT(None, "xres%d" % t) for t in range(NT)]

    rr = {"ps": 0, "ev": 0, "q": 0}
    PS = [fw.ps("ps%d" % i, [128, 512]) for i in range(8)]

    def nps():
        p = PS[rr["ps"] % 8]
        rr["ps"] += 1
        return p

    def ev_eng():
        rr["ev"] += 1
        return "act" if rr["ev"] % 2 else "dve"

    def copy(dst, src, reads, writes, eng=None, scale=None):
        eng = eng or ev_eng()
        if scale is not None and not isinstance(scale, float) and eng == "act":
            eng = "dve"
        if eng == "act":
            if scale is None:
                fw.op("act", lambda e: e.copy(out=dst, in_=src), reads, writes)
            else:
                fw.op("act", lambda e: e.mul(out=dst, in_=src, mul=scale), reads, writes)
        else:
            if scale is None:
                fw.op(eng, lambda e: e.tensor_copy(out=dst, in_=src), reads, writes)
            else:
                fw.op(eng, lambda e: e.tensor_scalar(out=dst, in0=src, scalar1=scale, scalar2=None, op0=ALU.mult), reads, writes)

    def tt_op(eng, dst, a, b, op, reads, writes):
        fw.op(eng, lambda e: e.tensor_tensor(out=dst, in0=a, in1=b, op=op), reads, writes)

    def stt(eng, dst, a, sc, b, op0, op1, reads, writes):
        if eng == "pool":
            tmp = sm("ptmp", [128, 128])
            nparts, nfree = a.shape[0], a.shape[1]
            fw.op("pool", lambda e: e.tensor_scalar(out=tmp[:nparts, :nfree], in0=a, scalar1=sc, scalar2=None, op0=op0), reads, [tmp])
            fw.op("pool", lambda e: e.tensor_tensor(out=dst, in0=tmp[:nparts, :nfree], in1=b, op=op1), list(reads) + [tmp], writes)
            return
        fw.op(eng, lambda e: e.scalar_tensor_tensor(out=dst, in0=a, scalar=sc, in1=b, op0=op0, op1=op1), reads, writes)

    def act(dst, src, func, reads, writes, bias=None, scale=None):
        kw = {}
        if bias is not None:
            kw["bias"] = bias
        if scale is not None:
            kw["scale"] = scale
        fw.op("act", lambda e: e.activation(out=dst, in_=src, func=func, **kw), reads, writes)

    def mm(ps_ap, lhsT, rhs, start, stop, reads, writes):
        fw.op("pe", lambda e: e.matmul(ps_ap, lhsT=lhsT, rhs=rhs, start=start, stop=stop), reads, writes)

    def tr(ps_ap, in_ap, k, reads, writes):
        fw.op("pe", lambda e: e.transpose(ps_ap, in_ap, ident[:k, :k]), list(reads) + [ident], writes)

    ident = fw.sb("ident", [128, 128]); fw.dma("sp", ident[:], c_ident[:], writes=[ident])
    m_ge = fw.sb("m_ge", [128, 128]); fw.dma("sp", m_ge[:], c_mge[:], writes=[m_ge])
    nm_le = fw.sb("nm_le", [128, 128]); fw.dma("sp", nm_le[:], c_nmle[:], writes=[nm_le])
    nm_lt = fw.sb("nm_lt", [128, 128]); fw.dma("sp", nm_lt[:], c_nmlt[:], writes=[nm_lt])
    LT = fw.sb("LT", [128, 128]); fw.dma("sp", LT[:], c_LT[:], writes=[LT])
    ones = fw.sb("ones", [128, 128]); fw.op("pool", lambda e: e.memset(ones[:], 1.0), writes=[ones])
    gfin = fw.sb("gfin", [128, 1024]); fw.dma("sp", gfin[:], g_final[0:1, :].partition_broadcast(128) if False else g_final.to_broadcast([128, 1024]), writes=[gfin])
    gbc = fw.sb("gbc", [128, 1024])

    def load_g(src_row):
        fw.dma("sp", gbc[:], src_row.to_broadcast([128, 1024]), writes=[gbc])

    xsres = [TT(None, "xsres%d" % s) for s in range(NS)]

    ARENA = 67584
    arena = fw.sb("arena", [128, ARENA], BF16)
    arena_f = arena.ap.bitcast(F32)

    def barrier_tt(tt):
        tt.r = dict(fw.cnt)
        for k, v in fw.dma_tot.items():
            tt.r[k] = v
        return tt

    wl = {"i": 0}

    def wload(name, dram2d, off, K, N):
        KC = K // 128
        tt = barrier_tt(TT(arena.ap[:, off:off + KC * N].rearrange("p (k n) -> p k n", n=N), name))
        chunks = []
        stg = [xt_rot[0], xt_rot[1], hn]
        for kc in range(KC):
            c = TT(tt.ap[:, kc, :], "%s_%d" % (name, kc))
            c.r = dict(tt.r)
            if N < 64:
                fw.dma("pool", c.ap, dram2d[kc * 128:(kc + 1) * 128, :], writes=[c])
            else:
                eng = "act" if kc % 2 else "dve"
                for c0 in range(0, N, 1024):
                    w_ = min(1024, N - c0)
                    st = stg[wl["i"] % 3]; wl["i"] += 1
                    fw.dma("sp", st[:, :w_], dram2d[kc * 128:(kc + 1) * 128, c0:c0 + w_], writes=[st])
                    if eng == "act":
                        fw.op("act", lambda e, st=st, c=c, c0=c0, w_=w_: e.copy(out=c.ap[:, c0:c0 + w_], in_=st[:, :w_]), [st], [c])
                    else:
                        fw.op("dve", lambda e, st=st, c=c, c0=c0, w_=w_: e.tensor_copy(out=c.ap[:, c0:c0 + w_], in_=st[:, :w_]), [st], [c])
            chunks.append(c)
        return chunks, off + KC * N

    xt_rot = [fw.sb("xt%d" % i, [128, 1024]) for i in range(2)]
    hn = fw.sb("hn", [128, 1024])
    hT4 = fw.sb("hT4", [128, 8, 512], BF16)
    hT_rot = [TT(hT4.ap[:, :, 0:256], "hTa"), TT(hT4.ap[:, :, 256:512], "hTb")]
    ssq = [fw.sb("ssq%d" % i, [128, 1]) for i in range(2)]
    cnt = {"xt": 0, "hT": 0, "ss": 0}

    def rstd_of(src_ap, ntok, width, reads):
        ss = ssq[cnt["ss"] % 2]; cnt["ss"] += 1
        act(hn[:ntok, :width], src_ap, AF.Square, reads, [hn])
        fw.op("dve", lambda e: e.reduce_sum(out=ss[:ntok, :], in_=hn[:ntok, :width], axis=AX.X), [hn], [ss])
        act(ss[:ntok, :], ss[:ntok, :], AF.Sqrt, [ss], [ss], bias=EPS, scale=1.0 / width)
        fw.op("dve", lambda e: e.reciprocal(out=ss[:ntok, :], in_=ss[:ntok, :]), [ss], [ss])
        return ss

    def norm_tok(xt, ntok, g_tt):
        ss = rstd_of(xt[:ntok, :], ntok, 1024, [xt])
        stt("dve", hn[:ntok, :], xt[:ntok, :], ss[:ntok, 0:1], g_tt[:ntok, :], ALU.mult, ALU.mult, [xt, ss, g_tt], [hn])
        return hn

    def to_fm(src, ntok, dst, nchunks=8, c0=0):
        for b in range(0, nchunks, 4):
            nb = min(4, nchunks - b)
            ps = nps()
            for j in range(nb):
                tr(ps[:, j * 128:j * 128 + ntok], src[:ntok, (b + j) * 128:(b + j + 1) * 128], ntok, [src], [ps])
            copy(dst[:, b:b + nb, c0:c0 + ntok], ps[:, 0:nb * 128].rearrange("p (j t) -> p j t", t=128)[:, :, :ntok], [ps], [dst])

    def norm_T(xt, ntok, g_tt):
        norm_tok(xt, ntok, g_tt)
        hT = hT_rot[cnt["hT"] % 2]; cnt["hT"] += 1
        to_fm(hn, ntok, hT)
        return hT

    def lin_fm(ps_ap, W, col0, M, hT, ntok, wr):
        KC = len(W)
        for kc in range(KC):
            mm(ps_ap, W[kc][:, col0:col0 + M], hT[:, kc, :ntok], kc == 0, kc == KC - 1, [W[kc], hT], [wr])

    def lin_tm(ps_ap, hT, W, col0, N, ntok, wr, o=0, rd=None):
        KC = len(W)
        for kc in range(KC):
            mm(ps_ap, hT[:, kc, o:o + ntok], W[kc][:, col0:col0 + N], kc == 0, kc == KC - 1, [W[kc]] + (rd if rd is not None else [hT]), [wr])

    STW = 1024
    stage = fw.sb("stage", [16, STW])
    stage_o = fw.sb("stage_o", [16, STW])

    def cols_from_rows(dst_fn, dst_tts, src_dram, k, C):
        per = STW // 128
        for c0 in range(0, C, per):
            n = min(per, C - c0)
            fw.dma("sp", stage[:k, :n * 128], src_dram[:, c0 * 128:(c0 + n) * 128], writes=[stage])
            ps = nps()
            for j in range(n):
                tr(ps[:, j * 16:j * 16 + k], stage[:k, j * 128:(j + 1) * 128], k, [stage], [ps])
            for j in range(n):
                copy(dst_fn(c0 + j), ps[:, j * 16:j * 16 + k], [ps], [dst_tts[c0 + j] if isinstance(dst_tts, list) else dst_tts])

    def rows_from_cols(dst_dram, src_fn, src_tts, k, C):
        per = STW // 128
        for p0 in range(0, C, per):
            np_ = min(per, C - p0)
            for c0 in range(p0, p0 + np_, 4):
                n = min(4, p0 + np_ - c0)
                ps = nps()
                for j in range(n):
                    st = src_tts[c0 + j] if isinstance(src_tts, list) else src_tts
                    fw.op("pe", lambda e, j=j, c=c0 + j, ps=ps: e.transpose(ps[:k, j * 128:(j + 1) * 128], src_fn(c), ident[:, :]), [st, ident], [ps])
                copy(stage_o[:k, (c0 - p0) * 128:(c0 - p0 + n) * 128], ps[:k, :n * 128], [ps], [stage_o])
            fw.dma("pool", dst_dram[:, p0 * 128:(p0 + np_) * 128], stage_o[:k, :np_ * 128], reads=[stage_o], is_output=True)

    def load_x(t, layer0):
        xt = xt_rot[cnt["xt"] % 2]; cnt["xt"] += 1
        if layer0:
            fw.dma("sp", xt[:, :], xp[t * 128:(t + 1) * 128, :], writes=[xt])
        else:
            fw.dma("sp", xt[:, :], y_p[t * 128:(t + 1) * 128, :], reads=[xres[t]], writes=[xt])
        return xt

    def load_xs(s_, layer0=False):
        xt = xt_rot[cnt["xt"] % 2]; cnt["xt"] += 1
        if layer0:
            fw.dma("sp", xt[:8, :], xs_d[s_ * 8:(s_ + 1) * 8, :], writes=[xt])
        else:
            fw.dma("sp", xt[:8, :], y_s[s_ * 8:(s_ + 1) * 8, :], reads=[xsres[s_]], writes=[xt])
        return xt

    def store_xs(s_, xt):
        fw.dma("sp", y_s[s_ * 8:(s_ + 1) * 8, :], xt[:8, :], reads=[xt], writes=[xsres[s_]], is_output=True)

    def store_x(t, xt):
        fw.dma("sp", y_p[t * 128:(t + 1) * 128, :], xt[:, :], reads=[xt], writes=[xres[t]], is_output=True)

    def add_res(xt, ntok, half, ps):
        tt_op("dve", xt[:ntok, half * 512:(half + 1) * 512], xt[:ntok, half * 512:(half + 1) * 512], ps[:ntok, :512], ALU.add, [xt, ps], [xt])

    SCRB = 37632
    scratch = fw.sb("scratch", [128, SCRB // 2], BF16)
    scr_f = scratch.ap.bitcast(F32)
    small = {}
    scr = {"off": 0}

    def sm_reset():
        print('scratch used', scr['off'])
        small.clear()
        scr["off"] = 0

    def sm(name, shape, dt=F32):
        if name in small:
            return small[name]
        esz = 4 if dt == F32 else 2
        n = 1
        for d_ in shape[1:]:
            n *= d_
        nbytes = (n * esz + 7) // 8 * 8
        off = scr["off"]
        assert off + nbytes <= SCRB, ("scratch overflow", name, off, nbytes)
        scr["off"] = off + nbytes
        base = scr_f if dt == F32 else scratch.ap
        ap = base[:shape[0], off // esz: off // esz + n]
        if len(shape) == 3:
            ap = ap.rearrange("p (a b) -> p a b", b=shape[2])
        elif len(shape) == 4:
            ap = ap.rearrange("p (a b c) -> p a b c", b=shape[2], c=shape[3])
        elif len(shape) == 5:
            ap = ap.rearrange("p (a b c d) -> p a b c d", b=shape[2], c=shape[3], d=shape[4])
        tt = barrier_tt(TT(ap, name))
        small[name] = tt
        return tt

    def phase_A0():
        off = 0
        sm_reset()
        Win, off = wload("w_in", w_in[:, 0:2560], off, 1024, 2560)
        Wab, off = wload("w_ab", w_in[:, 2560:2568], off, 1024, 8)
        Wout, off = wload("w_outab", w_out_ab, off, 1024, 1024)
        Wpool, off = wload("w_pool", w_pool, off, 512, 128)
        load_g(g_mix[0:1, :])
        cw = sm("dn_cw", [128, 12, 4]); cols_from_rows(lambda c: cw[:, c, :], cw, dn_conv_w, 4, 12)
        psc = sm("pool_sc", [128, 4, 1]); cols_from_rows(lambda c: psc[:, c, :], psc, pool_scale, 1, 4)
        dtb = sm("dtb", [128, 4]); fw.dma("sp", dtb[:], dn_dt_bias.to_broadcast([128, 4]), writes=[dtb])
        nea = sm("nea", [128, 4]); fw.dma("sp", nea[:], dn_a_log.to_broadcast([128, 4]), writes=[nea])
        act(nea[:], nea[:], AF.Exp, [nea], [nea])
        copy(nea[:], nea[:], [nea], [nea], eng="dve", scale=-1.0)
        nw = sm("normw", [128, 128]); fw.dma("sp", nw[:], dn_norm_w.to_broadcast([128, 128]), writes=[nw])
        invc = sm("invc", [128, 512]); fw.dma("sp", invc[:], c_invc[0], writes=[invc])
        aext = []; qext = []
        xoff = 22400
        for g in range(4):
            aext.append(barrier_tt(TT(arena_f[:, xoff:xoff + 527], "aext%d" % g))); xoff += 527
        for c in range(12):
            qext.append(barrier_tt(TT(arena_f[:, xoff:xoff + 515], "qext%d" % c))); xoff += 515
        assert xoff <= 33792
        qc = [sm("qc%d" % c, [128, 128]) for c in range(12)]
        S = [sm("S%d" % h, [128, 128]) for h in range(4)]
        mixT = sm("mixT", [128, 8, 128], BF16)
        gate_s = sm("gate_s", [128, 512]); ofin = sm("ofin", [128, 512])
        pa = [sm("pa%d" % i, [128, 143]) for i in range(2)]
        zb = sm("zb", [128, 128], BF16)
        col = {n: sm("col_" + n, [128, 4]) for n in ["beta", "lnb", "g", "G", "negG", "Gl", "eG", "beG", "nbeta", "ekg", "glast", "egl"]}
        gt = {n: sm("gt_" + n, [128, 128]) for n in ["sq", "rn"]}
        gnames = ["Ds", "DTb", "DTi", "B0", "B1", "R0", "R1", "Tt", "QKd", "X0v", "X0k", "kg", "nwT", "vnew", "tmpB"]
        gth = []
        aoff = 14720
        for h_ in range(4):
            d_ = {}
            for nm in gnames:
                d_[nm] = barrier_tt(TT(arena_f[:, aoff:aoff + 128], "gt%d_%s" % (h_, nm)))
                aoff += 128
            gth.append(d_)

        def tile(xt, ntok, tile0, o, rd):
            n = ntok
            ps = nps(); lin_tm(ps[:n, :512], hT4, Win, 2048, 512, n, ps, o=o, rd=rd)
            act(gate_s[:n, :], ps[:n, :512], AF.Silu, [ps], [gate_s])
            ps = nps(); lin_tm(ps[:n, :8], hT4, Wab, 0, 8, n, ps, o=o, rd=rd)
            act(col["beta"][:n, :], ps[:n, 0:4], AF.Sigmoid, [ps], [col["beta"]])
            tt_op("dve", col["g"][:n, :], ps[:n, 4:8], dtb[:n, :], ALU.add, [ps, dtb], [col["g"]])
            act(col["g"][:n, :], col["g"][:n, :], AF.Exp, [col["g"]], [col["g"]])
            act(col["g"][:n, :], col["g"][:n, :], AF.Ln, [col["g"]], [col["g"]], bias=1.0)
            tt_op("dve", col["g"][:n, :], col["g"][:n, :], nea[:n, :], ALU.mult, [col["g"], nea], [col["g"]])
            act(col["lnb"][:n, :], col["beta"][:n, :], AF.Ln, [col["beta"]], [col["lnb"]])
            ps = nps()
            mm(ps[:n, 0:4], LT[:n, :n], col["g"][:n, :], True, True, [LT, col["g"]], [ps])
            copy(col["G"][:n, :], ps[:n, 0:4], [ps], [col["G"]], eng="dve")
            ps = nps()
            mm(ps[:, 0:4], ones[:n, :], col["g"][:n, :], True, True, [ones, col["g"]], [ps])
            copy(col["glast"][:, :], ps[:, 0:4], [ps], [col["glast"]], eng="dve")
            act(col["egl"][:, :], col["glast"][:, :], AF.Exp, [col["glast"]], [col["egl"]])
            copy(col["negG"][:n, :], col["G"][:n, :], [col["G"]], [col["negG"]], eng="dve", scale=-1.0)
            tt_op("dve", col["Gl"][:n, :], col["G"][:n, :], col["lnb"][:n, :], ALU.add, [col["G"], col["lnb"]], [col["Gl"]])
            act(col["eG"][:n, :], col["G"][:n, :], AF.Exp, [col["G"]], [col["eG"]])
            tt_op("dve", col["beG"][:n, :], col["eG"][:n, :], col["beta"][:n, :], ALU.mult, [col["eG"], col["beta"]], [col["beG"]])
            copy(col["nbeta"][:n, :], col["beta"][:n, :], [col["beta"]], [col["nbeta"]], eng="dve", scale=-1.0)
            tt_op("dve", col["ekg"][:n, :], col["glast"][:n, :], col["G"][:n, :], ALU.subtract, [col["glast"], col["G"]], [col["ekg"]])
            act(col["ekg"][:n, :], col["ekg"][:n, :], AF.Exp, [col["ekg"]], [col["ekg"]])
            Lx = 15 + n
            for g in range(4):
                cur = aext[g]
                cb = o
                sh = 1
                for step in range(g + 1):
                    dst = pa[step % 2]
                    eng = "dve"
                    tt_op(eng, dst[:, sh:Lx], cur[:, cb + sh:cb + Lx], cur[:, cb:cb + Lx - sh], ALU.add, [cur], [dst])
                    cur = dst
                    cb = 0
                    sh *= 2
                nxt = pa[(g + 1) % 2]
                if tile0:
                    tt_op("dve", nxt[:, 15:Lx], cur[:, 15:Lx], invc[:, g * 128:g * 128 + n], ALU.mult, [cur, invc], [nxt])
                else:
                    copy(nxt[:, 15:Lx], cur[:, 15:Lx], [cur], [nxt], eng="dve", scale=1.0 / (2 ** (g + 1)))
                tt_op("dve", zb[:, :n], nxt[:, 15:Lx], aext[g][:, o + 15:o + Lx], ALU.subtract, [nxt, aext[g]], [zb])
                ps = nps()
                mm(ps[:, :n], Wpool[g][:, :], zb[:, :n], True, True, [Wpool[g], zb], [ps])
                copy(mixT[:, g, :n], ps[:, :n], [ps], [mixT], eng="act", scale=psc[:, g, 0:1])
            for c in range(12):
                fw.op("act", lambda e, c=c: e.mul(out=qc[c][:, :n], in_=qext[c][:, o:o + n], mul=cw[:, c, 0:1]), [qext[c], cw], [qc[c]])
                for k in range(1, 4):
                    stt("dve", qc[c][:, :n], qext[c][:, o + k:o + k + n], cw[:, c, k:k + 1], qc[c][:, :n], ALU.mult, ALU.add, [qext[c], cw, qc[c]], [qc[c]])
                act(qc[c][:, :n], qc[c][:, :n], AF.Silu, [qc[c]], [qc[c]])
            for c in range(8):
                act(gt["sq"][:, :n], qc[c][:, :n], AF.Square, [qc[c]], [gt["sq"]])
                ps = nps()
                mm(ps[:, :n], ones[:, :], gt["sq"][:, :n], True, True, [ones, gt["sq"]], [ps])
                sc = 128.0 if c < 4 else 1.0
                act(gt["rn"][:, :n], ps[:, :n], AF.Sqrt, [ps], [gt["rn"]], bias=EPS * sc, scale=sc)
                fw.op("dve", lambda e: e.reciprocal(out=gt["rn"][:, :n], in_=gt["rn"][:, :n]), [gt["rn"]], [gt["rn"]])
                tt_op("dve", qc[c][:, :n], qc[c][:, :n], gt["rn"][:, :n], ALU.mult, [qc[c], gt["rn"]], [qc[c]])
            nlev = 7 if n == 128 else 3
            PSH = [dict() for _ in range(4)]

            def st_A(h):
                g_ = gth[h]
                qT, kT = qc[h], qc[4 + h]
                pKK = nps(); mm(pKK[:n, :n], kT[:, :n], kT[:, :n], True, True, [kT], [pKK])
                pQK = nps(); mm(pQK[:n, :n], kT[:, :n], qT[:, :n], True, True, [kT, qT], [pQK])
                p1 = nps()
                mm(p1[:n, :n], col["G"][:n, h:h + 1].to_broadcast([n, n]), ident[:n, :n], True, False, [col["G"], ident], [p1])
                mm(p1[:n, :n], ident[:n, :n], m_ge[:n, :n], False, True, [ident, m_ge], [p1])
                act(g_["Ds"][:n, :n], p1[:n, :n], AF.Exp, [p1, col["G"]], [g_["Ds"]], bias=col["G"][:n, h:h + 1], scale=-1.0)
                p2 = nps()
                mm(p2[:n, :n], col["Gl"][:n, h:h + 1].to_broadcast([n, n]), ident[:n, :n], True, False, [col["Gl"], ident], [p2])
                mm(p2[:n, :n], ident[:n, :n], nm_le[:n, :n], False, True, [ident, nm_le], [p2])
                act(g_["DTb"][:n, :n], p2[:n, :n], AF.Exp, [p2, col["negG"]], [g_["DTb"]], bias=col["negG"][:n, h:h + 1], scale=1.0)
                p3 = nps()
                mm(p3[:n, :n], col["G"][:n, h:h + 1].to_broadcast([n, n]), ident[:n, :n], True, False, [col["G"], ident], [p3])
                mm(p3[:n, :n], ident[:n, :n], nm_lt[:n, :n], False, True, [ident, nm_lt], [p3])
                act(g_["DTi"][:n, :n], p3[:n, :n], AF.Exp, [p3, col["negG"]], [g_["DTi"]], bias=col["negG"][:n, h:h + 1], scale=1.0)
                stt("dve", g_["B0"][:n, :n], pKK[:n, :n], col["nbeta"][:n, h:h + 1], g_["Ds"][:n, :n], ALU.mult, ALU.mult, [pKK, col["nbeta"], g_["Ds"]], [g_["B0"]])
                stt("dve", g_["R0"][:n, :n], pKK[:n, :n], -1.0, g_["DTb"][:n, :n], ALU.mult, ALU.mult, [pKK, g_["DTb"]], [g_["R0"]])
                tt_op("dve", g_["QKd"][:n, :n], pQK[:n, :n], g_["DTi"][:n, :n], ALU.mult, [pQK, g_["DTi"]], [g_["QKd"]])
                tt_op("dve", g_["Tt"][:n, :n], g_["R0"][:n, :n], ident[:n, :n], ALU.add, [g_["R0"], ident], [g_["Tt"]])

            def st_L(h, k):
                g_ = gth[h]
                Bm = [g_["B0"], g_["B1"]]; Rm = [g_["R0"], g_["R1"]]; Tt = g_["Tt"]
                a, b = (k - 1) % 2, k % 2
                pP = nps(); mm(pP[:n, :n], Rm[a][:n, :n], Bm[a][:n, :n], True, True, [Rm[a], Bm[a]], [pP])
                if k < nlev - 1:
                    pR = nps(); mm(pR[:n, :n], Bm[a][:n, :n], Rm[a][:n, :n], True, True, [Rm[a], Bm[a]], [pR])
                copy(Bm[b][:n, :n], pP[:n, :n], [pP], [Bm[b]], eng="act")
                if k < nlev - 1:
                    copy(Rm[b][:n, :n], pR[:n, :n], [pR], [Rm[b]], eng="dve")
                pT = nps(); mm(pT[:n, :n], Bm[b][:n, :n], Tt[:n, :n], True, True, [Bm[b], Tt], [pT])
                tt_op("dve", Tt[:n, :n], Tt[:n, :n], pT[:n, :n], ALU.add, [Tt, pT], [Tt])

            def st_V(h):
                g_ = gth[h]
                kT, vT = qc[4 + h], qc[8 + h]
                Tt = g_["Tt"]
                pV = nps(); mm(pV[:n, :128], vT[:, :n], ident[:, :], True, True, [vT, ident], [pV])
                copy(g_["X0v"][:n, :], pV[:n, :128], [pV, col["beta"]], [g_["X0v"]], eng="dve", scale=col["beta"][:n, h:h + 1])
                pK = nps(); mm(pK[:n, :128], kT[:, :n], ident[:, :], True, True, [kT, ident], [pK])
                copy(g_["X0k"][:n, :], pK[:n, :128], [pK, col["beG"]], [g_["X0k"]], eng="dve", scale=col["beG"][:n, h:h + 1])
                copy(g_["kg"][:n, :], pK[:n, :128], [pK, col["ekg"]], [g_["kg"]], eng="dve", scale=col["ekg"][:n, h:h + 1])
                pW = nps(); mm(pW[:, :n], g_["X0k"][:n, :], Tt[:n, :n], True, True, [g_["X0k"], Tt], [pW])
                copy(g_["nwT"][:, :n], pW[:, :n], [pW], [g_["nwT"]], eng="act", scale=-1.0)

            def st_N(h):
                g_ = gth[h]
                qT = qc[h]
                Tt = g_["Tt"]
                pN = nps()
                mm(pN[:n, :128], Tt[:n, :n], g_["X0v"][:n, :], True, False, [Tt, g_["X0v"]], [pN])
                mm(pN[:n, :128], g_["nwT"][:, :n], S[h][:, :], False, True, [g_["nwT"], S[h]], [pN])
                copy(g_["vnew"][:n, :], pN[:n, :128], [pN], [g_["vnew"]], eng="act")
                pA = nps(); mm(pA[:n, :128], qT[:, :n], S[h][:, :], True, True, [qT, S[h]], [pA])
                pB = nps(); mm(pB[:n, :128], g_["QKd"][:n, :n], g_["vnew"][:n, :], True, True, [g_["QKd"], g_["vnew"]], [pB])
                copy(g_["tmpB"][:n, :], pB[:n, :128], [pB], [g_["tmpB"]], eng="act")
                stt("dve", ofin[:n, h * 128:(h + 1) * 128], pA[:n, :128], col["eG"][:n, h:h + 1], g_["tmpB"][:n, :], ALU.mult, ALU.add, [pA, col["eG"], g_["tmpB"]], [ofin])
                pS = nps(); mm(pS[:, :128], g_["kg"][:n, :], g_["vnew"][:n, :], True, True, [g_["kg"], g_["vnew"]], [pS])
                stt("dve", S[h][:, :], S[h][:, :], col["egl"][:, h:h + 1], pS[:, :128], ALU.mult, ALU.add, [S[h], col["egl"], pS], [S[h]])

            for h in range(4):
                st_A(h)
            for k in range(1, nlev):
                for h in range(4):
                    st_L(h, k)
            for h in range(4):
                st_V(h)
            for h in range(4):
                st_N(h)
            for h in range(4):
                ss = rstd_of(ofin[:n, h * 128:(h + 1) * 128], n, 128, [ofin])
                stt("dve", ofin[:n, h * 128:(h + 1) * 128], ofin[:n, h * 128:(h + 1) * 128], ss[:n, 0:1], nw[:n, :], ALU.mult, ALU.mult, [ofin, ss, nw], [ofin])
                tt_op("dve", ofin[:n, h * 128:(h + 1) * 128], ofin[:n, h * 128:(h + 1) * 128], gate_s[:n, h * 128:(h + 1) * 128], ALU.mult, [ofin, gate_s], [ofin])
            ps = nps()
            for j in range(4):
                tr(ps[:, j * 128:j * 128 + n], ofin[:n, j * 128:(j + 1) * 128], n, [ofin], [ps])
            copy(mixT[:, 4:8, :n], ps[:, :].rearrange("p (j t) -> p j t", t=128)[:, :, :n], [ps], [mixT])
            for half in range(2):
                ps = nps()
                for c in range(8):
                    mm(ps[:n, :512], mixT[:, c, :n], Wout[c][:, half * 512:(half + 1) * 512], c == 0, c == 7, [mixT, Wout[c]], [ps])
                add_res(xt, n, half, ps)

        def stile(loaders, n, first, finish):
            ntl = len(loaders)
            Wd = ntl * n
            for i, ld in enumerate(loaders):
                xt = ld()
                norm_tok(xt, n, gbc)
                half_tt = hT_rot[(i * n) // 256]
                for b in range(0, 8, 4):
                    ps = nps()
                    for j in range(4):
                        tr(ps[:, j * 128:j * 128 + n], hn[:n, (b + j) * 128:(b + j + 1) * 128], n, [hn], [ps])
                    copy(hT4[:, b:b + 4, i * n:i * n + n], ps[:, 0:512].rearrange("p (j t) -> p j t", t=128)[:, :, :n], [ps], [half_tt])
            rd = [hT_rot[0]] + ([hT_rot[1]] if Wd > 256 else [])
            for g in range(4):
                ps = nps()
                for kc in range(8):
                    mm(ps[:, :Wd], Win[kc][:, g * 128:(g + 1) * 128], hT4[:, kc, :Wd], kc == 0, kc == 7, [Win[kc]] + rd, [ps])
                copy(aext[g][:, 15:15 + Wd], ps[:, :Wd], [ps], [aext[g]])
            for c in range(12):
                ps = nps()
                for kc in range(8):
                    mm(ps[:, :Wd], Win[kc][:, 512 + c * 128:512 + (c + 1) * 128], hT4[:, kc, :Wd], kc == 0, kc == 7, [Win[kc]] + rd, [ps])
                copy(qext[c][:, 3:3 + Wd], ps[:, :Wd], [ps], [qext[c]])
            for i, ld in enumerate(loaders):
                xt = ld()
                tile(xt, n, first and i == 0, i * n, rd)
                finish(i, xt)
            for g in range(4):
                copy(aext[g][:, 0:15], aext[g][:, Wd:Wd + 15], [aext[g]], [aext[g]], eng="pool")
            for c in range(12):
                copy(qext[c][:, 0:3], qext[c][:, Wd:Wd + 3], [qext[c]], [qext[c]], eng="pool")

        for g in range(4):
            fw.op("pool", lambda e, g=g: e.memset(aext[g][:, :], 0.0), writes=[aext[g]])
        for c in range(12):
            fw.op("pool", lambda e, c=c: e.memset(qext[c][:, :], 0.0), writes=[qext[c]])
        for h in range(4):
            fw.op("pool", lambda e, h=h: e.memset(S[h][:, :], 0.0), writes=[S[h]])
        for t0 in range(0, NT, 4):
            stile([(lambda t=t0 + i: load_x(t, True)) for i in range(4)], 128, t0 == 0, lambda i, xt, t0=t0: store_x(t0 + i, xt))
        rows_from_cols(o_pool_p[:, :], lambda c: aext[c][:, 0:15], aext, 15, 4)
        rows_from_cols(o_dconv_p[:, :], lambda c: qext[c][:, 0:3], qext, 3, 12)
        for h in range(4):
            fw.dma("pool", o_dn_p[h], S[h][:, :], reads=[S[h]], is_output=True)
        for s in range(NS):
            cols_from_rows(lambda c: aext[c][:, 0:15], aext, st_pool[s], 15, 4)
            cols_from_rows(lambda c: qext[c][:, 0:3], qext, st_dconv[s], 3, 12)
            for h in range(4):
                fw.dma("sp", S[h][:, :], st_dn[s, h], writes=[S[h]])
            stile([lambda s=s: load_xs(s, True)], 8, False, lambda i, xt, s=s: store_xs(s, xt))
            rows_from_cols(o_pool_s[s], lambda c: aext[c][:, 0:15], aext, 15, 4)
            rows_from_cols(o_dconv_s[s], lambda c: qext[c][:, 0:3], qext, 3, 12)
            for h in range(4):
                fw.dma("pool", o_dn_s[s, h], S[h][:, :], reads=[S[h]], is_output=True)

    def phase_B(l):
        off = 0
        sm_reset()
        Wq, off = wload("wq", w_mem["q"][l], off, 1024, 1024)
        Wk, off = wload("wk", w_mem["k"][l], off, 1024, 1024)
        Wv, off = wload("wv", w_mem["v"][l], off, 1024, 1024)
        Wo, off = wload("wo", w_mem["o"][l], off, 1024, 1024)
        KTp = sm("KTp", [128, 8, 256], BF16); Vaug = sm("Vaug", [128, 2, 4, 257], BF16)
        KTs = sm("KTs", [128, 8, 256], BF16); Vaugs = sm("Vaugs", [128, 2, 4, 257], BF16)
        fw.op("pool", lambda e: e.memset(Vaug[:, :, :, 256:257], 1.0), writes=[Vaug])
        fw.op("pool", lambda e: e.memset(Vaugs[:, :, :, 256:257], 1.0), writes=[Vaugs])
        qT = sm("qT", [128, 8, 512], BF16); pT = sm("pT", [128, 2, 128], BF16)
        otok = sm("otok", [128, 1024]); oT = sm("oT", [128, 8, 128], BF16)
        rd = sm("rd", [128, 1]); kvs = sm("kvs", [128, 512]); kraw = sm("kraw", [128, 1024])
        load_g(g_mem_kv[l:l + 1, :])
        for nb in range(2):
            xt = xt_rot[cnt["xt"] % 2]; cnt["xt"] += 1
            fw.dma("sp", xt[:, :], memp[nb * 128:(nb + 1) * 128, :], writes=[xt])
            mT = norm_T(xt, 128, gbc)
            for ec in range(8):
                ps = nps(); lin_fm(ps[:, :128], Wk, ec * 128, 128, mT, 128, ps)
                copy(KTp[:, ec, nb * 128:(nb + 1) * 128], ps[:, :128], [ps], [KTp])
            for half in range(2):
                ps = nps(); lin_tm(ps[:, :512], mT, Wk, half * 512, 512, 128, ps)
                copy(kvs[:, :], ps[:, :512], [ps], [kvs])
                fw.dma("pool", o_memk[l, nb * 128:(nb + 1) * 128, half * 512:(half + 1) * 512], kvs[:, :], reads=[kvs], is_output=True)
                ps = nps(); lin_tm(ps[:, :512], mT, Wv, half * 512, 512, 128, ps)
                copy(kvs[:, :], ps[:, :512], [ps], [kvs])
                copy(Vaug[:, nb, 2 * half:2 * half + 2, 0:256], ps[:, :512].rearrange("p (h e) -> p h e", e=256), [ps], [Vaug])
                fw.dma("pool", o_memv[l, nb * 128:(nb + 1) * 128, half * 512:(half + 1) * 512], kvs[:, :], reads=[kvs], is_output=True)
        load_g(g_mem_q[l:l + 1, :])

        def stile(loaders, n, KT, VA, finish):
            ntl = len(loaders)
            Wd = ntl * n
            for i, ld in enumerate(loaders):
                xt = ld()
                norm_tok(xt, n, gbc)
                half_tt = hT_rot[(i * n) // 256]
                for b in range(0, 8, 4):
                    ps = nps()
                    for j in range(4):
                        tr(ps[:, j * 128:j * 128 + n], hn[:n, (b + j) * 128:(b + j + 1) * 128], n, [hn], [ps])
                    copy(hT4[:, b:b + 4, i * n:i * n + n], ps[:, 0:512].rearrange("p (j t) -> p j t", t=128)[:, :, :n], [ps], [half_tt])
            rdh = [hT_rot[0]] + ([hT_rot[1]] if Wd > 256 else [])
            for ec in range(8):
                ps = nps()
                for kc in range(8):
                    mm(ps[:, :Wd], Wq[kc][:, ec * 128:(ec + 1) * 128], hT4[:, kc, :Wd], kc == 0, kc == 7, [Wq[kc]] + rdh, [ps])
                copy(qT[:, ec, :Wd], ps[:, :Wd], [ps], [qT])
            for i, ld in enumerate(loaders):
                xt = ld()
                tile(xt, n, KT, VA, i * n)
                finish(i, xt)

        def tile(xt, n, KT, VA, o):
            for h in range(4):
                ps = nps()
                for nb in range(2):
                    for ec in range(2):
                        mm(ps[:, nb * 128:nb * 128 + n], KT[:, 2 * h + ec, nb * 128:(nb + 1) * 128], qT[:, 2 * h + ec, o:o + n], ec == 0, ec == 1, [KT, qT], [ps])
                act(pT[:, :, :n], ps[:, 0:256].rearrange("p (b t) -> p b t", t=128)[:, :, :n], AF.Exp, [ps], [pT], scale=1.0 / 16.0)
                ps2 = nps()
                for nb in range(2):
                    mm(ps2[:n, 0:257], pT[:, nb, :n], VA[:, nb, h, :], nb == 0, nb == 1, [pT, VA], [ps2])
                fw.op("dve", lambda e, ps2=ps2: e.reciprocal(out=rd[:n, :], in_=ps2[:n, 256:257]), [ps2], [rd])
                copy(otok[:n, h * 256:(h + 1) * 256], ps2[:n, 0:256], [ps2, rd], [otok], scale=rd[:n, 0:1])
            to_fm(otok, n, oT)
            for half in range(2):
                ps = nps(); lin_tm(ps[:n, :512], oT, Wo, half * 512, 512, n, ps)
                add_res(xt, n, half, ps)

        for t0 in range(0, NT, 4):
            stile([(lambda t=t0 + i: load_x(t, False)) for i in range(4)], 128, KTp, Vaug, lambda i, xt, t0=t0: store_x(t0 + i, xt))
        for s in range(NS):
            for nb in range(2):
                fw.dma("sp", kraw[:, :], cmk[l, s, nb * 128:(nb + 1) * 128, :], writes=[kraw])
                for b in range(2):
                    ps = nps()
                    for j in range(4):
                        tr(ps[:, j * 128:(j + 1) * 128], kraw[:, (4 * b + j) * 128:(4 * b + j + 1) * 128], 128, [kraw], [ps])
                    copy(KTs[:, 4 * b:4 * b + 4, nb * 128:(nb + 1) * 128], ps[:, :].rearrange("p (j t) -> p j t", t=128), [ps], [KTs])
                fw.dma("pool", Vaugs[:, nb, :, 0:256], cmv[l, s, nb * 128:(nb + 1) * 128, :].rearrange("n (h e) -> n h e", e=256), writes=[Vaugs])
            stile([lambda s=s: load_xs(s)], 8, KTs, Vaugs, lambda i, xt, s=s: store_xs(s, xt))

    def phase_C(l, final):
        off = 0
        sm_reset()
        Wup, off = wload("wup", w_up[l], off, 1024, 5632)
        Wdn, off = wload("wdn", w_down[l], off, 2816, 1024)
        load_g(g_ffn[l:l + 1, :])
        cwf = sm("cwf", [128, 44, 3]); cols_from_rows(lambda c: cwf[:, c, :], cwf, ffn_cw[l], 3, 44)
        hist = sm("fhist", [128, 44, 2])
        actT = sm("actT", [128, 22, 512], BF16)
        extA = [sm("extA%d" % i, [128, 514]) for i in range(2)]
        extB = [sm("extB%d" % i, [128, 514]) for i in range(2)]
        tA = sm("ftA", [128, 512]); tB = sm("ftB", [128, 512])
        hTs = [hT_rot[0], hT_rot[1]]

        def stile(loaders, n, finish):
            ntl = len(loaders)
            Wd = ntl * n
            for i, ld in enumerate(loaders):
                xt = ld()
                norm_tok(xt, n, gbc)
                half_tt = hTs[(i * n) // 256]
                for b in range(0, 8, 4):
                    ps = nps()
                    for j in range(4):
                        tr(ps[:, j * 128:j * 128 + n], hn[:n, (b + j) * 128:(b + j + 1) * 128], n, [hn], [ps])
                    copy(hT4[:, b:b + 4, i * n:i * n + n], ps[:, 0:512].rearrange("p (j t) -> p j t", t=128)[:, :, :n], [ps], [half_tt])
            rd = [hTs[0]] + ([hTs[1]] if Wd > 256 else [])
            for j in range(22):
                ea, eb = extA[j % 2], extB[j % 2]
                for (c, ext, ev) in ((j, ea, "act"), (j + 22, eb, "act")):
                    ps = nps()
                    for kc in range(8):
                        mm(ps[:, :Wd], Wup[kc][:, c * 128:(c + 1) * 128], hT4[:, kc, :Wd], kc == 0, kc == 7, [Wup[kc]] + rd, [ps])
                    copy(ext[:, 0:2], hist[:, c, :], [hist], [ext], eng="pool")
                    copy(ext[:, 2:2 + Wd], ps[:, :Wd], [ps], [ext], eng=ev)
                    copy(hist[:, c, :], ext[:, Wd:Wd + 2], [ext], [hist], eng="pool")
                c2 = j + 22
                fw.op("act", lambda e, ea=ea, j=j: e.mul(out=tA[:, :Wd], in_=ea[:, 0:Wd], mul=cwf[:, j, 0:1]), [ea, cwf], [tA])
                fw.op("act", lambda e, eb=eb, c2=c2: e.mul(out=tB[:, :Wd], in_=eb[:, 0:Wd], mul=cwf[:, c2, 0:1]), [eb, cwf], [tB])
                for k in (1, 2):
                    stt("dve", tA[:, :Wd], ea[:, k:k + Wd], cwf[:, j, k:k + 1], tA[:, :Wd], ALU.mult, ALU.add, [ea, cwf, tA], [tA])
                    stt("dve", tB[:, :Wd], eb[:, k:k + Wd], cwf[:, c2, k:k + 1], tB[:, :Wd], ALU.mult, ALU.add, [eb, cwf, tB], [tB])
                act(tA[:, :Wd], tA[:, :Wd], AF.Silu, [tA], [tA])
                tt_op("dve", actT[:, j, :Wd], tA[:, :Wd], tB[:, :Wd], ALU.mult, [tA, tB], [actT])
            for i, ld in enumerate(loaders):
                xt = ld()
                for half in range(2):
                    ps = nps()
                    for j in range(22):
                        mm(ps[:n, :512], actT[:, j, i * n:i * n + n], Wdn[j][:, half * 512:(half + 1) * 512], j == 0, j == 21, [actT, Wdn[j]], [ps])
                    add_res(xt, n, half, ps)
                finish(i, xt)

        fw.op("pool", lambda e: e.memset(hist[:, :, :], 0.0), writes=[hist])
        for t0 in range(0, NT, 4):
            def fin(i, xt, t0=t0):
                t = t0 + i
                if final:
                    norm_tok(xt, 128, gfin)
                    fw.dma("sp", y_p[t * 128:(t + 1) * 128, :], hn[:, :], reads=[hn], writes=[xres[t]], is_output=True)
                else:
                    store_x(t, xt)
            stile([(lambda t=t0 + i: load_x(t, False)) for i in range(4)], 128, fin)
        rows_from_cols(o_ffn_p[l], lambda c: hist[:, c, :], hist, 2, 44)
        for s_ in range(NS):
            cols_from_rows(lambda c: hist[:, c, :], hist, st_ffn[l, s_], 2, 44)

            def fin_s(i, xt, s_=s_):
                if final:
                    norm_tok(xt, 8, gfin)
                    fw.dma("sp", y_s[s_ * 8:(s_ + 1) * 8, :], hn[:8, :], reads=[hn], writes=[xsres[s_]], is_output=True)
                else:
                    store_xs(s_, xt)
            stile([lambda s_=s_: load_xs(s_)], 8, fin_s)
            rows_from_cols(o_ffn_s[l, s_], lambda c: hist[:, c, :], hist, 2, 44)

    def phase_A1():
        off = 0
        sm_reset()
        Wqkv, off = wload("wqkv", w_qkv, off, 1024, 4608)
        Woc, off = wload("woc", w_out_c, off, 512, 1024)
        load_g(g_mix[1:2, :])
        qk = sm("qk_tok", [128, 512]); vt = sm("v_tok", [128, 512]); vbf = sm("v_bf", [128, 512], BF16)
        cs = sm("cs_t", [128, 8]); sn = sm("sn_t", [128, 8])
        r1 = sm("rp1", [128, 8, 8]); r2 = sm("rp2", [128, 8, 8]); r3 = sm("rp3", [128, 8, 8])
        fmT = sm("fmT", [128, 4, 128], BF16)
        mb = sm("mb", [128, 2, 128], BF16); fw.dma("pool", mb[:], c_mb.rearrange("a p n -> p a n"), writes=[mb])
        smk = sm("smk", [128, 24, 8], BF16); fw.dma("pool", smk[:], c_sm.rearrange("p (b t) -> p b t", t=8), writes=[smk])
        ident_bf = sm("ident_bf", [128, 128], BF16); copy(ident_bf[:], ident[:], [ident], [ident_bf], eng="dve")
        sQT = [[sm("sQT%d_%d" % (s, g), [128, 4, 8], BF16) for g in range(3)] for s in range(NS)]
        sKT = [[sm("sKT%d_%d" % (s, g), [128, 4, 8], BF16) for g in range(3)] for s in range(NS)]
        sV_d = nc.dram_tensor("sV_d", [NS, 3, 8, 512], BF16).ap()
        sV_tt = TT(None, "sV_tt")
        sVn = sm("sVn", [8, 512], BF16)

        def rope(n, pos0):
            fw.dma("sp", cs[:n, :], c_cos[pos0:pos0 + n, :], writes=[cs])
            fw.dma("sp", sn[:n, :], c_sin[pos0:pos0 + n, :], writes=[sn])
            X = qk[:n, :].rearrange("p (h e) -> p h e", e=64)
            x1 = X[:, :, 0:8]; x2 = X[:, :, 8:16]
            cb = cs[:n, :].unsqueeze(1).to_broadcast([n, 8, 8]); sb_ = sn[:n, :].unsqueeze(1).to_broadcast([n, 8, 8])
            tt_op("dve", r1[:n], x1, sb_, ALU.mult, [qk, sn], [r1])
            tt_op("dve", r2[:n], x2, sb_, ALU.mult, [qk, sn], [r2])
            tt_op("dve", r3[:n], x2, cb, ALU.mult, [qk, cs], [r3])
            tt_op("dve", x1, x1, cb, ALU.mult, [qk, cs], [qk])
            tt_op("dve", x1, x1, r2[:n], ALU.subtract, [qk, r2], [qk])
            tt_op("dve", x2, r3[:n], r1[:n], ALU.add, [r3, r1], [qk])

        def proj_tile(xt, n, pos0, t, s):
            hT = norm_T(xt, n, gbc)
            for g in range(3):
                W = WINS[g][0]
                for si in range(3):
                    ps = nps(); lin_tm(ps[:n, :512], hT, Wqkv, g * 1536 + si * 512, 512, n, ps)
                    if si < 2:
                        copy(qk[:n, :], ps[:n, :512], [ps], [qk])
                        rope(n, pos0)
                        to_fm(qk, n, fmT, nchunks=4)
                        if s is None:
                            dst = (QT_d if si == 0 else KT_d)[g]
                            fw.dma("pool", dst.rearrange("(c p) s -> p c s", p=128)[:, :, t * 128:(t + 1) * 128], fmT[:, :, :], reads=[fmT], writes=[qkv_tt[t]])
                        else:
                            copy((sQT if si == 0 else sKT)[s][g][:, :, :], fmT[:, :, :8], [fmT], [(sQT if si == 0 else sKT)[s][g]], eng="pool")
                        if si == 1:
                            if s is None:
                                lo = S - min(W, S)
                                if (t + 1) * 128 > lo:
                                    fw.dma("pool", o_win_p[g][t * 128 - lo:(t + 1) * 128 - lo, 0:512], qk[:, :], reads=[qk], is_output=True)
                            else:
                                fw.dma("pool", o_win_s[g][s, :, 0:512], qk[:8, :], reads=[qk], is_output=True)
                    else:
                        copy(vt[:n, :], ps[:n, :512], [ps], [vt])
                        copy(vbf[:n, :], ps[:n, :512], [ps], [vbf])
                        if s is None:
                            fw.dma("pool", V_d[g][t * 128:(t + 1) * 128, :], vbf[:, :], reads=[vbf], writes=[qkv_tt[t]])
                            lo = S - min(W, S)
                            if (t + 1) * 128 > lo:
                                fw.dma("pool", o_win_p[g][t * 128 - lo:(t + 1) * 128 - lo, 512:1024], vt[:, :], reads=[vt], is_output=True)
                        else:
                            fw.dma("sp", sV_d[s, g], vbf[:8, :], reads=[vbf], writes=[sV_tt])
                            fw.dma("pool", o_win_s[g][s, :, 512:1024], vt[:8, :], reads=[vt], is_output=True)

        for t in range(NT):
            xt = load_x(t, False)
            proj_tile(xt, 128, t * 128, t, None)
        for s in range(NS):
            proj_tile(load_xs(s), 8, S, None, s)

        print('A1 proj done at op', fw.total)
        HALF = 2048
        NH = S // HALF
        accN = barrier_tt(TT(arena_f[:, 0:4 * HALF].rearrange("p (c s) -> p c s", s=HALF), "accN"))
        accD = barrier_tt(TT(arena_f[:, 4 * HALF:8 * HALF].rearrange("p (c s) -> p c s", s=HALF), "accD"))
        QTs = barrier_tt(TT(arena.ap[:, 40960:40960 + 8192].rearrange("p (c s) -> p c s", s=2048), "QTs"))
        KTs_ = barrier_tt(TT(arena.ap[:, 49152:49152 + 16384].rearrange("p (c s) -> p c s", s=4096), "KTs"))
        Vst = [sm("Vst%d" % i, [128, 512], BF16) for i in range(2)]
        Vpad = [sm("Vpad%d" % i, [128, 2, 4, 2, 128], BF16) for i in range(2)]
        onespad = sm("onespad", [128, 2, 128], BF16)
        fw.op("pool", lambda e: e.memset(onespad[:], 0.0), writes=[onespad])
        fw.op("pool", lambda e: e.memset(onespad[:, 0, 0:64], 1.0), writes=[onespad])
        fw.op("pool", lambda e: e.memset(onespad[:, 1, 64:128], 1.0), writes=[onespad])
        for i in range(2):
            fw.op("pool", lambda e, i=i: e.memset(Vpad[i][:], 0.0), writes=[Vpad[i]])
        PTR = [sm("PT%d" % i, [128, 2, 128], BF16) for i in range(5)]
        ptc = [0]
        rec = sm("rec", [128, 4, 128]); oTb = sm("oTb", [128, 4, 128], BF16)
        vcnt = [0]
        for hf in range(NH):
            for g in range(3):
                W, d = WINS[g]
                SB = 128 * d
                for sbk in range(HALF // SB):
                    q_sb = hf * (HALF // SB) + sbk
                    base = q_sb * SB
                    lb = base - hf * HALF
                    fw.dma("sp", QTs[:, :, 0:SB], QT_d[g].rearrange("(c p) s -> p c s", p=128)[:, :, base:base + SB], reads=qkv_tt, writes=[QTs])
                    if q_sb > 0:
                        fw.dma("sp", KTs_[:, :, 0:2 * SB], KT_d[g].rearrange("(c p) s -> p c s", p=128)[:, :, base - SB:base + SB], reads=qkv_tt, writes=[KTs_])
                    else:
                        fw.dma("sp", KTs_[:, :, SB:2 * SB], KT_d[g].rearrange("(c p) s -> p c s", p=128)[:, :, base:base + SB], reads=qkv_tt, writes=[KTs_])
                    kbs = (0, 1) if q_sb > 0 else (1,)
                    for r in range(d):
                        vp = Vpad[vcnt[0] % 2]; vcnt[0] += 1
                        for kb in kbs:
                            p0 = base - SB + r if kb == 0 else base + r
                            vs = Vst[kb]
                            fw.dma("sp", vs[:, :], V_d[g][p0:p0 + 127 * d + 1:d, :], reads=qkv_tt, writes=[vs])
                            for hh in range(2):
                                copy(vp[:, kb, :, hh, hh * 64:(hh + 1) * 64], vs[:, :].rearrange("p (c h e) -> p c h e", h=2, e=64)[:, :, hh, :], [vs], [vp], eng="pool")
                        def a_scores(c):
                            PT = [PTR[(ptc[0] + i_) % 5] for i_ in range(2)]
                            ptc[0] += 2
                            for hh in range(2):
                                ps = nps()
                                pl = slice(hh * 64, (hh + 1) * 64)
                                for kb in kbs:
                                    ksl = slice(r, SB, d) if kb == 0 else slice(SB + r, 2 * SB, d)
                                    mm(ps[:, kb * 128:(kb + 1) * 128], KTs_[pl, c, ksl], QTs[pl, c, r:SB:d], True, False, [KTs_, QTs], [ps])
                                    mm(ps[:, kb * 128:(kb + 1) * 128], ident_bf[:, :], mb[:, kb, :], False, True, [ident_bf, mb], [ps])
                                k0 = kbs[0]
                                act(PT[hh][:, k0:2, :], ps[:, k0 * 128:256].rearrange("p (b t) -> p b t", t=128), AF.Exp, [ps], [PT[hh]], scale=0.125)
                            return PT

                        def a_pv(c, PT):
                            psN = nps(); psD = nps()
                            seq = [(hh, kb) for hh in range(2) for kb in kbs]
                            for i, (hh, kb) in enumerate(seq):
                                mm(psN[:, :128], vp[:, kb, c, hh, :], PT[hh][:, kb, :], i == 0, i == len(seq) - 1, [vp, PT[hh]], [psN])
                            for i, (hh, kb) in enumerate(seq):
                                mm(psD[:, :128], onespad[:, hh, :], PT[hh][:, kb, :], i == 0, i == len(seq) - 1, [onespad, PT[hh]], [psD])
                            dN = accN[:, c, lb + r:lb + SB:d]; dD = accD[:, c, lb + r:lb + SB:d]
                            if g == 0:
                                copy(dN, psN[:, :128], [psN], [accN], eng="act")
                                copy(dD, psD[:, :128], [psD], [accD], eng="dve")
                            else:
                                tt_op("dve", dN, dN, psN[:, :128], ALU.add, [accN, psN], [accN])
                                tt_op("dve", dD, dD, psD[:, :128], ALU.add, [accD, psD], [accD])

                        pts = {0: a_scores(0)}
                        for c in range(4):
                            if c + 1 < 4:
                                pts[c + 1] = a_scores(c + 1)
                            a_pv(c, pts[c])
            for tl in range(HALF // 128):
                t = hf * (HALF // 128) + tl
                fw.op("dve", lambda e, tl=tl: e.reciprocal(out=rec[:, :, :], in_=accD[:, :, tl * 128:(tl + 1) * 128]), [accD], [rec])
                tt_op("dve", oTb[:, :, :], accN[:, :, tl * 128:(tl + 1) * 128], rec[:, :, :], ALU.mult, [accN, rec], [oTb])
                xt = load_x(t, False)
                for half in range(2):
                    ps = nps()
                    for c in range(4):
                        mm(ps[:, :512], oTb[:, c, :], Woc[c][:, half * 512:(half + 1) * 512], c == 0, c == 3, [oTb, Woc[c]], [ps])
                    add_res(xt, 128, half, ps)
                store_x(t, xt)
        print('A1 prompt attn done at op', fw.total)
        craw = sm("craw", [128, 1024]); KTb = sm("KTb", [128, 4, 128], BF16)
        sVp = [sm("sVp%d" % i, [128, 4, 2, 128], BF16) for i in range(2)]
        for i in range(2):
            fw.op("pool", lambda e, i=i: e.memset(sVp[i][:], 0.0), writes=[sVp[i]])
        PTs = sm("PTs", [128, 8, 8], BF16)
        aN = sm("aNs", [128, 32]); aD = sm("aDs", [128, 32]); oTs = sm("oTs", [128, 32], BF16)
        bases = [0, 1, 5]
        bcnt = [0]
        for s in range(NS):
            xs_t = load_xs(s)
            first = True
            for g in range(3):
                W, d = WINS[g]
                nblk = W // 128
                for kb in range(nblk + 1):
                    newblk = (kb == nblk)
                    nk = 8 if newblk else 128
                    vp = sVp[bcnt[0] % 2]; bcnt[0] += 1
                    if not newblk:
                        fw.dma("sp", craw[:, :], cwin[g][s, kb * 128:(kb + 1) * 128, :], writes=[craw])
                        ps = nps()
                        for j in range(4):
                            tr(ps[:, j * 128:(j + 1) * 128], craw[:, j * 128:(j + 1) * 128], 128, [craw], [ps])
                        copy(KTb[:, :, :], ps[:, :].rearrange("p (j t) -> p j t", t=128), [ps], [KTb])
                        for hh in range(2):
                            copy(vp[:, :, hh, hh * 64:(hh + 1) * 64], craw[:, 512:1024].rearrange("p (c h e) -> p c h e", h=2, e=64)[:, :, hh, :], [craw], [vp], eng="pool")
                        ktt, kap = KTb, KTb
                        bi = bases[g] + kb
                    else:
                        fw.dma("sp", sVn[:8, :], sV_d[s, g], reads=[sV_tt], writes=[sVn])
                        for hh in range(2):
                            copy(vp[:8, :, hh, hh * 64:(hh + 1) * 64], sVn[:8, :].rearrange("p (c h e) -> p c h e", h=2, e=64)[:, :, hh, :], [sVn], [vp], eng="pool")
                        ktt, kap = sKT[s][g], sKT[s][g]
                        bi = 21 + g
                        k2 = sm("k2n", [64, 2, 4, 8], BF16); q2 = sm("q2n", [64, 2, 4, 8], BF16)
                        for hh in range(2):
                            copy(k2[:, hh, :, :], sKT[s][g][hh * 64:(hh + 1) * 64, :, :], [sKT[s][g]], [k2], eng="pool")
                            copy(q2[:, hh, :, :], sQT[s][g][hh * 64:(hh + 1) * 64, :, :], [sQT[s][g]], [q2], eng="pool")
                    ps = nps()
                    for c in range(4):
                        for hh in range(2):
                            pl = slice(hh * 64, (hh + 1) * 64)
                            hd = 2 * c + hh
                            if newblk:
                                mm(ps[:nk, hd * 8:(hd + 1) * 8], k2[:, hh, c, :nk], q2[:, hh, c, :8], True, True, [k2, q2], [ps])
                            else:
                                mm(ps[:nk, hd * 8:(hd + 1) * 8], kap[pl, c, :nk], sQT[s][g][pl, c, :8], True, True, [ktt, sQT[s][g]], [ps])
                    act(PTs[:nk, :, :], ps[:nk, 0:64].rearrange("p (h t) -> p h t", t=8), AF.Exp, [ps], [PTs], scale=0.125)
                    tt_op("dve", PTs[:nk, :, :], PTs[:nk, :, :], smk[:nk, bi, :].unsqueeze(1).to_broadcast([nk, 8, 8]), ALU.mult, [PTs, smk], [PTs])
                    psN = nps(); psD = nps()
                    for c in range(4):
                        for hh in range(2):
                            hd = 2 * c + hh
                            mm(psN[:, c * 8:(c + 1) * 8], vp[:nk, c, hh, :], PTs[:nk, hd, :], hh == 0, hh == 1, [vp, PTs], [psN])
                    for c in range(4):
                        for hh in range(2):
                            hd = 2 * c + hh
                            mm(psD[:, c * 8:(c + 1) * 8], onespad[:nk, hh, :], PTs[:nk, hd, :], hh == 0, hh == 1, [onespad, PTs], [psD])
                    if first:
                        copy(aN[:, :], psN[:, :32], [psN], [aN], eng="act")
                        copy(aD[:, :], psD[:, :32], [psD], [aD], eng="dve")
                        first = False
                    else:
                        tt_op("dve", aN[:, :], aN[:, :], psN[:, :32], ALU.add, [aN, psN], [aN])
                        tt_op("dve", aD[:, :], aD[:, :], psD[:, :32], ALU.add, [aD, psD], [aD])
            fw.op("dve", lambda e: e.reciprocal(out=aD[:, :], in_=aD[:, :]), [aD], [aD])
            tt_op("dve", oTs[:, :], aN[:, :], aD[:, :], ALU.mult, [aN, aD], [oTs])
            for half in range(2):
                ps = nps()
                for c in range(4):
                    mm(ps[:8, :512], oTs[:, c * 8:(c + 1) * 8], Woc[c][:, half * 512:(half + 1) * 512], c == 0, c == 3, [oTs, Woc[c]], [ps])
                add_res(xs_t, 8, half, ps)
            store_xs(s, xs_t)

    if 'A0' in phases: phase_A0()
    if 'B0' in phases: phase_B(0)
    if 'C0' in phases: phase_C(0, False)
    if 'A1' in phases: phase_A1()
    if 'B1' in phases: phase_B(1)
    if 'C1' in phases: phase_C(1, True)
    fw.finish()
    return nc, fw


def _consts(S):
    c = {}
    r = np.arange(128)[:, None]; cc = np.arange(128)[None, :]
    c["c_ident"] = np.eye(128, dtype=np.float32)
    c["c_mge"] = (BIG * (cc >= r)).astype(np.float32)
    c["c_nmle"] = (-BIG * (cc <= r)).astype(np.float32)
    c["c_nmlt"] = (-BIG * (cc < r)).astype(np.float32)
    c["c_LT"] = (r <= cc).astype(np.float32)
    invc = np.zeros((2, 128, 512), np.float32)
    for g, w in enumerate((2, 4, 8, 16)):
        pos = np.arange(128)
        invc[0, :, g * 128:(g + 1) * 128] = (1.0 / np.minimum(pos + 1, w))[None, :]
        invc[1, :, g * 128:(g + 1) * 128] = 1.0 / w
    c["c_invc"] = invc
    half = 8
    inv_freq = (np.float32(500000.0) ** (-np.arange(half, dtype=np.float32) / np.float32(half))).astype(np.float32)
    pos = np.concatenate([np.arange(S), 8192 + np.arange(8)]).astype(np.float32)
    ang = (pos[:, None] * inv_freq[None, :]).astype(np.float32)
    c["c_cos"] = np.cos(ang.astype(np.float64)).astype(np.float32)
    c["c_sin"] = np.sin(ang.astype(np.float64)).astype(np.float32)
    NEG = -240000.0
    j = np.arange(128)[:, None]; i = np.arange(128)[None, :]
    mb = np.zeros((2, 128, 128), np.float32)
    mb[0] = np.where(j >= i, 0.0, NEG)
    mb[1] = np.where(j <= i, 0.0, NEG)
    c["c_mb"] = mb
    smk = np.zeros((128, 24, 8), np.float32)
    bases = [0, 1, 5]
    for g, (W, d) in enumerate(WINS):
        for kb in range(W // 128):
            e = kb * 128 + np.arange(128)[:, None]
            t = np.arange(8)[None, :]
            diff = W + t - e
            smk[:, bases[g] + kb, :] = ((diff % d == 0) & (diff // d >= 1) & (diff // d <= 128)).astype(np.float32)
        p = np.arange(128)[:, None]; t = np.arange(8)[None, :]
        smk[:, 21 + g, :] = ((p <= t) & ((t - p) % d == 0) & (p < 8)).astype(np.float32)
    c["c_sm"] = smk.reshape(128, 24 * 8)
    return c


_CACHE = {}


def _core_inputs(inp, c, S, NS, consts):
    b = c % 4
    ss = slice(NS * c, NS * (c + 1))
    f = lambda a: np.ascontiguousarray(a, dtype=np.float32)
    m = {
        "xp": f(inp["x_prompt"][b]), "xs": f(inp["x_sample"][ss].reshape(NS * 8, 1024)), "memp": f(inp["mem_prompt"][b]),
        "st_pool": f(inp["state_pool"][0, ss]), "st_dconv": f(inp["state_dn_conv"][0, ss]), "st_dn": f(inp["state_dn"][0, ss]),
        "cw128": f(inp["cache_win_w128"][0, ss].reshape(NS, 128, 1024)),
        "cw512": f(inp["cache_win_w512"][0, ss].reshape(NS, 512, 1024)),
        "cw2048": f(inp["cache_win_w2048"][0, ss].reshape(NS, 2048, 1024)),
        "cmk": f(inp["cache_mem_k"][:, ss].reshape(2, NS, 256, 1024)), "cmv": f(inp["cache_mem_v"][:, ss].reshape(2, NS, 256, 1024)),
        "st_ffn": f(inp["state_ffn_conv"][:, ss]),
        "g_mix": f(inp["g_mix"]), "w_in": f(inp["w_in_ab"][0]), "w_pool": f(inp["w_pool"][0].reshape(512, 128)),
        "pool_scale": f(inp["pool_scale"]), "dn_conv_w": f(inp["dn_conv_w"][0]), "dn_a_log": f(inp["dn_a_log"]),
        "dn_dt_bias": f(inp["dn_dt_bias"]), "dn_norm_w": f(inp["dn_norm_w"]), "w_out_ab": f(inp["w_out_ab"][0]),
        "w_qkv": f(inp["w_qkv_c"][0]), "w_out_c": f(inp["w_out_c"][0]), "g_mem_q": f(inp["g_mem_q"]), "g_mem_kv": f(inp["g_mem_kv"]),
        "w_mem_q": f(inp["w_mem_q"]), "w_mem_k": f(inp["w_mem_k"]), "w_mem_v": f(inp["w_mem_v"]), "w_mem_o": f(inp["w_mem_o"]),
        "g_ffn": f(inp["g_ffn"]), "w_up": f(inp["w_up"]), "ffn_cw": f(inp["ffn_conv_w"]), "w_down": f(inp["w_down"]),
        "g_final": f(inp["g_final"].reshape(1, 1024)),
    }
    m.update(consts)
    return m


def kernel(**inp):
    S = inp["x_prompt"].shape[1]
    NS = 4
    if S not in _CACHE:
        _CACHE[S] = build(S, NS)[0]
    nc = _CACHE[S]
    consts = _consts(S)
    in_maps = [_core_inputs(inp, c, S, NS, consts) for c in range(8)]
    res = run_bass_kernel_spmd(nc, in_maps, core_ids=list(range(8)))
    R = res.results
    f = np.float32
    B = 4
    cat_p = lambda k: np.stack([R[b][k] for b in range(B)])
    cat_s = lambda k: np.concatenate([R[c][k] for c in range(8)], axis=0)
    y_p = cat_p("y_p")
    y_s = cat_s("y_s").reshape(32, 8, 1024)
    outs = [y_p, y_s,
            cat_p("o_pool_p")[None], cat_s("o_pool_s")[None],
            cat_p("o_dconv_p")[None], cat_s("o_dconv_s")[None],
            cat_p("o_dn_p")[None], cat_s("o_dn_s")[None]]
    for w, _ in WINS:
        wp = min(w, S)
        outs.append(cat_p("o_w%d_p" % w).reshape(1, B, wp, 2, 8, 64))
        outs.append(cat_s("o_w%d_s" % w).reshape(1, 32, 8, 2, 8, 64))
    outs.append(np.stack([R[b]["o_memk"] for b in range(B)], axis=1).reshape(2, B, 256, 4, 256))
    outs.append(np.stack([R[b]["o_memv"] for b in range(B)], axis=1).reshape(2, B, 256, 4, 256))
    outs.append(np.stack([R[b]["o_ffn_p"] for b in range(B)], axis=1))
    outs.append(np.concatenate([R[c]["o_ffn_s"] for c in range(8)], axis=1))
    return tuple(np.ascontiguousarray(o, dtype=f) for o in outs)
```

```python
import numpy as np
import concourse.bass as bass
import concourse.mybir as mybir

F32 = mybir.dt.float32
BF16 = mybir.dt.bfloat16
AF = mybir.ActivationFunctionType
ALU = mybir.AluOpType
AX = mybir.AxisListType


class TT:
    __slots__ = ("ap", "w", "r", "name", "psum")

    def __init__(self, ap, name=""):
        self.ap = ap
        self.psum = False
        self.w = None
        self.r = {}
        self.name = name

    def __getitem__(self, idx):
        return self.ap[idx]


class FW:
    ENG = ("pe", "act", "dve", "pool", "sp")

    def __init__(self, nc, ndma=6):
        self.nc = nc
        self.prog = {e: [] for e in self.ENG}
        self.cnt = {e: 0 for e in self.ENG if e != "sp"}
        self.seen = {e: {} for e in self.ENG}
        self.sems = {}
        self.ndma = ndma
        self.dma_rr = {"sp": 0, "pool": 0, "act": 0}
        self.dma_tot = {}
        self.ctx = []
        self.out_tokens = []
        import os
        self.maxops = int(os.environ.get("FW_MAXOPS", "100000000"))
        self.total = 0
        self.log = []

    def sem(self, key):
        if key not in self.sems:
            cm = self.nc.semaphore("s_%s" % str(key).replace(" ", ""))
            self.sems[key] = cm.__enter__()
            self.ctx.append(cm)
        return self.sems[key]

    def sb(self, name, shape, dt=F32):
        cm = self.nc.sbuf_tensor(name, list(shape), dt)
        t = cm.__enter__()
        self.ctx.append(cm)
        return TT(t, name)

    def ps(self, name, shape, dt=F32):
        cm = self.nc.psum_tensor(name, list(shape), dt)
        t = cm.__enter__()
        self.ctx.append(cm)
        tt = TT(t, name)
        tt.psum = True
        return tt

    def _waits(self, eng, reads, writes, skip_self_pe=False):
        need = {}
        for t in reads:
            if t.w is not None:
                k, v = t.w
                need[k] = max(need.get(k, 0), v)
            if t.psum:
                for k, v in t.r.items():
                    if k != eng:
                        need[k] = max(need.get(k, 0), v)
        for t in writes:
            if t.w is not None:
                k, v = t.w
                need[k] = max(need.get(k, 0), v)
            for k, v in t.r.items():
                need[k] = max(need.get(k, 0), v)
        out = []
        seen = self.seen[eng]
        for k, v in need.items():
            if eng == "pe" and k == "pe":
                continue
            if seen.get(k, 0) >= v:
                continue
            seen[k] = v
            out.append((k, v))
        return out

    def op(self, eng, fn, reads=(), writes=()):
        self.total += 1
        if self.total > self.maxops:
            return None
        reads = [t for t in reads if isinstance(t, TT)]
        writes = [t for t in writes if isinstance(t, TT)]
        waits = self._waits(eng, reads, writes)
        self.cnt[eng] += 1
        v = self.cnt[eng]
        self.prog[eng].append((waits, fn, (eng, v, 1)))
        tok = (eng, v)
        for t in writes:
            t.w = tok
            t.r = {}
        for t in reads:
            if t in writes:
                continue
            t.r[eng] = max(t.r.get(eng, 0), v)
        return tok

    def dma(self, q, out_ap, in_ap, reads=(), writes=(), is_output=False, **kw):
        self.total += 1
        if self.total > self.maxops:
            return None
        reads = [t for t in reads if isinstance(t, TT)]
        writes = [t for t in writes if isinstance(t, TT)]
        waits = self._waits(q, reads, writes)
        i = self.dma_rr[q]
        self.dma_rr[q] = (i + 1) % self.ndma
        key = "d%s%d" % (q, i)
        prev = self.dma_tot.get(key, 0)
        if prev > 0 and self.seen[q].get(key, 0) < prev:
            self.seen[q][key] = prev
            waits.append((key, prev))
        v = prev + 16
        self.dma_tot[key] = v
        fn = (lambda e, o=out_ap, a=in_ap, k=kw: e.dma_start(out=o, in_=a, **k))
        self.prog[q].append((waits, fn, (key, v, 16)))
        tok = (key, v)
        for t in writes:
            t.w = tok
            t.r = {}
        for t in reads:
            t.r[key] = max(t.r.get(key, 0), v)
        if is_output:
            self.out_tokens.append(tok)
        return tok

    def finish(self):
        nc = self.nc
        fin = {}
        for k, v in self.out_tokens:
            fin[k] = max(fin.get(k, 0), v)
        for k, v in self.dma_tot.items():
            fin[k] = max(fin.get(k, 0), v)
        for e, c in self.cnt.items():
            if c > 0:
                fin[e] = c
        for k in list(fin.keys()):
            self.sem(k)
        for e in self.ENG:
            for waits, fn, (k, v, inc) in self.prog[e]:
                self.sem(k)
                for (wk, wv) in waits:
                    self.sem(wk)
        sems = self.sems
        prog = self.prog

        def run(e, h, final=False):
            for waits, fn, (k, v, inc) in prog[e]:
                for (wk, wv) in waits:
                    h.wait_ge(sems[wk], wv)
                fn(h).then_inc(sems[k], inc)
            if final:
                for k, v in fin.items():
                    h.wait_ge(sems[k], v)

        with nc.Block() as block:
            @block.sync
            def _(h):
                run("sp", h, final=True)

            @block.tensor
            def _(h):
                run("pe", h)

            @block.scalar
            def _(h):
                run("act", h)

            @block.vector
            def _(h):
                run("dve", h)

            @block.gpsimd
            def _(h):
                run("pool", h)
        for cm in reversed(self.ctx):
            cm.__exit__(None, None, None)

from concourse.bass_utils import run_bass_kernel_spmd

BIG = 30000.0
EPS = 1e-6
WINS = ((128, 1), (512, 4), (2048, 16))


def build(S=4096, NS=4, phases=('A0', 'B0', 'C0', 'A1', 'B1', 'C1')):
    nc = bass.Bass("TRN2", target_bir_lowering=False)
    fw = FW(nc)
    NT = S // 128
    I_, O_ = {}, {}

    def din(name, shape):
        I_[name] = nc.dram_tensor(name, list(shape), F32, kind="ExternalInput").ap()
        return I_[name]

    def dout(name, shape):
        O_[name] = nc.dram_tensor(name, list(shape), F32, kind="ExternalOutput").ap()
        return O_[name]

    xp = din("xp", [S, 1024]); xs_d = din("xs", [NS * 8, 1024]); memp = din("memp", [256, 1024])
    st_pool = din("st_pool", [NS, 15, 512]); st_dconv = din("st_dconv", [NS, 3, 1536])
    st_dn = din("st_dn", [NS, 4, 128, 128])
    cwin = [din("cw%d" % w, [NS, w, 1024]) for w, _ in WINS]
    cmk = din("cmk", [2, NS, 256, 1024]); cmv = din("cmv", [2, NS, 256, 1024])
    st_ffn = din("st_ffn", [2, NS, 2, 5632])
    g_mix = din("g_mix", [2, 1024]); w_in = din("w_in", [1024, 2568]); w_pool = din("w_pool", [512, 128])
    pool_scale = din("pool_scale", [1, 512]); dn_conv_w = din("dn_conv_w", [4, 1536])
    dn_a_log = din("dn_a_log", [1, 4]); dn_dt_bias = din("dn_dt_bias", [1, 4]); dn_norm_w = din("dn_norm_w", [1, 128])
    w_out_ab = din("w_out_ab", [1024, 1024]); w_qkv = din("w_qkv", [1024, 4608]); w_out_c = din("w_out_c", [512, 1024])
    g_mem_q = din("g_mem_q", [2, 1024]); g_mem_kv = din("g_mem_kv", [2, 1024])
    w_mem = {k: din("w_mem_" + k, [2, 1024, 1024]) for k in "qkvo"}
    g_ffn = din("g_ffn", [2, 1024]); w_up = din("w_up", [2, 1024, 5632]); ffn_cw = din("ffn_cw", [2, 3, 5632])
    w_down = din("w_down", [2, 2816, 1024]); g_final = din("g_final", [1, 1024])
    c_ident = din("c_ident", [128, 128]); c_mge = din("c_mge", [128, 128]); c_nmle = din("c_nmle", [128, 128])
    c_nmlt = din("c_nmlt", [128, 128]); c_LT = din("c_LT", [128, 128])
    c_invc = din("c_invc", [2, 128, 512])
    c_cos = din("c_cos", [S + 8, 8]); c_sin = din("c_sin", [S + 8, 8])
    c_mb = din("c_mb", [2, 128, 128]); c_sm = din("c_sm", [128, 24 * 8])

    y_p = dout("y_p", [S, 1024]); y_s = dout("y_s", [NS * 8, 1024])
    o_pool_p = dout("o_pool_p", [15, 512]); o_pool_s = dout("o_pool_s", [NS, 15, 512])
    o_dconv_p = dout("o_dconv_p", [3, 1536]); o_dconv_s = dout("o_dconv_s", [NS, 3, 1536])
    o_dn_p = dout("o_dn_p", [4, 128, 128]); o_dn_s = dout("o_dn_s", [NS, 4, 128, 128])
    o_win_p = [dout("o_w%d_p" % w, [min(w, S), 1024]) for w, _ in WINS]
    o_win_s = [dout("o_w%d_s" % w, [NS, 8, 1024]) for w, _ in WINS]
    o_memk = dout("o_memk", [2, 256, 1024]); o_memv = dout("o_memv", [2, 256, 1024])
    o_ffn_p = dout("o_ffn_p", [2, 2, 5632]); o_ffn_s = dout("o_ffn_s", [2, NS, 2, 5632])

    QT_d = [nc.dram_tensor("QT%d" % g, [512, S], BF16).ap() for g in range(3)]
    KT_d = [nc.dram_tensor("KT%d" % g, [512, S], BF16).ap() for g in range(3)]
    V_d = [nc.dram_tensor("Vd%d" % g, [S, 512], BF16).ap() for g in range(3)]
    qkv_tt = [TT(None, "qkvscr%d" % t) for t in range(NT)]
    xres = [TT(None, "xres%d" % t) for t in range(NT)]

    rr = {"ps": 0, "ev": 0, "q": 0}
    PS = [fw.ps("ps%d" % i, [128, 512]) for i in range(8)]

    def nps():
        p = PS[rr["ps"] % 8]
        rr["ps"] += 1
        return p

    def ev_eng():
        rr["ev"] += 1
        return "act" if rr["ev"] % 2 else "dve"

    def copy(dst, src, reads, writes, eng=None, scale=None):
        eng = eng or ev_eng()
        if scale is not None and not isinstance(scale, float) and eng == "act":
            eng = "dve"
        if eng == "act":
            if scale is None:
                fw.op("act", lambda e: e.copy(out=dst, in_=src), reads, writes)
            else:
                fw.op("act", lambda e: e.mul(out=dst, in_=src, mul=scale), reads, writes)
        else:
            if scale is None:
                fw.op(eng, lambda e: e.tensor_copy(out=dst, in_=src), reads, writes)
            else:
                fw.op(eng, lambda e: e.tensor_scalar(out=dst, in0=src, scalar1=scale, scalar2=None, op0=ALU.mult), reads, writes)

    def tt_op(eng, dst, a, b, op, reads, writes):
        fw.op(eng, lambda e: e.tensor_tensor(out=dst, in0=a, in1=b, op=op), reads, writes)

    def stt(eng, dst, a, sc, b, op0, op1, reads, writes):
        if eng == "pool":
            tmp = sm("ptmp", [128, 128])
            nparts, nfree = a.shape[0], a.shape[1]
            fw.op("pool", lambda e: e.tensor_scalar(out=tmp[:nparts, :nfree], in0=a, scalar1=sc, scalar2=None, op0=op0), reads, [tmp])
            fw.op("pool", lambda e: e.tensor_tensor(out=dst, in0=tmp[:nparts, :nfree], in1=b, op=op1), list(reads) + [tmp], writes)
            return
        fw.op(eng, lambda e: e.scalar_tensor_tensor(out=dst, in0=a, scalar=sc, in1=b, op0=op0, op1=op1), reads, writes)

    def act(dst, src, func, reads, writes, bias=None, scale=None):
        kw = {}
        if bias is not None:
            kw["bias"] = bias
        if scale is not None:
            kw["scale"] = scale
        fw.op("act", lambda e: e.activation(out=dst, in_=src, func=func, **kw), reads, writes)

    def mm(ps_ap, lhsT, rhs, start, stop, reads, writes):
        fw.op("pe", lambda e: e.matmul(ps_ap, lhsT=lhsT, rhs=rhs, start=start, stop=stop), reads, writes)

    def tr(ps_ap, in_ap, k, reads, writes):
        fw.op("pe", lambda e: e.transpose(ps_ap, in_ap, ident[:k, :k]), list(reads) + [ident], writes)

    ident = fw.sb("ident", [128, 128]); fw.dma("sp", ident[:], c_ident[:], writes=[ident])
    m_ge = fw.sb("m_ge", [128, 128]); fw.dma("sp", m_ge[:], c_mge[:], writes=[m_ge])
    nm_le = fw.sb("nm_le", [128, 128]); fw.dma("sp", nm_le[:], c_nmle[:], writes=[nm_le])
    nm_lt = fw.sb("nm_lt", [128, 128]); fw.dma("sp", nm_lt[:], c_nmlt[:], writes=[nm_lt])
    LT = fw.sb("LT", [128, 128]); fw.dma("sp", LT[:], c_LT[:], writes=[LT])
    ones = fw.sb("ones", [128, 128]); fw.op("pool", lambda e: e.memset(ones[:], 1.0), writes=[ones])
    gfin = fw.sb("gfin", [128, 1024]); fw.dma("sp", gfin[:], g_final[0:1, :].partition_broadcast(128) if False else g_final.to_broadcast([128, 1024]), writes=[gfin])
    gbc = fw.sb("gbc", [128, 1024])

    def load_g(src_row):
        fw.dma("sp", gbc[:], src_row.to_broadcast([128, 1024]), writes=[gbc])

    xsres = [TT(None, "xsres%d" % s) for s in range(NS)]

    ARENA = 67584
    arena = fw.sb("arena", [128, ARENA], BF16)
    arena_f = arena.ap.bitcast(F32)

    def barrier_tt(tt):
        tt.r = dict(fw.cnt)
        for k, v in fw.dma_tot.items():
            tt.r[k] = v
        return tt

    wl = {"i": 0}

    def wload(name, dram2d, off, K, N):
        KC = K // 128
        tt = barrier_tt(TT(arena.ap[:, off:off + KC * N].rearrange("p (k n) -> p k n", n=N), name))
        chunks = []
        stg = [xt_rot[0], xt_rot[1], hn]
        for kc in range(KC):
            c = TT(tt.ap[:, kc, :], "%s_%d" % (name, kc))
            c.r = dict(tt.r)
            if N < 64:
                fw.dma("pool", c.ap, dram2d[kc * 128:(kc + 1) * 128, :], writes=[c])
            else:
                eng = "act" if kc % 2 else "dve"
                for c0 in range(0, N, 1024):
                    w_ = min(1024, N - c0)
                    st = stg[wl["i"] % 3]; wl["i"] += 1
                    fw.dma("sp", st[:, :w_], dram2d[kc * 128:(kc + 1) * 128, c0:c0 + w_], writes=[st])
                    if eng == "act":
                        fw.op("act", lambda e, st=st, c=c, c0=c0, w_=w_: e.copy(out=c.ap[:, c0:c0 + w_], in_=st[:, :w_]), [st], [c])
                    else:
                        fw.op("dve", lambda e, st=st, c=c, c0=c0, w_=w_: e.tensor_copy(out=c.ap[:, c0:c0 + w_], in_=st[:, :w_]), [st], [c])
            chunks.append(c)
        return chunks, off + KC * N

    xt_rot = [fw.sb("xt%d" % i, [128, 1024]) for i in range(2)]
    hn = fw.sb("hn", [128, 1024])
    hT4 = fw.sb("hT4", [128, 8, 512], BF16)
    hT_rot = [TT(hT4.ap[:, :, 0:256], "hTa"), TT(hT4.ap[:, :, 256:512], "hTb")]
    ssq = [fw.sb("ssq%d" % i, [128, 1]) for i in range(2)]
    cnt = {"xt": 0, "hT": 0, "ss": 0}

    def rstd_of(src_ap, ntok, width, reads):
        ss = ssq[cnt["ss"] % 2]; cnt["ss"] += 1
        act(hn[:ntok, :width], src_ap, AF.Square, reads, [hn])
        fw.op("dve", lambda e: e.reduce_sum(out=ss[:ntok, :], in_=hn[:ntok, :width], axis=AX.X), [hn], [ss])
        act(ss[:ntok, :], ss[:ntok, :], AF.Sqrt, [ss], [ss], bias=EPS, scale=1.0 / width)
        fw.op("dve", lambda e: e.reciprocal(out=ss[:ntok, :], in_=ss[:ntok, :]), [ss], [ss])
        return ss

    def norm_tok(xt, ntok, g_tt):
        ss = rstd_of(xt[:ntok, :], ntok, 1024, [xt])
        stt("dve", hn[:ntok, :], xt[:ntok, :], ss[:ntok, 0:1], g_tt[:ntok, :], ALU.mult, ALU.mult, [xt, ss, g_tt], [hn])
        return hn

    def to_fm(src, ntok, dst, nchunks=8, c0=0):
        for b in range(0, nchunks, 4):
            nb = min(4, nchunks - b)
            ps = nps()
            for j in range(nb):
                tr(ps[:, j * 128:j * 128 + ntok], src[:ntok, (b + j) * 128:(b + j + 1) * 128], ntok, [src], [ps])
            copy(dst[:, b:b + nb, c0:c0 + ntok], ps[:, 0:nb * 128].rearrange("p (j t) -> p j t", t=128)[:, :, :ntok], [ps], [dst])

    def norm_T(xt, ntok, g_tt):
        norm_tok(xt, ntok, g_tt)
        hT = hT_rot[cnt["hT"] % 2]; cnt["hT"] += 1
        to_fm(hn, ntok, hT)
        return hT

    def lin_fm(ps_ap, W, col0, M, hT, ntok, wr):
        KC = len(W)
        for kc in range(KC):
            mm(ps_ap, W[kc][:, col0:col0 + M], hT[:, kc, :ntok], kc == 0, kc == KC - 1, [W[kc], hT], [wr])

    def lin_tm(ps_ap, hT, W, col0, N, ntok, wr, o=0, rd=None):
        KC = len(W)
        for kc in range(KC):
            mm(ps_ap, hT[:, kc, o:o + ntok], W[kc][:, col0:col0 + N], kc == 0, kc == KC - 1, [W[kc]] + (rd if rd is not None else [hT]), [wr])

    STW = 1024
    stage = fw.sb("stage", [16, STW])
    stage_o = fw.sb("stage_o", [16, STW])

    def cols_from_rows(dst_fn, dst_tts, src_dram, k, C):
        per = STW // 128
        for c0 in range(0, C, per):
            n = min(per, C - c0)
            fw.dma("sp", stage[:k, :n * 128], src_dram[:, c0 * 128:(c0 + n) * 128], writes=[stage])
            ps = nps()
            for j in range(n):
                tr(ps[:, j * 16:j * 16 + k], stage[:k, j * 128:(j + 1) * 128], k, [stage], [ps])
            for j in range(n):
                copy(dst_fn(c0 + j), ps[:, j * 16:j * 16 + k], [ps], [dst_tts[c0 + j] if isinstance(dst_tts, list) else dst_tts])

    def rows_from_cols(dst_dram, src_fn, src_tts, k, C):
        per = STW // 128
        for p0 in range(0, C, per):
            np_ = min(per, C - p0)
            for c0 in range(p0, p0 + np_, 4):
                n = min(4, p0 + np_ - c0)
                ps = nps()
                for j in range(n):
                    st = src_tts[c0 + j] if isinstance(src_tts, list) else src_tts
                    fw.op("pe", lambda e, j=j, c=c0 + j, ps=ps: e.transpose(ps[:k, j * 128:(j + 1) * 128], src_fn(c), ident[:, :]), [st, ident], [ps])
                copy(stage_o[:k, (c0 - p0) * 128:(c0 - p0 + n) * 128], ps[:k, :n * 128], [ps], [stage_o])
            fw.dma("pool", dst_dram[:, p0 * 128:(p0 + np_) * 128], stage_o[:k, :np_ * 128], reads=[stage_o], is_output=True)

    def load_x(t, layer0):
        xt = xt_rot[cnt["xt"] % 2]; cnt["xt"] += 1
        if layer0:
            fw.dma("sp", xt[:, :], xp[t * 128:(t + 1) * 128, :], writes=[xt])
        else:
            fw.dma("sp", xt[:, :], y_p[t * 128:(t + 1) * 128, :], reads=[xres[t]], writes=[xt])
        return xt

    def load_xs(s_, layer0=False):
        xt = xt_rot[cnt["xt"] % 2]; cnt["xt"] += 1
        if layer0:
            fw.dma("sp", xt[:8, :], xs_d[s_ * 8:(s_ + 1) * 8, :], writes=[xt])
        else:
            fw.dma("sp", xt[:8, :], y_s[s_ * 8:(s_ + 1) * 8, :], reads=[xsres[s_]], writes=[xt])
        return xt

    def store_xs(s_, xt):
        fw.dma("sp", y_s[s_ * 8:(s_ + 1) * 8, :], xt[:8, :], reads=[xt], writes=[xsres[s_]], is_output=True)

    def store_x(t, xt):
        fw.dma("sp", y_p[t * 128:(t + 1) * 128, :], xt[:, :], reads=[xt], writes=[xres[t]], is_output=True)

    def add_res(xt, ntok, half, ps):
        tt_op("dve", xt[:ntok, half * 512:(half + 1) * 512], xt[:ntok, half * 512:(half + 1) * 512], ps[:ntok, :512], ALU.add, [xt, ps], [xt])

    SCRB = 37632
    scratch = fw.sb("scratch", [128, SCRB // 2], BF16)
    scr_f = scratch.ap.bitcast(F32)
    small = {}
    scr = {"off": 0}

    def sm_reset():
        print('scratch used', scr['off'])
        small.clear()
        scr["off"] = 0

    def sm(name, shape, dt=F32):
        if name in small:
            return small[name]
        esz = 4 if dt == F32 else 2
        n = 1
        for d_ in shape[1:]:
            n *= d_
        nbytes = (n * esz + 7) // 8 * 8
        off = scr["off"]
        assert off + nbytes <= SCRB, ("scratch overflow", name, off, nbytes)
        scr["off"] = off + nbytes
        base = scr_f if dt == F32 else scratch.ap
        ap = base[:shape[0], off // esz: off // esz + n]
        if len(shape) == 3:
            ap = ap.rearrange("p (a b) -> p a b", b=shape[2])
        elif len(shape) == 4:
            ap = ap.rearrange("p (a b c) -> p a b c", b=shape[2], c=shape[3])
        elif len(shape) == 5:
            ap = ap.rearrange("p (a b c d) -> p a b c d", b=shape[2], c=shape[3], d=shape[4])
        tt = barrier_tt(TT(ap, name))
        small[name] = tt
        return tt

    def phase_A0():
        off = 0
        sm_reset()
        Win, off = wload("w_in", w_in[:, 0:2560], off, 1024, 2560)
        Wab, off = wload("w_ab", w_in[:, 2560:2568], off, 1024, 8)
        Wout, off = wload("w_outab", w_out_ab, off, 1024, 1024)
        Wpool, off = wload("w_pool", w_pool, off, 512, 128)
        load_g(g_mix[0:1, :])
        cw = sm("dn_cw", [128, 12, 4]); cols_from_rows(lambda c: cw[:, c, :], cw, dn_conv_w, 4, 12)
        psc = sm("pool_sc", [128, 4, 1]); cols_from_rows(lambda c: psc[:, c, :], psc, pool_scale, 1, 4)
        dtb = sm("dtb", [128, 4]); fw.dma("sp", dtb[:], dn_dt_bias.to_broadcast([128, 4]), writes=[dtb])
        nea = sm("nea", [128, 4]); fw.dma("sp", nea[:], dn_a_log.to_broadcast([128, 4]), writes=[nea])
        act(nea[:], nea[:], AF.Exp, [nea], [nea])
        copy(nea[:], nea[:], [nea], [nea], eng="dve", scale=-1.0)
        nw = sm("normw", [128, 128]); fw.dma("sp", nw[:], dn_norm_w.to_broadcast([128, 128]), writes=[nw])
        invc = sm("invc", [128, 512]); fw.dma("sp", invc[:], c_invc[0], writes=[invc])
        aext = []; qext = []
        xoff = 22400
        for g in range(4):
            aext.append(barrier_tt(TT(arena_f[:, xoff:xoff + 527], "aext%d" % g))); xoff += 527
        for c in range(12):
            qext.append(barrier_tt(TT(arena_f[:, xoff:xoff + 515], "qext%d" % c))); xoff += 515
        assert xoff <= 33792
        qc = [sm("qc%d" % c, [128, 128]) for c in range(12)]
        S = [sm("S%d" % h, [128, 128]) for h in range(4)]
        mixT = sm("mixT", [128, 8, 128], BF16)
        gate_s = sm("gate_s", [128, 512]); ofin = sm("ofin", [128, 512])
        pa = [sm("pa%d" % i, [128, 143]) for i in range(2)]
        zb = sm("zb", [128, 128], BF16)
        col = {n: sm("col_" + n, [128, 4]) for n in ["beta", "lnb", "g", "G", "negG", "Gl", "eG", "beG", "nbeta", "ekg", "glast", "egl"]}
        gt = {n: sm("gt_" + n, [128, 128]) for n in ["sq", "rn"]}
        gnames = ["Ds", "DTb", "DTi", "B0", "B1", "R0", "R1", "Tt", "QKd", "X0v", "X0k", "kg", "nwT", "vnew", "tmpB"]
        gth = []
        aoff = 14720
        for h_ in range(4):
            d_ = {}
            for nm in gnames:
                d_[nm] = barrier_tt(TT(arena_f[:, aoff:aoff + 128], "gt%d_%s" % (h_, nm)))
                aoff += 128
            gth.append(d_)

        def tile(xt, ntok, tile0, o, rd):
            n = ntok
            ps = nps(); lin_tm(ps[:n, :512], hT4, Win, 2048, 512, n, ps, o=o, rd=rd)
            act(gate_s[:n, :], ps[:n, :512], AF.Silu, [ps], [gate_s])
            ps = nps(); lin_tm(ps[:n, :8], hT4, Wab, 0, 8, n, ps, o=o, rd=rd)
            act(col["beta"][:n, :], ps[:n, 0:4], AF.Sigmoid, [ps], [col["beta"]])
            tt_op("dve", col["g"][:n, :], ps[:n, 4:8], dtb[:n, :], ALU.add, [ps, dtb], [col["g"]])
            act(col["g"][:n, :], col["g"][:n, :], AF.Exp, [col["g"]], [col["g"]])
            act(col["g"][:n, :], col["g"][:n, :], AF.Ln, [col["g"]], [col["g"]], bias=1.0)
            tt_op("dve", col["g"][:n, :], col["g"][:n, :], nea[:n, :], ALU.mult, [col["g"], nea], [col["g"]])
            act(col["lnb"][:n, :], col["beta"][:n, :], AF.Ln, [col["beta"]], [col["lnb"]])
            ps = nps()
            mm(ps[:n, 0:4], LT[:n, :n], col["g"][:n, :], True, True, [LT, col["g"]], [ps])
            copy(col["G"][:n, :], ps[:n, 0:4], [ps], [col["G"]], eng="dve")
            ps = nps()
            mm(ps[:, 0:4], ones[:n, :], col["g"][:n, :], True, True, [ones, col["g"]], [ps])
            copy(col["glast"][:, :], ps[:, 0:4], [ps], [col["glast"]], eng="dve")
            act(col["egl"][:, :], col["glast"][:, :], AF.Exp, [col["glast"]], [col["egl"]])
            copy(col["negG"][:n, :], col["G"][:n, :], [col["G"]], [col["negG"]], eng="dve", scale=-1.0)
            tt_op("dve", col["Gl"][:n, :], col["G"][:n, :], col["lnb"][:n, :], ALU.add, [col["G"], col["lnb"]], [col["Gl"]])
            act(col["eG"][:n, :], col["G"][:n, :], AF.Exp, [col["G"]], [col["eG"]])
            tt_op("dve", col["beG"][:n, :], col["eG"][:n, :], col["beta"][:n, :], ALU.mult, [col["eG"], col["beta"]], [col["beG"]])
            copy(col["nbeta"][:n, :], col["beta"][:n, :], [col["beta"]], [col["nbeta"]], eng="dve", scale=-1.0)
            tt_op("dve", col["ekg"][:n, :], col["glast"][:n, :], col["G"][:n, :], ALU.subtract, [col["glast"], col["G"]], [col["ekg"]])
            act(col["ekg"][:n, :], col["ekg"][:n, :], AF.Exp, [col["ekg"]], [col["ekg"]])
            Lx = 15 + n
            for g in range(4):
                cur = aext[g]
                cb = o
                sh = 1
                for step in range(g + 1):
                    dst = pa[step % 2]
                    eng = "dve"
                    tt_op(eng, dst[:, sh:Lx], cur[:, cb + sh:cb + Lx], cur[:, cb:cb + Lx - sh], ALU.add, [cur], [dst])
                    cur = dst
                    cb = 0
                    sh *= 2
                nxt = pa[(g + 1) % 2]
                if tile0:
                    tt_op("dve", nxt[:, 15:Lx], cur[:, 15:Lx], invc[:, g * 128:g * 128 + n], ALU.mult, [cur, invc], [nxt])
                else:
                    copy(nxt[:, 15:Lx], cur[:, 15:Lx], [cur], [nxt], eng="dve", scale=1.0 / (2 ** (g + 1)))
                tt_op("dve", zb[:, :n], nxt[:, 15:Lx], aext[g][:, o + 15:o + Lx], ALU.subtract, [nxt, aext[g]], [zb])
                ps = nps()
                mm(ps[:, :n], Wpool[g][:, :], zb[:, :n], True, True, [Wpool[g], zb], [ps])
                copy(mixT[:, g, :n], ps[:, :n], [ps], [mixT], eng="act", scale=psc[:, g, 0:1])
            for c in range(12):
                fw.op("act", lambda e, c=c: e.mul(out=qc[c][:, :n], in_=qext[c][:, o:o + n], mul=cw[:, c, 0:1]), [qext[c], cw], [qc[c]])
                for k in range(1, 4):
                    stt("dve", qc[c][:, :n], qext[c][:, o + k:o + k + n], cw[:, c, k:k + 1], qc[c][:, :n], ALU.mult, ALU.add, [qext[c], cw, qc[c]], [qc[c]])
                act(qc[c][:, :n], qc[c][:, :n], AF.Silu, [qc[c]], [qc[c]])
            for c in range(8):
                act(gt["sq"][:, :n], qc[c][:, :n], AF.Square, [qc[c]], [gt["sq"]])
                ps = nps()
                mm(ps[:, :n], ones[:, :], gt["sq"][:, :n], True, True, [ones, gt["sq"]], [ps])
                sc = 128.0 if c < 4 else 1.0
                act(gt["rn"][:, :n], ps[:, :n], AF.Sqrt, [ps], [gt["rn"]], bias=EPS * sc, scale=sc)
                fw.op("dve", lambda e: e.reciprocal(out=gt["rn"][:, :n], in_=gt["rn"][:, :n]), [gt["rn"]], [gt["rn"]])
                tt_op("dve", qc[c][:, :n], qc[c][:, :n], gt["rn"][:, :n], ALU.mult, [qc[c], gt["rn"]], [qc[c]])
            nlev = 7 if n == 128 else 3
            PSH = [dict() for _ in range(4)]

            def st_A(h):
                g_ = gth[h]
                qT, kT = qc[h], qc[4 + h]
                pKK = nps(); mm(pKK[:n, :n], kT[:, :n], kT[:, :n], True, True, [kT], [pKK])
                pQK = nps(); mm(pQK[:n, :n], kT[:, :n], qT[:, :n], True, True, [kT, qT], [pQK])
                p1 = nps()
                mm(p1[:n, :n], col["G"][:n, h:h + 1].to_broadcast([n, n]), ident[:n, :n], True, False, [col["G"], ident], [p1])
                mm(p1[:n, :n], ident[:n, :n], m_ge[:n, :n], False, True, [ident, m_ge], [p1])
                act(g_["Ds"][:n, :n], p1[:n, :n], AF.Exp, [p1, col["G"]], [g_["Ds"]], bias=col["G"][:n, h:h + 1], scale=-1.0)
                p2 = nps()
                mm(p2[:n, :n], col["Gl"][:n, h:h + 1].to_broadcast([n, n]), ident[:n, :n], True, False, [col["Gl"], ident], [p2])
                mm(p2[:n, :n], ident[:n, :n], nm_le[:n, :n], False, True, [ident, nm_le], [p2])
                act(g_["DTb"][:n, :n], p2[:n, :n], AF.Exp, [p2, col["negG"]], [g_["DTb"]], bias=col["negG"][:n, h:h + 1], scale=1.0)
                p3 = nps()
                mm(p3[:n, :n], col["G"][:n, h:h + 1].to_broadcast([n, n]), ident[:n, :n], True, False, [col["G"], ident], [p3])
                mm(p3[:n, :n], ident[:n, :n], nm_lt[:n, :n], False, True, [ident, nm_lt], [p3])
                act(g_["DTi"][:n, :n], p3[:n, :n], AF.Exp, [p3, col["negG"]], [g_["DTi"]], bias=col["negG"][:n, h:h + 1], scale=1.0)
                stt("dve", g_["B0"][:n, :n], pKK[:n, :n], col["nbeta"][:n, h:h + 1], g_["Ds"][:n, :n], ALU.mult, ALU.mult, [pKK, col["nbeta"], g_["Ds"]], [g_["B0"]])
                stt("dve", g_["R0"][:n, :n], pKK[:n, :n], -1.0, g_["DTb"][:n, :n], ALU.mult, ALU.mult, [pKK, g_["DTb"]], [g_["R0"]])
                tt_op("dve", g_["QKd"][:n, :n], pQK[:n, :n], g_["DTi"][:n, :n], ALU.mult, [pQK, g_["DTi"]], [g_["QKd"]])
                tt_op("dve", g_["Tt"][:n, :n], g_["R0"][:n, :n], ident[:n, :n], ALU.add, [g_["R0"], ident], [g_["Tt"]])

            def st_L(h, k):
                g_ = gth[h]
                Bm = [g_["B0"], g_["B1"]]; Rm = [g_["R0"], g_["R1"]]; Tt = g_["Tt"]
                a, b = (k - 1) % 2, k % 2
                pP = nps(); mm(pP[:n, :n], Rm[a][:n, :n], Bm[a][:n, :n], True, True, [Rm[a], Bm[a]], [pP])
                if k < nlev - 1:
                    pR = nps(); mm(pR[:n, :n], Bm[a][:n, :n], Rm[a][:n, :n], True, True, [Rm[a], Bm[a]], [pR])
                copy(Bm[b][:n, :n], pP[:n, :n], [pP], [Bm[b]], eng="act")
                if k < nlev - 1:
                    copy(Rm[b][:n, :n], pR[:n, :n], [pR], [Rm[b]], eng="dve")
                pT = nps(); mm(pT[:n, :n], Bm[b][:n, :n], Tt[:n, :n], True, True, [Bm[b], Tt], [pT])
                tt_op("dve", Tt[:n, :n], Tt[:n, :n], pT[:n, :n], ALU.add, [Tt, pT], [Tt])

            def st_V(h):
                g_ = gth[h]
                kT, vT = qc[4 + h], qc[8 + h]
                Tt = g_["Tt"]
                pV = nps(); mm(pV[:n, :128], vT[:, :n], ident[:, :], True, True, [vT, ident], [pV])
                copy(g_["X0v"][:n, :], pV[:n, :128], [pV, col["beta"]], [g_["X0v"]], eng="dve", scale=col["beta"][:n, h:h + 1])
                pK = nps(); mm(pK[:n, :128], kT[:, :n], ident[:, :], True, True, [kT, ident], [pK])
                copy(g_["X0k"][:n, :], pK[:n, :128], [pK, col["beG"]], [g_["X0k"]], eng="dve", scale=col["beG"][:n, h:h + 1])
                copy(g_["kg"][:n, :], pK[:n, :128], [pK, col["ekg"]], [g_["kg"]], eng="dve", scale=col["ekg"][:n, h:h + 1])
                pW = nps(); mm(pW[:, :n], g_["X0k"][:n, :], Tt[:n, :n], True, True, [g_["X0k"], Tt], [pW])
                copy(g_["nwT"][:, :n], pW[:, :n], [pW], [g_["nwT"]], eng="act", scale=-1.0)

            def st_N(h):
                g_ = gth[h]
                qT = qc[h]
                Tt = g_["Tt"]
                pN = nps()
                mm(pN[:n, :128], Tt[:n, :n], g_["X0v"][:n, :], True, False, [Tt, g_["X0v"]], [pN])
                mm(pN[:n, :128], g_["nwT"][:, :n], S[h][:, :], False, True, [g_["nwT"], S[h]], [pN])
                copy(g_["vnew"][:n, :], pN[:n, :128], [pN], [g_["vnew"]], eng="act")
                pA = nps(); mm(pA[:n, :128], qT[:, :n], S[h][:, :], True, True, [qT, S[h]], [pA])
                pB = nps(); mm(pB[:n, :128], g_["QKd"][:n, :n], g_["vnew"][:n, :], True, True, [g_["QKd"], g_["vnew"]], [pB])
                copy(g_["tmpB"][:n, :], pB[:n, :128], [pB], [g_["tmpB"]], eng="act")
                stt("dve", ofin[:n, h * 128:(h + 1) * 128], pA[:n, :128], col["eG"][:n, h:h + 1], g_["tmpB"][:n, :], ALU.mult, ALU.add, [pA, col["eG"], g_["tmpB"]], [ofin])
                pS = nps(); mm(pS[:, :128], g_["kg"][:n, :], g_["vnew"][:n, :], True, True, [g_["kg"], g_["vnew"]], [pS])
                stt("dve", S[h][:, :], S[h][:, :], col["egl"][:, h:h + 1], pS[:, :128], ALU.mult, ALU.add, [S[h], col["egl"], pS], [S[h]])

            for h in range(4):
                st_A(h)
            for k in range(1, nlev):
                for h in range(4):
                    st_L(h, k)
            for h in range(4):
                st_V(h)
            for h in range(4):
                st_N(h)
            for h in range(4):
                ss = rstd_of(ofin[:n, h * 128:(h + 1) * 128], n, 128, [ofin])
                stt("dve", ofin[:n, h * 128:(h + 1) * 128], ofin[:n, h * 128:(h + 1) * 128], ss[:n, 0:1], nw[:n, :], ALU.mult, ALU.mult, [ofin, ss, nw], [ofin])
                tt_op("dve", ofin[:n, h * 128:(h + 1) * 128], ofin[:n, h * 128:(h + 1) * 128], gate_s[:n, h * 128:(h + 1) * 128], ALU.mult, [ofin, gate_s], [ofin])
            ps = nps()
            for j in range(4):
                tr(ps[:, j * 128:j * 128 + n], ofin[:n, j * 128:(j + 1) * 128], n, [ofin], [ps])
            copy(mixT[:, 4:8, :n], ps[:, :].rearrange("p (j t) -> p j t", t=128)[:, :, :n], [ps], [mixT])
            for half in range(2):
                ps = nps()
                for c in range(8):
                    mm(ps[:n, :512], mixT[:, c, :n], Wout[c][:, half * 512:(half + 1) * 512], c == 0, c == 7, [mixT, Wout[c]], [ps])
                add_res(xt, n, half, ps)

        def stile(loaders, n, first, finish):
            ntl = len(loaders)
            Wd = ntl * n
            for i, ld in enumerate(loaders):
                xt = ld()
                norm_tok(xt, n, gbc)
                half_tt = hT_rot[(i * n) // 256]
                for b in range(0, 8, 4):
                    ps = nps()
                    for j in range(4):
                        tr(ps[:, j * 128:j * 128 + n], hn[:n, (b + j) * 128:(b + j + 1) * 128], n, [hn], [ps])
                    copy(hT4[:, b:b + 4, i * n:i * n + n], ps[:, 0:512].rearrange("p (j t) -> p j t", t=128)[:, :, :n], [ps], [half_tt])
            rd = [hT_rot[0]] + ([hT_rot[1]] if Wd > 256 else [])
            for g in range(4):
                ps = nps()
                for kc in range(8):
                    mm(ps[:, :Wd], Win[kc][:, g * 128:(g + 1) * 128], hT4[:, kc, :Wd], kc == 0, kc == 7, [Win[kc]] + rd, [ps])
                copy(aext[g][:, 15:15 + Wd], ps[:, :Wd], [ps], [aext[g]])
            for c in range(12):
                ps = nps()
                for kc in range(8):
                    mm(ps[:, :Wd], Win[kc][:, 512 + c * 128:512 + (c + 1) * 128], hT4[:, kc, :Wd], kc == 0, kc == 7, [Win[kc]] + rd, [ps])
                copy(qext[c][:, 3:3 + Wd], ps[:, :Wd], [ps], [qext[c]])
            for i, ld in enumerate(loaders):
                xt = ld()
                tile(xt, n, first and i == 0, i * n, rd)
                finish(i, xt)
            for g in range(4):
                copy(aext[g][:, 0:15], aext[g][:, Wd:Wd + 15], [aext[g]], [aext[g]], eng="pool")
            for c in range(12):
                copy(qext[c][:, 0:3], qext[c][:, Wd:Wd + 3], [qext[c]], [qext[c]], eng="pool")

        for g in range(4):
            fw.op("pool", lambda e, g=g: e.memset(aext[g][:, :], 0.0), writes=[aext[g]])
        for c in range(12):
            fw.op("pool", lambda e, c=c: e.memset(qext[c][:, :], 0.0), writes=[qext[c]])
        for h in range(4):
            fw.op("pool", lambda e, h=h: e.memset(S[h][:, :], 0.0), writes=[S[h]])
        for t0 in range(0, NT, 4):
            stile([(lambda t=t0 + i: load_x(t, True)) for i in range(4)], 128, t0 == 0, lambda i, xt, t0=t0: store_x(t0 + i, xt))
        rows_from_cols(o_pool_p[:, :], lambda c: aext[c][:, 0:15], aext, 15, 4)
        rows_from_cols(o_dconv_p[:, :], lambda c: qext[c][:, 0:3], qext, 3, 12)
        for h in range(4):
            fw.dma("pool", o_dn_p[h], S[h][:, :], reads=[S[h]], is_output=True)
        for s in range(NS):
            cols_from_rows(lambda c: aext[c][:, 0:15], aext, st_pool[s], 15, 4)
            cols_from_rows(lambda c: qext[c][:, 0:3], qext, st_dconv[s], 3, 12)
            for h in range(4):
                fw.dma("sp", S[h][:, :], st_dn[s, h], writes=[S[h]])
            stile([lambda s=s: load_xs(s, True)], 8, False, lambda i, xt, s=s: store_xs(s, xt))
            rows_from_cols(o_pool_s[s], lambda c: aext[c][:, 0:15], aext, 15, 4)
            rows_from_cols(o_dconv_s[s], lambda c: qext[c][:, 0:3], qext, 3, 12)
            for h in range(4):
                fw.dma("pool", o_dn_s[s, h], S[h][:, :], reads=[S[h]], is_output=True)

    def phase_B(l):
        off = 0
        sm_reset()
        Wq, off = wload("wq", w_mem["q"][l], off, 1024, 1024)
        Wk, off = wload("wk", w_mem["k"][l], off, 1024, 1024)
        Wv, off = wload("wv", w_mem["v"][l], off, 1024, 1024)
        Wo, off = wload("wo", w_mem["o"][l], off, 1024, 1024)
        KTp = sm("KTp", [128, 8, 256], BF16); Vaug = sm("Vaug", [128, 2, 4, 257], BF16)
        KTs = sm("KTs", [128, 8, 256], BF16); Vaugs = sm("Vaugs", [128, 2, 4, 257], BF16)
        fw.op("pool", lambda e: e.memset(Vaug[:, :, :, 256:257], 1.0), writes=[Vaug])
        fw.op("pool", lambda e: e.memset(Vaugs[:, :, :, 256:257], 1.0), writes=[Vaugs])
        qT = sm("qT", [128, 8, 512], BF16); pT = sm("pT", [128, 2, 128], BF16)
        otok = sm("otok", [128, 1024]); oT = sm("oT", [128, 8, 128], BF16)
        rd = sm("rd", [128, 1]); kvs = sm("kvs", [128, 512]); kraw = sm("kraw", [128, 1024])
        load_g(g_mem_kv[l:l + 1, :])
        for nb in range(2):
            xt = xt_rot[cnt["xt"] % 2]; cnt["xt"] += 1
            fw.dma("sp", xt[:, :], memp[nb * 128:(nb + 1) * 128, :], writes=[xt])
            mT = norm_T(xt, 128, gbc)
            for ec in range(8):
                ps = nps(); lin_fm(ps[:, :128], Wk, ec * 128, 128, mT, 128, ps)
                copy(KTp[:, ec, nb * 128:(nb + 1) * 128], ps[:, :128], [ps], [KTp])
            for half in range(2):
                ps = nps(); lin_tm(ps[:, :512], mT, Wk, half * 512, 512, 128, ps)
                copy(kvs[:, :], ps[:, :512], [ps], [kvs])
                fw.dma("pool", o_memk[l, nb * 128:(nb + 1) * 128, half * 512:(half + 1) * 512], kvs[:, :], reads=[kvs], is_output=True)
                ps = nps(); lin_tm(ps[:, :512], mT, Wv, half * 512, 512, 128, ps)
                copy(kvs[:, :], ps[:, :512], [ps], [kvs])
                copy(Vaug[:, nb, 2 * half:2 * half + 2, 0:256], ps[:, :512].rearrange("p (h e) -> p h e", e=256), [ps], [Vaug])
                fw.dma("pool", o_memv[l, nb * 128:(nb + 1) * 128, half * 512:(half + 1) * 512], kvs[:, :], reads=[kvs], is_output=True)
        load_g(g_mem_q[l:l + 1, :])

        def stile(loaders, n, KT, VA, finish):
            ntl = len(loaders)
            Wd = ntl * n
            for i, ld in enumerate(loaders):
                xt = ld()
                norm_tok(xt, n, gbc)
                half_tt = hT_rot[(i * n) // 256]
                for b in range(0, 8, 4):
                    ps = nps()
                    for j in range(4):
                        tr(ps[:, j * 128:j * 128 + n], hn[:n, (b + j) * 128:(b + j + 1) * 128], n, [hn], [ps])
                    copy(hT4[:, b:b + 4, i * n:i * n + n], ps[:, 0:512].rearrange("p (j t) -> p j t", t=128)[:, :, :n], [ps], [half_tt])
            rdh = [hT_rot[0]] + ([hT_rot[1]] if Wd > 256 else [])
            for ec in range(8):
                ps = nps()
                for kc in range(8):
                    mm(ps[:, :Wd], Wq[kc][:, ec * 128:(ec + 1) * 128], hT4[:, kc, :Wd], kc == 0, kc == 7, [Wq[kc]] + rdh, [ps])
                copy(qT[:, ec, :Wd], ps[:, :Wd], [ps], [qT])
            for i, ld in enumerate(loaders):
                xt = ld()
                tile(xt, n, KT, VA, i * n)
                finish(i, xt)

        def tile(xt, n, KT, VA, o):
            for h in range(4):
                ps = nps()
                for nb in range(2):
                    for ec in range(2):
                        mm(ps[:, nb * 128:nb * 128 + n], KT[:, 2 * h + ec, nb * 128:(nb + 1) * 128], qT[:, 2 * h + ec, o:o + n], ec == 0, ec == 1, [KT, qT], [ps])
                act(pT[:, :, :n], ps[:, 0:256].rearrange("p (b t) -> p b t", t=128)[:, :, :n], AF.Exp, [ps], [pT], scale=1.0 / 16.0)
                ps2 = nps()
                for nb in range(2):
                    mm(ps2[:n, 0:257], pT[:, nb, :n], VA[:, nb, h, :], nb == 0, nb == 1, [pT, VA], [ps2])
                fw.op("dve", lambda e, ps2=ps2: e.reciprocal(out=rd[:n, :], in_=ps2[:n, 256:257]), [ps2], [rd])
                copy(otok[:n, h * 256:(h + 1) * 256], ps2[:n, 0:256], [ps2, rd], [otok], scale=rd[:n, 0:1])
            to_fm(otok, n, oT)
            for half in range(2):
                ps = nps(); lin_tm(ps[:n, :512], oT, Wo, half * 512, 512, n, ps)
                add_res(xt, n, half, ps)

        for t0 in range(0, NT, 4):
            stile([(lambda t=t0 + i: load_x(t, False)) for i in range(4)], 128, KTp, Vaug, lambda i, xt, t0=t0: store_x(t0 + i, xt))
        for s in range(NS):
            for nb in range(2):
                fw.dma("sp", kraw[:, :], cmk[l, s, nb * 128:(nb + 1) * 128, :], writes=[kraw])
                for b in range(2):
                    ps = nps()
                    for j in range(4):
                        tr(ps[:, j * 128:(j + 1) * 128], kraw[:, (4 * b + j) * 128:(4 * b + j + 1) * 128], 128, [kraw], [ps])
                    copy(KTs[:, 4 * b:4 * b + 4, nb * 128:(nb + 1) * 128], ps[:, :].rearrange("p (j t) -> p j t", t=128), [ps], [KTs])
                fw.dma("pool", Vaugs[:, nb, :, 0:256], cmv[l, s, nb * 128:(nb + 1) * 128, :].rearrange("n (h e) -> n h e", e=256), writes=[Vaugs])
            stile([lambda s=s: load_xs(s)], 8, KTs, Vaugs, lambda i, xt, s=s: store_xs(s, xt))

    def phase_C(l, final):
        off = 0
        sm_reset()
        Wup, off = wload("wup", w_up[l], off, 1024, 5632)
        Wdn, off = wload("wdn", w_down[l], off, 2816, 1024)
        load_g(g_ffn[l:l + 1, :])
        cwf = sm("cwf", [128, 44, 3]); cols_from_rows(lambda c: cwf[:, c, :], cwf, ffn_cw[l], 3, 44)
        hist = sm("fhist", [128, 44, 2])
        actT = sm("actT", [128, 22, 512], BF16)
        extA = [sm("extA%d" % i, [128, 514]) for i in range(2)]
        extB = [sm("extB%d" % i, [128, 514]) for i in range(2)]
        tA = sm("ftA", [128, 512]); tB = sm("ftB", [128, 512])
        hTs = [hT_rot[0], hT_rot[1]]

        def stile(loaders, n, finish):
            ntl = len(loaders)
            Wd = ntl * n
            for i, ld in enumerate(loaders):
                xt = ld()
                norm_tok(xt, n, gbc)
                half_tt = hTs[(i * n) // 256]
                for b in range(0, 8, 4):
                    ps = nps()
                    for j in range(4):
                        tr(ps[:, j * 128:j * 128 + n], hn[:n, (b + j) * 128:(b + j + 1) * 128], n, [hn], [ps])
                    copy(hT4[:, b:b + 4, i * n:i * n + n], ps[:, 0:512].rearrange("p (j t) -> p j t", t=128)[:, :, :n], [ps], [half_tt])
            rd = [hTs[0]] + ([hTs[1]] if Wd > 256 else [])
            for j in range(22):
                ea, eb = extA[j % 2], extB[j % 2]
                for (c, ext, ev) in ((j, ea, "act"), (j + 22, eb, "act")):
                    ps = nps()
                    for kc in range(8):
                        mm(ps[:, :Wd], Wup[kc][:, c * 128:(c + 1) * 128], hT4[:, kc, :Wd], kc == 0, kc == 7, [Wup[kc]] + rd, [ps])
                    copy(ext[:, 0:2], hist[:, c, :], [hist], [ext], eng="pool")
                    copy(ext[:, 2:2 + Wd], ps[:, :Wd], [ps], [ext], eng=ev)
                    copy(hist[:, c, :], ext[:, Wd:Wd + 2], [ext], [hist], eng="pool")
                c2 = j + 22
                fw.op("act", lambda e, ea=ea, j=j: e.mul(out=tA[:, :Wd], in_=ea[:, 0:Wd], mul=cwf[:, j, 0:1]), [ea, cwf], [tA])
                fw.op("act", lambda e, eb=eb, c2=c2: e.mul(out=tB[:, :Wd], in_=eb[:, 0:Wd], mul=cwf[:, c2, 0:1]), [eb, cwf], [tB])
                for k in (1, 2):
                    stt("dve", tA[:, :Wd], ea[:, k:k + Wd], cwf[:, j, k:k + 1], tA[:, :Wd], ALU.mult, ALU.add, [ea, cwf, tA], [tA])
                    stt("dve", tB[:, :Wd], eb[:, k:k + Wd], cwf[:, c2, k:k + 1], tB[:, :Wd], ALU.mult, ALU.add, [eb, cwf, tB], [tB])
                act(tA[:, :Wd], tA[:, :Wd], AF.Silu, [tA], [tA])
                tt_op("dve", actT[:, j, :Wd], tA[:, :Wd], tB[:, :Wd], ALU.mult, [tA, tB], [actT])
            for i, ld in enumerate(loaders):
                xt = ld()
                for half in range(2):
                    ps = nps()
                    for j in range(22):
                        mm(ps[:n, :512], actT[:, j, i * n:i * n + n], Wdn[j][:, half * 512:(half + 1) * 512], j == 0, j == 21, [actT, Wdn[j]], [ps])
                    add_res(xt, n, half, ps)
                finish(i, xt)

        fw.op("pool", lambda e: e.memset(hist[:, :, :], 0.0), writes=[hist])
        for t0 in range(0, NT, 4):
            def fin(i, xt, t0=t0):
                t = t0 + i
                if final:
                    norm_tok(xt, 128, gfin)
                    fw.dma("sp", y_p[t * 128:(t + 1) * 128, :], hn[:, :], reads=[hn], writes=[xres[t]], is_output=True)
                else:
                    store_x(t, xt)
            stile([(lambda t=t0 + i: load_x(t, False)) for i in range(4)], 128, fin)
        rows_from_cols(o_ffn_p[l], lambda c: hist[:, c, :], hist, 2, 44)
        for s_ in range(NS):
            cols_from_rows(lambda c: hist[:, c, :], hist, st_ffn[l, s_], 2, 44)

            def fin_s(i, xt, s_=s_):
                if final:
                    norm_tok(xt, 8, gfin)
                    fw.dma("sp", y_s[s_ * 8:(s_ + 1) * 8, :], hn[:8, :], reads=[hn], writes=[xsres[s_]], is_output=True)
                else:
                    store_xs(s_, xt)
            stile([lambda s_=s_: load_xs(s_)], 8, fin_s)
            rows_from_cols(o_ffn_s[l, s_], lambda c: hist[:, c, :], hist, 2, 44)

    def phase_A1():
        off = 0
        sm_reset()
        Wqkv, off = wload("wqkv", w_qkv, off, 1024, 4608)
        Woc, off = wload("woc", w_out_c, off, 512, 1024)
        load_g(g_mix[1:2, :])
        qk = sm("qk_tok", [128, 512]); vt = sm("v_tok", [128, 512]); vbf = sm("v_bf", [128, 512], BF16)
        cs = sm("cs_t", [128, 8]); sn = sm("sn_t", [128, 8])
        r1 = sm("rp1", [128, 8, 8]); r2 = sm("rp2", [128, 8, 8]); r3 = sm("rp3", [128, 8, 8])
        fmT = sm("fmT", [128, 4, 128], BF16)
        mb = sm("mb", [128, 2, 128], BF16); fw.dma("pool", mb[:], c_mb.rearrange("a p n -> p a n"), writes=[mb])
        smk = sm("smk", [128, 24, 8], BF16); fw.dma("pool", smk[:], c_sm.rearrange("p (b t) -> p b t", t=8), writes=[smk])
        ident_bf = sm("ident_bf", [128, 128], BF16); copy(ident_bf[:], ident[:], [ident], [ident_bf], eng="dve")
        sQT = [[sm("sQT%d_%d" % (s, g), [128, 4, 8], BF16) for g in range(3)] for s in range(NS)]
        sKT = [[sm("sKT%d_%d" % (s, g), [128, 4, 8], BF16) for g in range(3)] for s in range(NS)]
        sV_d = nc.dram_tensor("sV_d", [NS, 3, 8, 512], BF16).ap()
        sV_tt = TT(None, "sV_tt")
        sVn = sm("sVn", [8, 512], BF16)

        def rope(n, pos0):
            X = qk[:n, :].rearrange("p (h e) -> p h e", e=64)
            x1 = X[:, :, 0:8]; x2 = X[:, :, 8:16]
            cb = cs[:n, :].unsqueeze(1).to_broadcast([n, 8, 8]); sb_ = sn[:n, :].unsqueeze(1).to_broadcast([n, 8, 8])
            tt_op("dve", r1[:n], x1, sb_, ALU.mult, [qk, sn], [r1])
            tt_op("dve", r2[:n], x2, sb_, ALU.mult, [qk, sn], [r2])
            tt_op("dve", r3[:n], x2, cb, ALU.mult, [qk, cs], [r3])
            tt_op("dve", x1, x1, cb, ALU.mult, [qk, cs], [qk])
            tt_op("dve", x1, x1, r2[:n], ALU.subtract, [qk, r2], [qk])
            tt_op("dve", x2, r3[:n], r1[:n], ALU.add, [r3, r1], [qk])

        def proj_tile(xt, n, pos0, t, s):
            fw.dma("sp", cs[:n, :], c_cos[pos0:pos0 + n, :], writes=[cs])
            fw.dma("sp", sn[:n, :], c_sin[pos0:pos0 + n, :], writes=[sn])
            hT = norm_T(xt, n, gbc)
            for g in range(3):
                W = WINS[g][0]
                for si in range(3):
                    ps = nps(); lin_tm(ps[:n, :512], hT, Wqkv, g * 1536 + si * 512, 512, n, ps)
                    if si < 2:
                        copy(qk[:n, :], ps[:n, :512], [ps], [qk])
                        rope(n, pos0)
                        to_fm(qk, n, fmT, nchunks=4)
                        if s is None:
                            dst = (QT_d if si == 0 else KT_d)[g]
                            fw.dma("pool", dst.rearrange("(c p) s -> p c s", p=128)[:, :, t * 128:(t + 1) * 128], fmT[:, :, :], reads=[fmT], writes=[qkv_tt[t]])
                        else:
                            copy((sQT if si == 0 else sKT)[s][g][:, :, :], fmT[:, :, :8], [fmT], [(sQT if si == 0 else sKT)[s][g]], eng="pool")
                        if si == 1:
                            if s is None:
                                lo = S - min(W, S)
                                if (t + 1) * 128 > lo:
                                    fw.dma("pool", o_win_p[g][t * 128 - lo:(t + 1) * 128 - lo, 0:512], qk[:, :], reads=[qk], is_output=True)
                            else:
                                fw.dma("pool", o_win_s[g][s, :, 0:512], qk[:8, :], reads=[qk], is_output=True)
                    else:
                        copy(vt[:n, :], ps[:n, :512], [ps], [vt])
                        copy(vbf[:n, :], ps[:n, :512], [ps], [vbf])
                        if s is None:
                            fw.dma("pool", V_d[g][t * 128:(t + 1) * 128, :], vbf[:, :], reads=[vbf], writes=[qkv_tt[t]])
                            lo = S - min(W, S)
                            if (t + 1) * 128 > lo:
                                fw.dma("pool", o_win_p[g][t * 128 - lo:(t + 1) * 128 - lo, 512:1024], vt[:, :], reads=[vt], is_output=True)
                        else:
                            fw.dma("sp", sV_d[s, g], vbf[:8, :], reads=[vbf], writes=[sV_tt])
                            fw.dma("pool", o_win_s[g][s, :, 512:1024], vt[:8, :], reads=[vt], is_output=True)

        for t in range(NT):
            xt = load_x(t, False)
            proj_tile(xt, 128, t * 128, t, None)
        for s in range(NS):
            proj_tile(load_xs(s), 8, S, None, s)

        print('A1 proj done at op', fw.total)
        HALF = 2048
        NH = S // HALF
        accN = barrier_tt(TT(arena_f[:, 0:4 * HALF].rearrange("p (c s) -> p c s", s=HALF), "accN"))
        accD = barrier_tt(TT(arena_f[:, 4 * HALF:8 * HALF].rearrange("p (c s) -> p c s", s=HALF), "accD"))
        QTs = barrier_tt(TT(arena.ap[:, 40960:40960 + 8192].rearrange("p (c s) -> p c s", s=2048), "QTs"))
        KTs_ = barrier_tt(TT(arena.ap[:, 49152:49152 + 16384].rearrange("p (c s) -> p c s", s=4096), "KTs"))
        Vst = [sm("Vst%d" % i, [128, 512], BF16) for i in range(2)]
        Vpad = [sm("Vpad%d" % i, [128, 2, 4, 2, 128], BF16) for i in range(2)]
        onespad = sm("onespad", [128, 2, 128], BF16)
        fw.op("pool", lambda e: e.memset(onespad[:], 0.0), writes=[onespad])
        fw.op("pool", lambda e: e.memset(onespad[:, 0, 0:64], 1.0), writes=[onespad])
        fw.op("pool", lambda e: e.memset(onespad[:, 1, 64:128], 1.0), writes=[onespad])
        for i in range(2):
            fw.op("pool", lambda e, i=i: e.memset(Vpad[i][:], 0.0), writes=[Vpad[i]])
        PTR = [sm("PT%d" % i, [128, 2, 128], BF16) for i in range(5)]
        ptc = [0]
        rec = sm("rec", [128, 4, 128]); oTb = sm("oTb", [128, 4, 128], BF16)
        vcnt = [0]
        for hf in range(NH):
            for g in range(3):
                W, d = WINS[g]
                SB = 128 * d
                for sbk in range(HALF // SB):
                    q_sb = hf * (HALF // SB) + sbk
                    base = q_sb * SB
                    lb = base - hf * HALF
                    fw.dma("sp", QTs[:, :, 0:SB], QT_d[g].rearrange("(c p) s -> p c s", p=128)[:, :, base:base + SB], reads=qkv_tt, writes=[QTs])
                    if q_sb > 0:
                        fw.dma("sp", KTs_[:, :, 0:2 * SB], KT_d[g].rearrange("(c p) s -> p c s", p=128)[:, :, base - SB:base + SB], reads=qkv_tt, writes=[KTs_])
                    else:
                        fw.dma("sp", KTs_[:, :, SB:2 * SB], KT_d[g].rearrange("(c p) s -> p c s", p=128)[:, :, base:base + SB], reads=qkv_tt, writes=[KTs_])
                    kbs = (0, 1) if q_sb > 0 else (1,)
                    for r in range(d):
                        vp = Vpad[vcnt[0] % 2]; vcnt[0] += 1
                        for kb in kbs:
                            p0 = base - SB + r if kb == 0 else base + r
                            vs = Vst[kb]
                            fw.dma("sp", vs[:, :], V_d[g][p0:p0 + 127 * d + 1:d, :], reads=qkv_tt, writes=[vs])
                            for hh in range(2):
                                copy(vp[:, kb, :, hh, hh * 64:(hh + 1) * 64], vs[:, :].rearrange("p (c h e) -> p c h e", h=2, e=64)[:, :, hh, :], [vs], [vp], eng="pool")
                        def a_scores(c):
                            PT = [PTR[(ptc[0] + i_) % 5] for i_ in range(2)]
                            ptc[0] += 2
                            for hh in range(2):
                                ps = nps()
                                pl = slice(hh * 64, (hh + 1) * 64)
                                for kb in kbs:
                                    ksl = slice(r, SB, d) if kb == 0 else slice(SB + r, 2 * SB, d)
                                    mm(ps[:, kb * 128:(kb + 1) * 128], KTs_[pl, c, ksl], QTs[pl, c, r:SB:d], True, False, [KTs_, QTs], [ps])
                                    mm(ps[:, kb * 128:(kb + 1) * 128], ident_bf[:, :], mb[:, kb, :], False, True, [ident_bf, mb], [ps])
                                k0 = kbs[0]
                                act(PT[hh][:, k0:2, :], ps[:, k0 * 128:256].rearrange("p (b t) -> p b t", t=128), AF.Exp, [ps], [PT[hh]], scale=0.125)
                            return PT

                        def a_pv(c, PT):
                            psN = nps(); psD = nps()
                            seq = [(hh, kb) for hh in range(2) for kb in kbs]
                            for i, (hh, kb) in enumerate(seq):
                                mm(psN[:, :128], vp[:, kb, c, hh, :], PT[hh][:, kb, :], i == 0, i == len(seq) - 1, [vp, PT[hh]], [psN])
                            for i, (hh, kb) in enumerate(seq):
                                mm(psD[:, :128], onespad[:, hh, :], PT[hh][:, kb, :], i == 0, i == len(seq) - 1, [onespad, PT[hh]], [psD])
                            dN = accN[:, c, lb + r:lb + SB:d]; dD = accD[:, c, lb + r:lb + SB:d]
                            if g == 0:
                                copy(dN, psN[:, :128], [psN], [accN], eng="act")
                                copy(dD, psD[:, :128], [psD], [accD], eng="dve")
                            else:
                                tt_op("dve", dN, dN, psN[:, :128], ALU.add, [accN, psN], [accN])
                                tt_op("dve", dD, dD, psD[:, :128], ALU.add, [accD, psD], [accD])

                        pts = {0: a_scores(0)}
                        for c in range(4):
                            if c + 1 < 4:
                                pts[c + 1] = a_scores(c + 1)
                            a_pv(c, pts[c])
            for tl in range(HALF // 128):
                t = hf * (HALF // 128) + tl
                fw.op("dve", lambda e, tl=tl: e.reciprocal(out=rec[:, :, :], in_=accD[:, :, tl * 128:(tl + 1) * 128]), [accD], [rec])
                tt_op("dve", oTb[:, :, :], accN[:, :, tl * 128:(tl + 1) * 128], rec[:, :, :], ALU.mult, [accN, rec], [oTb])
                xt = load_x(t, False)
                for half in range(2):
                    ps = nps()
                    for c in range(4):
                        mm(ps[:, :512], oTb[:, c, :], Woc[c][:, half * 512:(half + 1) * 512], c == 0, c == 3, [oTb, Woc[c]], [ps])
                    add_res(xt, 128, half, ps)
                store_x(t, xt)
        print('A1 prompt attn done at op', fw.total)
        craw = sm("craw", [128, 1024]); KTb = sm("KTb", [128, 4, 128], BF16)
        sVp = [sm("sVp%d" % i, [128, 4, 2, 128], BF16) for i in range(2)]
        for i in range(2):
            fw.op("pool", lambda e, i=i: e.memset(sVp[i][:], 0.0), writes=[sVp[i]])
        PTs = sm("PTs", [128, 8, 8], BF16)
        aN = sm("aNs", [128, 32]); aD = sm("aDs", [128, 32]); oTs = sm("oTs", [128, 32], BF16)
        bases = [0, 1, 5]
        bcnt = [0]
        for s in range(NS):
            xs_t = load_xs(s)
            first = True
            for g in range(3):
                W, d = WINS[g]
                nblk = W // 128
                for kb in range(nblk + 1):
                    newblk = (kb == nblk)
                    nk = 8 if newblk else 128
                    vp = sVp[bcnt[0] % 2]; bcnt[0] += 1
                    if not newblk:
                        fw.dma("sp", craw[:, :], cwin[g][s, kb * 128:(kb + 1) * 128, :], writes=[craw])
                        ps = nps()
                        for j in range(4):
                            tr(ps[:, j * 128:(j + 1) * 128], craw[:, j * 128:(j + 1) * 128], 128, [craw], [ps])
                        copy(KTb[:, :, :], ps[:, :].rearrange("p (j t) -> p j t", t=128), [ps], [KTb])
                        for hh in range(2):
                            copy(vp[:, :, hh, hh * 64:(hh + 1) * 64], craw[:, 512:1024].rearrange("p (c h e) -> p c h e", h=2, e=64)[:, :, hh, :], [craw], [vp], eng="pool")
                        ktt, kap = KTb, KTb
                        bi = bases[g] + kb
                    else:
                        fw.dma("sp", sVn[:8, :], sV_d[s, g], reads=[sV_tt], writes=[sVn])
                        for hh in range(2):
                            copy(vp[:8, :, hh, hh * 64:(hh + 1) * 64], sVn[:8, :].rearrange("p (c h e) -> p c h e", h=2, e=64)[:, :, hh, :], [sVn], [vp], eng="pool")
                        ktt, kap = sKT[s][g], sKT[s][g]
                        bi = 21 + g
                        k2 = sm("k2n", [64, 2, 4, 8], BF16); q2 = sm("q2n", [64, 2, 4, 8], BF16)
                        for hh in range(2):
                            copy(k2[:, hh, :, :], sKT[s][g][hh * 64:(hh + 1) * 64, :, :], [sKT[s][g]], [k2], eng="pool")
                            copy(q2[:, hh, :, :], sQT[s][g][hh * 64:(hh + 1) * 64, :, :], [sQT[s][g]], [q2], eng="pool")
                    ps = nps()
                    for c in range(4):
                        for hh in range(2):
                            pl = slice(hh * 64, (hh + 1) * 64)
                            hd = 2 * c + hh
                            if newblk:
                                mm(ps[:nk, hd * 8:(hd + 1) * 8], k2[:, hh, c, :nk], q2[:, hh, c, :8], True, True, [k2, q2], [ps])
                            else:
                                mm(ps[:nk, hd * 8:(hd + 1) * 8], kap[pl, c, :nk], sQT[s][g][pl, c, :8], True, True, [ktt, sQT[s][g]], [ps])
                    act(PTs[:nk, :, :], ps[:nk, 0:64].rearrange("p (h t) -> p h t", t=8), AF.Exp, [ps], [PTs], scale=0.125)
                    tt_op("dve", PTs[:nk, :, :], PTs[:nk, :, :], smk[:nk, bi, :].unsqueeze(1).to_broadcast([nk, 8, 8]), ALU.mult, [PTs, smk], [PTs])
                    psN = nps(); psD = nps()
                    for c in range(4):
                        for hh in range(2):
                            hd = 2 * c + hh
                            mm(psN[:, c * 8:(c + 1) * 8], vp[:nk, c, hh, :], PTs[:nk, hd, :], hh == 0, hh == 1, [vp, PTs], [psN])
                    for c in range(4):
                        for hh in range(2):
                            hd = 2 * c + hh
                            mm(psD[:, c * 8:(c + 1) * 8], onespad[:nk, hh, :], PTs[:nk, hd, :], hh == 0, hh == 1, [onespad, PTs], [psD])
                    if first:
                        copy(aN[:, :], psN[:, :32], [psN], [aN], eng="act")
                        copy(aD[:, :], psD[:, :32], [psD], [aD], eng="dve")
                        first = False
                    else:
                        tt_op("dve", aN[:, :], aN[:, :], psN[:, :32], ALU.add, [aN, psN], [aN])
                        tt_op("dve", aD[:, :], aD[:, :], psD[:, :32], ALU.add, [aD, psD], [aD])
            fw.op("dve", lambda e: e.reciprocal(out=aD[:, :], in_=aD[:, :]), [aD], [aD])
            tt_op("dve", oTs[:, :], aN[:, :], aD[:, :], ALU.mult, [aN, aD], [oTs])
            for half in range(2):
                ps = nps()
                for c in range(4):
                    mm(ps[:8, :512], oTs[:, c * 8:(c + 1) * 8], Woc[c][:, half * 512:(half + 1) * 512], c == 0, c == 3, [oTs, Woc[c]], [ps])
                add_res(xs_t, 8, half, ps)
            store_xs(s, xs_t)

    if 'A0' in phases: phase_A0()
    if 'B0' in phases: phase_B(0)
    if 'C0' in phases: phase_C(0, False)
    if 'A1' in phases: phase_A1()
    if 'B1' in phases: phase_B(1)
    if 'C1' in phases: phase_C(1, True)
    fw.finish()
    return nc, fw


def _consts(S):
    c = {}
    r = np.arange(128)[:, None]; cc = np.arange(128)[None, :]
    c["c_ident"] = np.eye(128, dtype=np.float32)
    c["c_mge"] = (BIG * (cc >= r)).astype(np.float32)
    c["c_nmle"] = (-BIG * (cc <= r)).astype(np.float32)
    c["c_nmlt"] = (-BIG * (cc < r)).astype(np.float32)
    c["c_LT"] = (r <= cc).astype(np.float32)
    invc = np.zeros((2, 128, 512), np.float32)
    for g, w in enumerate((2, 4, 8, 16)):
        pos = np.arange(128)
        invc[0, :, g * 128:(g + 1) * 128] = (1.0 / np.minimum(pos + 1, w))[None, :]
        invc[1, :, g * 128:(g + 1) * 128] = 1.0 / w
    c["c_invc"] = invc
    half = 8
    inv_freq = (np.float32(500000.0) ** (-np.arange(half, dtype=np.float32) / np.float32(half))).astype(np.float32)
    pos = np.concatenate([np.arange(S), 8192 + np.arange(8)]).astype(np.float32)
    ang = (pos[:, None] * inv_freq[None, :]).astype(np.float32)
    c["c_cos"] = np.cos(ang.astype(np.float64)).astype(np.float32)
    c["c_sin"] = np.sin(ang.astype(np.float64)).astype(np.float32)
    NEG = -240000.0
    j = np.arange(128)[:, None]; i = np.arange(128)[None, :]
    mb = np.zeros((2, 128, 128), np.float32)
    mb[0] = np.where(j >= i, 0.0, NEG)
    mb[1] = np.where(j <= i, 0.0, NEG)
    c["c_mb"] = mb
    smk = np.zeros((128, 24, 8), np.float32)
    bases = [0, 1, 5]
    for g, (W, d) in enumerate(WINS):
        for kb in range(W // 128):
            e = kb * 128 + np.arange(128)[:, None]
            t = np.arange(8)[None, :]
            diff = W + t - e
            smk[:, bases[g] + kb, :] = ((diff % d == 0) & (diff // d >= 1) & (diff // d <= 128)).astype(np.float32)
        p = np.arange(128)[:, None]; t = np.arange(8)[None, :]
        smk[:, 21 + g, :] = ((p <= t) & ((t - p) % d == 0) & (p < 8)).astype(np.float32)
    c["c_sm"] = smk.reshape(128, 24 * 8)
    return c


_CACHE = {}


def _core_inputs(inp, c, S, NS, consts):
    b = c % 4
    ss = slice(NS * c, NS * (c + 1))
    f = lambda a: np.ascontiguousarray(a, dtype=np.float32)
    m = {
        "xp": f(inp["x_prompt"][b]), "xs": f(inp["x_sample"][ss].reshape(NS * 8, 1024)), "memp": f(inp["mem_prompt"][b]),
        "st_pool": f(inp["state_pool"][0, ss]), "st_dconv": f(inp["state_dn_conv"][0, ss]), "st_dn": f(inp["state_dn"][0, ss]),
        "cw128": f(inp["cache_win_w128"][0, ss].reshape(NS, 128, 1024)),
        "cw512": f(inp["cache_win_w512"][0, ss].reshape(NS, 512, 1024)),
        "cw2048": f(inp["cache_win_w2048"][0, ss].reshape(NS, 2048, 1024)),
        "cmk": f(inp["cache_mem_k"][:, ss].reshape(2, NS, 256, 1024)), "cmv": f(inp["cache_mem_v"][:, ss].reshape(2, NS, 256, 1024)),
        "st_ffn": f(inp["state_ffn_conv"][:, ss]),
        "g_mix": f(inp["g_mix"]), "w_in": f(inp["w_in_ab"][0]), "w_pool": f(inp["w_pool"][0].reshape(512, 128)),
        "pool_scale": f(inp["pool_scale"]), "dn_conv_w": f(inp["dn_conv_w"][0]), "dn_a_log": f(inp["dn_a_log"]),
        "dn_dt_bias": f(inp["dn_dt_bias"]), "dn_norm_w": f(inp["dn_norm_w"]), "w_out_ab": f(inp["w_out_ab"][0]),
        "w_qkv": f(inp["w_qkv_c"][0]), "w_out_c": f(inp["w_out_c"][0]), "g_mem_q": f(inp["g_mem_q"]), "g_mem_kv": f(inp["g_mem_kv"]),
        "w_mem_q": f(inp["w_mem_q"]), "w_mem_k": f(inp["w_mem_k"]), "w_mem_v": f(inp["w_mem_v"]), "w_mem_o": f(inp["w_mem_o"]),
        "g_ffn": f(inp["g_ffn"]), "w_up": f(inp["w_up"]), "ffn_cw": f(inp["ffn_conv_w"]), "w_down": f(inp["w_down"]),
        "g_final": f(inp["g_final"].reshape(1, 1024)),
    }
    m.update(consts)
    return m


def kernel(**inp):
    S = inp["x_prompt"].shape[1]
    NS = 4
    if S not in _CACHE:
        _CACHE[S] = build(S, NS)[0]
    nc = _CACHE[S]
    consts = _consts(S)
    in_maps = [_core_inputs(inp, c, S, NS, consts) for c in range(8)]
    res = run_bass_kernel_spmd(nc, in_maps, core_ids=list(range(8)))
    R = res.results
    f = np.float32
    B = 4
    cat_p = lambda k: np.stack([R[b][k] for b in range(B)])
    cat_s = lambda k: np.concatenate([R[c][k] for c in range(8)], axis=0)
    y_p = cat_p("y_p")
    y_s = cat_s("y_s").reshape(32, 8, 1024)
    outs = [y_p, y_s,
            cat_p("o_pool_p")[None], cat_s("o_pool_s")[None],
            cat_p("o_dconv_p")[None], cat_s("o_dconv_s")[None],
            cat_p("o_dn_p")[None], cat_s("o_dn_s")[None]]
    for w, _ in WINS:
        wp = min(w, S)
        outs.append(cat_p("o_w%d_p" % w).reshape(1, B, wp, 2, 8, 64))
        outs.append(cat_s("o_w%d_s" % w).reshape(1, 32, 8, 2, 8, 64))
    outs.append(np.stack([R[b]["o_memk"] for b in range(B)], axis=1).reshape(2, B, 256, 4, 256))
    outs.append(np.stack([R[b]["o_memv"] for b in range(B)], axis=1).reshape(2, B, 256, 4, 256))
    outs.append(np.stack([R[b]["o_ffn_p"] for b in range(B)], axis=1))
    outs.append(np.concatenate([R[c]["o_ffn_s"] for c in range(8)], axis=1))
    return tuple(np.ascontiguousarray(o, dtype=f) for o in outs)
```

```python
import numpy as np
import concourse.bass as bass
import concourse.mybir as mybir

F32 = mybir.dt.float32
BF16 = mybir.dt.bfloat16
AF = mybir.ActivationFunctionType
ALU = mybir.AluOpType
AX = mybir.AxisListType


class TT:
    __slots__ = ("ap", "w", "r", "name", "psum")

    def __init__(self, ap, name=""):
        self.ap = ap
        self.psum = False
        self.w = None
        self.r = {}
        self.name = name

    def __getitem__(self, idx):
        return self.ap[idx]


class FW:
    ENG = ("pe", "act", "dve", "pool", "sp")

    def __init__(self, nc, ndma=6):
        self.nc = nc
        self.prog = {e: [] for e in self.ENG}
        self.cnt = {e: 0 for e in self.ENG if e != "sp"}
        self.seen = {e: {} for e in self.ENG}
        self.sems = {}
        self.ndma = ndma
        self.dma_rr = {"sp": 0, "pool": 0, "act": 0}
        self.dma_tot = {}
        self.ctx = []
        self.out_tokens = []
        import os
        self.maxops = int(os.environ.get("FW_MAXOPS", "100000000"))
        self.total = 0
        self.log = []

    def sem(self, key):
        if key not in self.sems:
            cm = self.nc.semaphore("s_%s" % str(key).replace(" ", ""))
            self.sems[key] = cm.__enter__()
            self.ctx.append(cm)
        return self.sems[key]

    def sb(self, name, shape, dt=F32):
        cm = self.nc.sbuf_tensor(name, list(shape), dt)
        t = cm.__enter__()
        self.ctx.append(cm)
        return TT(t, name)

    def ps(self, name, shape, dt=F32):
        cm = self.nc.psum_tensor(name, list(shape), dt)
        t = cm.__enter__()
        self.ctx.append(cm)
        tt = TT(t, name)
        tt.psum = True
        return tt

    def _waits(self, eng, reads, writes, skip_self_pe=False):
        need = {}
        for t in reads:
            if t.w is not None:
                k, v = t.w
                need[k] = max(need.get(k, 0), v)
            if t.psum:
                for k, v in t.r.items():
                    if k != eng:
                        need[k] = max(need.get(k, 0), v)
        for t in writes:
            if t.w is not None:
                k, v = t.w
                need[k] = max(need.get(k, 0), v)
            for k, v in t.r.items():
                need[k] = max(need.get(k, 0), v)
        out = []
        seen = self.seen[eng]
        for k, v in need.items():
            if eng == "pe" and k == "pe":
                continue
            if seen.get(k, 0) >= v:
                continue
            seen[k] = v
            out.append((k, v))
        return out

    def op(self, eng, fn, reads=(), writes=()):
        self.total += 1
        if self.total > self.maxops:
            return None
        reads = [t for t in reads if isinstance(t, TT)]
        writes = [t for t in writes if isinstance(t, TT)]
        waits = self._waits(eng, reads, writes)
        self.cnt[eng] += 1
        v = self.cnt[eng]
        self.prog[eng].append((waits, fn, (eng, v, 1)))
        tok = (eng, v)
        for t in writes:
            t.w = tok
            t.r = {}
        for t in reads:
            if t in writes:
                continue
            t.r[eng] = max(t.r.get(eng, 0), v)
        return tok

    def dma(self, q, out_ap, in_ap, reads=(), writes=(), is_output=False, **kw):
        self.total += 1
        if self.total > self.maxops:
            return None
        reads = [t for t in reads if isinstance(t, TT)]
        writes = [t for t in writes if isinstance(t, TT)]
        waits = self._waits(q, reads, writes)
        i = self.dma_rr[q]
        self.dma_rr[q] = (i + 1) % self.ndma
        key = "d%s%d" % (q, i)
        prev = self.dma_tot.get(key, 0)
        if prev > 0 and self.seen[q].get(key, 0) < prev:
            self.seen[q][key] = prev
            waits.append((key, prev))
        v = prev + 16
        self.dma_tot[key] = v
        fn = (lambda e, o=out_ap, a=in_ap, k=kw: e.dma_start(out=o, in_=a, **k))
        self.prog[q].append((waits, fn, (key, v, 16)))
        tok = (key, v)
        for t in writes:
            t.w = tok
            t.r = {}
        for t in reads:
            t.r[key] = max(t.r.get(key, 0), v)
        if is_output:
            self.out_tokens.append(tok)
        return tok

    def finish(self):
        nc = self.nc
        fin = {}
        for k, v in self.out_tokens:
            fin[k] = max(fin.get(k, 0), v)
        for k, v in self.dma_tot.items():
            fin[k] = max(fin.get(k, 0), v)
        for e, c in self.cnt.items():
            if c > 0:
                fin[e] = c
        for k in list(fin.keys()):
            self.sem(k)
        for e in self.ENG:
            for waits, fn, (k, v, inc) in self.prog[e]:
                self.sem(k)
                for (wk, wv) in waits:
                    self.sem(wk)
        sems = self.sems
        prog = self.prog

        def run(e, h, final=False):
            for waits, fn, (k, v, inc) in prog[e]:
                for (wk, wv) in waits:
                    h.wait_ge(sems[wk], wv)
                fn(h).then_inc(sems[k], inc)
            if final:
                for k, v in fin.items():
                    h.wait_ge(sems[k], v)

        with nc.Block() as block:
            @block.sync
            def _(h):
                run("sp", h, final=True)

            @block.tensor
            def _(h):
                run("pe", h)

            @block.scalar
            def _(h):
                run("act", h)

            @block.vector
            def _(h):
                run("dve", h)

            @block.gpsimd
            def _(h):
                run("pool", h)
        for cm in reversed(self.ctx):
            cm.__exit__(None, None, None)

from concourse.bass_utils import run_bass_kernel_spmd

BIG = 30000.0
EPS = 1e-6
WINS = ((128, 1), (512, 4), (2048, 16))


def build(S=4096, NS=4, phases=('A0', 'B0', 'C0', 'A1', 'B1', 'C1')):
    nc = bass.Bass("TRN2", target_bir_lowering=False)
    fw = FW(nc)
    NT = S // 128
    I_, O_ = {}, {}

    def din(name, shape):
        I_[name] = nc.dram_tensor(name, list(shape), F32, kind="ExternalInput").ap()
        return I_[name]

    def dout(name, shape):
        O_[name] = nc.dram_tensor(name, list(shape), F32, kind="ExternalOutput").ap()
        return O_[name]

    xp = din("xp", [S, 1024]); xs_d = din("xs", [NS * 8, 1024]); memp = din("memp", [256, 1024])
    st_pool = din("st_pool", [NS, 15, 512]); st_dconv = din("st_dconv", [NS, 3, 1536])
    st_dn = din("st_dn", [NS, 4, 128, 128])
    cwin = [din("cw%d" % w, [NS, w, 1024]) for w, _ in WINS]
    cmk = din("cmk", [2, NS, 256, 1024]); cmv = din("cmv", [2, NS, 256, 1024])
    st_ffn = din("st_ffn", [2, NS, 2, 5632])
    g_mix = din("g_mix", [2, 1024]); w_in = din("w_in", [1024, 2568]); w_pool = din("w_pool", [512, 128])
    pool_scale = din("pool_scale", [1, 512]); dn_conv_w = din("dn_conv_w", [4, 1536])
    dn_a_log = din("dn_a_log", [1, 4]); dn_dt_bias = din("dn_dt_bias", [1, 4]); dn_norm_w = din("dn_norm_w", [1, 128])
    w_out_ab = din("w_out_ab", [1024, 1024]); w_qkv = din("w_qkv", [1024, 4608]); w_out_c = din("w_out_c", [512, 1024])
    g_mem_q = din("g_mem_q", [2, 1024]); g_mem_kv = din("g_mem_kv", [2, 1024])
    w_mem = {k: din("w_mem_" + k, [2, 1024, 1024]) for k in "qkvo"}
    g_ffn = din("g_ffn", [2, 1024]); w_up = din("w_up", [2, 1024, 5632]); ffn_cw = din("ffn_cw", [2, 3, 5632])
    w_down = din("w_down", [2, 2816, 1024]); g_final = din("g_final", [1, 1024])
    c_ident = din("c_ident", [128, 128]); c_mge = din("c_mge", [128, 128]); c_nmle = din("c_nmle", [128, 128])
    c_nmlt = din("c_nmlt", [128, 128]); c_LT = din("c_LT", [128, 128])
    c_invc = din("c_invc", [2, 128, 512])
    c_cos = din("c_cos", [S + 8, 8]); c_sin = din("c_sin", [S + 8, 8])
    c_mb = din("c_mb", [2, 128, 128]); c_sm = din("c_sm", [128, 24 * 8])

    y_p = dout("y_p", [S, 1024]); y_s = dout("y_s", [NS * 8, 1024])
    o_pool_p = dout("o_pool_p", [15, 512]); o_pool_s = dout("o_pool_s", [NS, 15, 512])
    o_dconv_p = dout("o_dconv_p", [3, 1536]); o_dconv_s = dout("o_dconv_s", [NS, 3, 1536])
    o_dn_p = dout("o_dn_p", [4, 128, 128]); o_dn_s = dout("o_dn_s", [NS, 4, 128, 128])
    o_win_p = [dout("o_w%d_p" % w, [min(w, S), 1024]) for w, _ in WINS]
    o_win_s = [dout("o_w%d_s" % w, [NS, 8, 1024]) for w, _ in WINS]
    o_memk = dout("o_memk", [2, 256, 1024]); o_memv = dout("o_memv", [2, 256, 1024])
    o_ffn_p = dout("o_ffn_p", [2, 2, 5632]); o_ffn_s = dout("o_ffn_s", [2, NS, 2, 5632])

    QT_d = [nc.dram_tensor("QT%d" % g, [512, S], BF16).ap() for g in range(3)]
    KT_d = [nc.dram_tensor("KT%d" % g, [512, S], BF16).ap() for g in range(3)]
    V_d = [nc.dram_tensor("Vd%d" % g, [S, 512], BF16).ap() for g in range(3)]
    qkv_tt = [TT(None, "qkvscr%d" % t) for t in range(NT)]
    xres = [TT(None, "xres%d" % t) for t in range(NT)]

    rr = {"ps": 0, "ev": 0, "q": 0}
    PS = [fw.ps("ps%d" % i, [128, 512]) for i in range(8)]

    def nps():
        p = PS[rr["ps"] % 8]
        rr["ps"] += 1
        return p

    def ev_eng():
        rr["ev"] += 1
        return "act" if rr["ev"] % 2 else "dve"

    def copy(dst, src, reads, writes, eng=None, scale=None):
        eng = eng or ev_eng()
        if scale is not None and not isinstance(scale, float) and eng == "act":
            eng = "dve"
        if eng == "act":
            if scale is None:
                fw.op("act", lambda e: e.copy(out=dst, in_=src), reads, writes)
            else:
                fw.op("act", lambda e: e.mul(out=dst, in_=src, mul=scale), reads, writes)
        else:
            if scale is None:
                fw.op(eng, lambda e: e.tensor_copy(out=dst, in_=src), reads, writes)
            else:
                fw.op(eng, lambda e: e.tensor_scalar(out=dst, in0=src, scalar1=scale, scalar2=None, op0=ALU.mult), reads, writes)

    def tt_op(eng, dst, a, b, op, reads, writes):
        fw.op(eng, lambda e: e.tensor_tensor(out=dst, in0=a, in1=b, op=op), reads, writes)

    def stt(eng, dst, a, sc, b, op0, op1, reads, writes):
        if eng == "pool":
            tmp = sm("ptmp", [128, 128])
            nparts, nfree = a.shape[0], a.shape[1]
            fw.op("pool", lambda e: e.tensor_scalar(out=tmp[:nparts, :nfree], in0=a, scalar1=sc, scalar2=None, op0=op0), reads, [tmp])
            fw.op("pool", lambda e: e.tensor_tensor(out=dst, in0=tmp[:nparts, :nfree], in1=b, op=op1), list(reads) + [tmp], writes)
            return
        fw.op(eng, lambda e: e.scalar_tensor_tensor(out=dst, in0=a, scalar=sc, in1=b, op0=op0, op1=op1), reads, writes)

    def act(dst, src, func, reads, writes, bias=None, scale=None):
        kw = {}
        if bias is not None:
            kw["bias"] = bias
        if scale is not None:
            kw["scale"] = scale
        fw.op("act", lambda e: e.activation(out=dst, in_=src, func=func, **kw), reads, writes)

    def mm(ps_ap, lhsT, rhs, start, stop, reads, writes):
        fw.op("pe", lambda e: e.matmul(ps_ap, lhsT=lhsT, rhs=rhs, start=start, stop=stop), reads, writes)

    def tr(ps_ap, in_ap, k, reads, writes):
        fw.op("pe", lambda e: e.transpose(ps_ap, in_ap, ident[:k, :k]), list(reads) + [ident], writes)

    ident = fw.sb("ident", [128, 128]); fw.dma("sp", ident[:], c_ident[:], writes=[ident])
    m_ge = fw.sb("m_ge", [128, 128]); fw.dma("sp", m_ge[:], c_mge[:], writes=[m_ge])
    nm_le = fw.sb("nm_le", [128, 128]); fw.dma("sp", nm_le[:], c_nmle[:], writes=[nm_le])
    nm_lt = fw.sb("nm_lt", [128, 128]); fw.dma("sp", nm_lt[:], c_nmlt[:], writes=[nm_lt])
    LT = fw.sb("LT", [128, 128]); fw.dma("sp", LT[:], c_LT[:], writes=[LT])
    ones = fw.sb("ones", [128, 128]); fw.op("pool", lambda e: e.memset(ones[:], 1.0), writes=[ones])
    gfin = fw.sb("gfin", [128, 1024]); fw.dma("sp", gfin[:], g_final[0:1, :].partition_broadcast(128) if False else g_final.to_broadcast([128, 1024]), writes=[gfin])
    gbc = fw.sb("gbc", [128, 1024])

    def load_g(src_row):
        fw.dma("sp", gbc[:], src_row.to_broadcast([128, 1024]), writes=[gbc])

    xsres = [TT(None, "xsres%d" % s) for s in range(NS)]

    ARENA = 67584
    arena = fw.sb("arena", [128, ARENA], BF16)
    arena_f = arena.ap.bitcast(F32)

    def barrier_tt(tt):
        tt.r = dict(fw.cnt)
        for k, v in fw.dma_tot.items():
            tt.r[k] = v
        return tt

    wl = {"i": 0}

    def wload(name, dram2d, off, K, N):
        KC = K // 128
        tt = barrier_tt(TT(arena.ap[:, off:off + KC * N].rearrange("p (k n) -> p k n", n=N), name))
        chunks = []
        stg = [xt_rot[0], xt_rot[1], hn]
        for kc in range(KC):
            c = TT(tt.ap[:, kc, :], "%s_%d" % (name, kc))
            c.r = dict(tt.r)
            if N < 64:
                fw.dma("pool", c.ap, dram2d[kc * 128:(kc + 1) * 128, :], writes=[c])
            else:
                eng = "act" if kc % 2 else "dve"
                for c0 in range(0, N, 1024):
                    w_ = min(1024, N - c0)
                    st = stg[wl["i"] % 3]; wl["i"] += 1
                    fw.dma("sp", st[:, :w_], dram2d[kc * 128:(kc + 1) * 128, c0:c0 + w_], writes=[st])
                    if eng == "act":
                        fw.op("act", lambda e, st=st, c=c, c0=c0, w_=w_: e.copy(out=c.ap[:, c0:c0 + w_], in_=st[:, :w_]), [st], [c])
                    else:
                        fw.op("dve", lambda e, st=st, c=c, c0=c0, w_=w_: e.tensor_copy(out=c.ap[:, c0:c0 + w_], in_=st[:, :w_]), [st], [c])
            chunks.append(c)
        return chunks, off + KC * N

    xt_rot = [fw.sb("xt%d" % i, [128, 1024]) for i in range(2)]
    hn = fw.sb("hn", [128, 1024])
    hT4 = fw.sb("hT4", [128, 8, 512], BF16)
    hT_rot = [TT(hT4.ap[:, :, 0:256], "hTa"), TT(hT4.ap[:, :, 256:512], "hTb")]
    ssq = [fw.sb("ssq%d" % i, [128, 1]) for i in range(2)]
    cnt = {"xt": 0, "hT": 0, "ss": 0}

    def rstd_of(src_ap, ntok, width, reads):
        ss = ssq[cnt["ss"] % 2]; cnt["ss"] += 1
        act(hn[:ntok, :width], src_ap, AF.Square, reads, [hn])
        fw.op("dve", lambda e: e.reduce_sum(out=ss[:ntok, :], in_=hn[:ntok, :width], axis=AX.X), [hn], [ss])
        act(ss[:ntok, :], ss[:ntok, :], AF.Sqrt, [ss], [ss], bias=EPS, scale=1.0 / width)
        fw.op("dve", lambda e: e.reciprocal(out=ss[:ntok, :], in_=ss[:ntok, :]), [ss], [ss])
        return ss

    def norm_tok(xt, ntok, g_tt):
        ss = rstd_of(xt[:ntok, :], ntok, 1024, [xt])
        stt("dve", hn[:ntok, :], xt[:ntok, :], ss[:ntok, 0:1], g_tt[:ntok, :], ALU.mult, ALU.mult, [xt, ss, g_tt], [hn])
        return hn

    def to_fm(src, ntok, dst, nchunks=8, c0=0):
        for b in range(0, nchunks, 4):
            nb = min(4, nchunks - b)
            ps = nps()
            for j in range(nb):
                tr(ps[:, j * 128:j * 128 + ntok], src[:ntok, (b + j) * 128:(b + j + 1) * 128], ntok, [src], [ps])
            copy(dst[:, b:b + nb, c0:c0 + ntok], ps[:, 0:nb * 128].rearrange("p (j t) -> p j t", t=128)[:, :, :ntok], [ps], [dst])

    def norm_T(xt, ntok, g_tt):
        norm_tok(xt, ntok, g_tt)
        hT = hT_rot[cnt["hT"] % 2]; cnt["hT"] += 1
        to_fm(hn, ntok, hT)
        return hT

    def lin_fm(ps_ap, W, col0, M, hT, ntok, wr):
        KC = len(W)
        for kc in range(KC):
            mm(ps_ap, W[kc][:, col0:col0 + M], hT[:, kc, :ntok], kc == 0, kc == KC - 1, [W[kc], hT], [wr])

    def lin_tm(ps_ap, hT, W, col0, N, ntok, wr, o=0, rd=None):
        KC = len(W)
        for kc in range(KC):
            mm(ps_ap, hT[:, kc, o:o + ntok], W[kc][:, col0:col0 + N], kc == 0, kc == KC - 1, [W[kc]] + (rd if rd is not None else [hT]), [wr])

    STW = 1024
    stage = fw.sb("stage", [16, STW])
    stage_o = fw.sb("stage_o", [16, STW])

    def cols_from_rows(dst_fn, dst_tts, src_dram, k, C):
        per = STW // 128
        for c0 in range(0, C, per):
            n = min(per, C - c0)
            fw.dma("sp", stage[:k, :n * 128], src_dram[:, c0 * 128:(c0 + n) * 128], writes=[stage])
            ps = nps()
            for j in range(n):
                tr(ps[:, j * 16:j * 16 + k], stage[:k, j * 128:(j + 1) * 128], k, [stage], [ps])
            for j in range(n):
                copy(dst_fn(c0 + j), ps[:, j * 16:j * 16 + k], [ps], [dst_tts[c0 + j] if isinstance(dst_tts, list) else dst_tts])

    def rows_from_cols(dst_dram, src_fn, src_tts, k, C):
        per = STW // 128
        for p0 in range(0, C, per):
            np_ = min(per, C - p0)
            for c0 in range(p0, p0 + np_, 4):
                n = min(4, p0 + np_ - c0)
                ps = nps()
                for j in range(n):
                    st = src_tts[c0 + j] if isinstance(src_tts, list) else src_tts
                    fw.op("pe", lambda e, j=j, c=c0 + j, ps=ps: e.transpose(ps[:k, j * 128:(j + 1) * 128], src_fn(c), ident[:, :]), [st, ident], [ps])
                copy(stage_o[:k, (c0 - p0) * 128:(c0 - p0 + n) * 128], ps[:k, :n * 128], [ps], [stage_o])
            fw.dma("pool", dst_dram[:, p0 * 128:(p0 + np_) * 128], stage_o[:k, :np_ * 128], reads=[stage_o], is_output=True)

    def load_x(t, layer0):
        xt = xt_rot[cnt["xt"] % 2]; cnt["xt"] += 1
        if layer0:
            fw.dma("sp", xt[:, :], xp[t * 128:(t + 1) * 128, :], writes=[xt])
        else:
            fw.dma("sp", xt[:, :], y_p[t * 128:(t + 1) * 128, :], reads=[xres[t]], writes=[xt])
        return xt

    def load_xs(s_, layer0=False):
        xt = xt_rot[cnt["xt"] % 2]; cnt["xt"] += 1
        if layer0:
            fw.dma("sp", xt[:8, :], xs_d[s_ * 8:(s_ + 1) * 8, :], writes=[xt])
        else:
            fw.dma("sp", xt[:8, :], y_s[s_ * 8:(s_ + 1) * 8, :], reads=[xsres[s_]], writes=[xt])
        return xt

    def store_xs(s_, xt):
        fw.dma("pool", y_s[s_ * 8:(s_ + 1) * 8, :], xt[:8, :], reads=[xt], writes=[xsres[s_]], is_output=True)

    def store_x(t, xt):
        fw.dma("pool", y_p[t * 128:(t + 1) * 128, :], xt[:, :], reads=[xt], writes=[xres[t]], is_output=True)

    def add_res(xt, ntok, half, ps):
        tt_op("dve", xt[:ntok, half * 512:(half + 1) * 512], xt[:ntok, half * 512:(half + 1) * 512], ps[:ntok, :512], ALU.add, [xt, ps], [xt])

    SCRB = 37632
    scratch = fw.sb("scratch", [128, SCRB // 2], BF16)
    scr_f = scratch.ap.bitcast(F32)
    small = {}
    scr = {"off": 0}

    def sm_reset():
        print('scratch used', scr['off'])
        small.clear()
        scr["off"] = 0

    def sm(name, shape, dt=F32):
        if name in small:
            return small[name]
        esz = 4 if dt == F32 else 2
        n = 1
        for d_ in shape[1:]:
            n *= d_
        nbytes = (n * esz + 7) // 8 * 8
        off = scr["off"]
        assert off + nbytes <= SCRB, ("scratch overflow", name, off, nbytes)
        scr["off"] = off + nbytes
        base = scr_f if dt == F32 else scratch.ap
        ap = base[:shape[0], off // esz: off // esz + n]
        if len(shape) == 3:
            ap = ap.rearrange("p (a b) -> p a b", b=shape[2])
        elif len(shape) == 4:
            ap = ap.rearrange("p (a b c) -> p a b c", b=shape[2], c=shape[3])
        elif len(shape) == 5:
            ap = ap.rearrange("p (a b c d) -> p a b c d", b=shape[2], c=shape[3], d=shape[4])
        tt = barrier_tt(TT(ap, name))
        small[name] = tt
        return tt

    def phase_A0():
        off = 0
        sm_reset()
        Win, off = wload("w_in", w_in[:, 0:2560], off, 1024, 2560)
        Wab, off = wload("w_ab", w_in[:, 2560:2568], off, 1024, 8)
        Wout, off = wload("w_outab", w_out_ab, off, 1024, 1024)
        Wpool, off = wload("w_pool", w_pool, off, 512, 128)
        load_g(g_mix[0:1, :])
        cw = sm("dn_cw", [128, 12, 4]); cols_from_rows(lambda c: cw[:, c, :], cw, dn_conv_w, 4, 12)
        psc = sm("pool_sc", [128, 4, 1]); cols_from_rows(lambda c: psc[:, c, :], psc, pool_scale, 1, 4)
        dtb = sm("dtb", [128, 4]); fw.dma("sp", dtb[:], dn_dt_bias.to_broadcast([128, 4]), writes=[dtb])
        nea = sm("nea", [128, 4]); fw.dma("sp", nea[:], dn_a_log.to_broadcast([128, 4]), writes=[nea])
        act(nea[:], nea[:], AF.Exp, [nea], [nea])
        copy(nea[:], nea[:], [nea], [nea], eng="dve", scale=-1.0)
        nw = sm("normw", [128, 128]); fw.dma("sp", nw[:], dn_norm_w.to_broadcast([128, 128]), writes=[nw])
        invc = sm("invc", [128, 512]); fw.dma("sp", invc[:], c_invc[0], writes=[invc])
        aext = []; qext = []
        xoff = 22400
        for g in range(4):
            aext.append(barrier_tt(TT(arena_f[:, xoff:xoff + 527], "aext%d" % g))); xoff += 527
        for c in range(12):
            qext.append(barrier_tt(TT(arena_f[:, xoff:xoff + 515], "qext%d" % c))); xoff += 515
        assert xoff <= 33792
        qc = [sm("qc%d" % c, [128, 128]) for c in range(12)]
        S = [sm("S%d" % h, [128, 128]) for h in range(4)]
        mixT = sm("mixT", [128, 8, 128], BF16)
        gate_s = sm("gate_s", [128, 512]); ofin = sm("ofin", [128, 512])
        pa = [sm("pa%d" % i, [128, 143]) for i in range(2)]
        zb = sm("zb", [128, 128], BF16)
        col = {n: sm("col_" + n, [128, 4]) for n in ["beta", "lnb", "g", "G", "negG", "Gl", "eG", "beG", "nbeta", "ekg", "glast", "egl"]}
        gt = {n: sm("gt_" + n, [128, 128]) for n in ["sq", "rn"]}
        gnames = ["Ds", "DTb", "DTi", "B0", "B1", "R0", "R1", "Tt", "QKd", "X0v", "X0k", "kg", "nwT", "vnew", "tmpB"]
        gth = []
        aoff = 14720
        for h_ in range(4):
            d_ = {}
            for nm in gnames:
                d_[nm] = barrier_tt(TT(arena_f[:, aoff:aoff + 128], "gt%d_%s" % (h_, nm)))
                aoff += 128
            gth.append(d_)

        def tile(xt, ntok, tile0, o, rd):
            n = ntok
            ps = nps(); lin_tm(ps[:n, :512], hT4, Win, 2048, 512, n, ps, o=o, rd=rd)
            act(gate_s[:n, :], ps[:n, :512], AF.Silu, [ps], [gate_s])
            ps = nps(); lin_tm(ps[:n, :8], hT4, Wab, 0, 8, n, ps, o=o, rd=rd)
            act(col["beta"][:n, :], ps[:n, 0:4], AF.Sigmoid, [ps], [col["beta"]])
            tt_op("dve", col["g"][:n, :], ps[:n, 4:8], dtb[:n, :], ALU.add, [ps, dtb], [col["g"]])
            act(col["g"][:n, :], col["g"][:n, :], AF.Exp, [col["g"]], [col["g"]])
            act(col["g"][:n, :], col["g"][:n, :], AF.Ln, [col["g"]], [col["g"]], bias=1.0)
            tt_op("dve", col["g"][:n, :], col["g"][:n, :], nea[:n, :], ALU.mult, [col["g"], nea], [col["g"]])
            act(col["lnb"][:n, :], col["beta"][:n, :], AF.Ln, [col["beta"]], [col["lnb"]])
            ps = nps()
            mm(ps[:n, 0:4], LT[:n, :n], col["g"][:n, :], True, True, [LT, col["g"]], [ps])
            copy(col["G"][:n, :], ps[:n, 0:4], [ps], [col["G"]], eng="dve")
            ps = nps()
            mm(ps[:, 0:4], ones[:n, :], col["g"][:n, :], True, True, [ones, col["g"]], [ps])
            copy(col["glast"][:, :], ps[:, 0:4], [ps], [col["glast"]], eng="dve")
            act(col["egl"][:, :], col["glast"][:, :], AF.Exp, [col["glast"]], [col["egl"]])
            copy(col["negG"][:n, :], col["G"][:n, :], [col["G"]], [col["negG"]], eng="dve", scale=-1.0)
            tt_op("dve", col["Gl"][:n, :], col["G"][:n, :], col["lnb"][:n, :], ALU.add, [col["G"], col["lnb"]], [col["Gl"]])
            act(col["eG"][:n, :], col["G"][:n, :], AF.Exp, [col["G"]], [col["eG"]])
            tt_op("dve", col["beG"][:n, :], col["eG"][:n, :], col["beta"][:n, :], ALU.mult, [col["eG"], col["beta"]], [col["beG"]])
            copy(col["nbeta"][:n, :], col["beta"][:n, :], [col["beta"]], [col["nbeta"]], eng="dve", scale=-1.0)
            tt_op("dve", col["ekg"][:n, :], col["glast"][:n, :], col["G"][:n, :], ALU.subtract, [col["glast"], col["G"]], [col["ekg"]])
            act(col["ekg"][:n, :], col["ekg"][:n, :], AF.Exp, [col["ekg"]], [col["ekg"]])
            Lx = 15 + n
            for g in range(4):
                cur = aext[g]
                cb = o
                sh = 1
                for step in range(g + 1):
                    dst = pa[step % 2]
                    eng = "dve"
                    tt_op(eng, dst[:, sh:Lx], cur[:, cb + sh:cb + Lx], cur[:, cb:cb + Lx - sh], ALU.add, [cur], [dst])
                    cur = dst
                    cb = 0
                    sh *= 2
                nxt = pa[(g + 1) % 2]
                if tile0:
                    tt_op("dve", nxt[:, 15:Lx], cur[:, 15:Lx], invc[:, g * 128:g * 128 + n], ALU.mult, [cur, invc], [nxt])
                else:
                    copy(nxt[:, 15:Lx], cur[:, 15:Lx], [cur], [nxt], eng="dve", scale=1.0 / (2 ** (g + 1)))
                tt_op("dve", zb[:, :n], nxt[:, 15:Lx], aext[g][:, o + 15:o + Lx], ALU.subtract, [nxt, aext[g]], [zb])
                ps = nps()
                mm(ps[:, :n], Wpool[g][:, :], zb[:, :n], True, True, [Wpool[g], zb], [ps])
                copy(mixT[:, g, :n], ps[:, :n], [ps], [mixT], eng="act", scale=psc[:, g, 0:1])
            for c in range(12):
                fw.op("act", lambda e, c=c: e.mul(out=qc[c][:, :n], in_=qext[c][:, o:o + n], mul=cw[:, c, 0:1]), [qext[c], cw], [qc[c]])
                for k in range(1, 4):
                    stt("dve", qc[c][:, :n], qext[c][:, o + k:o + k + n], cw[:, c, k:k + 1], qc[c][:, :n], ALU.mult, ALU.add, [qext[c], cw, qc[c]], [qc[c]])
                act(qc[c][:, :n], qc[c][:, :n], AF.Silu, [qc[c]], [qc[c]])
            for c in range(8):
                act(gt["sq"][:, :n], qc[c][:, :n], AF.Square, [qc[c]], [gt["sq"]])
                ps = nps()
                mm(ps[:, :n], ones[:, :], gt["sq"][:, :n], True, True, [ones, gt["sq"]], [ps])
                sc = 128.0 if c < 4 else 1.0
                act(gt["rn"][:, :n], ps[:, :n], AF.Sqrt, [ps], [gt["rn"]], bias=EPS * sc, scale=sc)
                fw.op("dve", lambda e: e.reciprocal(out=gt["rn"][:, :n], in_=gt["rn"][:, :n]), [gt["rn"]], [gt["rn"]])
                tt_op("dve", qc[c][:, :n], qc[c][:, :n], gt["rn"][:, :n], ALU.mult, [qc[c], gt["rn"]], [qc[c]])
            nlev = 7 if n == 128 else 3
            PSH = [dict() for _ in range(4)]

            def st_A(h):
                g_ = gth[h]
                qT, kT = qc[h], qc[4 + h]
                pKK = nps(); mm(pKK[:n, :n], kT[:, :n], kT[:, :n], True, True, [kT], [pKK])
                pQK = nps(); mm(pQK[:n, :n], kT[:, :n], qT[:, :n], True, True, [kT, qT], [pQK])
                p1 = nps()
                mm(p1[:n, :n], col["G"][:n, h:h + 1].to_broadcast([n, n]), ident[:n, :n], True, False, [col["G"], ident], [p1])
                mm(p1[:n, :n], ident[:n, :n], m_ge[:n, :n], False, True, [ident, m_ge], [p1])
                act(g_["Ds"][:n, :n], p1[:n, :n], AF.Exp, [p1, col["G"]], [g_["Ds"]], bias=col["G"][:n, h:h + 1], scale=-1.0)
                p2 = nps()
                mm(p2[:n, :n], col["Gl"][:n, h:h + 1].to_broadcast([n, n]), ident[:n, :n], True, False, [col["Gl"], ident], [p2])
                mm(p2[:n, :n], ident[:n, :n], nm_le[:n, :n], False, True, [ident, nm_le], [p2])
                act(g_["DTb"][:n, :n], p2[:n, :n], AF.Exp, [p2, col["negG"]], [g_["DTb"]], bias=col["negG"][:n, h:h + 1], scale=1.0)
                p3 = nps()
                mm(p3[:n, :n], col["G"][:n, h:h + 1].to_broadcast([n, n]), ident[:n, :n], True, False, [col["G"], ident], [p3])
                mm(p3[:n, :n], ident[:n, :n], nm_lt[:n, :n], False, True, [ident, nm_lt], [p3])
                act(g_["DTi"][:n, :n], p3[:n, :n], AF.Exp, [p3, col["negG"]], [g_["DTi"]], bias=col["negG"][:n, h:h + 1], scale=1.0)
                stt("dve", g_["B0"][:n, :n], pKK[:n, :n], col["nbeta"][:n, h:h + 1], g_["Ds"][:n, :n], ALU.mult, ALU.mult, [pKK, col["nbeta"], g_["Ds"]], [g_["B0"]])
                stt("dve", g_["R0"][:n, :n], pKK[:n, :n], -1.0, g_["DTb"][:n, :n], ALU.mult, ALU.mult, [pKK, g_["DTb"]], [g_["R0"]])
                tt_op("dve", g_["QKd"][:n, :n], pQK[:n, :n], g_["DTi"][:n, :n], ALU.mult, [pQK, g_["DTi"]], [g_["QKd"]])
                tt_op("dve", g_["Tt"][:n, :n], g_["R0"][:n, :n], ident[:n, :n], ALU.add, [g_["R0"], ident], [g_["Tt"]])

            def st_L(h, k):
                g_ = gth[h]
                Bm = [g_["B0"], g_["B1"]]; Rm = [g_["R0"], g_["R1"]]; Tt = g_["Tt"]
                a, b = (k - 1) % 2, k % 2
                pP = nps(); mm(pP[:n, :n], Rm[a][:n, :n], Bm[a][:n, :n], True, True, [Rm[a], Bm[a]], [pP])
                if k < nlev - 1:
                    pR = nps(); mm(pR[:n, :n], Bm[a][:n, :n], Rm[a][:n, :n], True, True, [Rm[a], Bm[a]], [pR])
                copy(Bm[b][:n, :n], pP[:n, :n], [pP], [Bm[b]], eng="act")
                if k < nlev - 1:
                    copy(Rm[b][:n, :n], pR[:n, :n], [pR], [Rm[b]], eng="dve")
                pT = nps(); mm(pT[:n, :n], Bm[b][:n, :n], Tt[:n, :n], True, True, [Bm[b], Tt], [pT])
                tt_op("dve", Tt[:n, :n], Tt[:n, :n], pT[:n, :n], ALU.add, [Tt, pT], [Tt])

            def st_V(h):
                g_ = gth[h]
                kT, vT = qc[4 + h], qc[8 + h]
                Tt = g_["Tt"]
                pV = nps(); mm(pV[:n, :128], vT[:, :n], ident[:, :], True, True, [vT, ident], [pV])
                copy(g_["X0v"][:n, :], pV[:n, :128], [pV, col["beta"]], [g_["X0v"]], eng="dve", scale=col["beta"][:n, h:h + 1])
                pK = nps(); mm(pK[:n, :128], kT[:, :n], ident[:, :], True, True, [kT, ident], [pK])
                copy(g_["X0k"][:n, :], pK[:n, :128], [pK, col["beG"]], [g_["X0k"]], eng="dve", scale=col["beG"][:n, h:h + 1])
                copy(g_["kg"][:n, :], pK[:n, :128], [pK, col["ekg"]], [g_["kg"]], eng="dve", scale=col["ekg"][:n, h:h + 1])
                pW = nps(); mm(pW[:, :n], g_["X0k"][:n, :], Tt[:n, :n], True, True, [g_["X0k"], Tt], [pW])
                copy(g_["nwT"][:, :n], pW[:, :n], [pW], [g_["nwT"]], eng="act", scale=-1.0)

            def st_N(h):
                g_ = gth[h]
                qT = qc[h]
                Tt = g_["Tt"]
                pN = nps()
                mm(pN[:n, :128], Tt[:n, :n], g_["X0v"][:n, :], True, False, [Tt, g_["X0v"]], [pN])
                mm(pN[:n, :128], g_["nwT"][:, :n], S[h][:, :], False, True, [g_["nwT"], S[h]], [pN])
                copy(g_["vnew"][:n, :], pN[:n, :128], [pN], [g_["vnew"]], eng="act")
                pA = nps(); mm(pA[:n, :128], qT[:, :n], S[h][:, :], True, True, [qT, S[h]], [pA])
                pB = nps(); mm(pB[:n, :128], g_["QKd"][:n, :n], g_["vnew"][:n, :], True, True, [g_["QKd"], g_["vnew"]], [pB])
                copy(g_["tmpB"][:n, :], pB[:n, :128], [pB], [g_["tmpB"]], eng="act")
                stt("dve", ofin[:n, h * 128:(h + 1) * 128], pA[:n, :128], col["eG"][:n, h:h + 1], g_["tmpB"][:n, :], ALU.mult, ALU.add, [pA, col["eG"], g_["tmpB"]], [ofin])
                pS = nps(); mm(pS[:, :128], g_["kg"][:n, :], g_["vnew"][:n, :], True, True, [g_["kg"], g_["vnew"]], [pS])
                stt("dve", S[h][:, :], S[h][:, :], col["egl"][:, h:h + 1], pS[:, :128], ALU.mult, ALU.add, [S[h], col["egl"], pS], [S[h]])

            for h in range(4):
                st_A(h)
            for k in range(1, nlev):
                for h in range(4):
                    st_L(h, k)
            for h in range(4):
                st_V(h)
            for h in range(4):
                st_N(h)
            for h in range(4):
                ss = rstd_of(ofin[:n, h * 128:(h + 1) * 128], n, 128, [ofin])
                stt("dve", ofin[:n, h * 128:(h + 1) * 128], ofin[:n, h * 128:(h + 1) * 128], ss[:n, 0:1], nw[:n, :], ALU.mult, ALU.mult, [ofin, ss, nw], [ofin])
                tt_op("dve", ofin[:n, h * 128:(h + 1) * 128], ofin[:n, h * 128:(h + 1) * 128], gate_s[:n, h * 128:(h + 1) * 128], ALU.mult, [ofin, gate_s], [ofin])
            ps = nps()
            for j in range(4):
                tr(ps[:, j * 128:j * 128 + n], ofin[:n, j * 128:(j + 1) * 128], n, [ofin], [ps])
            copy(mixT[:, 4:8, :n], ps[:, :].rearrange("p (j t) -> p j t", t=128)[:, :, :n], [ps], [mixT])
            for half in range(2):
                ps = nps()
                for c in range(8):
                    mm(ps[:n, :512], mixT[:, c, :n], Wout[c][:, half * 512:(half + 1) * 512], c == 0, c == 7, [mixT, Wout[c]], [ps])
                add_res(xt, n, half, ps)

        def stile(loaders, n, first, finish):
            ntl = len(loaders)
            Wd = ntl * n
            for i, ld in enumerate(loaders):
                xt = ld()
                norm_tok(xt, n, gbc)
                half_tt = hT_rot[(i * n) // 256]
                for b in range(0, 8, 4):
                    ps = nps()
                    for j in range(4):
                        tr(ps[:, j * 128:j * 128 + n], hn[:n, (b + j) * 128:(b + j + 1) * 128], n, [hn], [ps])
                    copy(hT4[:, b:b + 4, i * n:i * n + n], ps[:, 0:512].rearrange("p (j t) -> p j t", t=128)[:, :, :n], [ps], [half_tt])
            rd = [hT_rot[0]] + ([hT_rot[1]] if Wd > 256 else [])
            for g in range(4):
                ps = nps()
                for kc in range(8):
                    mm(ps[:, :Wd], Win[kc][:, g * 128:(g + 1) * 128], hT4[:, kc, :Wd], kc == 0, kc == 7, [Win[kc]] + rd, [ps])
                copy(aext[g][:, 15:15 + Wd], ps[:, :Wd], [ps], [aext[g]])
            for c in range(12):
                ps = nps()
                for kc in range(8):
                    mm(ps[:, :Wd], Win[kc][:, 512 + c * 128:512 + (c + 1) * 128], hT4[:, kc, :Wd], kc == 0, kc == 7, [Win[kc]] + rd, [ps])
                copy(qext[c][:, 3:3 + Wd], ps[:, :Wd], [ps], [qext[c]])
            for i, ld in enumerate(loaders):
                xt = ld()
                tile(xt, n, first and i == 0, i * n, rd)
                finish(i, xt)
            for g in range(4):
                copy(aext[g][:, 0:15], aext[g][:, Wd:Wd + 15], [aext[g]], [aext[g]], eng="pool")
            for c in range(12):
                copy(qext[c][:, 0:3], qext[c][:, Wd:Wd + 3], [qext[c]], [qext[c]], eng="pool")

        for g in range(4):
            fw.op("pool", lambda e, g=g: e.memset(aext[g][:, :], 0.0), writes=[aext[g]])
        for c in range(12):
            fw.op("pool", lambda e, c=c: e.memset(qext[c][:, :], 0.0), writes=[qext[c]])
        for h in range(4):
            fw.op("pool", lambda e, h=h: e.memset(S[h][:, :], 0.0), writes=[S[h]])
        for t0 in range(0, NT, 4):
            stile([(lambda t=t0 + i: load_x(t, True)) for i in range(4)], 128, t0 == 0, lambda i, xt, t0=t0: store_x(t0 + i, xt))
        rows_from_cols(o_pool_p[:, :], lambda c: aext[c][:, 0:15], aext, 15, 4)
        rows_from_cols(o_dconv_p[:, :], lambda c: qext[c][:, 0:3], qext, 3, 12)
        for h in range(4):
            fw.dma("pool", o_dn_p[h], S[h][:, :], reads=[S[h]], is_output=True)
        for s in range(NS):
            cols_from_rows(lambda c: aext[c][:, 0:15], aext, st_pool[s], 15, 4)
            cols_from_rows(lambda c: qext[c][:, 0:3], qext, st_dconv[s], 3, 12)
            for h in range(4):
                fw.dma("sp", S[h][:, :], st_dn[s, h], writes=[S[h]])
            stile([lambda s=s: load_xs(s, True)], 8, False, lambda i, xt, s=s: store_xs(s, xt))
            rows_from_cols(o_pool_s[s], lambda c: aext[c][:, 0:15], aext, 15, 4)
            rows_from_cols(o_dconv_s[s], lambda c: qext[c][:, 0:3], qext, 3, 12)
            for h in range(4):
                fw.dma("pool", o_dn_s[s, h], S[h][:, :], reads=[S[h]], is_output=True)

    def phase_B(l):
        off = 0
        sm_reset()
        Wq, off = wload("wq", w_mem["q"][l], off, 1024, 1024)
        Wk, off = wload("wk", w_mem["k"][l], off, 1024, 1024)
        Wv, off = wload("wv", w_mem["v"][l], off, 1024, 1024)
        Wo, off = wload("wo", w_mem["o"][l], off, 1024, 1024)
        KTp = sm("KTp", [128, 8, 256], BF16); Vaug = sm("Vaug", [128, 2, 4, 257], BF16)
        KTs = sm("KTs", [128, 8, 256], BF16); Vaugs = sm("Vaugs", [128, 2, 4, 257], BF16)
        fw.op("pool", lambda e: e.memset(Vaug[:, :, :, 256:257], 1.0), writes=[Vaug])
        fw.op("pool", lambda e: e.memset(Vaugs[:, :, :, 256:257], 1.0), writes=[Vaugs])
        qT = sm("qT", [128, 8, 512], BF16); pT = sm("pT", [128, 2, 128], BF16)
        otok = sm("otok", [128, 1024]); oT = sm("oT", [128, 8, 128], BF16)
        rd = sm("rd", [128, 1]); kvs = sm("kvs", [128, 512]); kraw = sm("kraw", [128, 1024])
        load_g(g_mem_kv[l:l + 1, :])
        for nb in range(2):
            xt = xt_rot[cnt["xt"] % 2]; cnt["xt"] += 1
            fw.dma("sp", xt[:, :], memp[nb * 128:(nb + 1) * 128, :], writes=[xt])
            mT = norm_T(xt, 128, gbc)
            for ec in range(8):
                ps = nps(); lin_fm(ps[:, :128], Wk, ec * 128, 128, mT, 128, ps)
                copy(KTp[:, ec, nb * 128:(nb + 1) * 128], ps[:, :128], [ps], [KTp])
            for half in range(2):
                ps = nps(); lin_tm(ps[:, :512], mT, Wk, half * 512, 512, 128, ps)
                copy(kvs[:, :], ps[:, :512], [ps], [kvs])
                fw.dma("pool", o_memk[l, nb * 128:(nb + 1) * 128, half * 512:(half + 1) * 512], kvs[:, :], reads=[kvs], is_output=True)
                ps = nps(); lin_tm(ps[:, :512], mT, Wv, half * 512, 512, 128, ps)
                copy(kvs[:, :], ps[:, :512], [ps], [kvs])
                copy(Vaug[:, nb, 2 * half:2 * half + 2, 0:256], ps[:, :512].rearrange("p (h e) -> p h e", e=256), [ps], [Vaug])
                fw.dma("pool", o_memv[l, nb * 128:(nb + 1) * 128, half * 512:(half + 1) * 512], kvs[:, :], reads=[kvs], is_output=True)
        load_g(g_mem_q[l:l + 1, :])

        def stile(loaders, n, KT, VA, finish):
            ntl = len(loaders)
            Wd = ntl * n
            for i, ld in enumerate(loaders):
                xt = ld()
                norm_tok(xt, n, gbc)
                half_tt = hT_rot[(i * n) // 256]
                for b in range(0, 8, 4):
                    ps = nps()
                    for j in range(4):
                        tr(ps[:, j * 128:j * 128 + n], hn[:n, (b + j) * 128:(b + j + 1) * 128], n, [hn], [ps])
                    copy(hT4[:, b:b + 4, i * n:i * n + n], ps[:, 0:512].rearrange("p (j t) -> p j t", t=128)[:, :, :n], [ps], [half_tt])
            rdh = [hT_rot[0]] + ([hT_rot[1]] if Wd > 256 else [])
            for ec in range(8):
                ps = nps()
                for kc in range(8):
                    mm(ps[:, :Wd], Wq[kc][:, ec * 128:(ec + 1) * 128], hT4[:, kc, :Wd], kc == 0, kc == 7, [Wq[kc]] + rdh, [ps])
                copy(qT[:, ec, :Wd], ps[:, :Wd], [ps], [qT])
            for i, ld in enumerate(loaders):
                xt = ld()
                tile(xt, n, KT, VA, i * n)
                finish(i, xt)

        def tile(xt, n, KT, VA, o):
            for h in range(4):
                ps = nps()
                for nb in range(2):
                    for ec in range(2):
                        mm(ps[:, nb * 128:nb * 128 + n], KT[:, 2 * h + ec, nb * 128:(nb + 1) * 128], qT[:, 2 * h + ec, o:o + n], ec == 0, ec == 1, [KT, qT], [ps])
                act(pT[:, :, :n], ps[:, 0:256].rearrange("p (b t) -> p b t", t=128)[:, :, :n], AF.Exp, [ps], [pT], scale=1.0 / 16.0)
                ps2 = nps()
                for nb in range(2):
                    mm(ps2[:n, 0:257], pT[:, nb, :n], VA[:, nb, h, :], nb == 0, nb == 1, [pT, VA], [ps2])
                fw.op("dve", lambda e, ps2=ps2: e.reciprocal(out=rd[:n, :], in_=ps2[:n, 256:257]), [ps2], [rd])
                copy(otok[:n, h * 256:(h + 1) * 256], ps2[:n, 0:256], [ps2, rd], [otok], scale=rd[:n, 0:1])
            to_fm(otok, n, oT)
            for half in range(2):
                ps = nps(); lin_tm(ps[:n, :512], oT, Wo, half * 512, 512, n, ps)
                add_res(xt, n, half, ps)

        for t0 in range(0, NT, 4):
            stile([(lambda t=t0 + i: load_x(t, False)) for i in range(4)], 128, KTp, Vaug, lambda i, xt, t0=t0: store_x(t0 + i, xt))
        for s in range(NS):
            for nb in range(2):
                fw.dma("sp", kraw[:, :], cmk[l, s, nb * 128:(nb + 1) * 128, :], writes=[kraw])
                for b in range(2):
                    ps = nps()
                    for j in range(4):
                        tr(ps[:, j * 128:(j + 1) * 128], kraw[:, (4 * b + j) * 128:(4 * b + j + 1) * 128], 128, [kraw], [ps])
                    copy(KTs[:, 4 * b:4 * b + 4, nb * 128:(nb + 1) * 128], ps[:, :].rearrange("p (j t) -> p j t", t=128), [ps], [KTs])
                fw.dma("pool", Vaugs[:, nb, :, 0:256], cmv[l, s, nb * 128:(nb + 1) * 128, :].rearrange("n (h e) -> n h e", e=256), writes=[Vaugs])
            stile([lambda s=s: load_xs(s)], 8, KTs, Vaugs, lambda i, xt, s=s: store_xs(s, xt))

    def phase_C(l, final):
        off = 0
        sm_reset()
        Wup, off = wload("wup", w_up[l], off, 1024, 5632)
        Wdn, off = wload("wdn", w_down[l], off, 2816, 1024)
        load_g(g_ffn[l:l + 1, :])
        cwf = sm("cwf", [128, 44, 3]); cols_from_rows(lambda c: cwf[:, c, :], cwf, ffn_cw[l], 3, 44)
        hist = sm("fhist", [128, 44, 2])
        actT = sm("actT", [128, 22, 512], BF16)
        extA = [sm("extA%d" % i, [128, 514]) for i in range(2)]
        extB = [sm("extB%d" % i, [128, 514]) for i in range(2)]
        tA = sm("ftA", [128, 512]); tB = sm("ftB", [128, 512])
        hTs = [hT_rot[0], hT_rot[1]]

        def stile(loaders, n, finish):
            ntl = len(loaders)
            Wd = ntl * n
            for i, ld in enumerate(loaders):
                xt = ld()
                norm_tok(xt, n, gbc)
                half_tt = hTs[(i * n) // 256]
                for b in range(0, 8, 4):
                    ps = nps()
                    for j in range(4):
                        tr(ps[:, j * 128:j * 128 + n], hn[:n, (b + j) * 128:(b + j + 1) * 128], n, [hn], [ps])
                    copy(hT4[:, b:b + 4, i * n:i * n + n], ps[:, 0:512].rearrange("p (j t) -> p j t", t=128)[:, :, :n], [ps], [half_tt])
            rd = [hTs[0]] + ([hTs[1]] if Wd > 256 else [])
            for j in range(22):
                ea, eb = extA[j % 2], extB[j % 2]
                for (c, ext, ev) in ((j, ea, "act"), (j + 22, eb, "act")):
                    ps = nps()
                    for kc in range(8):
                        mm(ps[:, :Wd], Wup[kc][:, c * 128:(c + 1) * 128], hT4[:, kc, :Wd], kc == 0, kc == 7, [Wup[kc]] + rd, [ps])
                    copy(ext[:, 0:2], hist[:, c, :], [hist], [ext], eng="pool")
                    copy(ext[:, 2:2 + Wd], ps[:, :Wd], [ps], [ext], eng=ev)
                    copy(hist[:, c, :], ext[:, Wd:Wd + 2], [ext], [hist], eng="pool")
                c2 = j + 22
                fw.op("act", lambda e, ea=ea, j=j: e.mul(out=tA[:, :Wd], in_=ea[:, 0:Wd], mul=cwf[:, j, 0:1]), [ea, cwf], [tA])
                fw.op("act", lambda e, eb=eb, c2=c2: e.mul(out=tB[:, :Wd], in_=eb[:, 0:Wd], mul=cwf[:, c2, 0:1]), [eb, cwf], [tB])
                for k in (1, 2):
                    stt("dve", tA[:, :Wd], ea[:, k:k + Wd], cwf[:, j, k:k + 1], tA[:, :Wd], ALU.mult, ALU.add, [ea, cwf, tA], [tA])
                    stt("dve", tB[:, :Wd], eb[:, k:k + Wd], cwf[:, c2, k:k + 1], tB[:, :Wd], ALU.mult, ALU.add, [eb, cwf, tB], [tB])
                act(tA[:, :Wd], tA[:, :Wd], AF.Silu, [tA], [tA])
                tt_op("dve", actT[:, j, :Wd], tA[:, :Wd], tB[:, :Wd], ALU.mult, [tA, tB], [actT])
            for i, ld in enumerate(loaders):
                xt = ld()
                for half in range(2):
                    ps = nps()
                    for j in range(22):
                        mm(ps[:n, :512], actT[:, j, i * n:i * n + n], Wdn[j][:, half * 512:(half + 1) * 512], j == 0, j == 21, [actT, Wdn[j]], [ps])
                    add_res(xt, n, half, ps)
                finish(i, xt)

        fw.op("pool", lambda e: e.memset(hist[:, :, :], 0.0), writes=[hist])
        for t0 in range(0, NT, 4):
            def fin(i, xt, t0=t0):
                t = t0 + i
                if final:
                    norm_tok(xt, 128, gfin)
                    fw.dma("pool", y_p[t * 128:(t + 1) * 128, :], hn[:, :], reads=[hn], writes=[xres[t]], is_output=True)
                else:
                    store_x(t, xt)
            stile([(lambda t=t0 + i: load_x(t, False)) for i in range(4)], 128, fin)
        rows_from_cols(o_ffn_p[l], lambda c: hist[:, c, :], hist, 2, 44)
        for s_ in range(NS):
            cols_from_rows(lambda c: hist[:, c, :], hist, st_ffn[l, s_], 2, 44)

            def fin_s(i, xt, s_=s_):
                if final:
                    norm_tok(xt, 8, gfin)
                    fw.dma("pool", y_s[s_ * 8:(s_ + 1) * 8, :], hn[:8, :], reads=[hn], writes=[xsres[s_]], is_output=True)
                else:
                    store_xs(s_, xt)
            stile([lambda s_=s_: load_xs(s_)], 8, fin_s)
            rows_from_cols(o_ffn_s[l, s_], lambda c: hist[:, c, :], hist, 2, 44)

    def phase_A1():
        off = 0
        sm_reset()
        Wqkv, off = wload("wqkv", w_qkv, off, 1024, 4608)
        Woc, off = wload("woc", w_out_c, off, 512, 1024)
        load_g(g_mix[1:2, :])
        qk = sm("qk_tok", [128, 512]); vt = sm("v_tok", [128, 512]); vbf = sm("v_bf", [128, 512], BF16)
        cs = sm("cs_t", [128, 8]); sn = sm("sn_t", [128, 8])
        r1 = sm("rp1", [128, 8, 8]); r2 = sm("rp2", [128, 8, 8]); r3 = sm("rp3", [128, 8, 8])
        fmT = sm("fmT", [128, 4, 128], BF16)
        mb = sm("mb", [128, 2, 128], BF16); fw.dma("pool", mb[:], c_mb.rearrange("a p n -> p a n"), writes=[mb])
        smk = sm("smk", [128, 24, 8], BF16); fw.dma("pool", smk[:], c_sm.rearrange("p (b t) -> p b t", t=8), writes=[smk])
        ident_bf = sm("ident_bf", [128, 128], BF16); copy(ident_bf[:], ident[:], [ident], [ident_bf], eng="dve")
        sQT = [[sm("sQT%d_%d" % (s, g), [128, 4, 8], BF16) for g in range(3)] for s in range(NS)]
        sKT = [[sm("sKT%d_%d" % (s, g), [128, 4, 8], BF16) for g in range(3)] for s in range(NS)]
        sV_d = nc.dram_tensor("sV_d", [NS, 3, 8, 512], BF16).ap()
        sV_tt = TT(None, "sV_tt")
        sVn = sm("sVn", [8, 512], BF16)

        def rope(n, pos0):
            fw.dma("sp", cs[:n, :], c_cos[pos0:pos0 + n, :], writes=[cs])
            fw.dma("sp", sn[:n, :], c_sin[pos0:pos0 + n, :], writes=[sn])
            X = qk[:n, :].rearrange("p (h e) -> p h e", e=64)
            x1 = X[:, :, 0:8]; x2 = X[:, :, 8:16]
            cb = cs[:n, :].unsqueeze(1).to_broadcast([n, 8, 8]); sb_ = sn[:n, :].unsqueeze(1).to_broadcast([n, 8, 8])
            tt_op("dve", r1[:n], x1, sb_, ALU.mult, [qk, sn], [r1])
            tt_op("dve", r2[:n], x2, sb_, ALU.mult, [qk, sn], [r2])
            tt_op("dve", r3[:n], x2, cb, ALU.mult, [qk, cs], [r3])
            tt_op("dve", x1, x1, cb, ALU.mult, [qk, cs], [qk])
            tt_op("dve", x1, x1, r2[:n], ALU.subtract, [qk, r2], [qk])
            tt_op("dve", x2, r3[:n], r1[:n], ALU.add, [r3, r1], [qk])

        def proj_tile(xt, n, pos0, t, s):
            hT = norm_T(xt, n, gbc)
            for g in range(3):
                W = WINS[g][0]
                for si in range(3):
                    ps = nps(); lin_tm(ps[:n, :512], hT, Wqkv, g * 1536 + si * 512, 512, n, ps)
                    if si < 2:
                        copy(qk[:n, :], ps[:n, :512], [ps], [qk])
                        rope(n, pos0)
                        to_fm(qk, n, fmT, nchunks=4)
                        if s is None:
                            dst = (QT_d if si == 0 else KT_d)[g]
                            fw.dma("pool", dst.rearrange("(c p) s -> p c s", p=128)[:, :, t * 128:(t + 1) * 128], fmT[:, :, :], reads=[fmT], writes=[qkv_tt[t]])
                        else:
                            copy((sQT if si == 0 else sKT)[s][g][:, :, :], fmT[:, :, :8], [fmT], [(sQT if si == 0 else sKT)[s][g]], eng="pool")
                        if si == 1:
                            if s is None:
                                lo = S - min(W, S)
                                if (t + 1) * 128 > lo:
                                    fw.dma("pool", o_win_p[g][t * 128 - lo:(t + 1) * 128 - lo, 0:512], qk[:, :], reads=[qk], is_output=True)
                            else:
                                fw.dma("pool", o_win_s[g][s, :, 0:512], qk[:8, :], reads=[qk], is_output=True)
                    else:
                        copy(vt[:n, :], ps[:n, :512], [ps], [vt])
                        copy(vbf[:n, :], ps[:n, :512], [ps], [vbf])
                        if s is None:
                            fw.dma("pool", V_d[g][t * 128:(t + 1) * 128, :], vbf[:, :], reads=[vbf], writes=[qkv_tt[t]])
                            lo = S - min(W, S)
                            if (t + 1) * 128 > lo:
                                fw.dma("pool", o_win_p[g][t * 128 - lo:(t + 1) * 128 - lo, 512:1024], vt[:, :], reads=[vt], is_output=True)
                        else:
                            fw.dma("sp", sV_d[s, g], vbf[:8, :], reads=[vbf], writes=[sV_tt])
                            fw.dma("pool", o_win_s[g][s, :, 512:1024], vt[:8, :], reads=[vt], is_output=True)

        for t in range(NT):
            xt = load_x(t, False)
            proj_tile(xt, 128, t * 128, t, None)
        for s in range(NS):
            proj_tile(load_xs(s), 8, S, None, s)

        print('A1 proj done at op', fw.total)
        HALF = 2048
        NH = S // HALF
        accN = barrier_tt(TT(arena_f[:, 0:4 * HALF].rearrange("p (c s) -> p c s", s=HALF), "accN"))
        accD = barrier_tt(TT(arena_f[:, 4 * HALF:8 * HALF].rearrange("p (c s) -> p c s", s=HALF), "accD"))
        QTs = barrier_tt(TT(arena.ap[:, 40960:40960 + 8192].rearrange("p (c s) -> p c s", s=2048), "QTs"))
        KTs_ = barrier_tt(TT(arena.ap[:, 49152:49152 + 16384].rearrange("p (c s) -> p c s", s=4096), "KTs"))
        Vst = [sm("Vst%d" % i, [128, 512], BF16) for i in range(2)]
        Vpad = [sm("Vpad%d" % i, [128, 2, 4, 2, 128], BF16) for i in range(2)]
        onespad = sm("onespad", [128, 2, 128], BF16)
        fw.op("pool", lambda e: e.memset(onespad[:], 0.0), writes=[onespad])
        fw.op("pool", lambda e: e.memset(onespad[:, 0, 0:64], 1.0), writes=[onespad])
        fw.op("pool", lambda e: e.memset(onespad[:, 1, 64:128], 1.0), writes=[onespad])
        for i in range(2):
            fw.op("pool", lambda e, i=i: e.memset(Vpad[i][:], 0.0), writes=[Vpad[i]])
        PTR = [sm("PT%d" % i, [128, 2, 128], BF16) for i in range(5)]
        ptc = [0]
        rec = sm("rec", [128, 4, 128]); oTb = sm("oTb", [128, 4, 128], BF16)
        vcnt = [0]
        for hf in range(NH):
            for g in range(3):
                W, d = WINS[g]
                SB = 128 * d
                for sbk in range(HALF // SB):
                    q_sb = hf * (HALF // SB) + sbk
                    base = q_sb * SB
                    lb = base - hf * HALF
                    fw.dma("sp", QTs[:, :, 0:SB], QT_d[g].rearrange("(c p) s -> p c s", p=128)[:, :, base:base + SB], reads=qkv_tt, writes=[QTs])
                    if q_sb > 0:
                        fw.dma("sp", KTs_[:, :, 0:2 * SB], KT_d[g].rearrange("(c p) s -> p c s", p=128)[:, :, base - SB:base + SB], reads=qkv_tt, writes=[KTs_])
                    else:
                        fw.dma("sp", KTs_[:, :, SB:2 * SB], KT_d[g].rearrange("(c p) s -> p c s", p=128)[:, :, base:base + SB], reads=qkv_tt, writes=[KTs_])
                    kbs = (0, 1) if q_sb > 0 else (1,)
                    for r in range(d):
                        vp = Vpad[vcnt[0] % 2]; vcnt[0] += 1
                        for kb in kbs:
                            p0 = base - SB + r if kb == 0 else base + r
                            vs = Vst[kb]
                            fw.dma("sp", vs[:, :], V_d[g][p0:p0 + 127 * d + 1:d, :], reads=qkv_tt, writes=[vs])
                            for hh in range(2):
                                copy(vp[:, kb, :, hh, hh * 64:(hh + 1) * 64], vs[:, :].rearrange("p (c h e) -> p c h e", h=2, e=64)[:, :, hh, :], [vs], [vp], eng="pool")
                        def a_scores(c):
                            PT = [PTR[(ptc[0] + i_) % 5] for i_ in range(2)]
                            ptc[0] += 2
                            for hh in range(2):
                                ps = nps()
                                pl = slice(hh * 64, (hh + 1) * 64)
                                for kb in kbs:
                                    ksl = slice(r, SB, d) if kb == 0 else slice(SB + r, 2 * SB, d)
                                    mm(ps[:, kb * 128:(kb + 1) * 128], KTs_[pl, c, ksl], QTs[pl, c, r:SB:d], True, False, [KTs_, QTs], [ps])
                                    mm(ps[:, kb * 128:(kb + 1) * 128], ident_bf[:, :], mb[:, kb, :], False, True, [ident_bf, mb], [ps])
                                k0 = kbs[0]
                                act(PT[hh][:, k0:2, :], ps[:, k0 * 128:256].rearrange("p (b t) -> p b t", t=128), AF.Exp, [ps], [PT[hh]], scale=0.125)
                            return PT

                        def a_pv(c, PT):
                            psN = nps(); psD = nps()
                            seq = [(hh, kb) for hh in range(2) for kb in kbs]
                            for i, (hh, kb) in enumerate(seq):
                                mm(psN[:, :128], vp[:, kb, c, hh, :], PT[hh][:, kb, :], i == 0, i == len(seq) - 1, [vp, PT[hh]], [psN])
                            for i, (hh, kb) in enumerate(seq):
                                mm(psD[:, :128], onespad[:, hh, :], PT[hh][:, kb, :], i == 0, i == len(seq) - 1, [onespad, PT[hh]], [psD])
                            dN = accN[:, c, lb + r:lb + SB:d]; dD = accD[:, c, lb + r:lb + SB:d]
                            if g == 0:
                                copy(dN, psN[:, :128], [psN], [accN], eng="act")
                                copy(dD, psD[:, :128], [psD], [accD], eng="dve")
                            else:
                                tt_op("dve", dN, dN, psN[:, :128], ALU.add, [accN, psN], [accN])
                                tt_op("dve", dD, dD, psD[:, :128], ALU.add, [accD, psD], [accD])

                        pts = {0: a_scores(0)}
                        for c in range(4):
                            if c + 1 < 4:
                                pts[c + 1] = a_scores(c + 1)
                            a_pv(c, pts[c])
            for tl in range(HALF // 128):
                t = hf * (HALF // 128) + tl
                fw.op("dve", lambda e, tl=tl: e.reciprocal(out=rec[:, :, :], in_=accD[:, :, tl * 128:(tl + 1) * 128]), [accD], [rec])
                tt_op("dve", oTb[:, :, :], accN[:, :, tl * 128:(tl + 1) * 128], rec[:, :, :], ALU.mult, [accN, rec], [oTb])
                xt = load_x(t, False)
                for half in range(2):
                    ps = nps()
                    for c in range(4):
                        mm(ps[:, :512], oTb[:, c, :], Woc[c][:, half * 512:(half + 1) * 512], c == 0, c == 3, [oTb, Woc[c]], [ps])
                    add_res(xt, 128, half, ps)
                store_x(t, xt)
        print('A1 prompt attn done at op', fw.total)
        craw = sm("craw", [128, 1024]); KTb = sm("KTb", [128, 4, 128], BF16)
        sVp = [sm("sVp%d" % i, [128, 4, 2, 128], BF16) for i in range(2)]
        for i in range(2):
            fw.op("pool", lambda e, i=i: e.memset(sVp[i][:], 0.0), writes=[sVp[i]])
        PTs = sm("PTs", [128, 8, 8], BF16)
        aN = sm("aNs", [128, 32]); aD = sm("aDs", [128, 32]); oTs = sm("oTs", [128, 32], BF16)
        bases = [0, 1, 5]
        bcnt = [0]
        for s in range(NS):
            xs_t = load_xs(s)
            first = True
            for g in range(3):
                W, d = WINS[g]
                nblk = W // 128
                for kb in range(nblk + 1):
                    newblk = (kb == nblk)
                    nk = 8 if newblk else 128
                    vp = sVp[bcnt[0] % 2]; bcnt[0] += 1
                    if not newblk:
                        fw.dma("sp", craw[:, :], cwin[g][s, kb * 128:(kb + 1) * 128, :], writes=[craw])
                        ps = nps()
                        for j in range(4):
                            tr(ps[:, j * 128:(j + 1) * 128], craw[:, j * 128:(j + 1) * 128], 128, [craw], [ps])
                        copy(KTb[:, :, :], ps[:, :].rearrange("p (j t) -> p j t", t=128), [ps], [KTb])
                        for hh in range(2):
                            copy(vp[:, :, hh, hh * 64:(hh + 1) * 64], craw[:, 512:1024].rearrange("p (c h e) -> p c h e", h=2, e=64)[:, :, hh, :], [craw], [vp], eng="pool")
                        ktt, kap = KTb, KTb
                        bi = bases[g] + kb
                    else:
                        fw.dma("sp", sVn[:8, :], sV_d[s, g], reads=[sV_tt], writes=[sVn])
                        for hh in range(2):
                            copy(vp[:8, :, hh, hh * 64:(hh + 1) * 64], sVn[:8, :].rearrange("p (c h e) -> p c h e", h=2, e=64)[:, :, hh, :], [sVn], [vp], eng="pool")
                        ktt, kap = sKT[s][g], sKT[s][g]
                        bi = 21 + g
                        k2 = sm("k2n", [64, 2, 4, 8], BF16); q2 = sm("q2n", [64, 2, 4, 8], BF16)
                        for hh in range(2):
                            copy(k2[:, hh, :, :], sKT[s][g][hh * 64:(hh + 1) * 64, :, :], [sKT[s][g]], [k2], eng="pool")
                            copy(q2[:, hh, :, :], sQT[s][g][hh * 64:(hh + 1) * 64, :, :], [sQT[s][g]], [q2], eng="pool")
                    ps = nps()
                    for c in range(4):
                        for hh in range(2):
                            pl = slice(hh * 64, (hh + 1) * 64)
                            hd = 2 * c + hh
                            if newblk:
                                mm(ps[:nk, hd * 8:(hd + 1) * 8], k2[:, hh, c, :nk], q2[:, hh, c, :8], True, True, [k2, q2], [ps])
                            else:
                                mm(ps[:nk, hd * 8:(hd + 1) * 8], kap[pl, c, :nk], sQT[s][g][pl, c, :8], True, True, [ktt, sQT[s][g]], [ps])
                    act(PTs[:nk, :, :], ps[:nk, 0:64].rearrange("p (h t) -> p h t", t=8), AF.Exp, [ps], [PTs], scale=0.125)
                    tt_op("dve", PTs[:nk, :, :], PTs[:nk, :, :], smk[:nk, bi, :].unsqueeze(1).to_broadcast([nk, 8, 8]), ALU.mult, [PTs, smk], [PTs])
                    psN = nps(); psD = nps()
                    for c in range(4):
                        for hh in range(2):
                            hd = 2 * c + hh
                            mm(psN[:, c * 8:(c + 1) * 8], vp[:nk, c, hh, :], PTs[:nk, hd, :], hh == 0, hh == 1, [vp, PTs], [psN])
                    for c in range(4):
                        for hh in range(2):
                            hd = 2 * c + hh
                            mm(psD[:, c * 8:(c + 1) * 8], onespad[:nk, hh, :], PTs[:nk, hd, :], hh == 0, hh == 1, [onespad, PTs], [psD])
                    if first:
                        copy(aN[:, :], psN[:, :32], [psN], [aN], eng="act")
                        copy(aD[:, :], psD[:, :32], [psD], [aD], eng="dve")
                        first = False
                    else:
                        tt_op("dve", aN[:, :], aN[:, :], psN[:, :32], ALU.add, [aN, psN], [aN])
                        tt_op("dve", aD[:, :], aD[:, :], psD[:, :32], ALU.add, [aD, psD], [aD])
            fw.op("dve", lambda e: e.reciprocal(out=aD[:, :], in_=aD[:, :]), [aD], [aD])
            tt_op("dve", oTs[:, :], aN[:, :], aD[:, :], ALU.mult, [aN, aD], [oTs])
            for half in range(2):
                ps = nps()
                for c in range(4):
                    mm(ps[:8, :512], oTs[:, c * 8:(c + 1) * 8], Woc[c][:, half * 512:(half + 1) * 512], c == 0, c == 3, [oTs, Woc[c]], [ps])
                add_res(xs_t, 8, half, ps)
            store_xs(s, xs_t)

    if 'A0' in phases: phase_A0()
    if 'B0' in phases: phase_B(0)
    if 'C0' in phases: phase_C(0, False)
    if 'A1' in phases: phase_A1()
    if 'B1' in phases: phase_B(1)
    if 'C1' in phases: phase_C(1, True)
    fw.finish()
    return nc, fw


def _consts(S):
    c = {}
    r = np.arange(128)[:, None]; cc = np.arange(128)[None, :]
    c["c_ident"] = np.eye(128, dtype=np.float32)
    c["c_mge"] = (BIG * (cc >= r)).astype(np.float32)
    c["c_nmle"] = (-BIG * (cc <= r)).astype(np.float32)
    c["c_nmlt"] = (-BIG * (cc < r)).astype(np.float32)
    c["c_LT"] = (r <= cc).astype(np.float32)
    invc = np.zeros((2, 128, 512), np.float32)
    for g, w in enumerate((2, 4, 8, 16)):
        pos = np.arange(128)
        invc[0, :, g * 128:(g + 1) * 128] = (1.0 / np.minimum(pos + 1, w))[None, :]
        invc[1, :, g * 128:(g + 1) * 128] = 1.0 / w
    c["c_invc"] = invc
    half = 8
    inv_freq = (np.float32(500000.0) ** (-np.arange(half, dtype=np.float32) / np.float32(half))).astype(np.float32)
    pos = np.concatenate([np.arange(S), 8192 + np.arange(8)]).astype(np.float32)
    ang = (pos[:, None] * inv_freq[None, :]).astype(np.float32)
    c["c_cos"] = np.cos(ang.astype(np.float64)).astype(np.float32)
    c["c_sin"] = np.sin(ang.astype(np.float64)).astype(np.float32)
    NEG = -240000.0
    j = np.arange(128)[:, None]; i = np.arange(128)[None, :]
    mb = np.zeros((2, 128, 128), np.float32)
    mb[0] = np.where(j >= i, 0.0, NEG)
    mb[1] = np.where(j <= i, 0.0, NEG)
    c["c_mb"] = mb
    smk = np.zeros((128, 24, 8), np.float32)
    bases = [0, 1, 5]
    for g, (W, d) in enumerate(WINS):
        for kb in range(W // 128):
            e = kb * 128 + np.arange(128)[:, None]
            t = np.arange(8)[None, :]
            diff = W + t - e
            smk[:, bases[g] + kb, :] = ((diff % d == 0) & (diff // d >= 1) & (diff // d <= 128)).astype(np.float32)
        p = np.arange(128)[:, None]; t = np.arange(8)[None, :]
        smk[:, 21 + g, :] = ((p <= t) & ((t - p) % d == 0) & (p < 8)).astype(np.float32)
    c["c_sm"] = smk.reshape(128, 24 * 8)
    return c


_CACHE = {}


def _core_inputs(inp, c, S, NS, consts):
    b = c % 4
    ss = slice(NS * c, NS * (c + 1))
    f = lambda a: np.ascontiguousarray(a, dtype=np.float32)
    m = {
        "xp": f(inp["x_prompt"][b]), "xs": f(inp["x_sample"][ss].reshape(NS * 8, 1024)), "memp": f(inp["mem_prompt"][b]),
        "st_pool": f(inp["state_pool"][0, ss]), "st_dconv": f(inp["state_dn_conv"][0, ss]), "st_dn": f(inp["state_dn"][0, ss]),
        "cw128": f(inp["cache_win_w128"][0, ss].reshape(NS, 128, 1024)),
        "cw512": f(inp["cache_win_w512"][0, ss].reshape(NS, 512, 1024)),
        "cw2048": f(inp["cache_win_w2048"][0, ss].reshape(NS, 2048, 1024)),
        "cmk": f(inp["cache_mem_k"][:, ss].reshape(2, NS, 256, 1024)), "cmv": f(inp["cache_mem_v"][:, ss].reshape(2, NS, 256, 1024)),
        "st_ffn": f(inp["state_ffn_conv"][:, ss]),
        "g_mix": f(inp["g_mix"]), "w_in": f(inp["w_in_ab"][0]), "w_pool": f(inp["w_pool"][0].reshape(512, 128)),
        "pool_scale": f(inp["pool_scale"]), "dn_conv_w": f(inp["dn_conv_w"][0]), "dn_a_log": f(inp["dn_a_log"]),
        "dn_dt_bias": f(inp["dn_dt_bias"]), "dn_norm_w": f(inp["dn_norm_w"]), "w_out_ab": f(inp["w_out_ab"][0]),
        "w_qkv": f(inp["w_qkv_c"][0]), "w_out_c": f(inp["w_out_c"][0]), "g_mem_q": f(inp["g_mem_q"]), "g_mem_kv": f(inp["g_mem_kv"]),
        "w_mem_q": f(inp["w_mem_q"]), "w_mem_k": f(inp["w_mem_k"]), "w_mem_v": f(inp["w_mem_v"]), "w_mem_o": f(inp["w_mem_o"]),
        "g_ffn": f(inp["g_ffn"]), "w_up": f(inp["w_up"]), "ffn_cw": f(inp["ffn_conv_w"]), "w_down": f(inp["w_down"]),
        "g_final": f(inp["g_final"].reshape(1, 1024)),
    }
    m.update(consts)
    return m


def kernel(**inp):
    S = inp["x_prompt"].shape[1]
    NS = 4
    if S not in _CACHE:
        _CACHE[S] = build(S, NS)[0]
    nc = _CACHE[S]
    consts = _consts(S)
    in_maps = [_core_inputs(inp, c, S, NS, consts) for c in range(8)]
    res = run_bass_kernel_spmd(nc, in_maps, core_ids=list(range(8)))
    R = res.results
    f = np.float32
    B = 4
    cat_p = lambda k: np.stack([R[b][k] for b in range(B)])
    cat_s = lambda k: np.concatenate([R[c][k] for c in range(8)], axis=0)
    y_p = cat_p("y_p")
    y_s = cat_s("y_s").reshape(32, 8, 1024)
    outs = [y_p, y_s,
            cat_p("o_pool_p")[None], cat_s("o_pool_s")[None],
            cat_p("o_dconv_p")[None], cat_s("o_dconv_s")[None],
            cat_p("o_dn_p")[None], cat_s("o_dn_s")[None]]
    for w, _ in WINS:
        wp = min(w, S)
        outs.append(cat_p("o_w%d_p" % w).reshape(1, B, wp, 2, 8, 64))
        outs.append(cat_s("o_w%d_s" % w).reshape(1, 32, 8, 2, 8, 64))
    outs.append(np.stack([R[b]["o_memk"] for b in range(B)], axis=1).reshape(2, B, 256, 4, 256))
    outs.append(np.stack([R[b]["o_memv"] for b in range(B)], axis=1).reshape(2, B, 256, 4, 256))
    outs.append(np.stack([R[b]["o_ffn_p"] for b in range(B)], axis=1))
    outs.append(np.concatenate([R[c]["o_ffn_s"] for c in range(8)], axis=1))
    return tuple(np.ascontiguousarray(o, dtype=f) for o in outs)
```
